# Optimizing a Trainium2 kernel written in Bass

```python
import math
import jax, jax.numpy as jnp
from jax import lax
import numpy as np

D_MODEL = 1024
BATCH = 32
SEQ = 2048
DEPTH = 1

PLE_DIM = 256
RMS_EPS = 1e-6
ROPE_THETA = 10000.0
RET_HEADS = 8
RET_DK = D_MODEL // RET_HEADS
RET_DV = 2 * RET_DK
RET_CHUNK = 128
RET_QK_W = RET_HEADS * RET_DK
RET_V_W = RET_HEADS * RET_DV
NSA_HEADS = 16
NSA_GROUPS = 2
NSA_HPG = NSA_HEADS // NSA_GROUPS
NSA_DH = 64
NSA_Q_W = NSA_HEADS * NSA_DH
NSA_KV_W = NSA_GROUPS * NSA_DH
NSA_GATE_W = 3 * NSA_HEADS
CMP_LEN = 32
CMP_STRIDE = 16
CMP_HIDDEN = 256
SEL_BLOCK = 64
SEL_TOPN = 8
WINDOW = 512
NSA_QBLOCK = 64
FORCE_SCORE = 1e6
NEG_INF = -1e30
MLP_HIDDEN = 4 * D_MODEL
IN_SPLITS = [RET_QK_W, RET_QK_W, RET_V_W, RET_V_W, NSA_Q_W] + [NSA_KV_W] * 6 + [NSA_GATE_W]
D_IN = sum(IN_SPLITS)

kernel_name = "hybrid_retention_nsa_gated_block"


def rms_norm(x, g):
    x32 = x.astype(jnp.float32)
    y = x32 * lax.rsqrt(jnp.mean(x32 * x32, axis=-1, keepdims=True) + RMS_EPS)
    return (y * g.astype(jnp.float32)).astype(x.dtype)


def rope_tables(positions, dim):
    inv = ROPE_THETA ** (-jnp.arange(0, dim, 2, dtype=jnp.float32) / dim)
    ang = positions.astype(jnp.float32)[..., None] * inv
    return jnp.cos(ang)[:, :, None, :], jnp.sin(ang)[:, :, None, :]


def apply_rope(x, cos, sin):
    x32 = x.astype(jnp.float32)
    x1, x2 = jnp.split(x32, 2, axis=-1)
    return jnp.concatenate([x1 * cos - x2 * sin, x2 * cos + x1 * sin], axis=-1).astype(x.dtype)


def masked_softmax(s, mask, axis):
    s32 = jnp.where(mask, s.astype(jnp.float32), NEG_INF)
    return jax.nn.softmax(s32, axis=axis) * mask


def retention(q, k, v, g, gn_g):
    B, S = q.shape[:2]
    C = RET_CHUNK
    N = S // C
    dt = q.dtype
    log_g = jnp.log(1.0 - 2.0 ** (-5.0 - jnp.arange(RET_HEADS, dtype=jnp.float32)))
    idx = jnp.arange(C, dtype=jnp.float32)
    diff = idx[:, None] - idx[None, :]
    decay = jnp.where(diff >= 0, jnp.exp(jnp.maximum(diff, 0.0)[None] * log_g[:, None, None]), 0.0).astype(dt)
    xi = jnp.exp((idx + 1.0)[None] * log_g[:, None]).astype(dt)
    zeta = jnp.exp((C - 1.0 - idx)[None] * log_g[:, None]).astype(dt)
    g_chunk = jnp.exp(C * log_g).astype(dt)
    k = k * (RET_DK ** -0.5)

    def chunks(t, d):
        return t.reshape(B, N, C, RET_HEADS, d).transpose(1, 0, 3, 2, 4)

    qc = chunks(q, RET_DK)
    kc = chunks(k, RET_DK)
    vc = chunks(v.reshape(B, S, RET_HEADS, RET_DV), RET_DV)

    def step(R, inp):
        qi, ki, vi = inp
        inner = jnp.einsum('bhnd,bhmd->bhnm', qi, ki) * decay[None]
        o = (jnp.einsum('bhnm,bhme->bhne', inner, vi)
             + jnp.einsum('bhnd,bhde->bhne', qi, R) * xi[None, :, :, None])
        R = g_chunk[None, :, None, None] * R + jnp.einsum('bhmd,bhme->bhde', ki * zeta[None, :, :, None], vi)
        return R, o

    R0 = jnp.zeros((B, RET_HEADS, RET_DK, RET_DV), dt)
    _, o = lax.scan(step, R0, (qc, kc, vc))
    o32 = o.transpose(1, 0, 3, 2, 4).reshape(B, S, RET_HEADS, RET_DV).astype(jnp.float32)
    mu = jnp.mean(o32, axis=-1, keepdims=True)
    var = jnp.mean(jnp.square(o32 - mu), axis=-1, keepdims=True)
    o32 = ((o32 - mu) * lax.rsqrt(var + RMS_EPS)).reshape(B, S, RET_V_W) * gn_g.astype(jnp.float32)
    return (o32 * jax.nn.silu(g.astype(jnp.float32))).astype(dt)


def compress(k, pe, w1, w2):
    S = k.shape[2]
    n_cmp = (S - CMP_LEN) // CMP_STRIDE + 1
    idx = jnp.arange(n_cmp)[:, None] * CMP_STRIDE + jnp.arange(CMP_LEN)[None, :]
    blk = k[:, :, idx] + pe
    blk = blk.reshape(blk.shape[0], blk.shape[1], n_cmp, CMP_LEN * NSA_DH)
    return jax.nn.gelu(blk @ w1) @ w2


def nsa(q, k_cmp, v_cmp, k_sel, v_sel, k_win, v_win, gates,
        cmp_pe_k, cmp_k_w1, cmp_k_w2, cmp_pe_v, cmp_v_w1, cmp_v_w2):
    B, S = q.shape[:2]
    dt = q.dtype
    G, Hg, dh, QB = NSA_GROUPS, NSA_HPG, NSA_DH, NSA_QBLOCK
    scale = dh ** -0.5
    qg = q.reshape(B, S, G, Hg, dh).transpose(0, 2, 3, 1, 4)
    tr = lambda t: t.transpose(0, 2, 1, 3)
    Kc = compress(tr(k_cmp), cmp_pe_k, cmp_k_w1, cmp_k_w2)
    Vc = compress(tr(v_cmp), cmp_pe_v, cmp_v_w1, cmp_v_w2)
    n_cmp = Kc.shape[2]
    c_start = jnp.arange(n_cmp) * CMP_STRIDE
    c_end = c_start + CMP_LEN - 1
    n_blk = S // SEL_BLOCK
    b_start = jnp.arange(n_blk) * SEL_BLOCK
    overlap = ((c_start[:, None] < b_start[None, :] + SEL_BLOCK)
               & (c_end[:, None] >= b_start[None, :])).astype(jnp.float32)
    n_sel = min(SEL_TOPN, n_blk)
    ks_blocks = tr(k_sel).reshape(B, G, n_blk, SEL_BLOCK, dh)
    vs_blocks = tr(v_sel).reshape(B, G, n_blk, SEL_BLOCK, dh)
    pad = ((0, 0), (0, 0), (WINDOW, 0), (0, 0))
    kw_pad = jnp.pad(tr(k_win), pad)
    vw_pad = jnp.pad(tr(v_win), pad)
    n_qb = S // QB
    q_blocks = qg.reshape(B, G, Hg, n_qb, QB, dh).transpose(3, 0, 1, 2, 4, 5)
    g_blocks = jax.nn.sigmoid(gates).reshape(B, n_qb, QB, 3, G, Hg).transpose(1, 3, 0, 4, 5, 2)
    bi = jnp.arange(B)[:, None, None, None]
    gi = jnp.arange(G)[None, :, None, None]
    blk_ids = jnp.arange(n_blk)

    def block_fn(inp):
        qb, gb, qi = inp
        t = qi * QB + jnp.arange(QB)
        s_c = jnp.einsum('bghqd,bgcd->bghqc', qb, Kc) * scale
        p_c = masked_softmax(s_c, c_end[None, :] <= t[:, None], -1)
        o_c = jnp.einsum('bghqc,bgcd->bghqd', p_c.astype(dt), Vc)
        imp = jnp.einsum('bgqc,cn->bgqn', p_c.sum(axis=2), overlap)
        cur = t // SEL_BLOCK
        forced = ((blk_ids[None] == 0) | (blk_ids[None] == cur[:, None])
                  | (blk_ids[None] == cur[:, None] - 1))
        valid = blk_ids[None] <= cur[:, None]
        score = jnp.where(forced, FORCE_SCORE, jnp.where(valid, imp, -1.0))
        _, sel = lax.top_k(score, n_sel)
        k_g = ks_blocks[bi, gi, sel]
        v_g = vs_blocks[bi, gi, sel]
        kpos = sel[..., None] * SEL_BLOCK + jnp.arange(SEL_BLOCK)
        s_mask = (kpos <= t[None, None, :, None, None])[:, :, None]
        s_s = jnp.einsum('bghqd,bgqnkd->bghqnk', qb, k_g) * scale
        p_s = masked_softmax(s_s, s_mask, (-2, -1))
        o_s = jnp.einsum('bghqnk,bgqnkd->bghqd', p_s.astype(dt), v_g)
        k_w = lax.dynamic_slice_in_dim(kw_pad, qi * QB, WINDOW + QB, axis=2)
        v_w = lax.dynamic_slice_in_dim(vw_pad, qi * QB, WINDOW + QB, axis=2)
        wpos = qi * QB - WINDOW + jnp.arange(WINDOW + QB)
        w_mask = ((wpos[None] >= 0) & (wpos[None] <= t[:, None])
                  & (wpos[None] > t[:, None] - WINDOW))
        s_w = jnp.einsum('bghqd,bgkd->bghqk', qb, k_w) * scale
        p_w = masked_softmax(s_w, w_mask, -1)
        o_w = jnp.einsum('bghqk,bgkd->bghqd', p_w.astype(dt), v_w)
        return gb[0][..., None] * o_c + gb[1][..., None] * o_s + gb[2][..., None] * o_w

    out = lax.map(block_fn, (q_blocks, g_blocks, jnp.arange(n_qb)))
    return out.transpose(1, 0, 4, 2, 3, 5).reshape(B, S, NSA_Q_W)


def setup_inputs(seed: int = 0) -> dict:
    key = jax.random.key(seed)
    ks = jax.random.split(key, 24)
    nrm = lambda k, shape, fan_in: jax.random.normal(k, shape, jnp.float32) * (fan_in ** -0.5)
    gain = lambda k, shape: 1.0 + 0.02 * jax.random.normal(k, shape, jnp.float32)
    L = DEPTH
    offs = jax.random.randint(ks[2], (BATCH, 1), 0, 1024, dtype=jnp.int32)
    return {
        "x": jax.random.normal(ks[0], (BATCH, SEQ, D_MODEL), jnp.float32),
        "p": jax.random.normal(ks[1], (DEPTH, BATCH, SEQ, PLE_DIM), jnp.float32),
        "positions": offs + jnp.arange(SEQ, dtype=jnp.int32)[None, :],
        "norm_mix_g": gain(ks[3], (L, D_MODEL)),
        "w_in": nrm(ks[4], (L, D_MODEL, D_IN), D_MODEL),
        "ret_gn_g": gain(ks[5], (L, RET_V_W)),
        "w_ret_o": nrm(ks[6], (L, RET_V_W, D_MODEL), RET_V_W),
        "cmp_pe_k": 0.1 * jax.random.normal(ks[7], (L, CMP_LEN, NSA_DH), jnp.float32),
        "cmp_k_w1": nrm(ks[8], (L, CMP_LEN * NSA_DH, CMP_HIDDEN), CMP_LEN * NSA_DH),
        "cmp_k_w2": nrm(ks[9], (L, CMP_HIDDEN, NSA_DH), CMP_HIDDEN),
        "cmp_pe_v": 0.1 * jax.random.normal(ks[10], (L, CMP_LEN, NSA_DH), jnp.float32),
        "cmp_v_w1": nrm(ks[11], (L, CMP_LEN * NSA_DH, CMP_HIDDEN), CMP_LEN * NSA_DH),
        "cmp_v_w2": nrm(ks[12], (L, CMP_HIDDEN, NSA_DH), CMP_HIDDEN),
        "w_nsa_o": nrm(ks[13], (L, NSA_Q_W, D_MODEL), NSA_Q_W),
        "w_merge_gate": nrm(ks[14], (L, D_MODEL, 2 * D_MODEL), D_MODEL),
        "w_out": nrm(ks[15], (L, D_MODEL, D_MODEL), D_MODEL),
        "norm_mlp_g": gain(ks[16], (L, D_MODEL)),
        "w_mlp_up": nrm(ks[17], (L, D_MODEL, MLP_HIDDEN), D_MODEL),
        "w_mlp_down": nrm(ks[18], (L, MLP_HIDDEN, D_MODEL), MLP_HIDDEN),
        "norm_ple_g": gain(ks[19], (L, D_MODEL)),
        "w_ple_gate": nrm(ks[20], (L, D_MODEL, D_MODEL), D_MODEL),
        "w_ple_proj": nrm(ks[21], (L, PLE_DIM, D_MODEL), PLE_DIM),
        "norm_final_g": gain(ks[22], (D_MODEL,)),
    }


def reference(x, p, positions, norm_mix_g, w_in, ret_gn_g, w_ret_o, cmp_pe_k, cmp_k_w1, cmp_k_w2,
              cmp_pe_v, cmp_v_w1, cmp_v_w2, w_nsa_o, w_merge_gate, w_out, norm_mlp_g, w_mlp_up,
              w_mlp_down, norm_ple_g, w_ple_gate, w_ple_proj, norm_final_g):
    B, S, _ = x.shape
    split_points = np.cumsum(IN_SPLITS)[:-1].tolist()
    cos_r, sin_r = rope_tables(positions, RET_DK)
    cos_n, sin_n = rope_tables(positions, NSA_DH)
    for i in range(DEPTH):
        h = rms_norm(x, norm_mix_g[i])
        proj = h @ w_in[i]
        (rq, rk, rv, rg, nq, kc, vc, ksl, vsl, kw, vw, ngate) = jnp.split(proj, split_points, axis=-1)
        rq = apply_rope(rq.reshape(B, S, RET_HEADS, RET_DK), cos_r, sin_r)
        rk = apply_rope(rk.reshape(B, S, RET_HEADS, RET_DK), cos_r, sin_r)
        o_ret = retention(rq, rk, rv, rg, ret_gn_g[i]) @ w_ret_o[i]
        kvr = lambda t: t.reshape(B, S, NSA_GROUPS, NSA_DH)
        nq = apply_rope(nq.reshape(B, S, NSA_HEADS, NSA_DH), cos_n, sin_n)
        kc = apply_rope(kvr(kc), cos_n, sin_n)
        ksl = apply_rope(kvr(ksl), cos_n, sin_n)
        kw = apply_rope(kvr(kw), cos_n, sin_n)
        o_nsa = nsa(nq, kc, kvr(vc), ksl, kvr(vsl), kw, kvr(vw), ngate,
                    cmp_pe_k[i], cmp_k_w1[i], cmp_k_w2[i], cmp_pe_v[i], cmp_v_w1[i], cmp_v_w2[i]) @ w_nsa_o[i]
        g_ret, g_nsa = jnp.split(jax.nn.sigmoid(h @ w_merge_gate[i]), 2, axis=-1)
        x = x + (g_ret * o_ret + g_nsa * o_nsa) @ w_out[i]
        h2 = rms_norm(x, norm_mlp_g[i])
        x = x + jnp.square(jax.nn.relu(h2 @ w_mlp_up[i])) @ w_mlp_down[i]
        x = x + (p[i] @ w_ple_proj[i]) * jax.nn.sigmoid(rms_norm(x, norm_ple_g[i]) @ w_ple_gate[i])
    return rms_norm(x, norm_final_g)
```

```python
import math
import numpy as np
from contextlib import ExitStack
import concourse.bass as bass
import concourse.mybir as mybir
from concourse.bass_utils import run_bass_kernel_spmd

F32 = mybir.dt.float32
BF16 = mybir.dt.bfloat16
I32 = mybir.dt.int32
AF = mybir.ActivationFunctionType
ALU = mybir.AluOpType

NCORES = 8
SEQ = 2048
DM = 1024
TB = 512
NT = TB // 128
EPS = 1e-6
TWO_PI = 2.0 * math.pi
C1 = 6.28125
C2 = TWO_PI - C1
PIECE = 4096
BIGM = 30000.0


class _Op:
    __slots__ = ("eng", "fn", "deps", "dma_sem", "seg", "token", "needs_inc", "is_dma", "ninc")

    def __init__(self, eng, fn, deps, dma_sem, seg, ninc):
        self.eng = eng
        self.fn = fn
        self.deps = deps
        self.dma_sem = dma_sem
        self.seg = seg
        self.token = None
        self.needs_inc = False
        self.is_dma = dma_sem is not None
        self.ninc = ninc


class _Rec:
    def __init__(self):
        self.call = None

    def __getattr__(self, name):
        def f(*a, **k):
            self.call = (name, a, k)
            return self
        return f


class Prog:
    def __init__(self, nc, stack):
        self.nc = nc
        self.stack = stack
        self.ops = []
        self.last_write = {}
        self.readers = {}
        self.seg = 0
        self.eng_obj = {"pe": nc.tensor, "act": nc.scalar, "dve": nc.vector,
                        "pool": nc.gpsimd, "sp": nc.sync}
        self.sems = {}
        self.dma_sems = {}
        self.fence_ops = set()
        self.ps_last = {}
        self.touch = {}

    def next_segment(self):
        self.seg += 1

    def dma_sem(self, name):
        if name not in self.dma_sems:
            s = self.stack.enter_context(self.nc.semaphore("d_" + name))
            self.dma_sems[name] = [s, 0]
        return name

    def fence(self):
        per_eng = {}
        dmas = set()
        for k in list(self.touch.keys()):
            ids = []
            w = self.last_write.pop(k, None)
            if w is not None:
                ids.append(w)
            ids.extend(self.readers.pop(k, []))
            for i in ids:
                o = self.ops[i]
                if o.is_dma:
                    dmas.add(i)
                else:
                    if per_eng.get(o.eng, -1) < i:
                        per_eng[o.eng] = i
        for i in self.fence_ops:
            o = self.ops[i]
            if o.is_dma:
                dmas.add(i)
            elif per_eng.get(o.eng, -1) < i:
                per_eng[o.eng] = i
        self.fence_ops = set(per_eng.values()) | dmas
        self.touch = {}

    def op(self, eng, fn, reads=(), writes=(), dma_sem=None, ninc=1):
        import os as _os
        _lim = int(_os.environ.get("LIMIT", "0"))
        if _lim and len(self.ops) >= _lim:
            return None
        deps = set()
        lw = self.last_write
        rd = self.readers
        for k in reads:
            w = lw.get(k)
            if w is not None:
                deps.add(w)
            if isinstance(k, tuple) and k[0] == "A" and k not in self.touch:
                deps.update(self.fence_ops)
        for k in writes:
            w = lw.get(k)
            if w is not None:
                deps.add(w)
            r = rd.get(k)
            if r:
                deps.update(r)
            if isinstance(k, tuple) and k[0] == "A" and k not in self.touch:
                deps.update(self.fence_ops)
        idx = len(self.ops)
        for k in tuple(reads) + tuple(writes):
            if isinstance(k, tuple) and k[0] == "ps":
                ent = self.ps_last.setdefault(k, {})
                for e2, i2 in ent.items():
                    if e2 != eng:
                        deps.add(i2)
                ent[eng] = idx
        rec = _Rec()
        fn(rec)
        assert rec.call is not None
        self.ops.append(_Op(eng, rec.call, deps, dma_sem, self.seg, ninc))
        for k in reads:
            if isinstance(k, tuple) and k[0] == "A":
                self.touch[k] = None
            if isinstance(k, tuple) and k[0] == "const":
                continue
            rd.setdefault(k, []).append(idx)
        for k in writes:
            if isinstance(k, tuple) and k[0] == "A":
                self.touch[k] = None
            lw[k] = idx
            rd[k] = []
        return idx

    @staticmethod
    def _skip(do, o):
        return do.eng == "pe" and o.eng == "pe" and not do.is_dma and not o.is_dma

    def emit(self, final_wait_eng="sp"):
        nc = self.nc
        ops = self.ops
        for o in ops:
            for d in o.deps:
                do = ops[d]
                if self._skip(do, o):
                    continue
                do.needs_inc = True
        counters = {}
        for o in ops:
            if o.is_dma:
                ent = self.dma_sems[o.dma_sem]
                ent[1] += 16 * o.ninc
                o.token = (ent[0], ent[1], o.dma_sem)
                o.needs_inc = True
            elif o.needs_inc:
                key = (o.eng, o.seg)
                if key not in self.sems:
                    self.sems[key] = self.stack.enter_context(nc.semaphore("p_%s_%d" % key))
                counters[key] = counters.get(key, 0) + 1
                o.token = (self.sems[key], counters[key], key)
        waited = {e: {} for e in self.eng_obj}
        nwaits = 0
        for o in ops:
            eobj = self.eng_obj[o.eng]
            need = {}
            wd = waited[o.eng]
            for d in o.deps:
                do = ops[d]
                if self._skip(do, o):
                    continue
                sem, val, key = do.token
                if wd.get(key, 0) >= val:
                    continue
                if need.get(key, (None, 0))[1] < val:
                    need[key] = (sem, val)
            for key, (sem, val) in need.items():
                eobj.wait_ge(sem, val)
                wd[key] = val
                nwaits += 1
            mname, margs, mkw = o.fn
            inst = getattr(eobj, mname)(*margs, **mkw)
            if o.is_dma:
                insts = inst if isinstance(inst, (list, tuple)) else [inst]
                assert len(insts) == o.ninc
                for i in insts:
                    i.then_inc(o.token[0], 16)
            elif o.needs_inc:
                inst.then_inc(o.token[0], 1)
        eobj = self.eng_obj[final_wait_eng]
        for name, (sem, val) in self.dma_sems.items():
            if val > 0:
                eobj.wait_ge(sem, val)
        return dict(n_ops=len(ops), n_waits=nwaits, n_sems=len(self.sems) + len(self.dma_sems))


def _layout(names_sizes):
    off = {}
    o = 0
    for n, s in names_sizes:
        off[n] = (o, s)
        o += s
    return off, o


CF_ITEMS = [("g_mix", 8), ("g_mlp", 8), ("g_ple", 8), ("gn_g", 16), ("zs", 8), ("inv", 96),
            ("pek", 32), ("pev", 32), ("AC", 512), ("g_final", 1024), ("mhalf", 8)]
CF_OFF, CF_W = _layout(CF_ITEMS)
CB_ITEMS = [("decayT", 1024), ("xi", 1024), ("tri", 128), ("old", 128), ("maskC", 2048),
            ("VM", 512), ("ident", 128), ("E", 2048), ("ov", 32), ("ck2", 256), ("cv2", 128)]
CB_OFF, CB_W = _layout(CB_ITEMS)


def host_consts(inp):
    f32 = np.float32
    cf = np.zeros((128, CF_W), f32)
    cb = np.zeros((128, CB_W), f32)

    def putf(name, arr):
        o, s = CF_OFF[name]
        cf[:, o:o + s] = np.asarray(arr, f32).reshape(128, s)

    def putb(name, arr):
        o, s = CB_OFF[name]
        cb[:, o:o + s] = np.asarray(arr, f32).reshape(128, s)

    colmaj = lambda g, k: np.asarray(g, f32).reshape(k, 128).T
    putf("g_mix", colmaj(inp["norm_mix_g"][0], 8))
    putf("g_mlp", colmaj(inp["norm_mlp_g"][0], 8))
    putf("g_ple", colmaj(inp["norm_ple_g"][0], 8))
    putf("gn_g", colmaj(inp["ret_gn_g"][0], 16))
    log_g = np.log(1.0 - 2.0 ** (-5.0 - np.arange(8, dtype=f32))).astype(f32)
    idx = np.arange(128, dtype=f32)
    diff = idx[:, None] - idx[None, :]
    decay = np.where(diff[None] >= 0, np.exp(np.maximum(diff, 0.0)[None] * log_g[:, None, None]), 0.0)
    sc = 128.0 ** -0.5
    putb("decayT", np.transpose(decay, (2, 0, 1)) * sc)
    xi = np.exp((idx + 1.0)[None] * log_g[:, None])
    putb("xi", np.broadcast_to(xi[None], (128, 8, 128)))
    zeta = np.exp((127.0 - idx)[None] * log_g[:, None])
    putf("zs", zeta.T * sc)
    inv_r = (f32(10000.0) ** (-np.arange(0, 128, 2, dtype=f32) / f32(128))).astype(f32)
    inv_n = (f32(10000.0) ** (-np.arange(0, 64, 2, dtype=f32) / f32(64))).astype(f32)
    putf("inv", np.broadcast_to(np.concatenate([inv_r, inv_n])[None], (128, 96)))
    pek = np.asarray(inp["cmp_pe_k"][0], f32)
    pev = np.asarray(inp["cmp_pe_v"][0], f32)
    putf("pek", np.concatenate([pek.T, pek.T], 0))
    putf("pev", np.concatenate([pev.T, pev.T], 0))
    putf("g_final", np.broadcast_to(np.asarray(inp["norm_final_g"], f32)[None], (128, 1024)))
    putf("mhalf", np.full((128, 8), -0.5, f32))
    q = np.arange(128)
    putb("tri", np.where(q[:, None] <= q[None, :], 0.0, -BIGM))
    putb("old", np.where(q[:, None] > q[None, :], 0.0, -BIGM))
    slot = np.arange(128)
    c = slot - 1
    gt = np.arange(16)
    t_abs = gt[:, None] * 128 + q[None, :]
    mC = ((16 * c[:, None, None] + 31) <= t_abs[None]) & (slot[:, None, None] >= 1)
    putb("maskC", np.where(mC, 0.0, -BIGM))
    blk = np.arange(32)
    cur = (t_abs.T // 64)
    forced = (blk[None, None] == 0) | (blk[None, None] == cur[..., None]) | (blk[None, None] == cur[..., None] - 1)
    valid = blk[None, None] <= cur[..., None]
    putb("VM", (valid & ~forced).astype(f32))
    putf("AC", np.where(forced, 1e6, np.where(valid, 0.0, -1.0)))
    putb("ident", np.eye(128, dtype=f32))
    E = np.zeros((128, 16, 128), f32)
    key = np.arange(128)
    for kt in range(16):
        for b in range(32):
            E[b, kt, :] = BIGM * (b == 2 * kt + key // 64)
        E[32, kt, :] = -BIGM
    putb("E", E)
    ov = ((16 * c[:, None] < 64 * (blk[None] + 1)) & (16 * c[:, None] + 31 >= 64 * blk[None]) & (slot[:, None] >= 1))
    putb("ov", ov.astype(f32))
    w2k = np.asarray(inp["cmp_k_w2"][0], f32)
    w2v = np.asarray(inp["cmp_v_w2"][0], f32)
    ck2 = np.zeros((128, 2, 128), f32)
    cv2 = np.zeros((128, 2, 64), f32)
    for hh in range(2):
        ck2[:, hh, 0:64] = w2k[hh * 128:(hh + 1) * 128]
        ck2[:, hh, 64:128] = w2k[hh * 128:(hh + 1) * 128]
        cv2[:, hh, :] = w2v[hh * 128:(hh + 1) * 128]
    putb("ck2", ck2)
    putb("cv2", cv2)
    return cf, cb


def piece_names():
    names = ["S1", "S2", "Q1", "Q2", "CK1", "CK2", "CV1", "CV2"]
    for hp in range(4):
        names += ["B%d" % hp, "A%d" % (2 * hp), "A%d" % (2 * hp + 1)]
    names += ["MG0", "MG1", "MG2", "MG3"]
    for i in range(4):
        if i % 2 == 0:
            names.append("NO%d" % (i // 2))
        names.append("RO%d" % i)
    names += ["WO0", "WO1"]
    for qd in range(4):
        names += ["UP%d" % (2 * qd), "UP%d" % (2 * qd + 1), "DN%d" % (2 * qd), "DN%d" % (2 * qd + 1)]
    names += ["PP", "PG0", "PG1"]
    return names


PIECES = piece_names()
PIDX = {n: i for i, n in enumerate(PIECES)}
NP_ = len(PIECES)


def host_pack(inp):
    f32 = np.float32
    W = np.zeros((NP_, 128, PIECE), f32)
    w_in = np.asarray(inp["w_in"][0], f32)
    o = 0
    sl = {}
    for n, s in [("rq", 1024), ("rk", 1024), ("rv", 2048), ("rg", 2048), ("nq", 1024), ("kc", 128),
                 ("vc", 128), ("ksl", 128), ("vsl", 128), ("kw", 128), ("vw", 128), ("ng", 48)]:
        sl[n] = w_in[:, o:o + s]
        o += s

    def kpiece(cols):
        out = np.zeros((128, 8, 512), f32)
        out[:, :, :cols.shape[1]] = cols.reshape(8, 128, -1).transpose(1, 0, 2)
        return out.reshape(128, PIECE)

    W[PIDX["S1"]] = kpiece(np.concatenate([sl["kc"], sl["ksl"], sl["kw"], sl["vc"]], 1))
    W[PIDX["S2"]] = kpiece(np.concatenate([sl["vsl"], sl["vw"], sl["ng"]], 1))
    nq = sl["nq"].reshape(1024, 16, 64)
    order = []
    for i in range(8):
        order += [i, 8 + i]
    nqp = nq[:, order, :].reshape(1024, 1024)
    W[PIDX["Q1"]] = kpiece(nqp[:, 0:512])
    W[PIDX["Q2"]] = kpiece(nqp[:, 512:1024])
    for h in range(8):
        W[PIDX["A%d" % h]] = kpiece(np.concatenate(
            [sl["rq"][:, h * 128:(h + 1) * 128], sl["rk"][:, h * 128:(h + 1) * 128],
             sl["rv"][:, h * 256:(h + 1) * 256]], 1))
    for hp in range(4):
        W[PIDX["B%d" % hp]] = kpiece(sl["rg"][:, hp * 512:(hp + 1) * 512])
    for nm, key in (("CK", "cmp_k_w1"), ("CV", "cmp_v_w1")):
        w1 = np.asarray(inp[key][0], f32).reshape(32, 64, 256)
        for half in range(2):
            blk = w1[half * 16:(half + 1) * 16]
            pc = np.concatenate([blk.transpose(1, 0, 2)] * 2, 0)
            W[PIDX["%s%d" % (nm, half + 1)]] = pc.reshape(128, PIECE)
    wm = np.asarray(inp["w_merge_gate"][0], f32)
    for i in range(4):
        W[PIDX["MG%d" % i]] = kpiece(wm[:, i * 512:(i + 1) * 512])
    wno = np.asarray(inp["w_nsa_o"][0], f32)
    rows = []
    for i in range(8):
        rows += list(range(i * 64, (i + 1) * 64)) + list(range((8 + i) * 64, (9 + i) * 64))
    wno = wno[rows, :]
    for i in range(2):
        W[PIDX["NO%d" % i]] = kpiece(wno[:, i * 512:(i + 1) * 512])
    wro = np.asarray(inp["w_ret_o"][0], f32)
    for i in range(4):
        pc = wro[:, i * 256:(i + 1) * 256].reshape(16, 128, 256).transpose(1, 0, 2)
        W[PIDX["RO%d" % i]] = pc.reshape(128, PIECE)
    wo = np.asarray(inp["w_out"][0], f32)
    for i in range(2):
        W[PIDX["WO%d" % i]] = kpiece(wo[:, i * 512:(i + 1) * 512])
    wu = np.asarray(inp["w_mlp_up"][0], f32)
    wd = np.asarray(inp["w_mlp_down"][0], f32)
    for i in range(8):
        W[PIDX["UP%d" % i]] = kpiece(wu[:, i * 512:(i + 1) * 512])
    for qd in range(4):
        for ch in range(2):
            W[PIDX["DN%d" % (2 * qd + ch)]] = kpiece(wd[qd * 1024:(qd + 1) * 1024, ch * 512:(ch + 1) * 512])
    wg = np.asarray(inp["w_ple_gate"][0], f32)
    for i in range(2):
        W[PIDX["PG%d" % i]] = kpiece(wg[:, i * 512:(i + 1) * 512])
    wp = np.asarray(inp["w_ple_proj"][0], f32)
    pp = np.zeros((128, PIECE), f32)
    pp[:, :2048] = wp.reshape(2, 128, 1024).transpose(1, 0, 2).reshape(128, 2048)
    W[PIDX["PP"]] = pp
    return W


def build_program(nseq=4, nblk=4, dump=None, stages=99):
    nc = bass.Bass("TRN2", target_bir_lowering=False)
    ntok = nseq * SEQ
    x_d = nc.dram_tensor("x", [ntok, DM], F32, kind="ExternalInput")
    p_d = nc.dram_tensor("p", [ntok, 256], F32, kind="ExternalInput")
    pos_d = nc.dram_tensor("posl", [128, nseq * 16], I32, kind="ExternalInput")
    wp_d = nc.dram_tensor("wpack", [NP_, 128, PIECE], F32, kind="ExternalInput")
    cf_d = nc.dram_tensor("cf", [128, CF_W], F32, kind="ExternalInput")
    cb_d = nc.dram_tensor("cb", [128, CB_W], F32, kind="ExternalInput")
    out_d = nc.dram_tensor("out", [ntok, DM], F32, kind="ExternalOutput")
    wbf_d = nc.dram_tensor("wbf", [NP_, 128, PIECE], BF16, kind="ExternalOutput")
    dumps = {}

    st = ExitStack()
    with st:
        P = Prog(nc, st)
        sbt = lambda n, s, d: st.enter_context(nc.sbuf_tensor(n, s, d))
        psum = st.enter_context(nc.psum_tensor("psum", [128, 4096], F32))

        def bank(b, n=512, off=0):
            return psum[:, b * 512 + off: b * 512 + off + n]

        def bankb(b, n=1024, off=0):
            return psum[:, b * 512:(b + 1) * 512].bitcast(BF16)[:, off:off + n]

        PS = lambda b: ("ps", b)

        cf = sbt("cf_s", [128, CF_W], F32)
        cb = sbt("cb_s", [128, CB_W], BF16)
        posi = sbt("posi", [128, nseq * 16], I32)
        posf = sbt("posf", [128, nseq * 16], F32)
        NSLOT = 4
        wring = [sbt("wring%d" % i, [128, PIECE], BF16) for i in range(NSLOT)]
        kslT = [sbt("kslT%d" % g_, [128, SEQ], BF16) for g_ in range(2)]
        kwT = [sbt("kwT%d" % g_, [128, SEQ], BF16) for g_ in range(2)]
        KcTz = sbt("KcTz", [128, 2, 128], BF16)
        vslA = sbt("vslA", [128, 16, 2, 65], BF16)
        vwA = sbt("vwA", [128, 16, 2, 65], BF16)
        kcT = sbt("kcT", [128, 16 + SEQ], BF16)
        vcT = sbt("vcT", [128, 16 + SEQ], BF16)
        hidk = sbt("hidk", [128, 2, 2, 128], BF16)
        hidv = sbt("hidv", [128, 2, 2, 128], BF16)
        KcT = sbt("KcT", [128, 2, 128], BF16)
        VcA = sbt("VcA", [128, 2, 97], BF16)
        Rst = sbt("Rst", [128, 8, 256], F32)
        hT = sbt("hT", [128, 8, TB], BF16)
        oretT = sbt("oretT", [128, 16, TB], BF16)
        onsaT = sbt("onsaT", [128, 8, TB], BF16)
        tabs = sbt("tabs", [128, NT, 2, 96], F32)
        cosR = sbt("cosR", [128, NT, 128], F32)
        sinR = sbt("sinR", [128, NT, 128], F32)
        cosN = sbt("cosN", [128, NT, 64], F32)
        sinN = sbt("sinN", [128, NT, 64], F32)
        ARENA_W = 15 * 1024
        arena = sbt("arena", [128, ARENA_W], F32)
        astate = {"off": 0}

        def cfv(name, *shape):
            o, s = CF_OFF[name]
            v = cf[:, o:o + s]
            return v

        def cbv(name):
            o, s = CB_OFF[name]
            return cb[:, o:o + s]

        def a_reset():
            astate["off"] = 0
            P.fence()

        def a_alloc(n, dtype):
            words = (n * (2 if dtype == BF16 else 4) + 3) // 4
            words = (words + 7) // 8 * 8
            o = astate["off"]
            assert o + words <= ARENA_W, ("arena overflow", o, words)
            astate["off"] = o + words
            v = arena[:, o:o + words]
            if dtype == BF16:
                v = v.bitcast(BF16)
            elif dtype == I32:
                v = v.bitcast(I32)
            return v[:, 0:n]

        for nme in ["cf", "cb", "pos", "x0", "x1", "p0", "p1", "out", "dump"] + ["w%d" % i for i in range(NSLOT)]:
            P.dma_sem(nme)

        def do_dump(name, ap, reads, shape, dtype=F32):
            if dump is None or name not in dump or name in dumps:
                return
            d = nc.dram_tensor("dump_" + name, list(shape), dtype, kind="ExternalOutput")
            dumps[name] = d
            P.op("sp", lambda e: e.dma_start(out=d.ap(), in_=ap), reads=reads, dma_sem="dump")

        P.op("sp", lambda e: e.dma_start(out=cf[:], in_=cf_d.ap()), writes=[("const", "cf")], dma_sem="cf")
        P.op("sp", lambda e: e.dma_start(out=posi[:], in_=pos_d.ap()), writes=["posi"], dma_sem="pos")
        P.op("dve", lambda e: e.tensor_copy(out=posf[:], in_=posi[:]), reads=["posi"], writes=[("const", "posf")])
        cbst = a_alloc(CB_W, F32)
        P.op("sp", lambda e: e.dma_start(out=cbst, in_=cb_d.ap()), writes=[("A", "cbst")], dma_sem="cb")
        P.op("dve", lambda e: e.tensor_copy(out=cb[:], in_=cbst), reads=[("A", "cbst")], writes=[("const", "cb")])
        a_reset()
        wst = [a_alloc(PIECE, F32) for _ in range(2)]
        wsb = [a_alloc(PIECE, BF16) for _ in range(2)]
        for i in range(3):
            P.dma_sem("wst%d" % i)
        for i in range(2):
            P.dma_sem("wsb%d" % i)
        import os as _os
        _skip = _os.environ.get("SKIP", "").split(",")
        for i in range(0 if "cast" in _skip else NP_):
            a3, b2 = i % 2, i % 2
            P.op("sp", (lambda i, a3: lambda e: e.dma_start(out=wst[a3], in_=wp_d.ap()[i]))(i, a3),
                 writes=[("A", "wst", a3)], dma_sem="wst%d" % a3)
            if i % 2 == 0:
                P.op("act", (lambda a3, b2: lambda e: e.copy(out=wsb[b2], in_=wst[a3]))(a3, b2),
                     reads=[("A", "wst", a3)], writes=[("A", "wsb", b2)])
            else:
                P.op("dve", (lambda a3, b2: lambda e: e.tensor_copy(out=wsb[b2], in_=wst[a3]))(a3, b2),
                     reads=[("A", "wst", a3)], writes=[("A", "wsb", b2)])
            P.op("sp", (lambda i, b2: lambda e: e.dma_start(out=wbf_d.ap()[i], in_=wsb[b2]))(i, b2),
                 reads=[("A", "wsb", b2)], writes=[("wbf", i)], dma_sem="wsb%d" % b2)
        for tname, t in (() if "memset" in _skip else (("kcT", kcT), ("vcT", vcT), ("hidk", hidk), ("hidv", hidv))):
            P.op("pool", (lambda t: lambda e: e.memset(t[:], 0.0))(t), writes=[tname])
        P.op("pool", lambda e: e.memset(VcA[:], 0.0), writes=["VcA"])
        P.op("pool", lambda e: e.memset(KcTz[:], 0.0), writes=["KcT"])
        for g_ in range(2):
            P.op("pool", lambda e: e.memset(kslT[g_][:], 0.0), writes=[("kslT", k_) for k_ in range(16)])
            P.op("pool", lambda e: e.memset(kwT[g_][:], 0.0), writes=[("kwT", k_) for k_ in range(16)])
        P.op("dve", lambda e: e.memset(VcA[:, :, 64:65], 1.0), reads=["VcA"], writes=["VcA"])
        for g in range(2):
            P.op("dve", (lambda g: lambda e: e.tensor_copy(out=VcA[:, g, 65:97], in_=cbv("ov")))(g),
                 reads=[("const", "cb"), "VcA"], writes=["VcA"])
        P.op("pool", lambda e: e.memset(vslA[:], 1.0), writes=["vslA"])
        P.op("pool", lambda e: e.memset(vwA[:], 1.0), writes=["vwA"])

        ncut = {1: 0, 2: 4, 3: 8, 4: 8, 5: 20}.get(stages, NP_)
        border = PIECES[:ncut]
        nstream = len(border) * nseq * nblk
        wstate = {"use": 0, "iss": 0}
        released = set()
        held = set()
        pending = []

        def try_issue(upto):
            while wstate["iss"] < min(upto, nstream):
                n = wstate["iss"]
                if n - NSLOT >= 0 and (n - NSLOT) not in released:
                    break
                wstate["iss"] += 1
                slot = n % NSLOT
                pi = PIDX[border[n % len(border)]]
                P.op("sp", lambda e: e.dma_start(out=wring[slot][:], in_=wbf_d.ap()[pi]),
                     reads=[("wbf", pi)], writes=[("wr", slot)], dma_sem="w%d" % slot)

        def wrelease(i):
            held.discard(i)
            released.add(i)
            try_issue(wstate["use"] + NSLOT)

        def wload(name, hold=False):
            i = wstate["use"]
            assert border[i % len(border)] == name, (name, border[i % len(border)])
            for q in list(pending):
                if q not in held:
                    released.add(q)
                    pending.remove(q)
            wstate["use"] += 1
            try_issue(i + NSLOT)
            assert wstate["iss"] > i, ("weight ring deadlock", name, i)
            pending.append(i)
            if hold:
                held.add(i)
            wload.last = i
            return wring[i % NSLOT], ("wr", i % NSLOT)

        CONST = [("const", "cf"), ("const", "cb"), ("const", "posf")]
        ident = cbv("ident")

        def transposes(src_fn, n, tb, keys_r):
            for k in range(n):
                P.op("pe", (lambda k: lambda e: e.transpose(out=bankb(tb, 128, k * 128), in_=src_fn(k), identity=ident))(k),
                     reads=keys_r + [("const", "cb")], writes=[PS(tb)])

        def bc(ap2d, dims):
            return bass.AP(ap2d.tensor, ap2d.offset, [list(ap2d.ap[0])] + [list(d) for d in dims])

        def rms_to_hT(src_ap, src_keys, gname, t, tb, hn, junk, ssq, rstd):
            P.op("act", lambda e: e.activation(out=junk, in_=src_ap, func=AF.Square, accum_out=ssq),
                 reads=src_keys, writes=[("A", "junk"), ("A", "ssq")])
            P.op("dve", lambda e: e.tensor_scalar(out=rstd, in0=ssq, scalar1=1.0 / DM, scalar2=EPS,
                                                  op0=ALU.mult, op1=ALU.add),
                 reads=[("A", "ssq")], writes=[("A", "rstd")])
            P.op("pool", lambda e: e.tensor_tensor(out=rstd, in0=rstd, in1=cfv("mhalf")[:, 0:1], op=ALU.pow),
                 reads=[("A", "rstd"), ("const", "cf")], writes=[("A", "rstd")])
            P.op("act", lambda e: e.activation(out=hn, in_=src_ap, func=AF.Copy, scale=rstd),
                 reads=src_keys + [("A", "rstd")], writes=[("A", "hn")])
            transposes(lambda k: hn[:, k * 128:(k + 1) * 128], 8, tb, [("A", "hn")])
            g = cfv(gname)
            P.op("dve", lambda e: e.tensor_tensor(
                out=hT[:, :, t * 128:(t + 1) * 128],
                in0=bankb(tb).rearrange("p (k c) -> p k c", k=8),
                in1=bc(g, [[1, 8], [0, 128]]), op=ALU.mult),
                reads=[PS(tb), ("const", "cf")], writes=[("hT", t)])

        def rope(e_unused, psv, nh, hd, cosv, sinv, outv, tmp1, tmp2, rkeys, wkeys, tkeys):
            h2 = hd // 2
            x3 = psv.rearrange("p (h d) -> p h d", h=nh)
            P.op("dve", lambda e: e.tensor_tensor(out=tmp1.rearrange("p (h d) -> p h d", h=nh), in0=x3,
                                                  in1=bc(cosv, [[0, nh], [1, hd]]), op=ALU.mult),
                 reads=rkeys + ["TABS"], writes=[tkeys[0]])
            t23 = tmp2.rearrange("p (h d) -> p h d", h=nh)
            P.op("dve", lambda e: e.tensor_tensor(out=t23[:, :, 0:h2], in0=x3[:, :, h2:hd],
                                                  in1=bc(sinv[:, 0:h2], [[0, nh], [1, h2]]), op=ALU.mult),
                 reads=rkeys + ["TABS"], writes=[tkeys[1]])
            P.op("dve", lambda e: e.tensor_tensor(out=t23[:, :, h2:hd], in0=x3[:, :, 0:h2],
                                                  in1=bc(sinv[:, h2:hd], [[0, nh], [1, h2]]), op=ALU.mult),
                 reads=rkeys + ["TABS"], writes=[tkeys[1]])
            P.op("dve", lambda e: e.tensor_tensor(out=outv, in0=tmp1, in1=tmp2, op=ALU.add),
                 reads=list(tkeys), writes=wkeys)

        for s in range(nseq if stages > 0 else 0):
            for j in range(nblk):
                P.next_segment()
                row0 = s * SEQ + j * TB
                T0 = j * TB
                a_reset()
                xt = [a_alloc(1024, F32), a_alloc(1024, F32)]
                hn = a_alloc(1024, BF16)
                junk = a_alloc(1024, BF16)
                ssq = a_alloc(1, F32)
                rstd = a_alloc(1, F32)
                ang = a_alloc(NT * 2 * 96, F32)
                angk = a_alloc(NT * 2 * 96, F32)
                angi = a_alloc(NT * 2 * 96, I32)
                ang4 = ang.rearrange("p (t a f) -> p t a f", t=NT, a=2)
                inv = cfv("inv")
                for t in range(0 if "tabs" in _skip else NT):
                    col = s * 16 + j * NT + t
                    P.op("dve", (lambda t, col: lambda e: e.tensor_scalar(
                        out=ang4[:, t, 0, :], in0=inv, scalar1=posf[:, col:col + 1], scalar2=None, op0=ALU.mult))(t, col),
                        reads=CONST, writes=[("A", "ang")])
                    P.op("dve", (lambda t, col: lambda e: e.tensor_scalar(
                        out=ang4[:, t, 1, :], in0=inv, scalar1=posf[:, col:col + 1], scalar2=math.pi / 2,
                        op0=ALU.mult, op1=ALU.add))(t, col),
                        reads=CONST, writes=[("A", "ang")])
                P.op("dve", lambda e: e.tensor_scalar(out=angk, in0=ang, scalar1=1.0 / TWO_PI, scalar2=None, op0=ALU.mult),
                     reads=[("A", "ang")], writes=[("A", "angk")])
                P.op("dve", lambda e: e.tensor_copy(out=angi, in_=angk), reads=[("A", "angk")], writes=[("A", "angi")])
                P.op("dve", lambda e: e.tensor_copy(out=angk, in_=angi), reads=[("A", "angi")], writes=[("A", "angk")])
                P.op("dve", lambda e: e.scalar_tensor_tensor(out=ang, in0=angk, scalar=-C1, in1=ang, op0=ALU.mult, op1=ALU.add),
                     reads=[("A", "angk"), ("A", "ang")], writes=[("A", "ang")])
                P.op("dve", lambda e: e.scalar_tensor_tensor(out=ang, in0=angk, scalar=-C2, in1=ang, op0=ALU.mult, op1=ALU.add),
                     reads=[("A", "angk"), ("A", "ang")], writes=[("A", "ang")])
                P.op("dve", lambda e: e.tensor_scalar(out=ang, in0=ang, scalar1=3.1415925, scalar2=-3.1415925,
                                                      op0=ALU.min, op1=ALU.max),
                     reads=[("A", "ang")], writes=[("A", "ang")])
                P.op("act", lambda e: e.activation(out=tabs[:].rearrange("p t a f -> p (t a f)"), in_=ang, func=AF.Sin),
                     reads=[("A", "ang")], writes=["tabs0"])
                P.op("dve", lambda e: e.tensor_copy(out=cosR[:, :, 0:64], in_=tabs[:, :, 1, 0:64]), reads=["tabs0"], writes=["TABS"])
                P.op("dve", lambda e: e.tensor_copy(out=cosR[:, :, 64:128], in_=tabs[:, :, 1, 0:64]), reads=["tabs0"], writes=["TABS"])
                P.op("dve", lambda e: e.tensor_scalar(out=sinR[:, :, 0:64], in0=tabs[:, :, 0, 0:64], scalar1=-1.0, scalar2=None, op0=ALU.mult),
                     reads=["tabs0"], writes=["TABS"])
                P.op("dve", lambda e: e.tensor_copy(out=sinR[:, :, 64:128], in_=tabs[:, :, 0, 0:64]), reads=["tabs0"], writes=["TABS"])
                P.op("dve", lambda e: e.tensor_copy(out=cosN[:, :, 0:32], in_=tabs[:, :, 1, 64:96]), reads=["tabs0"], writes=["TABS"])
                P.op("dve", lambda e: e.tensor_copy(out=cosN[:, :, 32:64], in_=tabs[:, :, 1, 64:96]), reads=["tabs0"], writes=["TABS"])
                P.op("dve", lambda e: e.tensor_scalar(out=sinN[:, :, 0:32], in0=tabs[:, :, 0, 64:96], scalar1=-1.0, scalar2=None, op0=ALU.mult),
                     reads=["tabs0"], writes=["TABS"])
                P.op("dve", lambda e: e.tensor_copy(out=sinN[:, :, 32:64], in_=tabs[:, :, 0, 64:96]), reads=["tabs0"], writes=["TABS"])
                if dump and "tabs" in dump:
                    do_dump("tabs", tabs[:].rearrange("p t a f -> p (t a f)"), ["tabs0"], [128, NT * 2 * 96])

                for t in range(0 if "norm" in _skip else NT):
                    xb = xt[t % 2]
                    P.op("sp", (lambda t, xb: lambda e: e.dma_start(out=xb, in_=x_d.ap()[row0 + t * 128: row0 + (t + 1) * 128, :]))(t, xb),
                         writes=[("A", "xt", t % 2)], dma_sem="x%d" % (t % 2))
                    rms_to_hT(xb, [("A", "xt", t % 2)], "g_mix", t, 6 + (t % 2), hn, junk, ssq, rstd)
                do_dump("hT", hT[:].rearrange("p k t -> p (k t)"), [("hT", t) for t in range(NT)], [128, 8 * TB], BF16)
                HT = [("hT", t) for t in range(NT)]
                if stages < 2:
                    continue

                sm = a_alloc(512, BF16)
                tmp1s = [a_alloc(512, F32), a_alloc(512, F32)]
                tmp2s = [a_alloc(512, F32), a_alloc(512, F32)]
                nq_tm = a_alloc(1024, BF16)
                nqT = a_alloc(8 * TB, BF16)
                nqT3 = nqT.rearrange("p (i t) -> p i t", i=8)
                sig = a_alloc(NT * 48, F32)
                sig3 = sig.rearrange("p (t c) -> p t c", t=NT)
                w, wk = wload("S1")
                w3 = w[:].rearrange("p (k c) -> p k c", k=8)
                for t in range(NT):
                    gtile = j * NT + t
                    b = t % 4
                    for k in range(8):
                        P.op("pe", (lambda t, k, b: lambda e: e.matmul(bank(b), lhsT=hT[:, k, t * 128:(t + 1) * 128], rhs=w3[:, k, :],
                                                                       start=(k == 0), stop=(k == 7)))(t, k, b),
                             reads=[("hT", t), wk], writes=[PS(b)])
                    rope(None, bank(b, 384), 6, 64, cosN[:, t, :], sinN[:, t, :], sm[:, 0:384],
                         tmp1s[t % 2][:, 0:384], tmp2s[t % 2][:, 0:384], [PS(b)], [("A", "sm")], [("A", "tmp1", t % 2), ("A", "tmp2", t % 2)])
                    P.op("act", (lambda b: lambda e: e.copy(out=sm[:, 384:512], in_=bank(b, 128, 384)))(b),
                         reads=[PS(b)], writes=[("A", "sm2")])
                    tb = 6 + (t % 2)
                    transposes(lambda k: sm[:, k * 128:(k + 1) * 128], 4, tb, [("A", "sm"), ("A", "sm2")])
                    c0 = T0 + t * 128
                    P.op("dve", (lambda tb, c0: lambda e: e.tensor_copy(out=kcT[:, 16 + c0:16 + c0 + 128], in_=bankb(tb, 128, 0)))(tb, c0),
                         reads=[PS(tb)], writes=["kcT"])
                    for g_ in range(2):
                        rs_ = slice(g_ * 64, (g_ + 1) * 64)
                        P.op("dve", lambda e: e.tensor_copy(out=kslT[g_][rs_, c0:c0 + 128], in_=bankb(tb, 128, 128)[rs_, :]),
                             reads=[PS(tb)], writes=[("kslT", gtile)])
                        P.op("dve", lambda e: e.tensor_copy(out=kwT[g_][rs_, c0:c0 + 128], in_=bankb(tb, 128, 256)[rs_, :]),
                             reads=[PS(tb)], writes=[("kwT", gtile)])
                    P.op("dve", (lambda tb, c0: lambda e: e.tensor_copy(out=vcT[:, 16 + c0:16 + c0 + 128], in_=bankb(tb, 128, 384)))(tb, c0),
                         reads=[PS(tb)], writes=["vcT"])
                w, wk = wload("S2")
                w3b = w[:].rearrange("p (k c) -> p k c", k=8)
                for t in range(NT):
                    gtile = j * NT + t
                    b = t % 4
                    for k in range(8):
                        P.op("pe", (lambda t, k, b, w3b: lambda e: e.matmul(bank(b, 304), lhsT=hT[:, k, t * 128:(t + 1) * 128], rhs=w3b[:, k, 0:304],
                                                                            start=(k == 0), stop=(k == 7)))(t, k, b, w3b),
                             reads=[("hT", t), wk], writes=[PS(b)])
                    P.op("act", (lambda b, gtile: lambda e: e.copy(out=vslA[:, gtile, :, 0:64],
                                                                   in_=bank(b, 128, 0).rearrange("p (g d) -> p g d", g=2)))(b, gtile),
                         reads=[PS(b)], writes=[("vslA", gtile)])
                    P.op("dve", (lambda b, gtile: lambda e: e.tensor_copy(out=vwA[:, gtile, :, 0:64],
                                                                          in_=bank(b, 128, 128).rearrange("p (g d) -> p g d", g=2)))(b, gtile),
                         reads=[PS(b)], writes=[("vwA", gtile)])
                    P.op("act", (lambda b, t: lambda e: e.activation(out=sig3[:, t, :], in_=bank(b, 48, 256), func=AF.Sigmoid))(b, t),
                         reads=[PS(b)], writes=[("A", "sig", t)])
                for qi, qn in enumerate(("Q1", "Q2")):
                    w, wk = wload(qn)
                    w3q = w[:].rearrange("p (k c) -> p k c", k=8)
                    for t in range(NT):
                        b = t % 4
                        for k in range(8):
                            P.op("pe", (lambda t, k, b, w3q: lambda e: e.matmul(bank(b), lhsT=hT[:, k, t * 128:(t + 1) * 128], rhs=w3q[:, k, :],
                                                                                start=(k == 0), stop=(k == 7)))(t, k, b, w3q),
                                 reads=[("hT", t), wk], writes=[PS(b)])
                        rope(None, bank(b), 8, 64, cosN[:, t, :], sinN[:, t, :], nq_tm[:, 0:512],
                             tmp1s[t % 2], tmp2s[t % 2], [PS(b)], [("A", "nq_tm")], [("A", "tmp1", t % 2), ("A", "tmp2", t % 2)])
                        tb = 6 + (t % 2)
                        transposes(lambda k: nq_tm[:, k * 128:(k + 1) * 128], 4, tb, [("A", "nq_tm")])
                        P.op("dve", (lambda tb, qi, t: lambda e: e.tensor_copy(
                            out=nqT3[:, qi * 4:(qi + 1) * 4, t * 128:(t + 1) * 128],
                            in_=bankb(tb, 512).rearrange("p (i c) -> p i c", i=4)))(tb, qi, t),
                            reads=[PS(tb)], writes=[("A", "nqT", t)])
                do_dump("kslT", kslT[0][:], [("kslT", j * NT + t) for t in range(NT)], [128, SEQ], BF16)
                do_dump("nqT", nqT, [("A", "nqT", t) for t in range(NT)], [128, 8 * TB], BF16)
                do_dump("sig", sig, [("A", "sig", t) for t in range(NT)], [128, NT * 48])
                if stages < 3:
                    continue

                kpe = a_alloc(32 * 32, BF16)
                kpe3 = kpe.rearrange("p (l c) -> p l c", l=32)
                gl = [a_alloc(128, F32) for _ in range(3)]
                s0 = 32 * j
                for nm, cache, pen, hid, w2n in (("CK", kcT, "pek", hidk, "ck2"), ("CV", vcT, "pev", hidv, "cv2")):
                    src = cache[:, 16 * s0: 16 * s0 + 1]
                    src = bass.AP(src.tensor, src.offset, [list(src.ap[0]), [1, 32], [16, 32]])
                    pe_ap = cfv(pen)
                    P.op("dve", (lambda src, pe_ap: lambda e: e.tensor_tensor(out=kpe3, in0=src, in1=bc(pe_ap, [[1, 32], [0, 32]]), op=ALU.add))(src, pe_ap),
                         reads=[nm[1] == "K" and "kcT" or "vcT", ("const", "cf")], writes=[("A", "kpe")])
                    wA, wkA = wload(nm + "1", hold=True)
                    iA = wload.last
                    wB, wkB = wload(nm + "2", hold=True)
                    iB = wload.last
                    for g in range(2):
                        first = True
                        for hh in range(2):
                            for l in range(32):
                                wsrc, wkey = (wA, wkA) if l < 16 else (wB, wkB)
                                w1v = wsrc[:].rearrange("p (l j) -> p l j", l=16)
                                P.op("pe", lambda e: e.matmul(
                                    bank(g, 32, hh * 32),
                                    lhsT=w1v[g * 64:(g + 1) * 64, l % 16, hh * 128:(hh + 1) * 128],
                                    rhs=kpe3[g * 64:(g + 1) * 64, l, :],
                                    start=first, stop=(l == 31), skip_group_check=True),
                                    reads=[("A", "kpe"), wkey], writes=[PS(g)])
                                first = False
                    wrelease(iA)
                    wrelease(iB)
                    xh = bass.AP(psum, 0, [[4096, 128], [512, 2], [1, 64]])
                    g3 = [t_.rearrange("p (g c) -> p g c", g=2) for t_ in gl]
                    PH = [PS(0), PS(1)]
                    P.op("act", lambda e: e.activation(out=g3[0], in_=xh, func=AF.Square), reads=PH, writes=[("A", "gl0")])
                    P.op("dve", lambda e: e.tensor_scalar(out=gl[0], in0=gl[0], scalar1=0.044715, scalar2=1.0, op0=ALU.mult, op1=ALU.add),
                         reads=[("A", "gl0")], writes=[("A", "gl0")])
                    P.op("dve", lambda e: e.tensor_tensor(out=g3[1], in0=xh, in1=g3[0], op=ALU.mult), reads=PH + [("A", "gl0")], writes=[("A", "gl1")])
                    P.op("act", lambda e: e.activation(out=gl[2], in_=gl[1], func=AF.Sigmoid, scale=1.5957691216), reads=[("A", "gl1")], writes=[("A", "gl2")])
                    xh4 = bass.AP(psum, 0, [[4096, 128], [512, 2], [32, 2], [1, 32]])
                    P.op("dve", lambda e: e.tensor_tensor(out=hid[:, :, :, s0:s0 + 32], in0=xh4,
                                                          in1=gl[2].rearrange("p (g h c) -> p g h c", g=2, h=2), op=ALU.mult),
                         reads=PH + [("A", "gl2")], writes=[nm])
                ck2 = cbv("ck2").rearrange("p (h d) -> p h d", h=2)
                cv2 = cbv("cv2").rearrange("p (h d) -> p h d", h=2)
                for g in range(2):
                    for hh in range(2):
                        P.op("pe", (lambda g, hh: lambda e: e.matmul(bank(2, 128, g * 128), lhsT=ck2[:, hh, :], rhs=hidk[:, g, hh, :],
                                                                     start=(g == 0 and hh == 0), stop=(hh == 1), skip_group_check=True))(g, hh),
                             reads=["CK", ("const", "cb")], writes=[PS(2)])
                for g in range(2):
                    for hh in range(2):
                        P.op("pe", (lambda g, hh: lambda e: e.matmul(bank(3, 64, g * 64), lhsT=hidv[:, g, hh, :], rhs=cv2[:, hh, :],
                                                                     start=(g == 0 and hh == 0), stop=(hh == 1), skip_group_check=True))(g, hh),
                             reads=["CV", ("const", "cb")], writes=[PS(3)])
                for g_ in range(2):
                    rs_ = slice(g_ * 64, (g_ + 1) * 64)
                    P.op("act", lambda e: e.copy(out=KcTz[rs_, g_, :], in_=bank(2, 128, g_ * 128)[rs_, :]), reads=[PS(2)], writes=["KcT"])
                P.op("dve", lambda e: e.tensor_copy(out=VcA[:, :, 0:64], in_=bank(3, 128).rearrange("p (g d) -> p g d", g=2)),
                     reads=[PS(3), "VcA"], writes=["VcA"])
                do_dump("KcT", KcTz[:].rearrange("p g c -> p (g c)"), ["KcT"], [128, 256], BF16)
                do_dump("VcA", VcA[:].rearrange("p g c -> p (g c)"), ["VcA"], [128, 2 * 97], BF16)
                if stages < 4:
                    continue

                pexp = [a_alloc(1024, BF16) for _ in range(3)]
                onsa = a_alloc(1024, F32)
                onsa4 = onsa.rearrange("p (i g d) -> p i g d", i=8, g=2)
                otmp = a_alloc(512, F32)
                onsab = a_alloc(1024, BF16)
                den = a_alloc(8, F32)
                fac = a_alloc(8, F32)
                impt = a_alloc(256, F32)
                imp = a_alloc(32, F32)
                top8 = a_alloc(8, F32)
                selm = a_alloc(32, BF16)
                selT = a_alloc(128, BF16)
                pcount = {"n": 0, "br": 0}
                triB = cbv("tri")
                oldB = cbv("old")
                maskC3 = cbv("maskC").rearrange("p (g q) -> p g q", g=16)
                VM3 = cbv("VM").rearrange("p (g b) -> p g b", g=16)
                AC3 = cfv("AC").rearrange("p (g b) -> p g b", g=16)
                E3 = cbv("E").rearrange("p (k c) -> p k c", k=16)
                P.op("dve", lambda e: e.memset(selT, 0.0), writes=[("A", "selT")])
                P.op("dve", lambda e: e.memset(selT[32:33, :], 1.0), reads=[("A", "selT")], writes=[("A", "selT")])

                def hb(ap2):
                    return bass.AP(ap2.tensor, ap2.offset, [list(ap2.ap[0]), [0, 4], [1, 128]])

                def stage_a(pr):
                    kind, t, g, gt, kt = pr["kind"], pr["t"], pr["g"], pr["gt"], pr["kt"]
                    n = pcount["n"]
                    pcount["n"] += 1
                    sb_ = (n % 2) * 2
                    pt = pexp[n % 3]
                    pk = ("A", "pexp", n % 3)
                    rows = slice(g * 64, (g + 1) * 64)
                    biases = []
                    if kind == "cmp":
                        kT_ap, kkeys = KcTz[:, g, :], ["KcT"]
                        biases.append((ident, hb(maskC3[:, gt, :]), [("const", "cb")]))
                    elif kind == "win":
                        kT_ap, kkeys = kwT[g][:, kt * 128:(kt + 1) * 128], [("kwT", kt)]
                        if kt == gt:
                            biases.append((ident, hb(triB), [("const", "cb")]))
                        elif kt == gt - 4:
                            biases.append((ident, hb(oldB), [("const", "cb")]))
                    else:
                        kT_ap, kkeys = kslT[g][:, kt * 128:(kt + 1) * 128], [("kslT", kt)]
                        biases.append((E3[:, kt, :], hb(selT), [("const", "cb"), ("A", "selT")]))
                        if kt == gt:
                            biases.append((ident, hb(triB), [("const", "cb")]))
                    for half in range(2):
                        for bi, (bl, br_, bkeys) in enumerate(biases):
                            P.op("pe", lambda e: e.matmul(bank(sb_ + half), lhsT=bl, rhs=br_, start=(bi == 0), stop=False),
                                 reads=bkeys, writes=[PS(sb_ + half)])
                        P.op("pe", lambda e: e.matmul(
                            bank(sb_ + half), lhsT=kT_ap,
                            rhs=nqT3[:, half * 4:(half + 1) * 4, t * 128:(t + 1) * 128],
                            start=(len(biases) == 0), stop=True),
                            reads=kkeys + [("A", "nqT", t)], writes=[PS(sb_ + half)])
                    P.op("act", lambda e: e.activation(out=pt, in_=psum[:, sb_ * 512:(sb_ + 2) * 512], func=AF.Exp, scale=0.125),
                         reads=[PS(sb_), PS(sb_ + 1)], writes=[pk])
                    pr["pt"], pr["pk"] = pt, pk

                def evac_branch(g, t, br, ncol, first_branch, ob0):
                    o4 = bass.AP(psum, ob0 * 512, [[4096, 128], [512, 2], [ncol, 4], [1, 64]])
                    d4 = bass.AP(psum, ob0 * 512 + 64, [[4096, 128], [512, 2], [ncol, 4]])
                    den3 = den.rearrange("p (a b) -> p a b", a=2)
                    OB = [PS(ob0), PS(ob0 + 1)]
                    P.op("dve", lambda e: e.tensor_scalar(out=den3, in0=d4, scalar1=1e-30, scalar2=None, op0=ALU.max),
                         reads=OB, writes=[("A", "den")])
                    P.op("dve", lambda e: e.reciprocal(out=den, in_=den), reads=[("A", "den")], writes=[("A", "den")])
                    if br == 0:
                        i4 = bass.AP(psum, ob0 * 512 + 65, [[4096, 128], [512, 2], [97, 4], [1, 32]])
                        db = bass.AP(den.tensor, den.offset, [list(den.ap[0]), [4, 2], [1, 4], [0, 32]])
                        gt = j * NT + t
                        P.op("dve", lambda e: e.tensor_tensor(out=impt.rearrange("p (a b c) -> p a b c", a=2, b=4), in0=i4, in1=db, op=ALU.mult),
                             reads=OB + [("A", "den")], writes=[("A", "impt")])
                        P.op("dve", lambda e: e.tensor_reduce(out=imp, in_=impt.rearrange("p (h c) -> p c h", h=8),
                                                              op=ALU.add, axis=mybir.AxisListType.X),
                             reads=[("A", "impt")], writes=[("A", "imp")])
                        P.op("dve", lambda e: e.tensor_tensor(out=imp, in0=imp, in1=VM3[:, gt, :], op=ALU.mult),
                             reads=[("A", "imp"), ("const", "cb")], writes=[("A", "imp")])
                        P.op("dve", lambda e: e.tensor_tensor(out=imp, in0=imp, in1=AC3[:, gt, :], op=ALU.add),
                             reads=[("A", "imp"), ("const", "cf")], writes=[("A", "imp")])
                        P.op("dve", lambda e: e.max(out=top8, in_=imp), reads=[("A", "imp")], writes=[("A", "top8")])
                        P.op("dve", lambda e: e.tensor_scalar(out=selm, in0=imp, scalar1=top8[:, 7:8], scalar2=None, op0=ALU.is_ge),
                             reads=[("A", "imp"), ("A", "top8")], writes=[("A", "selm")])
                        n = pcount["n"]
                        pcount["n"] += 1
                        tbk = (n % 2) * 2
                        P.op("pe", lambda e: e.transpose(out=bankb(tbk, 128, 0)[0:32, :], in_=selm, identity=ident),
                             reads=[("A", "selm"), ("const", "cb")], writes=[PS(tbk)])
                        P.op("dve", lambda e: e.tensor_copy(out=selT[0:32, :], in_=bankb(tbk, 128, 0)[0:32, :]), reads=[PS(tbk)], writes=[("A", "selT")])
                    gcol = br * 16 + g * 8
                    P.op("dve", lambda e: e.tensor_tensor(out=fac, in0=den, in1=sig3[:, t, gcol:gcol + 8], op=ALU.mult),
                         reads=[("A", "den"), ("A", "sig", t)], writes=[("A", "fac")])
                    fb = bass.AP(fac.tensor, fac.offset, [list(fac.ap[0]), [4, 2], [1, 4], [0, 64]])
                    dst = onsa4[:, :, g, :].rearrange("p (a b) d -> p a b d", a=2)
                    if first_branch:
                        P.op("dve", lambda e: e.tensor_tensor(out=dst, in0=o4, in1=fb, op=ALU.mult),
                             reads=OB + [("A", "fac")], writes=[("A", "onsa", g)])
                    else:
                        ot = otmp.rearrange("p (a b d) -> p a b d", a=2, b=4)
                        P.op("dve", lambda e: e.tensor_tensor(out=ot, in0=o4, in1=fb, op=ALU.mult),
                             reads=OB + [("A", "fac")], writes=[("A", "otmp")])
                        P.op("dve", lambda e: e.tensor_tensor(out=dst, in0=dst, in1=ot, op=ALU.add),
                             reads=[("A", "otmp"), ("A", "onsa", g)], writes=[("A", "onsa", g)])

                def stage_b(pr):
                    kind, t, g, gt, kt = pr["kind"], pr["t"], pr["g"], pr["gt"], pr["kt"]
                    if kind == "fin":
                        P.op("act", lambda e: e.copy(out=onsab, in_=onsa), reads=[("A", "onsa", 0), ("A", "onsa", 1)], writes=[("A", "onsab")])
                        n = pcount["n"]
                        pcount["n"] += 1
                        tbk = (n % 2) * 2
                        transposes(lambda k: onsab[:, k * 128:(k + 1) * 128], 8, tbk, [("A", "onsab")])
                        P.op("dve", lambda e: e.tensor_copy(out=onsaT[:, :, t * 128:(t + 1) * 128],
                                                            in_=bankb(tbk).rearrange("p (k c) -> p k c", k=8)),
                             reads=[PS(tbk)], writes=[("onsaT", t)])
                        return
                    if pr["first"]:
                        pr["ob0"] = 4 + 2 * (pcount["br"] % 2)
                        pcount["br"] += 1
                        cur["ob0"] = pr["ob0"]
                    ob0 = cur["ob0"]
                    pt, pk = pr["pt"], pr["pk"]
                    if kind == "cmp":
                        v_ap, vkeys, ncol = VcA[:, g, :], ["VcA"], 97
                    elif kind == "win":
                        v_ap, vkeys, ncol = vwA[:, kt, g, :], [("vwA", kt)], 65
                    else:
                        v_ap, vkeys, ncol = vslA[:, kt, g, :], [("vslA", kt)], 65
                    for h in range(8):
                        ob = ob0 + h // 4
                        P.op("pe", lambda e: e.matmul(
                            bank(ob, ncol, (h % 4) * ncol), lhsT=pt[:, h * 128:(h + 1) * 128], rhs=v_ap,
                            start=(pr["first"] and h % 4 == 0), stop=pr["last"], skip_group_check=True),
                            reads=[pk] + vkeys, writes=[PS(ob)])
                    if pr["last"]:
                        evac_branch(g, t, {"cmp": 0, "sel": 1, "win": 2}[kind], ncol, kind == "cmp", ob0)

                cur = {}
                plist = []
                for t in range(NT):
                    gt = j * NT + t
                    for g in range(2):
                        plist.append(dict(kind="cmp", t=t, g=g, gt=gt, kt=None, first=True, last=True))
                        kts = list(range(max(0, gt - 4), gt + 1))
                        for ii, kt in enumerate(kts):
                            plist.append(dict(kind="win", t=t, g=g, gt=gt, kt=kt, first=(ii == 0), last=(ii == len(kts) - 1)))
                        for kt in range(gt + 1):
                            plist.append(dict(kind="sel", t=t, g=g, gt=gt, kt=kt, first=(kt == 0), last=(kt == gt)))
                    plist.append(dict(kind="fin", t=t, g=None, gt=gt, kt=None))
                SKEW = 1
                for ii in range(len(plist) + SKEW):
                    if ii < len(plist) and plist[ii]["kind"] != "fin":
                        stage_a(plist[ii])
                    if ii - SKEW >= 0:
                        stage_b(plist[ii - SKEW])
                do_dump("onsaT", onsaT[:].rearrange("p k t -> p (k t)"), [("onsaT", t) for t in range(NT)], [128, 8 * TB], BF16)
                if stages < 5:
                    continue

                a_reset()
                S3 = [dict(rqk=a_alloc(NT * 256, BF16), rv=a_alloc(NT * 256, BF16)) for _ in range(3)]
                S2 = [dict(qT=a_alloc(TB, BF16), kT=a_alloc(TB, BF16), qxT=a_alloc(TB, BF16), kz=a_alloc(NT * 128, BF16),
                           inT=a_alloc(NT * 128, BF16), rbc=a_alloc(NT * 256, BF16)) for _ in range(2)]
                S2b = [dict(osb=a_alloc(NT * 256, F32), y=a_alloc(NT * 256, BF16), st=a_alloc(32, F32)) for _ in range(2)]
                gsg = [a_alloc(NT * 512, BF16) for _ in range(2)]
                rt1s = [a_alloc(256, F32), a_alloc(256, F32)]
                rt2s = [a_alloc(256, F32), a_alloc(256, F32)]
                decT = cbv("decayT").rearrange("p (h n) -> p h n", h=8)
                xi3 = cbv("xi").rearrange("p (h n) -> p h n", h=8)
                zs = cfv("zs")
                gng = cfv("gn_g")
                log_g = [math.log(1.0 - 2.0 ** (-5.0 - h)) for h in range(8)]
                gch = [math.exp(128.0 * lg) for lg in log_g]

                def ret_s0(h):
                    A3 = S3[h % 3]
                    K3 = lambda n, *x: ("A", n, h % 3) + tuple(x)
                    hp = h // 2
                    if h % 2 == 0:
                        w, wk = wload("B%d" % hp)
                        w3g = w[:].rearrange("p (k c) -> p k c", k=8)
                        gs = gsg[hp % 2].rearrange("p (t c) -> p t c", t=NT)
                        for t in range(NT):
                            b = t % 2
                            for k in range(8):
                                P.op("pe", lambda e: e.matmul(bank(b), lhsT=hT[:, k, t * 128:(t + 1) * 128], rhs=w3g[:, k, :],
                                                              start=(k == 0), stop=(k == 7)),
                                     reads=[("hT", t), wk], writes=[PS(b)])
                            P.op("act", lambda e: e.activation(out=gs[:, t, :], in_=bank(b), func=AF.Silu),
                                 reads=[PS(b)], writes=[("A", "gsg", hp % 2, t)])
                    w, wk = wload("A%d" % h)
                    w3a = w[:].rearrange("p (k c) -> p k c", k=8)
                    rqk3 = A3["rqk"].rearrange("p (t c) -> p t c", t=NT)
                    rv3 = A3["rv"].rearrange("p (t c) -> p t c", t=NT)
                    for t in range(NT):
                        b = t % 2
                        for k in range(8):
                            P.op("pe", lambda e: e.matmul(bank(b), lhsT=hT[:, k, t * 128:(t + 1) * 128], rhs=w3a[:, k, :],
                                                          start=(k == 0), stop=(k == 7)),
                                 reads=[("hT", t), wk], writes=[PS(b)])
                        rope(None, bank(b, 256), 2, 128, cosR[:, t, :], sinR[:, t, :], rqk3[:, t, :], rt1s[t % 2], rt2s[t % 2],
                             [PS(b)], [K3("rqk", t)], [("A", "rt1", t % 2), ("A", "rt2", t % 2)])
                        P.op("act", lambda e: e.copy(out=rv3[:, t, :], in_=bank(b, 256, 256)),
                             reads=[PS(b)], writes=[K3("rv", t)])

                def ret_s1(h):
                    A3 = S3[h % 3]
                    B = S2[h % 2]
                    K3 = lambda n, *x: ("A", n, h % 3) + tuple(x)
                    K = lambda n: ("A", n, h % 2)
                    rqk3 = A3["rqk"].rearrange("p (t c) -> p t c", t=NT)
                    rv3 = A3["rv"].rearrange("p (t c) -> p t c", t=NT)
                    for which in range(2):
                        for t in range(NT):
                            P.op("pe", lambda e: e.transpose(out=bankb(7, 128, (which * NT + t) * 128),
                                                             in_=rqk3[:, t, which * 128:(which + 1) * 128], identity=ident),
                                 reads=[K3("rqk", t), ("const", "cb")], writes=[PS(7)])
                    P.op("act", lambda e: e.copy(out=B["qT"], in_=bankb(7, 512, 0)), reads=[PS(7)], writes=[K("qT")])
                    P.op("dve", lambda e: e.tensor_tensor(out=B["qxT"].rearrange("p (t n) -> p t n", t=NT),
                                                          in0=bankb(7, 512, 0).rearrange("p (t n) -> p t n", t=NT),
                                                          in1=bass.AP(xi3.tensor, xi3[:, h, :].offset, [list(xi3.ap[0]), [0, NT], [1, 128]]),
                                                          op=ALU.mult),
                         reads=[PS(7), ("const", "cb")], writes=[K("qxT")])
                    P.op("act", lambda e: e.copy(out=B["kT"], in_=bankb(7, 512, 512)), reads=[PS(7)], writes=[K("kT")])
                    P.op("dve", lambda e: e.tensor_scalar(out=B["kz"].rearrange("p (t d) -> p t d", t=NT), in0=rqk3[:, :, 128:256],
                                                          scalar1=zs[:, h:h + 1], scalar2=None, op0=ALU.mult),
                         reads=[K3("rqk", t_) for t_ in range(NT)] + [("const", "cf")], writes=[K("kz")])
                    for t in range(NT):
                        P.op("pe", lambda e: e.matmul(bank(2, 128, t * 128), lhsT=B["kT"][:, t * 128:(t + 1) * 128],
                                                      rhs=B["qT"][:, t * 128:(t + 1) * 128], start=(t == 0), stop=True,
                                                      skip_group_check=True),
                             reads=[K("kT"), K("qT")], writes=[PS(2)])
                    for t in range(NT):
                        rbk = 3 + t // 2
                        P.op("pe", lambda e: e.matmul(bank(rbk, 256, (t % 2) * 256), lhsT=B["kz"][:, t * 128:(t + 1) * 128],
                                                      rhs=rv3[:, t, :], start=(t % 2 == 0), stop=True, skip_group_check=True),
                             reads=[K("kz"), K3("rv", t)], writes=[PS(rbk)])
                    P.op("dve", lambda e: e.tensor_tensor(out=B["inT"].rearrange("p (t n) -> p t n", t=NT),
                                                          in0=bank(2).rearrange("p (t n) -> p t n", t=NT),
                                                          in1=bass.AP(decT.tensor, decT[:, h, :].offset, [list(decT.ap[0]), [0, NT], [1, 128]]),
                                                          op=ALU.mult),
                         reads=[PS(2), ("const", "cb")], writes=[K("inT")])
                    rbc3 = B["rbc"].rearrange("p (t e) -> p t e", t=NT)
                    for t in range(NT):
                        rbk = 3 + t // 2
                        if t == 0:
                            P.op("act", lambda e: e.copy(out=rbc3[:, 0, :], in_=Rst[:, h, :]), reads=[("R", h)], writes=[K("rbc")])
                        P.op("dve", lambda e: e.scalar_tensor_tensor(out=Rst[:, h, :], in0=Rst[:, h, :], scalar=gch[h],
                                                                     in1=bank(rbk, 256, (t % 2) * 256), op0=ALU.mult, op1=ALU.add),
                             reads=[("R", h), PS(rbk), K("rbc")], writes=[("R", h)])
                        if t < NT - 1:
                            P.op("act", lambda e: e.copy(out=rbc3[:, t + 1, :], in_=Rst[:, h, :]), reads=[("R", h)], writes=[K("rbc")])

                def ret_s2(h):
                    A3 = S3[h % 3]
                    B = S2[h % 2]
                    C = S2b[h % 2]
                    K3 = lambda n, *x: ("A", n, h % 3) + tuple(x)
                    K = lambda n: ("A", n, h % 2)
                    hp = h // 2
                    rv3 = A3["rv"].rearrange("p (t c) -> p t c", t=NT)
                    rbc3 = B["rbc"].rearrange("p (t e) -> p t e", t=NT)
                    osb3 = C["osb"].rearrange("p (t e) -> p t e", t=NT)
                    y3 = C["y"].rearrange("p (t e) -> p t e", t=NT)
                    gs = gsg[hp % 2].rearrange("p (t c) -> p t c", t=NT)
                    stt = C["st"]
                    for t in range(NT):
                        ob = 5 + t // 2
                        oo = (t % 2) * 256
                        P.op("pe", lambda e: e.matmul(bank(ob, 256, oo), lhsT=B["inT"][:, t * 128:(t + 1) * 128], rhs=rv3[:, t, :],
                                                      start=(t % 2 == 0), stop=False, skip_group_check=True),
                             reads=[K("inT"), K3("rv", t)], writes=[PS(ob)])
                        P.op("pe", lambda e: e.matmul(bank(ob, 256, oo), lhsT=B["qxT"][:, t * 128:(t + 1) * 128], rhs=rbc3[:, t, :],
                                                      start=False, stop=True, skip_group_check=True),
                             reads=[K("qxT"), K("rbc")], writes=[PS(ob)])
                    for t in range(NT):
                        ob = 5 + t // 2
                        oo = (t % 2) * 256
                        P.op("act", lambda e: e.activation(out=osb3[:, t, :], in_=bank(ob, 256, oo), func=AF.Copy,
                                                           accum_out=stt[:, t:t + 1]),
                             reads=[PS(ob)], writes=[K("osb"), K("st")])
                        P.op("act", lambda e: e.activation(out=y3[:, t, :], in_=bank(ob, 256, oo), func=AF.Square,
                                                           accum_out=stt[:, 4 + t:5 + t]),
                             reads=[PS(ob)], writes=[K("y"), K("st")])
                    mean = stt[:, 8:12]
                    var = stt[:, 12:16]
                    rs = stt[:, 16:20]
                    nb = stt[:, 20:24]
                    P.op("dve", lambda e: e.tensor_scalar(out=mean, in0=stt[:, 0:4], scalar1=1.0 / 256, scalar2=None, op0=ALU.mult),
                         reads=[K("st")], writes=[K("st")])
                    P.op("dve", lambda e: e.tensor_tensor(out=var, in0=mean, in1=mean, op=ALU.mult), reads=[K("st")], writes=[K("st")])
                    P.op("dve", lambda e: e.scalar_tensor_tensor(out=var, in0=stt[:, 4:8], scalar=1.0 / 256, in1=var, op0=ALU.mult, op1=ALU.subtract),
                         reads=[K("st")], writes=[K("st")])
                    P.op("dve", lambda e: e.tensor_scalar(out=var, in0=var, scalar1=EPS, scalar2=None, op0=ALU.add), reads=[K("st")], writes=[K("st")])
                    P.op("pool", lambda e: e.tensor_tensor(out=rs, in0=var, in1=cfv("mhalf")[:, 0:4], op=ALU.pow),
                         reads=[K("st"), ("const", "cf")], writes=[K("st")])
                    P.op("dve", lambda e: e.scalar_tensor_tensor(out=nb, in0=mean, scalar=-1.0, in1=rs, op0=ALU.mult, op1=ALU.mult),
                         reads=[K("st")], writes=[K("st")])
                    for t in range(NT):
                        P.op("dve", lambda e: e.tensor_scalar(out=y3[:, t, :], in0=osb3[:, t, :], scalar1=rs[:, t:t + 1], scalar2=nb[:, t:t + 1],
                                                              op0=ALU.mult, op1=ALU.add),
                             reads=[K("osb"), K("st"), K("y")], writes=[K("y")])
                    P.op("dve", lambda e: e.tensor_tensor(out=y3, in0=y3, in1=gs[:, :, (h % 2) * 256:(h % 2 + 1) * 256], op=ALU.mult),
                         reads=[K("y")] + [("A", "gsg", hp % 2, t) for t in range(NT)], writes=[K("y")])

                def ret_s3(h):
                    C = S2b[h % 2]
                    K = lambda n: ("A", n, h % 2)
                    y3 = C["y"].rearrange("p (t e) -> p t e", t=NT)
                    for kc in range(2):
                        for t in range(NT):
                            P.op("pe", lambda e: e.transpose(out=bankb(7, 128, (kc * NT + t) * 128),
                                                             in_=y3[:, t, kc * 128:(kc + 1) * 128], identity=ident),
                                 reads=[K("y"), ("const", "cb")], writes=[PS(7)])
                    P.op("dve", lambda e: e.tensor_tensor(out=oretT[:, 2 * h:2 * h + 2, :],
                                                          in0=bankb(7).rearrange("p (k c) -> p k c", k=2),
                                                          in1=bass.AP(gng.tensor, gng[:, 2 * h:2 * h + 2].offset, [list(gng.ap[0]), [1, 2], [0, TB]]),
                                                          op=ALU.mult),
                         reads=[PS(7), ("const", "cf")], writes=[("oretT", h)])

                if j == 0:
                    P.op("pool", lambda e: e.memset(Rst[:], 0.0), reads=[("R", h) for h in range(8)], writes=[("R", h) for h in range(8)])
                for it in range(8 + 3):
                    if it < 8:
                        ret_s0(it)
                    if 0 <= it - 1 < 8:
                        ret_s1(it - 1)
                    if 0 <= it - 2 < 8:
                        ret_s2(it - 2)
                    if 0 <= it - 3 < 8:
                        ret_s3(it - 3)
                do_dump("oretT", oretT[:].rearrange("p k t -> p (k t)"), [("oretT", h) for h in range(8)], [128, 16 * TB], BF16)
                if stages < 6:
                    continue

                a_reset()
                U = a_alloc(16 * TB, BF16)
                U3 = U.rearrange("p (k t) -> p k t", k=16)
                mixT = a_alloc(8 * TB, BF16)
                mix3 = mixT.rearrange("p (k t) -> p k t", k=8)
                xres = a_alloc(NT * 1024, F32)
                xres3 = xres.rearrange("p (t c) -> p t c", t=NT)
                pT = a_alloc(2 * TB, BF16)
                pT3 = pT.rearrange("p (k t) -> p k t", k=2)
                pld = [a_alloc(256, F32), a_alloc(256, F32)]
                pbf = a_alloc(256, BF16)
                mt1 = a_alloc(512, F32)
                mt2 = a_alloc(512, F32)
                hn = a_alloc(1024, BF16)
                junk = a_alloc(1024, BF16)
                ssq = a_alloc(1, F32)
                rstd = a_alloc(1, F32)
                sgA = a_alloc(512, F32)
                for t in range(NT):
                    P.op("sp", (lambda t: lambda e: e.dma_start(out=xres3[:, t, :], in_=x_d.ap()[row0 + t * 128: row0 + (t + 1) * 128, :]))(t),
                         writes=[("A", "xres", t)], dma_sem="x%d" % (t % 2))
                for i in range(4):
                    w, wk = wload("MG%d" % i)
                    w3m = w[:].rearrange("p (k c) -> p k c", k=8)
                    for n in range(4):
                        b = n % 4
                        for k in range(8):
                            P.op("pe", (lambda n, k, b, w3m: lambda e: e.matmul(bank(b), lhsT=w3m[:, k, n * 128:(n + 1) * 128], rhs=hT[:, k, :],
                                                                                start=(k == 0), stop=(k == 7)))(n, k, b, w3m),
                                 reads=HT + [wk], writes=[PS(b)])
                        P.op("act", (lambda i, n, b: lambda e: e.activation(out=U3[:, i * 4 + n, :], in_=bank(b), func=AF.Sigmoid))(i, n, b),
                             reads=[PS(b)], writes=[("A", "U", i * 4 + n)])
                wno = None
                for i in range(4):
                    if i % 2 == 0:
                        wno, wnok = wload("NO%d" % (i // 2), hold=True)
                        iNO = wload.last
                        wno3 = wno[:].rearrange("p (k c) -> p k c", k=8)
                    wro, wrok = wload("RO%d" % i)
                    wro3 = wro[:].rearrange("p (k c) -> p k c", k=16)
                    for n2 in range(2):
                        n = 2 * i + n2
                        br_, bn_ = 4, 5
                        for k in range(16):
                            P.op("pe", (lambda k, n2, wro3: lambda e: e.matmul(bank(4), lhsT=wro3[:, k, n2 * 128:(n2 + 1) * 128], rhs=oretT[:, k, :],
                                                                               start=(k == 0), stop=(k == 15)))(k, n2, wro3),
                                 reads=[("oretT", k // 2), wrok], writes=[PS(4)])
                        cno = (i % 2) * 256 + n2 * 128
                        for k in range(8):
                            P.op("pe", (lambda k, cno, wno3: lambda e: e.matmul(bank(5), lhsT=wno3[:, k, cno:cno + 128], rhs=onsaT[:, k, :],
                                                                                start=(k == 0), stop=(k == 7)))(k, cno, wno3),
                                 reads=[("onsaT", t) for t in range(NT)] + [wnok], writes=[PS(5)])
                        P.op("dve", (lambda n: lambda e: e.tensor_tensor(out=mt1, in0=bank(4), in1=U3[:, n, :], op=ALU.mult))(n),
                             reads=[PS(4), ("A", "U", n)], writes=[("A", "mt1")])
                        P.op("dve", (lambda n: lambda e: e.tensor_tensor(out=mt2, in0=bank(5), in1=U3[:, 8 + n, :], op=ALU.mult))(n),
                             reads=[PS(5), ("A", "U", 8 + n)], writes=[("A", "mt2")])
                        P.op("dve", (lambda n: lambda e: e.tensor_tensor(out=mix3[:, n, :], in0=mt1, in1=mt2, op=ALU.add))(n),
                             reads=[("A", "mt1"), ("A", "mt2")], writes=[("A", "mix", n)])
                    if i % 2 == 1:
                        wrelease(iNO)
                MIX = [("A", "mix", n) for n in range(8)]
                for ch in range(2):
                    w, wk = wload("WO%d" % ch)
                    w3o = w[:].rearrange("p (k c) -> p k c", k=8)
                    for t in range(NT):
                        b = t % 4
                        for k in range(8):
                            P.op("pe", (lambda t, k, b, w3o: lambda e: e.matmul(bank(b), lhsT=mix3[:, k, t * 128:(t + 1) * 128], rhs=w3o[:, k, :],
                                                                                start=(k == 0), stop=(k == 7)))(t, k, b, w3o),
                                 reads=MIX + [wk], writes=[PS(b)])
                        P.op("dve", (lambda t, b, ch: lambda e: e.tensor_tensor(out=xres3[:, t, ch * 512:(ch + 1) * 512], in0=bank(b),
                                                                                in1=xres3[:, t, ch * 512:(ch + 1) * 512], op=ALU.add))(t, b, ch),
                             reads=[PS(b), ("A", "xres", t)], writes=[("A", "xres", t)])
                do_dump("x1", xres, [("A", "xres", t) for t in range(NT)], [128, NT * 1024])
                for t in range(NT):
                    rms_to_hT(xres3[:, t, :], [("A", "xres", t)], "g_mlp", t, 6 + (t % 2), hn, junk, ssq, rstd)
                U4 = U.rearrange("p (a k t) -> p a k t", a=2, k=8)
                for qd in range(4):
                    ub = qd % 2
                    for half in range(2):
                        w, wk = wload("UP%d" % (2 * qd + half))
                        w3u = w[:].rearrange("p (k c) -> p k c", k=8)
                        for n in range(4):
                            b = n % 4
                            for k in range(8):
                                P.op("pe", (lambda n, k, b, w3u: lambda e: e.matmul(bank(b), lhsT=w3u[:, k, n * 128:(n + 1) * 128], rhs=hT[:, k, :],
                                                                                    start=(k == 0), stop=(k == 7)))(n, k, b, w3u),
                                     reads=HT + [wk], writes=[PS(b)])
                            ui = half * 4 + n
                            P.op("act", (lambda b: lambda e: e.activation(out=sgA, in_=bank(b), func=AF.Relu))(b),
                                 reads=[PS(b)], writes=[("A", "sgA")])
                            P.op("dve", (lambda ub, ui: lambda e: e.tensor_tensor(out=U4[:, ub, ui, :], in0=sgA, in1=sgA, op=ALU.mult))(ub, ui),
                                 reads=[("A", "sgA")], writes=[("A", "U", ub * 8 + ui)])
                    for ch in range(2):
                        w, wk = wload("DN%d" % (2 * qd + ch))
                        w3d = w[:].rearrange("p (k c) -> p k c", k=8)
                        for t in range(NT):
                            b = 4 + t % 2
                            for k in range(8):
                                P.op("pe", (lambda t, k, b, w3d, ub: lambda e: e.matmul(bank(b), lhsT=U4[:, ub, k, t * 128:(t + 1) * 128], rhs=w3d[:, k, :],
                                                                                        start=(k == 0), stop=(k == 7)))(t, k, b, w3d, ub),
                                     reads=[("A", "U", ub * 8 + k), wk], writes=[PS(b)])
                            P.op("dve", (lambda t, b, ch: lambda e: e.tensor_tensor(out=xres3[:, t, ch * 512:(ch + 1) * 512], in0=bank(b),
                                                                                    in1=xres3[:, t, ch * 512:(ch + 1) * 512], op=ALU.add))(t, b, ch),
                                 reads=[PS(b), ("A", "xres", t)], writes=[("A", "xres", t)])
                do_dump("x2", xres, [("A", "xres", t) for t in range(NT)], [128, NT * 1024])
                for t in range(NT):
                    pb_ = pld[t % 2]
                    P.op("sp", (lambda t, pb_: lambda e: e.dma_start(out=pb_, in_=p_d.ap()[row0 + t * 128: row0 + (t + 1) * 128, :]))(t, pb_),
                         writes=[("A", "pld", t % 2)], dma_sem="p%d" % (t % 2))
                    P.op("act", (lambda pb_: lambda e: e.copy(out=pbf, in_=pb_))(pb_), reads=[("A", "pld", t % 2)], writes=[("A", "pbf")])
                    tb = 6 + (t % 2)
                    transposes(lambda k: pbf[:, k * 128:(k + 1) * 128], 2, tb, [("A", "pbf")])
                    P.op("dve", (lambda t, tb: lambda e: e.tensor_copy(out=pT3[:, :, t * 128:(t + 1) * 128],
                                                                       in_=bankb(tb, 256).rearrange("p (k c) -> p k c", k=2)))(t, tb),
                         reads=[PS(tb)], writes=[("A", "pT", t)])
                    rms_to_hT(xres3[:, t, :], [("A", "xres", t)], "g_ple", t, 6 + (t % 2), hn, junk, ssq, rstd)
                wpp, wppk = wload("PP", hold=True)
                iPP = wload.last
                wpp3 = wpp[:, 0:2048].rearrange("p (k c) -> p k c", k=2)
                for ch in range(2):
                    w, wk = wload("PG%d" % ch)
                    w3g = w[:].rearrange("p (k c) -> p k c", k=8)
                    for t in range(NT):
                        ba = t % 2
                        bb = 2 + t % 2
                        for k in range(8):
                            P.op("pe", (lambda t, k, ba, w3g: lambda e: e.matmul(bank(ba), lhsT=hT[:, k, t * 128:(t + 1) * 128], rhs=w3g[:, k, :],
                                                                                 start=(k == 0), stop=(k == 7)))(t, k, ba, w3g),
                                 reads=[("hT", t), wk], writes=[PS(ba)])
                        for k in range(2):
                            P.op("pe", (lambda t, k, bb, ch: lambda e: e.matmul(bank(bb), lhsT=pT3[:, k, t * 128:(t + 1) * 128],
                                                                                rhs=wpp3[:, k, ch * 512:(ch + 1) * 512],
                                                                                start=(k == 0), stop=(k == 1)))(t, k, bb, ch),
                                 reads=[("A", "pT", t), wppk], writes=[PS(bb)])
                        P.op("act", (lambda ba: lambda e: e.activation(out=sgA, in_=bank(ba), func=AF.Sigmoid))(ba),
                             reads=[PS(ba)], writes=[("A", "sgA")])
                        P.op("dve", (lambda bb: lambda e: e.tensor_tensor(out=mt1, in0=bank(bb), in1=sgA, op=ALU.mult))(bb),
                             reads=[PS(bb), ("A", "sgA")], writes=[("A", "mt1")])
                        P.op("dve", (lambda t, ch: lambda e: e.tensor_tensor(out=xres3[:, t, ch * 512:(ch + 1) * 512], in0=mt1,
                                                                              in1=xres3[:, t, ch * 512:(ch + 1) * 512], op=ALU.add))(t, ch),
                             reads=[("A", "mt1"), ("A", "xres", t)], writes=[("A", "xres", t)])
                wrelease(iPP)
                gfin = cfv("g_final")
                for t in range(NT):
                    P.op("act", (lambda t: lambda e: e.activation(out=junk, in_=xres3[:, t, :], func=AF.Square, accum_out=ssq))(t),
                         reads=[("A", "xres", t)], writes=[("A", "junk"), ("A", "ssq")])
                    P.op("dve", lambda e: e.tensor_scalar(out=rstd, in0=ssq, scalar1=1.0 / DM, scalar2=EPS, op0=ALU.mult, op1=ALU.add),
                         reads=[("A", "ssq")], writes=[("A", "rstd")])
                    P.op("pool", lambda e: e.tensor_tensor(out=rstd, in0=rstd, in1=cfv("mhalf")[:, 0:1], op=ALU.pow),
                         reads=[("A", "rstd"), ("const", "cf")], writes=[("A", "rstd")])
                    P.op("dve", (lambda t: lambda e: e.scalar_tensor_tensor(out=xres3[:, t, :], in0=xres3[:, t, :], scalar=rstd, in1=gfin,
                                                                            op0=ALU.mult, op1=ALU.mult))(t),
                         reads=[("A", "xres", t), ("A", "rstd"), ("const", "cf")], writes=[("A", "xres", t)])
                    P.op("sp", (lambda t: lambda e: e.dma_start(out=out_d.ap()[row0 + t * 128: row0 + (t + 1) * 128, :], in_=xres3[:, t, :]))(t),
                         reads=[("A", "xres", t)], dma_sem="out")
        info = P.emit()
    return nc, info, dumps


_CACHE = {}


def prepare_inputs(inputs, ncores=NCORES, nseq=None):
    x = np.asarray(inputs["x"], np.float32)
    p = np.asarray(inputs["p"], np.float32)[0]
    pos = np.asarray(inputs["positions"], np.int32)
    B = x.shape[0]
    nseq = B // ncores if nseq is None else nseq
    cf, cb = host_consts(inputs)
    wpack = host_pack(inputs)
    in_maps = []
    for c in range(ncores):
        xs = np.ascontiguousarray(x[c * nseq:(c + 1) * nseq].reshape(nseq * SEQ, DM))
        ps_ = np.ascontiguousarray(p[c * nseq:(c + 1) * nseq].reshape(nseq * SEQ, 256))
        pl = pos[c * nseq:(c + 1) * nseq].reshape(nseq, 16, 128).transpose(2, 0, 1).reshape(128, nseq * 16)
        in_maps.append({"x": xs, "p": ps_, "posl": np.ascontiguousarray(pl), "wpack": wpack, "cf": cf, "cb": cb})
    return in_maps, nseq


def kernel(**inputs):
    in_maps, nseq = prepare_inputs(inputs)
    key = ("full", nseq)
    if key not in _CACHE:
        _CACHE[key] = build_program(nseq=nseq)[0]
    nc = _CACHE[key]
    res = run_bass_kernel_spmd(nc, in_maps, core_ids=list(range(NCORES)))
    outs = [r["out"].reshape(nseq, SEQ, DM) for r in res.results]
    return np.concatenate(outs, axis=0).astype(np.float32)
```

```python
import math
import numpy as np
from contextlib import ExitStack
import concourse.bass as bass
import concourse.mybir as mybir
from concourse.bass_utils import run_bass_kernel_spmd

F32 = mybir.dt.float32
BF16 = mybir.dt.bfloat16
I32 = mybir.dt.int32
AF = mybir.ActivationFunctionType
ALU = mybir.AluOpType

NCORES = 8
SEQ = 2048
DM = 1024
TB = 512
NT = TB // 128
EPS = 1e-6
TWO_PI = 2.0 * math.pi
C1 = 6.28125
C2 = TWO_PI - C1
PIECE = 4096
BIGM = 30000.0


class _Op:
    __slots__ = ("eng", "fn", "deps", "dma_sem", "seg", "token", "needs_inc", "is_dma", "ninc")

    def __init__(self, eng, fn, deps, dma_sem, seg, ninc):
        self.eng = eng
        self.fn = fn
        self.deps = deps
        self.dma_sem = dma_sem
        self.seg = seg
        self.token = None
        self.needs_inc = False
        self.is_dma = dma_sem is not None
        self.ninc = ninc


class _Rec:
    def __init__(self):
        self.call = None

    def __getattr__(self, name):
        def f(*a, **k):
            self.call = (name, a, k)
            return self
        return f


class Prog:
    def __init__(self, nc, stack):
        self.nc = nc
        self.stack = stack
        self.ops = []
        self.last_write = {}
        self.readers = {}
        self.seg = 0
        self.eng_obj = {"pe": nc.tensor, "act": nc.scalar, "dve": nc.vector,
                        "pool": nc.gpsimd, "sp": nc.sync}
        self.sems = {}
        self.dma_sems = {}
        self.fence_ops = set()
        self.ps_last = {}
        self.touch = {}

    def next_segment(self):
        self.seg += 1

    def dma_sem(self, name):
        if name not in self.dma_sems:
            s = self.stack.enter_context(self.nc.semaphore("d_" + name))
            self.dma_sems[name] = [s, 0]
        return name

    def fence(self):
        per_eng = {}
        dmas = set()
        for k in list(self.touch.keys()):
            ids = []
            w = self.last_write.pop(k, None)
            if w is not None:
                ids.append(w)
            ids.extend(self.readers.pop(k, []))
            for i in ids:
                o = self.ops[i]
                if o.is_dma:
                    dmas.add(i)
                else:
                    if per_eng.get(o.eng, -1) < i:
                        per_eng[o.eng] = i
        for i in self.fence_ops:
            o = self.ops[i]
            if o.is_dma:
                dmas.add(i)
            elif per_eng.get(o.eng, -1) < i:
                per_eng[o.eng] = i
        self.fence_ops = set(per_eng.values()) | dmas
        self.touch = {}

    def op(self, eng, fn, reads=(), writes=(), dma_sem=None, ninc=1):
        import os as _os
        _lim = int(_os.environ.get("LIMIT", "0"))
        if _lim and len(self.ops) >= _lim:
            return None
        deps = set()
        lw = self.last_write
        rd = self.readers
        for k in reads:
            w = lw.get(k)
            if w is not None:
                deps.add(w)
            if isinstance(k, tuple) and k[0] == "A" and k not in self.touch:
                deps.update(self.fence_ops)
        for k in writes:
            w = lw.get(k)
            if w is not None:
                deps.add(w)
            r = rd.get(k)
            if r:
                deps.update(r)
            if isinstance(k, tuple) and k[0] == "A" and k not in self.touch:
                deps.update(self.fence_ops)
        idx = len(self.ops)
        for k in tuple(reads) + tuple(writes):
            if isinstance(k, tuple) and k[0] == "ps":
                ent = self.ps_last.setdefault(k, {})
                for e2, i2 in ent.items():
                    if e2 != eng:
                        deps.add(i2)
                ent[eng] = idx
        rec = _Rec()
        fn(rec)
        assert rec.call is not None
        self.ops.append(_Op(eng, rec.call, deps, dma_sem, self.seg, ninc))
        for k in reads:
            if isinstance(k, tuple) and k[0] == "A":
                self.touch[k] = None
            if isinstance(k, tuple) and k[0] == "const":
                continue
            rd.setdefault(k, []).append(idx)
        for k in writes:
            if isinstance(k, tuple) and k[0] == "A":
                self.touch[k] = None
            lw[k] = idx
            rd[k] = []
        return idx

    @staticmethod
    def _skip(do, o):
        return do.eng == "pe" and o.eng == "pe" and not do.is_dma and not o.is_dma

    def emit(self, final_wait_eng="sp"):
        nc = self.nc
        ops = self.ops
        for o in ops:
            for d in o.deps:
                do = ops[d]
                if self._skip(do, o):
                    continue
                do.needs_inc = True
        counters = {}
        for o in ops:
            if o.is_dma:
                ent = self.dma_sems[o.dma_sem]
                ent[1] += 16 * o.ninc
                o.token = (ent[0], ent[1], o.dma_sem)
                o.needs_inc = True
            elif o.needs_inc:
                key = (o.eng, o.seg)
                if key not in self.sems:
                    self.sems[key] = self.stack.enter_context(nc.semaphore("p_%s_%d" % key))
                counters[key] = counters.get(key, 0) + 1
                o.token = (self.sems[key], counters[key], key)
        waited = {e: {} for e in self.eng_obj}
        nwaits = 0
        for o in ops:
            eobj = self.eng_obj[o.eng]
            need = {}
            wd = waited[o.eng]
            for d in o.deps:
                do = ops[d]
                if self._skip(do, o):
                    continue
                sem, val, key = do.token
                if wd.get(key, 0) >= val:
                    continue
                if need.get(key, (None, 0))[1] < val:
                    need[key] = (sem, val)
            for key, (sem, val) in need.items():
                eobj.wait_ge(sem, val)
                wd[key] = val
                nwaits += 1
            mname, margs, mkw = o.fn
            inst = getattr(eobj, mname)(*margs, **mkw)
            if o.is_dma:
                insts = inst if isinstance(inst, (list, tuple)) else [inst]
                assert len(insts) == o.ninc
                for i in insts:
                    i.then_inc(o.token[0], 16)
            elif o.needs_inc:
                inst.then_inc(o.token[0], 1)
        eobj = self.eng_obj[final_wait_eng]
        for name, (sem, val) in self.dma_sems.items():
            if val > 0:
                eobj.wait_ge(sem, val)
        return dict(n_ops=len(ops), n_waits=nwaits, n_sems=len(self.sems) + len(self.dma_sems))


def _layout(names_sizes):
    off = {}
    o = 0
    for n, s in names_sizes:
        off[n] = (o, s)
        o += s
    return off, o


CF_ITEMS = [("g_mix", 8), ("g_mlp", 8), ("g_ple", 8), ("gn_g", 16), ("zs", 8), ("inv", 96),
            ("pek", 32), ("pev", 32), ("AC", 512), ("g_final", 1024), ("mhalf", 8)]
CF_OFF, CF_W = _layout(CF_ITEMS)
CB_ITEMS = [("decayT", 1024), ("xi", 1024), ("tri", 128), ("old", 128), ("maskC", 2048),
            ("VM", 512), ("ident", 128), ("E", 2048), ("ov", 32), ("ck2", 256), ("cv2", 128)]
CB_OFF, CB_W = _layout(CB_ITEMS)


def host_consts(inp):
    f32 = np.float32
    cf = np.zeros((128, CF_W), f32)
    cb = np.zeros((128, CB_W), f32)

    def putf(name, arr):
        o, s = CF_OFF[name]
        cf[:, o:o + s] = np.asarray(arr, f32).reshape(128, s)

    def putb(name, arr):
        o, s = CB_OFF[name]
        cb[:, o:o + s] = np.asarray(arr, f32).reshape(128, s)

    colmaj = lambda g, k: np.asarray(g, f32).reshape(k, 128).T
    putf("g_mix", colmaj(inp["norm_mix_g"][0], 8))
    putf("g_mlp", colmaj(inp["norm_mlp_g"][0], 8))
    putf("g_ple", colmaj(inp["norm_ple_g"][0], 8))
    putf("gn_g", colmaj(inp["ret_gn_g"][0], 16))
    log_g = np.log(1.0 - 2.0 ** (-5.0 - np.arange(8, dtype=f32))).astype(f32)
    idx = np.arange(128, dtype=f32)
    diff = idx[:, None] - idx[None, :]
    decay = np.where(diff[None] >= 0, np.exp(np.maximum(diff, 0.0)[None] * log_g[:, None, None]), 0.0)
    sc = 128.0 ** -0.5
    putb("decayT", np.transpose(decay, (2, 0, 1)) * sc)
    xi = np.exp((idx + 1.0)[None] * log_g[:, None])
    putb("xi", np.broadcast_to(xi[None], (128, 8, 128)))
    zeta = np.exp((127.0 - idx)[None] * log_g[:, None])
    putf("zs", zeta.T * sc)
    inv_r = (f32(10000.0) ** (-np.arange(0, 128, 2, dtype=f32) / f32(128))).astype(f32)
    inv_n = (f32(10000.0) ** (-np.arange(0, 64, 2, dtype=f32) / f32(64))).astype(f32)
    putf("inv", np.broadcast_to(np.concatenate([inv_r, inv_n])[None], (128, 96)))
    pek = np.asarray(inp["cmp_pe_k"][0], f32)
    pev = np.asarray(inp["cmp_pe_v"][0], f32)
    putf("pek", np.concatenate([pek.T, pek.T], 0))
    putf("pev", np.concatenate([pev.T, pev.T], 0))
    putf("g_final", np.broadcast_to(np.asarray(inp["norm_final_g"], f32)[None], (128, 1024)))
    putf("mhalf", np.full((128, 8), -0.5, f32))
    q = np.arange(128)
    putb("tri", np.where(q[:, None] <= q[None, :], 0.0, -BIGM))
    putb("old", np.where(q[:, None] > q[None, :], 0.0, -BIGM))
    slot = np.arange(128)
    c = slot - 1
    gt = np.arange(16)
    t_abs = gt[:, None] * 128 + q[None, :]
    mC = ((16 * c[:, None, None] + 31) <= t_abs[None]) & (slot[:, None, None] >= 1)
    putb("maskC", np.where(mC, 0.0, -BIGM))
    blk = np.arange(32)
    cur = (t_abs.T // 64)
    forced = (blk[None, None] == 0) | (blk[None, None] == cur[..., None]) | (blk[None, None] == cur[..., None] - 1)
    valid = blk[None, None] <= cur[..., None]
    putb("VM", (valid & ~forced).astype(f32))
    putf("AC", np.where(forced, 1e6, np.where(valid, 0.0, -1.0)))
    putb("ident", np.eye(128, dtype=f32))
    E = np.zeros((128, 16, 128), f32)
    key = np.arange(128)
    for kt in range(16):
        for b in range(32):
            E[b, kt, :] = BIGM * (b == 2 * kt + key // 64)
        E[32, kt, :] = -BIGM
    putb("E", E)
    ov = ((16 * c[:, None] < 64 * (blk[None] + 1)) & (16 * c[:, None] + 31 >= 64 * blk[None]) & (slot[:, None] >= 1))
    putb("ov", ov.astype(f32))
    w2k = np.asarray(inp["cmp_k_w2"][0], f32)
    w2v = np.asarray(inp["cmp_v_w2"][0], f32)
    ck2 = np.zeros((128, 2, 128), f32)
    cv2 = np.zeros((128, 2, 64), f32)
    for hh in range(2):
        ck2[:, hh, 0:64] = w2k[hh * 128:(hh + 1) * 128]
        ck2[:, hh, 64:128] = w2k[hh * 128:(hh + 1) * 128]
        cv2[:, hh, :] = w2v[hh * 128:(hh + 1) * 128]
    putb("ck2", ck2)
    putb("cv2", cv2)
    return cf, cb


def piece_names():
    names = ["S1", "S2", "Q1", "Q2", "CK1", "CK2", "CV1", "CV2"]
    for hp in range(4):
        names += ["B%d" % hp, "A%d" % (2 * hp), "A%d" % (2 * hp + 1)]
    names += ["MG0", "MG1", "MG2", "MG3"]
    for i in range(4):
        if i % 2 == 0:
            names.append("NO%d" % (i // 2))
        names.append("RO%d" % i)
    names += ["WO0", "WO1"]
    for qd in range(4):
        names += ["UP%d" % (2 * qd), "UP%d" % (2 * qd + 1), "DN%d" % (2 * qd), "DN%d" % (2 * qd + 1)]
    names += ["PP", "PG0", "PG1"]
    return names


PIECES = piece_names()
PIDX = {n: i for i, n in enumerate(PIECES)}
NP_ = len(PIECES)


def host_pack(inp):
    f32 = np.float32
    W = np.zeros((NP_, 128, PIECE), f32)
    w_in = np.asarray(inp["w_in"][0], f32)
    o = 0
    sl = {}
    for n, s in [("rq", 1024), ("rk", 1024), ("rv", 2048), ("rg", 2048), ("nq", 1024), ("kc", 128),
                 ("vc", 128), ("ksl", 128), ("vsl", 128), ("kw", 128), ("vw", 128), ("ng", 48)]:
        sl[n] = w_in[:, o:o + s]
        o += s

    def kpiece(cols):
        out = np.zeros((128, 8, 512), f32)
        out[:, :, :cols.shape[1]] = cols.reshape(8, 128, -1).transpose(1, 0, 2)
        return out.reshape(128, PIECE)

    W[PIDX["S1"]] = kpiece(np.concatenate([sl["kc"], sl["ksl"], sl["kw"], sl["vc"]], 1))
    W[PIDX["S2"]] = kpiece(np.concatenate([sl["vsl"], sl["vw"], sl["ng"]], 1))
    nq = sl["nq"].reshape(1024, 16, 64)
    order = []
    for i in range(8):
        order += [i, 8 + i]
    nqp = nq[:, order, :].reshape(1024, 1024)
    W[PIDX["Q1"]] = kpiece(nqp[:, 0:512])
    W[PIDX["Q2"]] = kpiece(nqp[:, 512:1024])
    for h in range(8):
        W[PIDX["A%d" % h]] = kpiece(np.concatenate(
            [sl["rq"][:, h * 128:(h + 1) * 128], sl["rk"][:, h * 128:(h + 1) * 128],
             sl["rv"][:, h * 256:(h + 1) * 256]], 1))
    for hp in range(4):
        W[PIDX["B%d" % hp]] = kpiece(sl["rg"][:, hp * 512:(hp + 1) * 512])
    for nm, key in (("CK", "cmp_k_w1"), ("CV", "cmp_v_w1")):
        w1 = np.asarray(inp[key][0], f32).reshape(32, 64, 256)
        for half in range(2):
            blk = w1[half * 16:(half + 1) * 16]
            pc = np.concatenate([blk.transpose(1, 0, 2)] * 2, 0)
            W[PIDX["%s%d" % (nm, half + 1)]] = pc.reshape(128, PIECE)
    wm = np.asarray(inp["w_merge_gate"][0], f32)
    for i in range(4):
        W[PIDX["MG%d" % i]] = kpiece(wm[:, i * 512:(i + 1) * 512])
    wno = np.asarray(inp["w_nsa_o"][0], f32)
    rows = []
    for i in range(8):
        rows += list(range(i * 64, (i + 1) * 64)) + list(range((8 + i) * 64, (9 + i) * 64))
    wno = wno[rows, :]
    for i in range(2):
        W[PIDX["NO%d" % i]] = kpiece(wno[:, i * 512:(i + 1) * 512])
    wro = np.asarray(inp["w_ret_o"][0], f32)
    for i in range(4):
        pc = wro[:, i * 256:(i + 1) * 256].reshape(16, 128, 256).transpose(1, 0, 2)
        W[PIDX["RO%d" % i]] = pc.reshape(128, PIECE)
    wo = np.asarray(inp["w_out"][0], f32)
    for i in range(2):
        W[PIDX["WO%d" % i]] = kpiece(wo[:, i * 512:(i + 1) * 512])
    wu = np.asarray(inp["w_mlp_up"][0], f32)
    wd = np.asarray(inp["w_mlp_down"][0], f32)
    for i in range(8):
        W[PIDX["UP%d" % i]] = kpiece(wu[:, i * 512:(i + 1) * 512])
    for qd in range(4):
        for ch in range(2):
            W[PIDX["DN%d" % (2 * qd + ch)]] = kpiece(wd[qd * 1024:(qd + 1) * 1024, ch * 512:(ch + 1) * 512])
    wg = np.asarray(inp["w_ple_gate"][0], f32)
    for i in range(2):
        W[PIDX["PG%d" % i]] = kpiece(wg[:, i * 512:(i + 1) * 512])
    wp = np.asarray(inp["w_ple_proj"][0], f32)
    pp = np.zeros((128, PIECE), f32)
    pp[:, :2048] = wp.reshape(2, 128, 1024).transpose(1, 0, 2).reshape(128, 2048)
    W[PIDX["PP"]] = pp
    return W


def build_program(nseq=4, nblk=4, dump=None, stages=99):
    nc = bass.Bass("TRN2", target_bir_lowering=False)
    ntok = nseq * SEQ
    x_d = nc.dram_tensor("x", [ntok, DM], F32, kind="ExternalInput")
    p_d = nc.dram_tensor("p", [ntok, 256], F32, kind="ExternalInput")
    pos_d = nc.dram_tensor("posl", [128, nseq * 16], I32, kind="ExternalInput")
    wp_d = nc.dram_tensor("wpack", [NP_, 128, PIECE], F32, kind="ExternalInput")
    cf_d = nc.dram_tensor("cf", [128, CF_W], F32, kind="ExternalInput")
    cb_d = nc.dram_tensor("cb", [128, CB_W], F32, kind="ExternalInput")
    out_d = nc.dram_tensor("out", [ntok, DM], F32, kind="ExternalOutput")
    wbf_d = nc.dram_tensor("wbf", [NP_, 128, PIECE], BF16, kind="ExternalOutput")
    dumps = {}

    st = ExitStack()
    with st:
        P = Prog(nc, st)
        sbt = lambda n, s, d: st.enter_context(nc.sbuf_tensor(n, s, d))
        psum = st.enter_context(nc.psum_tensor("psum", [128, 4096], F32))

        def bank(b, n=512, off=0):
            return psum[:, b * 512 + off: b * 512 + off + n]

        def bankb(b, n=1024, off=0):
            return psum[:, b * 512:(b + 1) * 512].bitcast(BF16)[:, off:off + n]

        PS = lambda b: ("ps", b)

        cf = sbt("cf_s", [128, CF_W], F32)
        cb = sbt("cb_s", [128, CB_W], BF16)
        posi = sbt("posi", [128, nseq * 16], I32)
        posf = sbt("posf", [128, nseq * 16], F32)
        NSLOT = 4
        wring = [sbt("wring%d" % i, [128, PIECE], BF16) for i in range(NSLOT)]
        kslT = [sbt("kslT%d" % g_, [128, SEQ], BF16) for g_ in range(2)]
        kwT = [sbt("kwT%d" % g_, [128, SEQ], BF16) for g_ in range(2)]
        KcTz = sbt("KcTz", [128, 2, 128], BF16)
        vslA = sbt("vslA", [128, 16, 2, 65], BF16)
        vwA = sbt("vwA", [128, 16, 2, 65], BF16)
        kcT = sbt("kcT", [128, 16 + SEQ], BF16)
        vcT = sbt("vcT", [128, 16 + SEQ], BF16)
        hidk = sbt("hidk", [128, 2, 2, 128], BF16)
        hidv = sbt("hidv", [128, 2, 2, 128], BF16)
        VcA = sbt("VcA", [128, 2, 97], BF16)
        Rst = sbt("Rst", [128, 8, 256], F32)
        hT = sbt("hT", [128, 8, TB], BF16)
        oretT = sbt("oretT", [128, 16, TB], BF16)
        onsaT = sbt("onsaT", [128, 8, TB], BF16)
        tabs = sbt("tabs", [128, NT, 2, 96], F32)
        cosR = sbt("cosR", [128, NT, 128], F32)
        sinR = sbt("sinR", [128, NT, 128], F32)
        cosN = sbt("cosN", [128, NT, 64], F32)
        sinN = sbt("sinN", [128, NT, 64], F32)
        ARENA_W = 16 * 1024
        arena = sbt("arena", [128, ARENA_W], F32)
        astate = {"off": 0}

        def cfv(name, *shape):
            o, s = CF_OFF[name]
            v = cf[:, o:o + s]
            return v

        def cbv(name):
            o, s = CB_OFF[name]
            return cb[:, o:o + s]

        def a_reset():
            astate["off"] = 0
            P.fence()

        def a_alloc(n, dtype):
            words = (n * (2 if dtype == BF16 else 4) + 3) // 4
            words = (words + 7) // 8 * 8
            o = astate["off"]
            assert o + words <= ARENA_W, ("arena overflow", o, words)
            astate["off"] = o + words
            v = arena[:, o:o + words]
            if dtype == BF16:
                v = v.bitcast(BF16)
            elif dtype == I32:
                v = v.bitcast(I32)
            return v[:, 0:n]

        for nme in ["cf", "cb", "pos", "x0", "x1", "p0", "p1", "out", "dump"] + ["w%d" % i for i in range(NSLOT)]:
            P.dma_sem(nme)

        def do_dump(name, ap, reads, shape, dtype=F32):
            if dump is None or name not in dump or name in dumps:
                return
            d = nc.dram_tensor("dump_" + name, list(shape), dtype, kind="ExternalOutput")
            dumps[name] = d
            P.op("sp", lambda e: e.dma_start(out=d.ap(), in_=ap), reads=reads, dma_sem="dump")

        P.op("sp", lambda e: e.dma_start(out=cf[:], in_=cf_d.ap()), writes=[("const", "cf")], dma_sem="cf")
        P.op("sp", lambda e: e.dma_start(out=posi[:], in_=pos_d.ap()), writes=["posi"], dma_sem="pos")
        P.op("dve", lambda e: e.tensor_copy(out=posf[:], in_=posi[:]), reads=["posi"], writes=[("const", "posf")])
        cbst = a_alloc(CB_W, F32)
        P.op("sp", lambda e: e.dma_start(out=cbst, in_=cb_d.ap()), writes=[("A", "cbst")], dma_sem="cb")
        P.op("dve", lambda e: e.tensor_copy(out=cb[:], in_=cbst), reads=[("A", "cbst")], writes=[("const", "cb")])
        a_reset()
        wst = [a_alloc(PIECE, F32) for _ in range(2)]
        wsb = [a_alloc(PIECE, BF16) for _ in range(2)]
        for i in range(3):
            P.dma_sem("wst%d" % i)
        for i in range(2):
            P.dma_sem("wsb%d" % i)
        import os as _os
        _skip = _os.environ.get("SKIP", "").split(",")
        def _cast_load(i):
            P.op("sp", lambda e: e.dma_start(out=wst[i % 2], in_=wp_d.ap()[i]),
                 writes=[("A", "wst", i % 2)], dma_sem="wst%d" % (i % 2))

        ncast = 0 if "cast" in _skip else NP_
        for i in range(min(2, ncast)):
            _cast_load(i)
        for i in range(ncast):
            a3, b2 = i % 2, i % 2
            if i % 2 == 0:
                P.op("act", lambda e: e.copy(out=wsb[b2], in_=wst[a3]),
                     reads=[("A", "wst", a3)], writes=[("A", "wsb", b2)])
            else:
                P.op("dve", lambda e: e.tensor_copy(out=wsb[b2], in_=wst[a3]),
                     reads=[("A", "wst", a3)], writes=[("A", "wsb", b2)])
            P.op("sp", lambda e: e.dma_start(out=wbf_d.ap()[i], in_=wsb[b2]),
                 reads=[("A", "wsb", b2)], writes=[("wbf", i)], dma_sem="wsb%d" % b2)
            if i + 2 < ncast:
                _cast_load(i + 2)
        for tname, t in (() if "memset" in _skip else (("kcT", kcT), ("vcT", vcT), ("hidk", hidk), ("hidv", hidv))):
            P.op("pool", (lambda t: lambda e: e.memset(t[:], 0.0))(t), writes=[tname])
        P.op("pool", lambda e: e.memset(VcA[:], 0.0), writes=["VcA"])
        P.op("pool", lambda e: e.memset(KcTz[:], 0.0), writes=["KcT"])
        for g_ in range(2):
            P.op("pool", lambda e: e.memset(kslT[g_][:], 0.0), writes=[("kslT", k_) for k_ in range(16)])
            P.op("pool", lambda e: e.memset(kwT[g_][:], 0.0), writes=[("kwT", k_) for k_ in range(16)])
        P.op("dve", lambda e: e.memset(VcA[:, :, 64:65], 1.0), reads=["VcA"], writes=["VcA"])
        for g in range(2):
            P.op("dve", (lambda g: lambda e: e.tensor_copy(out=VcA[:, g, 65:97], in_=cbv("ov")))(g),
                 reads=[("const", "cb"), "VcA"], writes=["VcA"])
        P.op("pool", lambda e: e.memset(vslA[:], 1.0), writes=["vslA"])
        P.op("pool", lambda e: e.memset(vwA[:], 1.0), writes=["vwA"])

        ncut = {1: 0, 2: 4, 3: 8, 4: 8, 5: 20}.get(stages, NP_)
        border = PIECES[:ncut]
        nstream = len(border) * nseq * nblk
        wstate = {"use": 0, "iss": 0}
        released = set()
        held = set()
        pending = []

        def try_issue(upto):
            while wstate["iss"] < min(upto, nstream):
                n = wstate["iss"]
                if n - NSLOT >= 0 and (n - NSLOT) not in released:
                    break
                wstate["iss"] += 1
                slot = n % NSLOT
                pi = PIDX[border[n % len(border)]]
                P.op("sp", lambda e: e.dma_start(out=wring[slot][:], in_=wbf_d.ap()[pi]),
                     reads=[("wbf", pi)], writes=[("wr", slot)], dma_sem="w%d" % slot)

        def wrelease(i):
            held.discard(i)
            released.add(i)
            try_issue(wstate["use"] + NSLOT)

        def wload(name, hold=False):
            i = wstate["use"]
            assert border[i % len(border)] == name, (name, border[i % len(border)])
            for q in list(pending):
                if q not in held:
                    released.add(q)
                    pending.remove(q)
            wstate["use"] += 1
            try_issue(i + NSLOT)
            assert wstate["iss"] > i, ("weight ring deadlock", name, i)
            pending.append(i)
            if hold:
                held.add(i)
            wload.last = i
            return wring[i % NSLOT], ("wr", i % NSLOT)

        CONST = [("const", "cf"), ("const", "cb"), ("const", "posf")]
        ident = cbv("ident")

        def transposes(src_fn, n, tb, keys_r):
            for k in range(n):
                P.op("pe", (lambda k: lambda e: e.transpose(out=bankb(tb, 128, k * 128), in_=src_fn(k), identity=ident))(k),
                     reads=keys_r + [("const", "cb")], writes=[PS(tb)])

        def bc(ap2d, dims):
            return bass.AP(ap2d.tensor, ap2d.offset, [list(ap2d.ap[0])] + [list(d) for d in dims])

        def rms_to_hT(src_ap, src_keys, gname, t, tb, hn, junk, ssq, rstd):
            i2 = t % 2
            hn, junk, ssq, rstd = hn[i2], junk[i2], ssq[i2], rstd[i2]
            P.op("act", lambda e: e.activation(out=junk, in_=src_ap, func=AF.Square, accum_out=ssq),
                 reads=src_keys, writes=[("A", "junk"), ("A", "ssq", i2)])
            P.op("dve", lambda e: e.tensor_scalar(out=rstd, in0=ssq, scalar1=1.0 / DM, scalar2=EPS,
                                                  op0=ALU.mult, op1=ALU.add),
                 reads=[("A", "ssq", i2)], writes=[("A", "rstd", i2)])
            P.op("pool", lambda e: e.tensor_tensor(out=rstd, in0=rstd, in1=cfv("mhalf")[:, 0:1], op=ALU.pow),
                 reads=[("A", "rstd", i2), ("const", "cf")], writes=[("A", "rstd", i2)])
            P.op("act", lambda e: e.activation(out=hn, in_=src_ap, func=AF.Copy, scale=rstd),
                 reads=src_keys + [("A", "rstd", i2)], writes=[("A", "hn", i2)])
            transposes(lambda k: hn[:, k * 128:(k + 1) * 128], 8, tb, [("A", "hn", i2)])
            g = cfv(gname)
            P.op("dve", lambda e: e.tensor_tensor(
                out=hT[:, :, t * 128:(t + 1) * 128],
                in0=bankb(tb).rearrange("p (k c) -> p k c", k=8),
                in1=bc(g, [[1, 8], [0, 128]]), op=ALU.mult),
                reads=[PS(tb), ("const", "cf")], writes=[("hT", t)])

        def rope(e_unused, psv, nh, hd, cosv, sinv, outv, tmp1, tmp2, rkeys, wkeys, tkeys):
            h2 = hd // 2
            x3 = psv.rearrange("p (h d) -> p h d", h=nh)
            P.op("dve", lambda e: e.tensor_tensor(out=tmp1.rearrange("p (h d) -> p h d", h=nh), in0=x3,
                                                  in1=bc(cosv, [[0, nh], [1, hd]]), op=ALU.mult),
                 reads=rkeys + ["TABS"], writes=[tkeys[0]])
            t23 = tmp2.rearrange("p (h d) -> p h d", h=nh)
            P.op("dve", lambda e: e.tensor_tensor(out=t23[:, :, 0:h2], in0=x3[:, :, h2:hd],
                                                  in1=bc(sinv[:, 0:h2], [[0, nh], [1, h2]]), op=ALU.mult),
                 reads=rkeys + ["TABS"], writes=[tkeys[1]])
            P.op("dve", lambda e: e.tensor_tensor(out=t23[:, :, h2:hd], in0=x3[:, :, 0:h2],
                                                  in1=bc(sinv[:, h2:hd], [[0, nh], [1, h2]]), op=ALU.mult),
                 reads=rkeys + ["TABS"], writes=[tkeys[1]])
            P.op("dve", lambda e: e.tensor_tensor(out=outv, in0=tmp1, in1=tmp2, op=ALU.add),
                 reads=list(tkeys), writes=wkeys)

        for s in range(nseq if stages > 0 else 0):
            for j in range(nblk):
                P.next_segment()
                row0 = s * SEQ + j * TB
                T0 = j * TB
                a_reset()
                xt = [a_alloc(1024, F32), a_alloc(1024, F32)]
                hn = [a_alloc(1024, BF16), a_alloc(1024, BF16)]
                junk = [a_alloc(1024, BF16)] * 2
                ssq = [a_alloc(1, F32), a_alloc(1, F32)]
                rstd = [a_alloc(1, F32), a_alloc(1, F32)]
                ang = a_alloc(NT * 2 * 96, F32)
                angk = a_alloc(NT * 2 * 96, F32)
                angi = a_alloc(NT * 2 * 96, I32)
                ang4 = ang.rearrange("p (t a f) -> p t a f", t=NT, a=2)
                inv = cfv("inv")
                for t in range(0 if "tabs" in _skip else NT):
                    col = s * 16 + j * NT + t
                    P.op("dve", (lambda t, col: lambda e: e.tensor_scalar(
                        out=ang4[:, t, 0, :], in0=inv, scalar1=posf[:, col:col + 1], scalar2=None, op0=ALU.mult))(t, col),
                        reads=CONST, writes=[("A", "ang")])
                    P.op("dve", (lambda t, col: lambda e: e.tensor_scalar(
                        out=ang4[:, t, 1, :], in0=inv, scalar1=posf[:, col:col + 1], scalar2=math.pi / 2,
                        op0=ALU.mult, op1=ALU.add))(t, col),
                        reads=CONST, writes=[("A", "ang")])
                P.op("dve", lambda e: e.tensor_scalar(out=angk, in0=ang, scalar1=1.0 / TWO_PI, scalar2=None, op0=ALU.mult),
                     reads=[("A", "ang")], writes=[("A", "angk")])
                P.op("dve", lambda e: e.tensor_copy(out=angi, in_=angk), reads=[("A", "angk")], writes=[("A", "angi")])
                P.op("dve", lambda e: e.tensor_copy(out=angk, in_=angi), reads=[("A", "angi")], writes=[("A", "angk")])
                P.op("dve", lambda e: e.scalar_tensor_tensor(out=ang, in0=angk, scalar=-C1, in1=ang, op0=ALU.mult, op1=ALU.add),
                     reads=[("A", "angk"), ("A", "ang")], writes=[("A", "ang")])
                P.op("dve", lambda e: e.scalar_tensor_tensor(out=ang, in0=angk, scalar=-C2, in1=ang, op0=ALU.mult, op1=ALU.add),
                     reads=[("A", "angk"), ("A", "ang")], writes=[("A", "ang")])
                P.op("dve", lambda e: e.tensor_scalar(out=ang, in0=ang, scalar1=3.1415925, scalar2=-3.1415925,
                                                      op0=ALU.min, op1=ALU.max),
                     reads=[("A", "ang")], writes=[("A", "ang")])
                P.op("act", lambda e: e.activation(out=tabs[:].rearrange("p t a f -> p (t a f)"), in_=ang, func=AF.Sin),
                     reads=[("A", "ang")], writes=["tabs0"])
                P.op("dve", lambda e: e.tensor_copy(out=cosR[:, :, 0:64], in_=tabs[:, :, 1, 0:64]), reads=["tabs0"], writes=["TABS"])
                P.op("dve", lambda e: e.tensor_copy(out=cosR[:, :, 64:128], in_=tabs[:, :, 1, 0:64]), reads=["tabs0"], writes=["TABS"])
                P.op("dve", lambda e: e.tensor_scalar(out=sinR[:, :, 0:64], in0=tabs[:, :, 0, 0:64], scalar1=-1.0, scalar2=None, op0=ALU.mult),
                     reads=["tabs0"], writes=["TABS"])
                P.op("dve", lambda e: e.tensor_copy(out=sinR[:, :, 64:128], in_=tabs[:, :, 0, 0:64]), reads=["tabs0"], writes=["TABS"])
                P.op("dve", lambda e: e.tensor_copy(out=cosN[:, :, 0:32], in_=tabs[:, :, 1, 64:96]), reads=["tabs0"], writes=["TABS"])
                P.op("dve", lambda e: e.tensor_copy(out=cosN[:, :, 32:64], in_=tabs[:, :, 1, 64:96]), reads=["tabs0"], writes=["TABS"])
                P.op("dve", lambda e: e.tensor_scalar(out=sinN[:, :, 0:32], in0=tabs[:, :, 0, 64:96], scalar1=-1.0, scalar2=None, op0=ALU.mult),
                     reads=["tabs0"], writes=["TABS"])
                P.op("dve", lambda e: e.tensor_copy(out=sinN[:, :, 32:64], in_=tabs[:, :, 0, 64:96]), reads=["tabs0"], writes=["TABS"])
                if dump and "tabs" in dump:
                    do_dump("tabs", tabs[:].rearrange("p t a f -> p (t a f)"), ["tabs0"], [128, NT * 2 * 96])

                for t in range(0 if "norm" in _skip else NT):
                    xb = xt[t % 2]
                    P.op("sp", (lambda t, xb: lambda e: e.dma_start(out=xb, in_=x_d.ap()[row0 + t * 128: row0 + (t + 1) * 128, :]))(t, xb),
                         writes=[("A", "xt", t % 2)], dma_sem="x%d" % (t % 2))
                    rms_to_hT(xb, [("A", "xt", t % 2)], "g_mix", t, 6 + (t % 2), hn, junk, ssq, rstd)
                do_dump("hT", hT[:].rearrange("p k t -> p (k t)"), [("hT", t) for t in range(NT)], [128, 8 * TB], BF16)
                HT = [("hT", t) for t in range(NT)]
                if stages < 2:
                    continue

                sm = a_alloc(512, BF16)
                tmp1s = [a_alloc(512, F32), a_alloc(512, F32)]
                tmp2s = [a_alloc(512, F32), a_alloc(512, F32)]
                nq_tm = a_alloc(1024, BF16)
                nqT = a_alloc(8 * TB, BF16)
                nqT3 = nqT.rearrange("p (i t) -> p i t", i=8)
                sig = a_alloc(NT * 48, F32)
                sig3 = sig.rearrange("p (t c) -> p t c", t=NT)
                w, wk = wload("S1")
                w3 = w[:].rearrange("p (k c) -> p k c", k=8)
                for t in range(NT):
                    gtile = j * NT + t
                    b = t % 4
                    for k in range(8):
                        P.op("pe", (lambda t, k, b: lambda e: e.matmul(bank(b), lhsT=hT[:, k, t * 128:(t + 1) * 128], rhs=w3[:, k, :],
                                                                       start=(k == 0), stop=(k == 7)))(t, k, b),
                             reads=[("hT", t), wk], writes=[PS(b)])
                    rope(None, bank(b, 384), 6, 64, cosN[:, t, :], sinN[:, t, :], sm[:, 0:384],
                         tmp1s[t % 2][:, 0:384], tmp2s[t % 2][:, 0:384], [PS(b)], [("A", "sm")], [("A", "tmp1", t % 2), ("A", "tmp2", t % 2)])
                    P.op("act", (lambda b: lambda e: e.copy(out=sm[:, 384:512], in_=bank(b, 128, 384)))(b),
                         reads=[PS(b)], writes=[("A", "sm2")])
                    tb = 6 + (t % 2)
                    transposes(lambda k: sm[:, k * 128:(k + 1) * 128], 4, tb, [("A", "sm"), ("A", "sm2")])
                    c0 = T0 + t * 128
                    P.op("dve", (lambda tb, c0: lambda e: e.tensor_copy(out=kcT[:, 16 + c0:16 + c0 + 128], in_=bankb(tb, 128, 0)))(tb, c0),
                         reads=[PS(tb)], writes=["kcT"])
                    for g_ in range(2):
                        rs_ = slice(g_ * 64, (g_ + 1) * 64)
                        P.op("dve", lambda e: e.tensor_copy(out=kslT[g_][rs_, c0:c0 + 128], in_=bankb(tb, 128, 128)[rs_, :]),
                             reads=[PS(tb)], writes=[("kslT", gtile)])
                        P.op("dve", lambda e: e.tensor_copy(out=kwT[g_][rs_, c0:c0 + 128], in_=bankb(tb, 128, 256)[rs_, :]),
                             reads=[PS(tb)], writes=[("kwT", gtile)])
                    P.op("dve", (lambda tb, c0: lambda e: e.tensor_copy(out=vcT[:, 16 + c0:16 + c0 + 128], in_=bankb(tb, 128, 384)))(tb, c0),
                         reads=[PS(tb)], writes=["vcT"])
                w, wk = wload("S2")
                w3b = w[:].rearrange("p (k c) -> p k c", k=8)
                for t in range(NT):
                    gtile = j * NT + t
                    b = t % 4
                    for k in range(8):
                        P.op("pe", (lambda t, k, b, w3b: lambda e: e.matmul(bank(b, 304), lhsT=hT[:, k, t * 128:(t + 1) * 128], rhs=w3b[:, k, 0:304],
                                                                            start=(k == 0), stop=(k == 7)))(t, k, b, w3b),
                             reads=[("hT", t), wk], writes=[PS(b)])
                    P.op("act", (lambda b, gtile: lambda e: e.copy(out=vslA[:, gtile, :, 0:64],
                                                                   in_=bank(b, 128, 0).rearrange("p (g d) -> p g d", g=2)))(b, gtile),
                         reads=[PS(b)], writes=[("vslA", gtile)])
                    P.op("dve", (lambda b, gtile: lambda e: e.tensor_copy(out=vwA[:, gtile, :, 0:64],
                                                                          in_=bank(b, 128, 128).rearrange("p (g d) -> p g d", g=2)))(b, gtile),
                         reads=[PS(b)], writes=[("vwA", gtile)])
                    P.op("act", (lambda b, t: lambda e: e.activation(out=sig3[:, t, :], in_=bank(b, 48, 256), func=AF.Sigmoid))(b, t),
                         reads=[PS(b)], writes=[("A", "sig", t)])
                for qi, qn in enumerate(("Q1", "Q2")):
                    w, wk = wload(qn)
                    w3q = w[:].rearrange("p (k c) -> p k c", k=8)
                    for t in range(NT):
                        b = t % 4
                        for k in range(8):
                            P.op("pe", (lambda t, k, b, w3q: lambda e: e.matmul(bank(b), lhsT=hT[:, k, t * 128:(t + 1) * 128], rhs=w3q[:, k, :],
                                                                                start=(k == 0), stop=(k == 7)))(t, k, b, w3q),
                                 reads=[("hT", t), wk], writes=[PS(b)])
                        rope(None, bank(b), 8, 64, cosN[:, t, :], sinN[:, t, :], nq_tm[:, 0:512],
                             tmp1s[t % 2], tmp2s[t % 2], [PS(b)], [("A", "nq_tm")], [("A", "tmp1", t % 2), ("A", "tmp2", t % 2)])
                        tb = 6 + (t % 2)
                        transposes(lambda k: nq_tm[:, k * 128:(k + 1) * 128], 4, tb, [("A", "nq_tm")])
                        P.op("dve", (lambda tb, qi, t: lambda e: e.tensor_copy(
                            out=nqT3[:, qi * 4:(qi + 1) * 4, t * 128:(t + 1) * 128],
                            in_=bankb(tb, 512).rearrange("p (i c) -> p i c", i=4)))(tb, qi, t),
                            reads=[PS(tb)], writes=[("A", "nqT", t)])
                do_dump("kslT", kslT[0][:], [("kslT", j * NT + t) for t in range(NT)], [128, SEQ], BF16)
                do_dump("nqT", nqT, [("A", "nqT", t) for t in range(NT)], [128, 8 * TB], BF16)
                do_dump("sig", sig, [("A", "sig", t) for t in range(NT)], [128, NT * 48])
                if stages < 3:
                    continue

                kpe = a_alloc(32 * 32, BF16)
                kpe3 = kpe.rearrange("p (l c) -> p l c", l=32)
                gl = [a_alloc(128, F32) for _ in range(3)]
                s0 = 32 * j
                for nm, cache, pen, hid, w2n in (("CK", kcT, "pek", hidk, "ck2"), ("CV", vcT, "pev", hidv, "cv2")):
                    src = cache[:, 16 * s0: 16 * s0 + 1]
                    src = bass.AP(src.tensor, src.offset, [list(src.ap[0]), [1, 32], [16, 32]])
                    pe_ap = cfv(pen)
                    P.op("dve", (lambda src, pe_ap: lambda e: e.tensor_tensor(out=kpe3, in0=src, in1=bc(pe_ap, [[1, 32], [0, 32]]), op=ALU.add))(src, pe_ap),
                         reads=[nm[1] == "K" and "kcT" or "vcT", ("const", "cf")], writes=[("A", "kpe")])
                    wA, wkA = wload(nm + "1", hold=True)
                    iA = wload.last
                    wB, wkB = wload(nm + "2", hold=True)
                    iB = wload.last
                    for g in range(2):
                        first = True
                        for hh in range(2):
                            for l in range(32):
                                wsrc, wkey = (wA, wkA) if l < 16 else (wB, wkB)
                                w1v = wsrc[:].rearrange("p (l j) -> p l j", l=16)
                                P.op("pe", lambda e: e.matmul(
                                    bank(g, 32, hh * 32),
                                    lhsT=w1v[g * 64:(g + 1) * 64, l % 16, hh * 128:(hh + 1) * 128],
                                    rhs=kpe3[g * 64:(g + 1) * 64, l, :],
                                    start=first, stop=(l == 31), skip_group_check=True),
                                    reads=[("A", "kpe"), wkey], writes=[PS(g)])
                                first = False
                    wrelease(iA)
                    wrelease(iB)
                    xh = bass.AP(psum, 0, [[4096, 128], [512, 2], [1, 64]])
                    g3 = [t_.rearrange("p (g c) -> p g c", g=2) for t_ in gl]
                    PH = [PS(0), PS(1)]
                    P.op("act", lambda e: e.activation(out=g3[0], in_=xh, func=AF.Square), reads=PH, writes=[("A", "gl0")])
                    P.op("dve", lambda e: e.tensor_scalar(out=gl[0], in0=gl[0], scalar1=0.044715, scalar2=1.0, op0=ALU.mult, op1=ALU.add),
                         reads=[("A", "gl0")], writes=[("A", "gl0")])
                    P.op("dve", lambda e: e.tensor_tensor(out=g3[1], in0=xh, in1=g3[0], op=ALU.mult), reads=PH + [("A", "gl0")], writes=[("A", "gl1")])
                    P.op("act", lambda e: e.activation(out=gl[2], in_=gl[1], func=AF.Sigmoid, scale=1.5957691216), reads=[("A", "gl1")], writes=[("A", "gl2")])
                    xh4 = bass.AP(psum, 0, [[4096, 128], [512, 2], [32, 2], [1, 32]])
                    P.op("dve", lambda e: e.tensor_tensor(out=hid[:, :, :, s0:s0 + 32], in0=xh4,
                                                          in1=gl[2].rearrange("p (g h c) -> p g h c", g=2, h=2), op=ALU.mult),
                         reads=PH + [("A", "gl2")], writes=[nm])
                ck2 = cbv("ck2").rearrange("p (h d) -> p h d", h=2)
                cv2 = cbv("cv2").rearrange("p (h d) -> p h d", h=2)
                for g in range(2):
                    for hh in range(2):
                        P.op("pe", (lambda g, hh: lambda e: e.matmul(bank(2, 128, g * 128), lhsT=ck2[:, hh, :], rhs=hidk[:, g, hh, :],
                                                                     start=(g == 0 and hh == 0), stop=(hh == 1), skip_group_check=True))(g, hh),
                             reads=["CK", ("const", "cb")], writes=[PS(2)])
                for g in range(2):
                    for hh in range(2):
                        P.op("pe", (lambda g, hh: lambda e: e.matmul(bank(3, 64, g * 64), lhsT=hidv[:, g, hh, :], rhs=cv2[:, hh, :],
                                                                     start=(g == 0 and hh == 0), stop=(hh == 1), skip_group_check=True))(g, hh),
                             reads=["CV", ("const", "cb")], writes=[PS(3)])
                for g_ in range(2):
                    rs_ = slice(g_ * 64, (g_ + 1) * 64)
                    P.op("act", lambda e: e.copy(out=KcTz[rs_, g_, :], in_=bank(2, 128, g_ * 128)[rs_, :]), reads=[PS(2)], writes=["KcT"])
                P.op("dve", lambda e: e.tensor_copy(out=VcA[:, :, 0:64], in_=bank(3, 128).rearrange("p (g d) -> p g d", g=2)),
                     reads=[PS(3), "VcA"], writes=["VcA"])
                do_dump("KcT", KcTz[:].rearrange("p g c -> p (g c)"), ["KcT"], [128, 256], BF16)
                do_dump("VcA", VcA[:].rearrange("p g c -> p (g c)"), ["VcA"], [128, 2 * 97], BF16)
                if stages < 4:
                    continue

                pexp = [a_alloc(1024, BF16) for _ in range(3)]
                onsa = a_alloc(1024, F32)
                onsa4 = onsa.rearrange("p (i g d) -> p i g d", i=8, g=2)
                otmp = a_alloc(512, F32)
                onsab = a_alloc(1024, BF16)
                den = a_alloc(8, F32)
                fac = a_alloc(8, F32)
                impt = a_alloc(256, F32)
                imp = a_alloc(32, F32)
                top8 = a_alloc(8, F32)
                selm = a_alloc(32, BF16)
                selT = a_alloc(128, BF16)
                pcount = {"n": 0, "br": 0}
                triB = cbv("tri")
                oldB = cbv("old")
                maskC3 = cbv("maskC").rearrange("p (g q) -> p g q", g=16)
                VM3 = cbv("VM").rearrange("p (g b) -> p g b", g=16)
                AC3 = cfv("AC").rearrange("p (g b) -> p g b", g=16)
                E3 = cbv("E").rearrange("p (k c) -> p k c", k=16)
                P.op("dve", lambda e: e.memset(selT, 0.0), writes=[("A", "selT")])
                P.op("dve", lambda e: e.memset(selT[32:33, :], 1.0), reads=[("A", "selT")], writes=[("A", "selT")])

                def hb(ap2):
                    return bass.AP(ap2.tensor, ap2.offset, [list(ap2.ap[0]), [0, 4], [1, 128]])

                def stage_a(pr):
                    kind, t, g, gt, kt = pr["kind"], pr["t"], pr["g"], pr["gt"], pr["kt"]
                    n = pcount["n"]
                    pcount["n"] += 1
                    sb_ = (n % 2) * 2
                    pt = pexp[n % 3]
                    pk = ("A", "pexp", n % 3)
                    rows = slice(g * 64, (g + 1) * 64)
                    biases = []
                    if kind == "cmp":
                        kT_ap, kkeys = KcTz[:, g, :], ["KcT"]
                        biases.append((ident, hb(maskC3[:, gt, :]), [("const", "cb")]))
                    elif kind == "win":
                        kT_ap, kkeys = kwT[g][:, kt * 128:(kt + 1) * 128], [("kwT", kt)]
                        if kt == gt:
                            biases.append((ident, hb(triB), [("const", "cb")]))
                        elif kt == gt - 4:
                            biases.append((ident, hb(oldB), [("const", "cb")]))
                    else:
                        kT_ap, kkeys = kslT[g][:, kt * 128:(kt + 1) * 128], [("kslT", kt)]
                        biases.append((E3[:, kt, :], hb(selT), [("const", "cb"), ("A", "selT")]))
                        if kt == gt:
                            biases.append((ident, hb(triB), [("const", "cb")]))
                    for half in range(2):
                        for bi, (bl, br_, bkeys) in enumerate(biases):
                            P.op("pe", lambda e: e.matmul(bank(sb_ + half), lhsT=bl, rhs=br_, start=(bi == 0), stop=False),
                                 reads=bkeys, writes=[PS(sb_ + half)])
                        P.op("pe", lambda e: e.matmul(
                            bank(sb_ + half), lhsT=kT_ap,
                            rhs=nqT3[:, half * 4:(half + 1) * 4, t * 128:(t + 1) * 128],
                            start=(len(biases) == 0), stop=True),
                            reads=kkeys + [("A", "nqT", t)], writes=[PS(sb_ + half)])
                    P.op("act", lambda e: e.activation(out=pt, in_=psum[:, sb_ * 512:(sb_ + 2) * 512], func=AF.Exp, scale=0.125),
                         reads=[PS(sb_), PS(sb_ + 1)], writes=[pk])
                    pr["pt"], pr["pk"] = pt, pk

                def evac_branch(g, t, br, ncol, first_branch, ob0):
                    o4 = bass.AP(psum, ob0 * 512, [[4096, 128], [512, 2], [ncol, 4], [1, 64]])
                    d4 = bass.AP(psum, ob0 * 512 + 64, [[4096, 128], [512, 2], [ncol, 4]])
                    den3 = den.rearrange("p (a b) -> p a b", a=2)
                    OB = [PS(ob0), PS(ob0 + 1)]
                    P.op("dve", lambda e: e.tensor_scalar(out=den3, in0=d4, scalar1=1e-30, scalar2=None, op0=ALU.max),
                         reads=OB, writes=[("A", "den")])
                    P.op("dve", lambda e: e.reciprocal(out=den, in_=den), reads=[("A", "den")], writes=[("A", "den")])
                    if br == 0:
                        i4 = bass.AP(psum, ob0 * 512 + 65, [[4096, 128], [512, 2], [97, 4], [1, 32]])
                        db = bass.AP(den.tensor, den.offset, [list(den.ap[0]), [4, 2], [1, 4], [0, 32]])
                        gt = j * NT + t
                        P.op("dve", lambda e: e.tensor_tensor(out=impt.rearrange("p (a b c) -> p a b c", a=2, b=4), in0=i4, in1=db, op=ALU.mult),
                             reads=OB + [("A", "den")], writes=[("A", "impt")])
                        P.op("dve", lambda e: e.tensor_reduce(out=imp, in_=impt.rearrange("p (h c) -> p c h", h=8),
                                                              op=ALU.add, axis=mybir.AxisListType.X),
                             reads=[("A", "impt")], writes=[("A", "imp")])
                        P.op("dve", lambda e: e.tensor_tensor(out=imp, in0=imp, in1=VM3[:, gt, :], op=ALU.mult),
                             reads=[("A", "imp"), ("const", "cb")], writes=[("A", "imp")])
                        P.op("dve", lambda e: e.tensor_tensor(out=imp, in0=imp, in1=AC3[:, gt, :], op=ALU.add),
                             reads=[("A", "imp"), ("const", "cf")], writes=[("A", "imp")])
                        P.op("dve", lambda e: e.max(out=top8, in_=imp), reads=[("A", "imp")], writes=[("A", "top8")])
                        P.op("dve", lambda e: e.tensor_scalar(out=selm, in0=imp, scalar1=top8[:, 7:8], scalar2=None, op0=ALU.is_ge),
                             reads=[("A", "imp"), ("A", "top8")], writes=[("A", "selm")])
                        n = pcount["n"]
                        pcount["n"] += 1
                        tbk = (n % 2) * 2
                        P.op("pe", lambda e: e.transpose(out=bankb(tbk, 128, 0)[0:32, :], in_=selm, identity=ident),
                             reads=[("A", "selm"), ("const", "cb")], writes=[PS(tbk)])
                        P.op("dve", lambda e: e.tensor_copy(out=selT[0:32, :], in_=bankb(tbk, 128, 0)[0:32, :]), reads=[PS(tbk)], writes=[("A", "selT")])
                    gcol = br * 16 + g * 8
                    P.op("dve", lambda e: e.tensor_tensor(out=fac, in0=den, in1=sig3[:, t, gcol:gcol + 8], op=ALU.mult),
                         reads=[("A", "den"), ("A", "sig", t)], writes=[("A", "fac")])
                    fb = bass.AP(fac.tensor, fac.offset, [list(fac.ap[0]), [4, 2], [1, 4], [0, 64]])
                    dst = onsa4[:, :, g, :].rearrange("p (a b) d -> p a b d", a=2)
                    if first_branch:
                        P.op("dve", lambda e: e.tensor_tensor(out=dst, in0=o4, in1=fb, op=ALU.mult),
                             reads=OB + [("A", "fac")], writes=[("A", "onsa", g)])
                    else:
                        ot = otmp.rearrange("p (a b d) -> p a b d", a=2, b=4)
                        P.op("dve", lambda e: e.tensor_tensor(out=ot, in0=o4, in1=fb, op=ALU.mult),
                             reads=OB + [("A", "fac")], writes=[("A", "otmp")])
                        P.op("dve", lambda e: e.tensor_tensor(out=dst, in0=dst, in1=ot, op=ALU.add),
                             reads=[("A", "otmp"), ("A", "onsa", g)], writes=[("A", "onsa", g)])

                def stage_b(pr):
                    kind, t, g, gt, kt = pr["kind"], pr["t"], pr["g"], pr["gt"], pr["kt"]
                    if kind == "fin":
                        P.op("act", lambda e: e.copy(out=onsab, in_=onsa), reads=[("A", "onsa", 0), ("A", "onsa", 1)], writes=[("A", "onsab")])
                        n = pcount["n"]
                        pcount["n"] += 1
                        tbk = (n % 2) * 2
                        transposes(lambda k: onsab[:, k * 128:(k + 1) * 128], 8, tbk, [("A", "onsab")])
                        P.op("dve", lambda e: e.tensor_copy(out=onsaT[:, :, t * 128:(t + 1) * 128],
                                                            in_=bankb(tbk).rearrange("p (k c) -> p k c", k=8)),
                             reads=[PS(tbk)], writes=[("onsaT", t)])
                        return
                    if pr["first"]:
                        pr["ob0"] = 4 + 2 * (pcount["br"] % 2)
                        pcount["br"] += 1
                        cur["ob0"] = pr["ob0"]
                    ob0 = cur["ob0"]
                    pt, pk = pr["pt"], pr["pk"]
                    if kind == "cmp":
                        v_ap, vkeys, ncol = VcA[:, g, :], ["VcA"], 97
                    elif kind == "win":
                        v_ap, vkeys, ncol = vwA[:, kt, g, :], [("vwA", kt)], 65
                    else:
                        v_ap, vkeys, ncol = vslA[:, kt, g, :], [("vslA", kt)], 65
                    for h in range(8):
                        ob = ob0 + h // 4
                        P.op("pe", lambda e: e.matmul(
                            bank(ob, ncol, (h % 4) * ncol), lhsT=pt[:, h * 128:(h + 1) * 128], rhs=v_ap,
                            start=(pr["first"] and h % 4 == 0), stop=pr["last"], skip_group_check=True),
                            reads=[pk] + vkeys, writes=[PS(ob)])
                    if pr["last"]:
                        evac_branch(g, t, {"cmp": 0, "sel": 1, "win": 2}[kind], ncol, kind == "cmp", ob0)

                cur = {}
                plist = []
                for t in range(NT):
                    gt = j * NT + t
                    for g in range(2):
                        plist.append(dict(kind="cmp", t=t, g=g, gt=gt, kt=None, first=True, last=True))
                        kts = list(range(max(0, gt - 4), gt + 1))
                        for ii, kt in enumerate(kts):
                            plist.append(dict(kind="win", t=t, g=g, gt=gt, kt=kt, first=(ii == 0), last=(ii == len(kts) - 1)))
                        for kt in range(gt + 1):
                            plist.append(dict(kind="sel", t=t, g=g, gt=gt, kt=kt, first=(kt == 0), last=(kt == gt)))
                    plist.append(dict(kind="fin", t=t, g=None, gt=gt, kt=None))
                SKEW = 1
                for ii in range(len(plist) + SKEW):
                    if ii < len(plist) and plist[ii]["kind"] != "fin":
                        stage_a(plist[ii])
                    if ii - SKEW >= 0:
                        stage_b(plist[ii - SKEW])
                do_dump("onsaT", onsaT[:].rearrange("p k t -> p (k t)"), [("onsaT", t) for t in range(NT)], [128, 8 * TB], BF16)
                if stages < 5:
                    continue

                a_reset()
                S3 = [dict(rqk=a_alloc(NT * 256, BF16), rv=a_alloc(NT * 256, BF16)) for _ in range(3)]
                S2 = [dict(qT=a_alloc(TB, BF16), kT=a_alloc(TB, BF16), qxT=a_alloc(TB, BF16), kz=a_alloc(NT * 128, BF16),
                           inT=a_alloc(NT * 128, BF16), rbc=a_alloc(NT * 256, BF16)) for _ in range(2)]
                S2b = [dict(osb=a_alloc(NT * 256, F32), y=a_alloc(NT * 256, BF16), st=a_alloc(32, F32)) for _ in range(2)]
                gsg = [a_alloc(NT * 512, BF16) for _ in range(2)]
                rt1s = [a_alloc(256, F32), a_alloc(256, F32)]
                rt2s = [a_alloc(256, F32), a_alloc(256, F32)]
                decT = cbv("decayT").rearrange("p (h n) -> p h n", h=8)
                xi3 = cbv("xi").rearrange("p (h n) -> p h n", h=8)
                zs = cfv("zs")
                gng = cfv("gn_g")
                log_g = [math.log(1.0 - 2.0 ** (-5.0 - h)) for h in range(8)]
                gch = [math.exp(128.0 * lg) for lg in log_g]

                def ret_s0(h):
                    A3 = S3[h % 3]
                    K3 = lambda n, *x: ("A", n, h % 3) + tuple(x)
                    hp = h // 2
                    if h % 2 == 0:
                        w, wk = wload("B%d" % hp)
                        w3g = w[:].rearrange("p (k c) -> p k c", k=8)
                        gs = gsg[hp % 2].rearrange("p (t c) -> p t c", t=NT)
                        for t in range(NT):
                            b = t % 2
                            for k in range(8):
                                P.op("pe", lambda e: e.matmul(bank(b), lhsT=hT[:, k, t * 128:(t + 1) * 128], rhs=w3g[:, k, :],
                                                              start=(k == 0), stop=(k == 7)),
                                     reads=[("hT", t), wk], writes=[PS(b)])
                            P.op("act", lambda e: e.activation(out=gs[:, t, :], in_=bank(b), func=AF.Silu),
                                 reads=[PS(b)], writes=[("A", "gsg", hp % 2, t)])
                    w, wk = wload("A%d" % h)
                    w3a = w[:].rearrange("p (k c) -> p k c", k=8)
                    rqk3 = A3["rqk"].rearrange("p (t c) -> p t c", t=NT)
                    rv3 = A3["rv"].rearrange("p (t c) -> p t c", t=NT)
                    for t in range(NT):
                        b = t % 2
                        for k in range(8):
                            P.op("pe", lambda e: e.matmul(bank(b), lhsT=hT[:, k, t * 128:(t + 1) * 128], rhs=w3a[:, k, :],
                                                          start=(k == 0), stop=(k == 7)),
                                 reads=[("hT", t), wk], writes=[PS(b)])
                        rope(None, bank(b, 256), 2, 128, cosR[:, t, :], sinR[:, t, :], rqk3[:, t, :], rt1s[t % 2], rt2s[t % 2],
                             [PS(b)], [K3("rqk", t)], [("A", "rt1", t % 2), ("A", "rt2", t % 2)])
                        P.op("act", lambda e: e.copy(out=rv3[:, t, :], in_=bank(b, 256, 256)),
                             reads=[PS(b)], writes=[K3("rv", t)])

                def ret_s1(h):
                    A3 = S3[h % 3]
                    B = S2[h % 2]
                    K3 = lambda n, *x: ("A", n, h % 3) + tuple(x)
                    K = lambda n: ("A", n, h % 2)
                    rqk3 = A3["rqk"].rearrange("p (t c) -> p t c", t=NT)
                    rv3 = A3["rv"].rearrange("p (t c) -> p t c", t=NT)
                    for which in range(2):
                        for t in range(NT):
                            P.op("pe", lambda e: e.transpose(out=bankb(7, 128, (which * NT + t) * 128),
                                                             in_=rqk3[:, t, which * 128:(which + 1) * 128], identity=ident),
                                 reads=[K3("rqk", t), ("const", "cb")], writes=[PS(7)])
                    P.op("act", lambda e: e.copy(out=B["qT"], in_=bankb(7, 512, 0)), reads=[PS(7)], writes=[K("qT")])
                    P.op("dve", lambda e: e.tensor_tensor(out=B["qxT"].rearrange("p (t n) -> p t n", t=NT),
                                                          in0=bankb(7, 512, 0).rearrange("p (t n) -> p t n", t=NT),
                                                          in1=bass.AP(xi3.tensor, xi3[:, h, :].offset, [list(xi3.ap[0]), [0, NT], [1, 128]]),
                                                          op=ALU.mult),
                         reads=[PS(7), ("const", "cb")], writes=[K("qxT")])
                    P.op("act", lambda e: e.copy(out=B["kT"], in_=bankb(7, 512, 512)), reads=[PS(7)], writes=[K("kT")])
                    P.op("dve", lambda e: e.tensor_scalar(out=B["kz"].rearrange("p (t d) -> p t d", t=NT), in0=rqk3[:, :, 128:256],
                                                          scalar1=zs[:, h:h + 1], scalar2=None, op0=ALU.mult),
                         reads=[K3("rqk", t_) for t_ in range(NT)] + [("const", "cf")], writes=[K("kz")])
                    for t in range(NT):
                        P.op("pe", lambda e: e.matmul(bank(2, 128, t * 128), lhsT=B["kT"][:, t * 128:(t + 1) * 128],
                                                      rhs=B["qT"][:, t * 128:(t + 1) * 128], start=(t == 0), stop=True,
                                                      skip_group_check=True),
                             reads=[K("kT"), K("qT")], writes=[PS(2)])
                    for t in range(NT):
                        rbk = 3 + t // 2
                        P.op("pe", lambda e: e.matmul(bank(rbk, 256, (t % 2) * 256), lhsT=B["kz"][:, t * 128:(t + 1) * 128],
                                                      rhs=rv3[:, t, :], start=(t % 2 == 0), stop=True, skip_group_check=True),
                             reads=[K("kz"), K3("rv", t)], writes=[PS(rbk)])
                    P.op("dve", lambda e: e.tensor_tensor(out=B["inT"].rearrange("p (t n) -> p t n", t=NT),
                                                          in0=bank(2).rearrange("p (t n) -> p t n", t=NT),
                                                          in1=bass.AP(decT.tensor, decT[:, h, :].offset, [list(decT.ap[0]), [0, NT], [1, 128]]),
                                                          op=ALU.mult),
                         reads=[PS(2), ("const", "cb")], writes=[K("inT")])
                    rbc3 = B["rbc"].rearrange("p (t e) -> p t e", t=NT)
                    for t in range(NT):
                        rbk = 3 + t // 2
                        if t == 0:
                            P.op("act", lambda e: e.copy(out=rbc3[:, 0, :], in_=Rst[:, h, :]), reads=[("R", h)], writes=[K("rbc")])
                        P.op("dve", lambda e: e.scalar_tensor_tensor(out=Rst[:, h, :], in0=Rst[:, h, :], scalar=gch[h],
                                                                     in1=bank(rbk, 256, (t % 2) * 256), op0=ALU.mult, op1=ALU.add),
                             reads=[("R", h), PS(rbk), K("rbc")], writes=[("R", h)])
                        if t < NT - 1:
                            P.op("act", lambda e: e.copy(out=rbc3[:, t + 1, :], in_=Rst[:, h, :]), reads=[("R", h)], writes=[K("rbc")])

                def ret_s2(h):
                    A3 = S3[h % 3]
                    B = S2[h % 2]
                    C = S2b[h % 2]
                    K3 = lambda n, *x: ("A", n, h % 3) + tuple(x)
                    K = lambda n: ("A", n, h % 2)
                    hp = h // 2
                    rv3 = A3["rv"].rearrange("p (t c) -> p t c", t=NT)
                    rbc3 = B["rbc"].rearrange("p (t e) -> p t e", t=NT)
                    osb3 = C["osb"].rearrange("p (t e) -> p t e", t=NT)
                    y3 = C["y"].rearrange("p (t e) -> p t e", t=NT)
                    gs = gsg[hp % 2].rearrange("p (t c) -> p t c", t=NT)
                    stt = C["st"]
                    for t in range(NT):
                        ob = 5 + t // 2
                        oo = (t % 2) * 256
                        P.op("pe", lambda e: e.matmul(bank(ob, 256, oo), lhsT=B["inT"][:, t * 128:(t + 1) * 128], rhs=rv3[:, t, :],
                                                      start=(t % 2 == 0), stop=False, skip_group_check=True),
                             reads=[K("inT"), K3("rv", t)], writes=[PS(ob)])
                        P.op("pe", lambda e: e.matmul(bank(ob, 256, oo), lhsT=B["qxT"][:, t * 128:(t + 1) * 128], rhs=rbc3[:, t, :],
                                                      start=False, stop=True, skip_group_check=True),
                             reads=[K("qxT"), K("rbc")], writes=[PS(ob)])
                    for t in range(NT):
                        ob = 5 + t // 2
                        oo = (t % 2) * 256
                        P.op("act", lambda e: e.activation(out=osb3[:, t, :], in_=bank(ob, 256, oo), func=AF.Copy,
                                                           accum_out=stt[:, t:t + 1]),
                             reads=[PS(ob)], writes=[K("osb"), K("st")])
                        P.op("act", lambda e: e.activation(out=y3[:, t, :], in_=bank(ob, 256, oo), func=AF.Square,
                                                           accum_out=stt[:, 4 + t:5 + t]),
                             reads=[PS(ob)], writes=[K("y"), K("st")])
                    mean = stt[:, 8:12]
                    var = stt[:, 12:16]
                    rs = stt[:, 16:20]
                    nb = stt[:, 20:24]
                    P.op("dve", lambda e: e.tensor_scalar(out=mean, in0=stt[:, 0:4], scalar1=1.0 / 256, scalar2=None, op0=ALU.mult),
                         reads=[K("st")], writes=[K("st")])
                    P.op("dve", lambda e: e.tensor_tensor(out=var, in0=mean, in1=mean, op=ALU.mult), reads=[K("st")], writes=[K("st")])
                    P.op("dve", lambda e: e.scalar_tensor_tensor(out=var, in0=stt[:, 4:8], scalar=1.0 / 256, in1=var, op0=ALU.mult, op1=ALU.subtract),
                         reads=[K("st")], writes=[K("st")])
                    P.op("dve", lambda e: e.tensor_scalar(out=var, in0=var, scalar1=EPS, scalar2=None, op0=ALU.add), reads=[K("st")], writes=[K("st")])
                    P.op("pool", lambda e: e.tensor_tensor(out=rs, in0=var, in1=cfv("mhalf")[:, 0:4], op=ALU.pow),
                         reads=[K("st"), ("const", "cf")], writes=[K("st")])
                    P.op("dve", lambda e: e.scalar_tensor_tensor(out=nb, in0=mean, scalar=-1.0, in1=rs, op0=ALU.mult, op1=ALU.mult),
                         reads=[K("st")], writes=[K("st")])
                    for t in range(NT):
                        P.op("dve", lambda e: e.tensor_scalar(out=y3[:, t, :], in0=osb3[:, t, :], scalar1=rs[:, t:t + 1], scalar2=nb[:, t:t + 1],
                                                              op0=ALU.mult, op1=ALU.add),
                             reads=[K("osb"), K("st"), K("y")], writes=[K("y")])
                    P.op("dve", lambda e: e.tensor_tensor(out=y3, in0=y3, in1=gs[:, :, (h % 2) * 256:(h % 2 + 1) * 256], op=ALU.mult),
                         reads=[K("y")] + [("A", "gsg", hp % 2, t) for t in range(NT)], writes=[K("y")])

                def ret_s3(h):
                    C = S2b[h % 2]
                    K = lambda n: ("A", n, h % 2)
                    y3 = C["y"].rearrange("p (t e) -> p t e", t=NT)
                    for kc in range(2):
                        for t in range(NT):
                            P.op("pe", lambda e: e.transpose(out=bankb(7, 128, (kc * NT + t) * 128),
                                                             in_=y3[:, t, kc * 128:(kc + 1) * 128], identity=ident),
                                 reads=[K("y"), ("const", "cb")], writes=[PS(7)])
                    P.op("dve", lambda e: e.tensor_tensor(out=oretT[:, 2 * h:2 * h + 2, :],
                                                          in0=bankb(7).rearrange("p (k c) -> p k c", k=2),
                                                          in1=bass.AP(gng.tensor, gng[:, 2 * h:2 * h + 2].offset, [list(gng.ap[0]), [1, 2], [0, TB]]),
                                                          op=ALU.mult),
                         reads=[PS(7), ("const", "cf")], writes=[("oretT", h)])

                if j == 0:
                    P.op("pool", lambda e: e.memset(Rst[:], 0.0), reads=[("R", h) for h in range(8)], writes=[("R", h) for h in range(8)])
                for it in range(8 + 3):
                    if it < 8:
                        ret_s0(it)
                    if 0 <= it - 1 < 8:
                        ret_s1(it - 1)
                    if 0 <= it - 2 < 8:
                        ret_s2(it - 2)
                    if 0 <= it - 3 < 8:
                        ret_s3(it - 3)
                do_dump("oretT", oretT[:].rearrange("p k t -> p (k t)"), [("oretT", h) for h in range(8)], [128, 16 * TB], BF16)
                if stages < 6:
                    continue

                a_reset()
                U = a_alloc(16 * TB, BF16)
                U3 = U.rearrange("p (k t) -> p k t", k=16)
                mixT = a_alloc(8 * TB, BF16)
                mix3 = mixT.rearrange("p (k t) -> p k t", k=8)
                xres = a_alloc(NT * 1024, F32)
                xres3 = xres.rearrange("p (t c) -> p t c", t=NT)
                pT = a_alloc(2 * TB, BF16)
                pT3 = pT.rearrange("p (k t) -> p k t", k=2)
                pld = [a_alloc(256, F32), a_alloc(256, F32)]
                pbf = a_alloc(256, BF16)
                mt1s = [a_alloc(512, F32), a_alloc(512, F32)]
                mt2s = [a_alloc(512, F32), a_alloc(512, F32)]
                hn = [a_alloc(1024, BF16), a_alloc(1024, BF16)]
                junk = [a_alloc(1024, BF16)] * 2
                ssq = [a_alloc(1, F32), a_alloc(1, F32)]
                rstd = [a_alloc(1, F32), a_alloc(1, F32)]
                sgAs = [a_alloc(512, F32), a_alloc(512, F32)]
                for t in range(NT):
                    P.op("sp", (lambda t: lambda e: e.dma_start(out=xres3[:, t, :], in_=x_d.ap()[row0 + t * 128: row0 + (t + 1) * 128, :]))(t),
                         writes=[("A", "xres", t)], dma_sem="x%d" % (t % 2))
                for i in range(4):
                    w, wk = wload("MG%d" % i)
                    w3m = w[:].rearrange("p (k c) -> p k c", k=8)
                    for n in range(4):
                        b = n % 4
                        for k in range(8):
                            P.op("pe", (lambda n, k, b, w3m: lambda e: e.matmul(bank(b), lhsT=w3m[:, k, n * 128:(n + 1) * 128], rhs=hT[:, k, :],
                                                                                start=(k == 0), stop=(k == 7)))(n, k, b, w3m),
                                 reads=HT + [wk], writes=[PS(b)])
                        P.op("act", (lambda i, n, b: lambda e: e.activation(out=U3[:, i * 4 + n, :], in_=bank(b), func=AF.Sigmoid))(i, n, b),
                             reads=[PS(b)], writes=[("A", "U", i * 4 + n)])
                wno = None
                for i in range(4):
                    if i % 2 == 0:
                        wno, wnok = wload("NO%d" % (i // 2), hold=True)
                        iNO = wload.last
                        wno3 = wno[:].rearrange("p (k c) -> p k c", k=8)
                    wro, wrok = wload("RO%d" % i)
                    wro3 = wro[:].rearrange("p (k c) -> p k c", k=16)
                    for n2 in range(2):
                        n = 2 * i + n2
                        br_, bn_ = 4, 5
                        for k in range(16):
                            P.op("pe", lambda e: e.matmul(bank(4 + 2 * (n % 2)), lhsT=wro3[:, k, n2 * 128:(n2 + 1) * 128], rhs=oretT[:, k, :],
                                                          start=(k == 0), stop=(k == 15)),
                                 reads=[("oretT", k // 2), wrok], writes=[PS(4 + 2 * (n % 2))])
                        cno = (i % 2) * 256 + n2 * 128
                        for k in range(8):
                            P.op("pe", lambda e: e.matmul(bank(5 + 2 * (n % 2)), lhsT=wno3[:, k, cno:cno + 128], rhs=onsaT[:, k, :],
                                                          start=(k == 0), stop=(k == 7)),
                                 reads=[("onsaT", t) for t in range(NT)] + [wnok], writes=[PS(5 + 2 * (n % 2))])
                        bR, bN = 4 + 2 * (n % 2), 5 + 2 * (n % 2)
                        P.op("dve", lambda e: e.tensor_tensor(out=mt1s[n % 2], in0=bank(bR), in1=U3[:, n, :], op=ALU.mult),
                             reads=[PS(bR), ("A", "U", n)], writes=[("A", "mt1", n % 2)])
                        P.op("dve", lambda e: e.tensor_tensor(out=mt2s[n % 2], in0=bank(bN), in1=U3[:, 8 + n, :], op=ALU.mult),
                             reads=[PS(bN), ("A", "U", 8 + n)], writes=[("A", "mt2", n % 2)])
                        P.op("dve", lambda e: e.tensor_tensor(out=mix3[:, n, :], in0=mt1s[n % 2], in1=mt2s[n % 2], op=ALU.add),
                             reads=[("A", "mt1", n % 2), ("A", "mt2", n % 2)], writes=[("A", "mix", n)])
                    if i % 2 == 1:
                        wrelease(iNO)
                MIX = [("A", "mix", n) for n in range(8)]
                for ch in range(2):
                    w, wk = wload("WO%d" % ch)
                    w3o = w[:].rearrange("p (k c) -> p k c", k=8)
                    for t in range(NT):
                        b = t % 4
                        for k in range(8):
                            P.op("pe", (lambda t, k, b, w3o: lambda e: e.matmul(bank(b), lhsT=mix3[:, k, t * 128:(t + 1) * 128], rhs=w3o[:, k, :],
                                                                                start=(k == 0), stop=(k == 7)))(t, k, b, w3o),
                                 reads=MIX + [wk], writes=[PS(b)])
                        P.op("dve", (lambda t, b, ch: lambda e: e.tensor_tensor(out=xres3[:, t, ch * 512:(ch + 1) * 512], in0=bank(b),
                                                                                in1=xres3[:, t, ch * 512:(ch + 1) * 512], op=ALU.add))(t, b, ch),
                             reads=[PS(b), ("A", "xres", t)], writes=[("A", "xres", t)])
                do_dump("x1", xres, [("A", "xres", t) for t in range(NT)], [128, NT * 1024])
                for t in range(NT):
                    rms_to_hT(xres3[:, t, :], [("A", "xres", t)], "g_mlp", t, 6 + (t % 2), hn, junk, ssq, rstd)
                U4 = U.rearrange("p (a k t) -> p a k t", a=2, k=8)
                for qd in range(4):
                    ub = qd % 2
                    for half in range(2):
                        w, wk = wload("UP%d" % (2 * qd + half))
                        w3u = w[:].rearrange("p (k c) -> p k c", k=8)
                        for n in range(4):
                            b = n % 4
                            for k in range(8):
                                P.op("pe", (lambda n, k, b, w3u: lambda e: e.matmul(bank(b), lhsT=w3u[:, k, n * 128:(n + 1) * 128], rhs=hT[:, k, :],
                                                                                    start=(k == 0), stop=(k == 7)))(n, k, b, w3u),
                                     reads=HT + [wk], writes=[PS(b)])
                            ui = half * 4 + n
                            P.op("act", lambda e: e.activation(out=sgAs[n % 2], in_=bank(b), func=AF.Relu),
                                 reads=[PS(b)], writes=[("A", "sgA", n % 2)])
                            P.op("dve", lambda e: e.tensor_tensor(out=U4[:, ub, ui, :], in0=sgAs[n % 2], in1=sgAs[n % 2], op=ALU.mult),
                                 reads=[("A", "sgA", n % 2)], writes=[("A", "U", ub * 8 + ui)])
                    for ch in range(2):
                        w, wk = wload("DN%d" % (2 * qd + ch))
                        w3d = w[:].rearrange("p (k c) -> p k c", k=8)
                        for t in range(NT):
                            b = 4 + t % 2
                            for k in range(8):
                                P.op("pe", (lambda t, k, b, w3d, ub: lambda e: e.matmul(bank(b), lhsT=U4[:, ub, k, t * 128:(t + 1) * 128], rhs=w3d[:, k, :],
                                                                                        start=(k == 0), stop=(k == 7)))(t, k, b, w3d, ub),
                                     reads=[("A", "U", ub * 8 + k), wk], writes=[PS(b)])
                            P.op("dve", (lambda t, b, ch: lambda e: e.tensor_tensor(out=xres3[:, t, ch * 512:(ch + 1) * 512], in0=bank(b),
                                                                                    in1=xres3[:, t, ch * 512:(ch + 1) * 512], op=ALU.add))(t, b, ch),
                                 reads=[PS(b), ("A", "xres", t)], writes=[("A", "xres", t)])
                do_dump("x2", xres, [("A", "xres", t) for t in range(NT)], [128, NT * 1024])
                for t in range(NT):
                    pb_ = pld[t % 2]
                    P.op("sp", (lambda t, pb_: lambda e: e.dma_start(out=pb_, in_=p_d.ap()[row0 + t * 128: row0 + (t + 1) * 128, :]))(t, pb_),
                         writes=[("A", "pld", t % 2)], dma_sem="p%d" % (t % 2))
                    P.op("act", (lambda pb_: lambda e: e.copy(out=pbf, in_=pb_))(pb_), reads=[("A", "pld", t % 2)], writes=[("A", "pbf")])
                    tb = 6 + (t % 2)
                    transposes(lambda k: pbf[:, k * 128:(k + 1) * 128], 2, tb, [("A", "pbf")])
                    P.op("dve", (lambda t, tb: lambda e: e.tensor_copy(out=pT3[:, :, t * 128:(t + 1) * 128],
                                                                       in_=bankb(tb, 256).rearrange("p (k c) -> p k c", k=2)))(t, tb),
                         reads=[PS(tb)], writes=[("A", "pT", t)])
                    rms_to_hT(xres3[:, t, :], [("A", "xres", t)], "g_ple", t, 6 + (t % 2), hn, junk, ssq, rstd)
                wpp, wppk = wload("PP", hold=True)
                iPP = wload.last
                wpp3 = wpp[:, 0:2048].rearrange("p (k c) -> p k c", k=2)
                for ch in range(2):
                    w, wk = wload("PG%d" % ch)
                    w3g = w[:].rearrange("p (k c) -> p k c", k=8)
                    for t in range(NT):
                        ba = t % 2
                        bb = 2 + t % 2
                        for k in range(8):
                            P.op("pe", (lambda t, k, ba, w3g: lambda e: e.matmul(bank(ba), lhsT=hT[:, k, t * 128:(t + 1) * 128], rhs=w3g[:, k, :],
                                                                                 start=(k == 0), stop=(k == 7)))(t, k, ba, w3g),
                                 reads=[("hT", t), wk], writes=[PS(ba)])
                        for k in range(2):
                            P.op("pe", (lambda t, k, bb, ch: lambda e: e.matmul(bank(bb), lhsT=pT3[:, k, t * 128:(t + 1) * 128],
                                                                                rhs=wpp3[:, k, ch * 512:(ch + 1) * 512],
                                                                                start=(k == 0), stop=(k == 1)))(t, k, bb, ch),
                                 reads=[("A", "pT", t), wppk], writes=[PS(bb)])
                        P.op("act", lambda e: e.activation(out=sgAs[t % 2], in_=bank(ba), func=AF.Sigmoid),
                             reads=[PS(ba)], writes=[("A", "sgA", t % 2)])
                        P.op("dve", lambda e: e.tensor_tensor(out=mt1s[t % 2], in0=bank(bb), in1=sgAs[t % 2], op=ALU.mult),
                             reads=[PS(bb), ("A", "sgA", t % 2)], writes=[("A", "mt1", t % 2)])
                        P.op("dve", lambda e: e.tensor_tensor(out=xres3[:, t, ch * 512:(ch + 1) * 512], in0=mt1s[t % 2],
                                                              in1=xres3[:, t, ch * 512:(ch + 1) * 512], op=ALU.add),
                             reads=[("A", "mt1", t % 2), ("A", "xres", t)], writes=[("A", "xres", t)])
                wrelease(iPP)
                gfin = cfv("g_final")
                for t in range(NT):
                    i2 = t % 2
                    P.op("act", lambda e: e.activation(out=junk[i2], in_=xres3[:, t, :], func=AF.Square, accum_out=ssq[i2]),
                         reads=[("A", "xres", t)], writes=[("A", "junk"), ("A", "ssq", i2)])
                    P.op("dve", lambda e: e.tensor_scalar(out=rstd[i2], in0=ssq[i2], scalar1=1.0 / DM, scalar2=EPS, op0=ALU.mult, op1=ALU.add),
                         reads=[("A", "ssq", i2)], writes=[("A", "rstd", i2)])
                    P.op("pool", lambda e: e.tensor_tensor(out=rstd[i2], in0=rstd[i2], in1=cfv("mhalf")[:, 0:1], op=ALU.pow),
                         reads=[("A", "rstd", i2), ("const", "cf")], writes=[("A", "rstd", i2)])
                    P.op("dve", lambda e: e.scalar_tensor_tensor(out=xres3[:, t, :], in0=xres3[:, t, :], scalar=rstd[i2], in1=gfin,
                                                                 op0=ALU.mult, op1=ALU.mult),
                         reads=[("A", "xres", t), ("A", "rstd", i2), ("const", "cf")], writes=[("A", "xres", t)])
                    P.op("sp", lambda e: e.dma_start(out=out_d.ap()[row0 + t * 128: row0 + (t + 1) * 128, :], in_=xres3[:, t, :]),
                         reads=[("A", "xres", t)], dma_sem="out")
        info = P.emit()
    return nc, info, dumps


_CACHE = {}


def prepare_inputs(inputs, ncores=NCORES, nseq=None):
    x = np.asarray(inputs["x"], np.float32)
    p = np.asarray(inputs["p"], np.float32)[0]
    pos = np.asarray(inputs["positions"], np.int32)
    B = x.shape[0]
    nseq = B // ncores if nseq is None else nseq
    cf, cb = host_consts(inputs)
    wpack = host_pack(inputs)
    in_maps = []
    for c in range(ncores):
        xs = np.ascontiguousarray(x[c * nseq:(c + 1) * nseq].reshape(nseq * SEQ, DM))
        ps_ = np.ascontiguousarray(p[c * nseq:(c + 1) * nseq].reshape(nseq * SEQ, 256))
        pl = pos[c * nseq:(c + 1) * nseq].reshape(nseq, 16, 128).transpose(2, 0, 1).reshape(128, nseq * 16)
        in_maps.append({"x": xs, "p": ps_, "posl": np.ascontiguousarray(pl), "wpack": wpack, "cf": cf, "cb": cb})
    return in_maps, nseq


def kernel(**inputs):
    in_maps, nseq = prepare_inputs(inputs)
    key = ("full", nseq)
    if key not in _CACHE:
        _CACHE[key] = build_program(nseq=nseq)[0]
    nc = _CACHE[key]
    res = run_bass_kernel_spmd(nc, in_maps, core_ids=list(range(NCORES)))
    outs = [r["out"].reshape(nseq, SEQ, DM) for r in res.results]
    return np.concatenate(outs, axis=0).astype(np.float32)
```

```python
import math
import numpy as np
from contextlib import ExitStack
import concourse.bass as bass
import concourse.mybir as mybir
from concourse.bass_utils import run_bass_kernel_spmd

F32 = mybir.dt.float32
BF16 = mybir.dt.bfloat16
I32 = mybir.dt.int32
AF = mybir.ActivationFunctionType
ALU = mybir.AluOpType

NCORES = 8
SEQ = 2048
DM = 1024
TB = 512
NT = TB // 128
EPS = 1e-6
TWO_PI = 2.0 * math.pi
C1 = 6.28125
C2 = TWO_PI - C1
PIECE = 4096
BIGM = 30000.0


class _Op:
    __slots__ = ("eng", "fn", "deps", "dma_sem", "seg", "token", "needs_inc", "is_dma", "ninc")

    def __init__(self, eng, fn, deps, dma_sem, seg, ninc):
        self.eng = eng
        self.fn = fn
        self.deps = deps
        self.dma_sem = dma_sem
        self.seg = seg
        self.token = None
        self.needs_inc = False
        self.is_dma = dma_sem is not None
        self.ninc = ninc


class _Rec:
    def __init__(self):
        self.call = None

    def __getattr__(self, name):
        def f(*a, **k):
            self.call = (name, a, k)
            return self
        return f


class Prog:
    def __init__(self, nc, stack):
        self.nc = nc
        self.stack = stack
        self.ops = []
        self.last_write = {}
        self.readers = {}
        self.seg = 0
        self.eng_obj = {"pe": nc.tensor, "act": nc.scalar, "dve": nc.vector,
                        "pool": nc.gpsimd, "sp": nc.sync}
        self.sems = {}
        self.dma_sems = {}
        self.fence_ops = set()
        self.ps_last = {}
        self.touch = {}

    def next_segment(self):
        self.seg += 1

    def dma_sem(self, name):
        if name not in self.dma_sems:
            s = self.stack.enter_context(self.nc.semaphore("d_" + name))
            self.dma_sems[name] = [s, 0]
        return name

    def fence(self):
        per_eng = {}
        dmas = set()
        for k in list(self.touch.keys()):
            ids = []
            w = self.last_write.pop(k, None)
            if w is not None:
                ids.append(w)
            ids.extend(self.readers.pop(k, []))
            for i in ids:
                o = self.ops[i]
                if o.is_dma:
                    dmas.add(i)
                else:
                    if per_eng.get(o.eng, -1) < i:
                        per_eng[o.eng] = i
        for i in self.fence_ops:
            o = self.ops[i]
            if o.is_dma:
                dmas.add(i)
            elif per_eng.get(o.eng, -1) < i:
                per_eng[o.eng] = i
        self.fence_ops = set(per_eng.values()) | dmas
        self.touch = {}

    def op(self, eng, fn, reads=(), writes=(), dma_sem=None, ninc=1):
        import os as _os
        _lim = int(_os.environ.get("LIMIT", "0"))
        if _lim and len(self.ops) >= _lim:
            return None
        deps = set()
        lw = self.last_write
        rd = self.readers
        for k in reads:
            w = lw.get(k)
            if w is not None:
                deps.add(w)
            if isinstance(k, tuple) and k[0] == "A" and k not in self.touch:
                deps.update(self.fence_ops)
        for k in writes:
            w = lw.get(k)
            if w is not None:
                deps.add(w)
            r = rd.get(k)
            if r:
                deps.update(r)
            if isinstance(k, tuple) and k[0] == "A" and k not in self.touch:
                deps.update(self.fence_ops)
        idx = len(self.ops)
        for k in tuple(reads) + tuple(writes):
            if isinstance(k, tuple) and k[0] == "ps":
                ent = self.ps_last.setdefault(k, {})
                for e2, i2 in ent.items():
                    if e2 != eng:
                        deps.add(i2)
                ent[eng] = idx
        rec = _Rec()
        fn(rec)
        assert rec.call is not None
        self.ops.append(_Op(eng, rec.call, deps, dma_sem, self.seg, ninc))
        for k in reads:
            if isinstance(k, tuple) and k[0] == "A":
                self.touch[k] = None
            if isinstance(k, tuple) and k[0] == "const":
                continue
            rd.setdefault(k, []).append(idx)
        for k in writes:
            if isinstance(k, tuple) and k[0] == "A":
                self.touch[k] = None
            lw[k] = idx
            rd[k] = []
        return idx

    @staticmethod
    def _skip(do, o):
        return do.eng == "pe" and o.eng == "pe" and not do.is_dma and not o.is_dma

    def emit(self, final_wait_eng="sp"):
        nc = self.nc
        ops = self.ops
        for o in ops:
            for d in o.deps:
                do = ops[d]
                if self._skip(do, o):
                    continue
                do.needs_inc = True
        counters = {}
        for o in ops:
            if o.is_dma:
                ent = self.dma_sems[o.dma_sem]
                ent[1] += 16 * o.ninc
                o.token = (ent[0], ent[1], o.dma_sem)
                o.needs_inc = True
            elif o.needs_inc:
                key = (o.eng, o.seg)
                if key not in self.sems:
                    self.sems[key] = self.stack.enter_context(nc.semaphore("p_%s_%d" % key))
                counters[key] = counters.get(key, 0) + 1
                o.token = (self.sems[key], counters[key], key)
        waited = {e: {} for e in self.eng_obj}
        nwaits = 0
        for o in ops:
            eobj = self.eng_obj[o.eng]
            need = {}
            wd = waited[o.eng]
            for d in o.deps:
                do = ops[d]
                if self._skip(do, o):
                    continue
                sem, val, key = do.token
                if wd.get(key, 0) >= val:
                    continue
                if need.get(key, (None, 0))[1] < val:
                    need[key] = (sem, val)
            for key, (sem, val) in need.items():
                eobj.wait_ge(sem, val)
                wd[key] = val
                nwaits += 1
            mname, margs, mkw = o.fn
            inst = getattr(eobj, mname)(*margs, **mkw)
            if o.is_dma:
                insts = inst if isinstance(inst, (list, tuple)) else [inst]
                assert len(insts) == o.ninc
                for i in insts:
                    i.then_inc(o.token[0], 16)
            elif o.needs_inc:
                inst.then_inc(o.token[0], 1)
        eobj = self.eng_obj[final_wait_eng]
        for name, (sem, val) in self.dma_sems.items():
            if val > 0:
                eobj.wait_ge(sem, val)
        return dict(n_ops=len(ops), n_waits=nwaits, n_sems=len(self.sems) + len(self.dma_sems))


def _layout(names_sizes):
    off = {}
    o = 0
    for n, s in names_sizes:
        off[n] = (o, s)
        o += s
    return off, o


CF_ITEMS = [("g_mix", 8), ("g_mlp", 8), ("g_ple", 8), ("gn_g", 16), ("zs", 8), ("inv", 96),
            ("pek", 32), ("pev", 32), ("AC", 512), ("g_final", 1024), ("mhalf", 8)]
CF_OFF, CF_W = _layout(CF_ITEMS)
CB_ITEMS = [("decayT", 1024), ("xi", 1024), ("tri", 128), ("old", 128), ("maskC", 2048),
            ("VM", 512), ("ident", 128), ("E", 2048), ("ov", 32), ("ck2", 256), ("cv2", 128)]
CB_OFF, CB_W = _layout(CB_ITEMS)


def host_consts(inp):
    f32 = np.float32
    cf = np.zeros((128, CF_W), f32)
    cb = np.zeros((128, CB_W), f32)

    def putf(name, arr):
        o, s = CF_OFF[name]
        cf[:, o:o + s] = np.asarray(arr, f32).reshape(128, s)

    def putb(name, arr):
        o, s = CB_OFF[name]
        cb[:, o:o + s] = np.asarray(arr, f32).reshape(128, s)

    colmaj = lambda g, k: np.asarray(g, f32).reshape(k, 128).T
    putf("g_mix", colmaj(inp["norm_mix_g"][0], 8))
    putf("g_mlp", colmaj(inp["norm_mlp_g"][0], 8))
    putf("g_ple", colmaj(inp["norm_ple_g"][0], 8))
    putf("gn_g", colmaj(inp["ret_gn_g"][0], 16))
    log_g = np.log(1.0 - 2.0 ** (-5.0 - np.arange(8, dtype=f32))).astype(f32)
    idx = np.arange(128, dtype=f32)
    diff = idx[:, None] - idx[None, :]
    decay = np.where(diff[None] >= 0, np.exp(np.maximum(diff, 0.0)[None] * log_g[:, None, None]), 0.0)
    sc = 128.0 ** -0.5
    putb("decayT", np.transpose(decay, (2, 0, 1)) * sc)
    xi = np.exp((idx + 1.0)[None] * log_g[:, None])
    putb("xi", np.broadcast_to(xi[None], (128, 8, 128)))
    zeta = np.exp((127.0 - idx)[None] * log_g[:, None])
    putf("zs", zeta.T * sc)
    inv_r = (f32(10000.0) ** (-np.arange(0, 128, 2, dtype=f32) / f32(128))).astype(f32)
    inv_n = (f32(10000.0) ** (-np.arange(0, 64, 2, dtype=f32) / f32(64))).astype(f32)
    putf("inv", np.broadcast_to(np.concatenate([inv_r, inv_n])[None], (128, 96)))
    pek = np.asarray(inp["cmp_pe_k"][0], f32)
    pev = np.asarray(inp["cmp_pe_v"][0], f32)
    putf("pek", np.concatenate([pek.T, pek.T], 0))
    putf("pev", np.concatenate([pev.T, pev.T], 0))
    putf("g_final", np.broadcast_to(np.asarray(inp["norm_final_g"], f32)[None], (128, 1024)))
    putf("mhalf", np.full((128, 8), -0.5, f32))
    q = np.arange(128)
    putb("tri", np.where(q[:, None] <= q[None, :], 0.0, -BIGM))
    putb("old", np.where(q[:, None] > q[None, :], 0.0, -BIGM))
    slot = np.arange(128)
    c = slot - 1
    gt = np.arange(16)
    t_abs = gt[:, None] * 128 + q[None, :]
    mC = ((16 * c[:, None, None] + 31) <= t_abs[None]) & (slot[:, None, None] >= 1)
    putb("maskC", np.where(mC, 0.0, -BIGM))
    blk = np.arange(32)
    cur = (t_abs.T // 64)
    forced = (blk[None, None] == 0) | (blk[None, None] == cur[..., None]) | (blk[None, None] == cur[..., None] - 1)
    valid = blk[None, None] <= cur[..., None]
    putb("VM", (valid & ~forced).astype(f32))
    putf("AC", np.where(forced, 1e6, np.where(valid, 0.0, -1.0)))
    putb("ident", np.eye(128, dtype=f32))
    E = np.zeros((128, 16, 128), f32)
    key = np.arange(128)
    for kt in range(16):
        for b in range(32):
            E[b, kt, :] = BIGM * (b == 2 * kt + key // 64)
        E[32, kt, :] = -BIGM
    putb("E", E)
    ov = ((16 * c[:, None] < 64 * (blk[None] + 1)) & (16 * c[:, None] + 31 >= 64 * blk[None]) & (slot[:, None] >= 1))
    putb("ov", ov.astype(f32))
    w2k = np.asarray(inp["cmp_k_w2"][0], f32)
    w2v = np.asarray(inp["cmp_v_w2"][0], f32)
    ck2 = np.zeros((128, 2, 128), f32)
    cv2 = np.zeros((128, 2, 64), f32)
    for hh in range(2):
        ck2[:, hh, 0:64] = w2k[hh * 128:(hh + 1) * 128]
        ck2[:, hh, 64:128] = w2k[hh * 128:(hh + 1) * 128]
        cv2[:, hh, :] = w2v[hh * 128:(hh + 1) * 128]
    putb("ck2", ck2)
    putb("cv2", cv2)
    return cf, cb


def piece_names():
    names = ["S1", "S2", "Q1", "Q2", "CK1", "CK2", "CV1", "CV2"]
    for hp in range(4):
        names += ["B%d" % hp, "A%d" % (2 * hp), "A%d" % (2 * hp + 1)]
    names += ["MG0", "MG1", "MG2", "MG3"]
    for i in range(4):
        if i % 2 == 0:
            names.append("NO%d" % (i // 2))
        names.append("RO%d" % i)
    names += ["WO0", "WO1"]
    for qd in range(4):
        names += ["UP%d" % (2 * qd), "UP%d" % (2 * qd + 1), "DN%d" % (2 * qd), "DN%d" % (2 * qd + 1)]
    names += ["PP", "PG0", "PG1"]
    return names


PIECES = piece_names()
PIDX = {n: i for i, n in enumerate(PIECES)}
NP_ = len(PIECES)


def host_pack(inp):
    f32 = np.float32
    W = np.zeros((NP_, 128, PIECE), f32)
    w_in = np.asarray(inp["w_in"][0], f32)
    o = 0
    sl = {}
    for n, s in [("rq", 1024), ("rk", 1024), ("rv", 2048), ("rg", 2048), ("nq", 1024), ("kc", 128),
                 ("vc", 128), ("ksl", 128), ("vsl", 128), ("kw", 128), ("vw", 128), ("ng", 48)]:
        sl[n] = w_in[:, o:o + s]
        o += s

    def kpiece(cols):
        out = np.zeros((128, 8, 512), f32)
        out[:, :, :cols.shape[1]] = cols.reshape(8, 128, -1).transpose(1, 0, 2)
        return out.reshape(128, PIECE)

    W[PIDX["S1"]] = kpiece(np.concatenate([sl["kc"], sl["ksl"], sl["kw"], sl["vc"]], 1))
    W[PIDX["S2"]] = kpiece(np.concatenate([sl["vsl"], sl["vw"], sl["ng"]], 1))
    nq = sl["nq"].reshape(1024, 16, 64)
    order = []
    for i in range(8):
        order += [i, 8 + i]
    nqp = nq[:, order, :].reshape(1024, 1024)
    W[PIDX["Q1"]] = kpiece(nqp[:, 0:512])
    W[PIDX["Q2"]] = kpiece(nqp[:, 512:1024])
    for h in range(8):
        W[PIDX["A%d" % h]] = kpiece(np.concatenate(
            [sl["rq"][:, h * 128:(h + 1) * 128], sl["rk"][:, h * 128:(h + 1) * 128],
             sl["rv"][:, h * 256:(h + 1) * 256]], 1))
    for hp in range(4):
        W[PIDX["B%d" % hp]] = kpiece(sl["rg"][:, hp * 512:(hp + 1) * 512])
    for nm, key in (("CK", "cmp_k_w1"), ("CV", "cmp_v_w1")):
        w1 = np.asarray(inp[key][0], f32).reshape(32, 64, 256)
        for half in range(2):
            blk = w1[half * 16:(half + 1) * 16]
            pc = np.concatenate([blk.transpose(1, 0, 2)] * 2, 0)
            W[PIDX["%s%d" % (nm, half + 1)]] = pc.reshape(128, PIECE)
    wm = np.asarray(inp["w_merge_gate"][0], f32)
    for i in range(4):
        W[PIDX["MG%d" % i]] = kpiece(wm[:, i * 512:(i + 1) * 512])
    wno = np.asarray(inp["w_nsa_o"][0], f32)
    rows = []
    for i in range(8):
        rows += list(range(i * 64, (i + 1) * 64)) + list(range((8 + i) * 64, (9 + i) * 64))
    wno = wno[rows, :]
    for i in range(2):
        W[PIDX["NO%d" % i]] = kpiece(wno[:, i * 512:(i + 1) * 512])
    wro = np.asarray(inp["w_ret_o"][0], f32)
    for i in range(4):
        pc = wro[:, i * 256:(i + 1) * 256].reshape(16, 128, 256).transpose(1, 0, 2)
        W[PIDX["RO%d" % i]] = pc.reshape(128, PIECE)
    wo = np.asarray(inp["w_out"][0], f32)
    for i in range(2):
        W[PIDX["WO%d" % i]] = kpiece(wo[:, i * 512:(i + 1) * 512])
    wu = np.asarray(inp["w_mlp_up"][0], f32)
    wd = np.asarray(inp["w_mlp_down"][0], f32)
    for i in range(8):
        W[PIDX["UP%d" % i]] = kpiece(wu[:, i * 512:(i + 1) * 512])
    for qd in range(4):
        for ch in range(2):
            W[PIDX["DN%d" % (2 * qd + ch)]] = kpiece(wd[qd * 1024:(qd + 1) * 1024, ch * 512:(ch + 1) * 512])
    wg = np.asarray(inp["w_ple_gate"][0], f32)
    for i in range(2):
        W[PIDX["PG%d" % i]] = kpiece(wg[:, i * 512:(i + 1) * 512])
    wp = np.asarray(inp["w_ple_proj"][0], f32)
    pp = np.zeros((128, PIECE), f32)
    pp[:, :2048] = wp.reshape(2, 128, 1024).transpose(1, 0, 2).reshape(128, 2048)
    W[PIDX["PP"]] = pp
    return W


def build_program(nseq=4, nblk=4, dump=None, stages=99):
    nc = bass.Bass("TRN2", target_bir_lowering=False)
    ntok = nseq * SEQ
    x_d = nc.dram_tensor("x", [ntok, DM], F32, kind="ExternalInput")
    p_d = nc.dram_tensor("p", [ntok, 256], F32, kind="ExternalInput")
    pos_d = nc.dram_tensor("posl", [128, nseq * 16], I32, kind="ExternalInput")
    wp_d = nc.dram_tensor("wpack", [NP_, 128, PIECE], F32, kind="ExternalInput")
    cf_d = nc.dram_tensor("cf", [128, CF_W], F32, kind="ExternalInput")
    cb_d = nc.dram_tensor("cb", [128, CB_W], F32, kind="ExternalInput")
    out_d = nc.dram_tensor("out", [ntok, DM], F32, kind="ExternalOutput")
    wbf_d = nc.dram_tensor("wbf", [NP_, 128, PIECE], BF16, kind="ExternalOutput")
    dumps = {}

    st = ExitStack()
    with st:
        P = Prog(nc, st)
        sbt = lambda n, s, d: st.enter_context(nc.sbuf_tensor(n, s, d))
        psum = st.enter_context(nc.psum_tensor("psum", [128, 4096], F32))

        def bank(b, n=512, off=0):
            return psum[:, b * 512 + off: b * 512 + off + n]

        def bankb(b, n=1024, off=0):
            return psum[:, b * 512:(b + 1) * 512].bitcast(BF16)[:, off:off + n]

        PS = lambda b: ("ps", b)

        cf = sbt("cf_s", [128, CF_W], F32)
        cb = sbt("cb_s", [128, CB_W], BF16)
        posi = sbt("posi", [128, nseq * 16], I32)
        posf = sbt("posf", [128, nseq * 16], F32)
        NSLOT = 4
        wring = [sbt("wring%d" % i, [128, PIECE], BF16) for i in range(NSLOT)]
        kslT = [sbt("kslT%d" % g_, [128, SEQ], BF16) for g_ in range(2)]
        kwT = [sbt("kwT%d" % g_, [128, SEQ], BF16) for g_ in range(2)]
        KcTz = sbt("KcTz", [128, 2, 128], BF16)
        vslA = sbt("vslA", [128, 16, 2, 65], BF16)
        vwA = sbt("vwA", [128, 16, 2, 65], BF16)
        kcT = sbt("kcT", [128, 16 + SEQ], BF16)
        vcT = sbt("vcT", [128, 16 + SEQ], BF16)
        hidk = sbt("hidk", [128, 2, 2, 128], BF16)
        hidv = sbt("hidv", [128, 2, 2, 128], BF16)
        VcA = sbt("VcA", [128, 2, 97], BF16)
        Rst = sbt("Rst", [128, 8, 256], F32)
        hT = sbt("hT", [128, 8, TB], BF16)
        oretT = sbt("oretT", [128, 16, TB], BF16)
        onsaT = sbt("onsaT", [128, 8, TB], BF16)
        tabs = sbt("tabs", [128, NT, 2, 96], F32)
        cosR = sbt("cosR", [128, NT, 128], F32)
        sinR = sbt("sinR", [128, NT, 128], F32)
        cosN = sbt("cosN", [128, NT, 64], F32)
        sinN = sbt("sinN", [128, NT, 64], F32)
        ARENA_W = 16 * 1024
        arena = sbt("arena", [128, ARENA_W], F32)
        astate = {"off": 0}

        def cfv(name, *shape):
            o, s = CF_OFF[name]
            v = cf[:, o:o + s]
            return v

        def cbv(name):
            o, s = CB_OFF[name]
            return cb[:, o:o + s]

        def a_reset():
            astate["off"] = 0
            P.fence()

        def a_alloc(n, dtype):
            words = (n * (2 if dtype == BF16 else 4) + 3) // 4
            words = (words + 7) // 8 * 8
            o = astate["off"]
            assert o + words <= ARENA_W, ("arena overflow", o, words)
            astate["off"] = o + words
            v = arena[:, o:o + words]
            if dtype == BF16:
                v = v.bitcast(BF16)
            elif dtype == I32:
                v = v.bitcast(I32)
            return v[:, 0:n]

        for nme in ["cf", "cb", "pos", "x0", "x1", "p0", "p1", "out", "dump"] + ["w%d" % i for i in range(NSLOT)]:
            P.dma_sem(nme)

        def do_dump(name, ap, reads, shape, dtype=F32):
            if dump is None or name not in dump or name in dumps:
                return
            d = nc.dram_tensor("dump_" + name, list(shape), dtype, kind="ExternalOutput")
            dumps[name] = d
            P.op("sp", lambda e: e.dma_start(out=d.ap(), in_=ap), reads=reads, dma_sem="dump")

        P.op("sp", lambda e: e.dma_start(out=cf[:], in_=cf_d.ap()), writes=[("const", "cf")], dma_sem="cf")
        P.op("sp", lambda e: e.dma_start(out=posi[:], in_=pos_d.ap()), writes=["posi"], dma_sem="pos")
        P.op("dve", lambda e: e.tensor_copy(out=posf[:], in_=posi[:]), reads=["posi"], writes=[("const", "posf")])
        cbst = a_alloc(CB_W, F32)
        P.op("sp", lambda e: e.dma_start(out=cbst, in_=cb_d.ap()), writes=[("A", "cbst")], dma_sem="cb")
        P.op("dve", lambda e: e.tensor_copy(out=cb[:], in_=cbst), reads=[("A", "cbst")], writes=[("const", "cb")])
        a_reset()
        wst = [a_alloc(PIECE, F32) for _ in range(2)]
        wsb = [a_alloc(PIECE, BF16) for _ in range(2)]
        for i in range(3):
            P.dma_sem("wst%d" % i)
        for i in range(2):
            P.dma_sem("wsb%d" % i)
        import os as _os
        _skip = _os.environ.get("SKIP", "").split(",")
        def _cast_load(i):
            P.op("sp", lambda e: e.dma_start(out=wst[i % 2], in_=wp_d.ap()[i]),
                 writes=[("A", "wst", i % 2)], dma_sem="wst%d" % (i % 2))

        ncast = 0 if "cast" in _skip else NP_
        for i in range(min(2, ncast)):
            _cast_load(i)
        for i in range(ncast):
            a3, b2 = i % 2, i % 2
            if i % 2 == 0:
                P.op("act", lambda e: e.copy(out=wsb[b2], in_=wst[a3]),
                     reads=[("A", "wst", a3)], writes=[("A", "wsb", b2)])
            else:
                P.op("dve", lambda e: e.tensor_copy(out=wsb[b2], in_=wst[a3]),
                     reads=[("A", "wst", a3)], writes=[("A", "wsb", b2)])
            P.op("sp", lambda e: e.dma_start(out=wbf_d.ap()[i], in_=wsb[b2]),
                 reads=[("A", "wsb", b2)], writes=[("wbf", i)], dma_sem="wsb%d" % b2)
            if i + 2 < ncast:
                _cast_load(i + 2)
        for tname, t in (() if "memset" in _skip else (("kcT", kcT), ("vcT", vcT), ("hidk", hidk), ("hidv", hidv))):
            P.op("pool", (lambda t: lambda e: e.memset(t[:], 0.0))(t), writes=[tname])
        P.op("pool", lambda e: e.memset(VcA[:], 0.0), writes=["VcA"])
        P.op("pool", lambda e: e.memset(KcTz[:], 0.0), writes=["KcT"])
        for g_ in range(2):
            P.op("pool", lambda e: e.memset(kslT[g_][:], 0.0), writes=[("kslT", k_) for k_ in range(16)])
            P.op("pool", lambda e: e.memset(kwT[g_][:], 0.0), writes=[("kwT", k_) for k_ in range(16)])
        P.op("dve", lambda e: e.memset(VcA[:, :, 64:65], 1.0), reads=["VcA"], writes=["VcA"])
        for g in range(2):
            P.op("dve", (lambda g: lambda e: e.tensor_copy(out=VcA[:, g, 65:97], in_=cbv("ov")))(g),
                 reads=[("const", "cb"), "VcA"], writes=["VcA"])
        P.op("pool", lambda e: e.memset(vslA[:], 1.0), writes=["vslA"])
        P.op("pool", lambda e: e.memset(vwA[:], 1.0), writes=["vwA"])

        ncut = {1: 0, 2: 4, 3: 8, 4: 8, 5: 20}.get(stages, NP_)
        border = PIECES[:ncut]
        nstream = len(border) * nseq * nblk
        wstate = {"use": 0, "iss": 0}
        released = set()
        held = set()
        pending = []

        def try_issue(upto):
            while wstate["iss"] < min(upto, nstream):
                n = wstate["iss"]
                if n - NSLOT >= 0 and (n - NSLOT) not in released:
                    break
                wstate["iss"] += 1
                slot = n % NSLOT
                pi = PIDX[border[n % len(border)]]
                P.op("sp", lambda e: e.dma_start(out=wring[slot][:], in_=wbf_d.ap()[pi]),
                     reads=[("wbf", pi)], writes=[("wr", slot)], dma_sem="w%d" % slot)

        def wrelease(i):
            held.discard(i)
            released.add(i)
            try_issue(wstate["use"] + NSLOT)

        def wload(name, hold=False):
            i = wstate["use"]
            assert border[i % len(border)] == name, (name, border[i % len(border)])
            for q in list(pending):
                if q not in held:
                    released.add(q)
                    pending.remove(q)
            wstate["use"] += 1
            try_issue(i + NSLOT)
            assert wstate["iss"] > i, ("weight ring deadlock", name, i)
            pending.append(i)
            if hold:
                held.add(i)
            wload.last = i
            return wring[i % NSLOT], ("wr", i % NSLOT)

        CONST = [("const", "cf"), ("const", "cb"), ("const", "posf")]
        ident = cbv("ident")

        def transposes(src_fn, n, tb, keys_r):
            for k in range(n):
                P.op("pe", (lambda k: lambda e: e.transpose(out=bankb(tb, 128, k * 128), in_=src_fn(k), identity=ident))(k),
                     reads=keys_r + [("const", "cb")], writes=[PS(tb)])

        def bc(ap2d, dims):
            return bass.AP(ap2d.tensor, ap2d.offset, [list(ap2d.ap[0])] + [list(d) for d in dims])

        def rms_to_hT(src_ap, src_keys, gname, t, tb, hn, junk, ssq, rstd):
            i2 = t % 2
            hn, junk, ssq, rstd = hn[i2], junk[i2], ssq[i2], rstd[i2]
            P.op("act", lambda e: e.activation(out=junk, in_=src_ap, func=AF.Square, accum_out=ssq),
                 reads=src_keys, writes=[("A", "junk"), ("A", "ssq", i2)])
            P.op("dve", lambda e: e.tensor_scalar(out=rstd, in0=ssq, scalar1=1.0 / DM, scalar2=EPS,
                                                  op0=ALU.mult, op1=ALU.add),
                 reads=[("A", "ssq", i2)], writes=[("A", "rstd", i2)])
            P.op("pool", lambda e: e.tensor_tensor(out=rstd, in0=rstd, in1=cfv("mhalf")[:, 0:1], op=ALU.pow),
                 reads=[("A", "rstd", i2), ("const", "cf")], writes=[("A", "rstd", i2)])
            P.op("act", lambda e: e.activation(out=hn, in_=src_ap, func=AF.Copy, scale=rstd),
                 reads=src_keys + [("A", "rstd", i2)], writes=[("A", "hn", i2)])
            transposes(lambda k: hn[:, k * 128:(k + 1) * 128], 8, tb, [("A", "hn", i2)])
            g = cfv(gname)
            P.op("dve", lambda e: e.tensor_tensor(
                out=hT[:, :, t * 128:(t + 1) * 128],
                in0=bankb(tb).rearrange("p (k c) -> p k c", k=8),
                in1=bc(g, [[1, 8], [0, 128]]), op=ALU.mult),
                reads=[PS(tb), ("const", "cf")], writes=[("hT", t)])

        def rope(e_unused, psv, nh, hd, cosv, sinv, outv, tmp1, tmp2, rkeys, wkeys, tkeys):
            h2 = hd // 2
            x3 = psv.rearrange("p (h d) -> p h d", h=nh)
            P.op("dve", lambda e: e.tensor_tensor(out=tmp1.rearrange("p (h d) -> p h d", h=nh), in0=x3,
                                                  in1=bc(cosv, [[0, nh], [1, hd]]), op=ALU.mult),
                 reads=rkeys + ["TABS"], writes=[tkeys[0]])
            t23 = tmp2.rearrange("p (h d) -> p h d", h=nh)
            P.op("dve", lambda e: e.tensor_tensor(out=t23[:, :, 0:h2], in0=x3[:, :, h2:hd],
                                                  in1=bc(sinv[:, 0:h2], [[0, nh], [1, h2]]), op=ALU.mult),
                 reads=rkeys + ["TABS"], writes=[tkeys[1]])
            P.op("dve", lambda e: e.tensor_tensor(out=t23[:, :, h2:hd], in0=x3[:, :, 0:h2],
                                                  in1=bc(sinv[:, h2:hd], [[0, nh], [1, h2]]), op=ALU.mult),
                 reads=rkeys + ["TABS"], writes=[tkeys[1]])
            P.op("dve", lambda e: e.tensor_tensor(out=outv, in0=tmp1, in1=tmp2, op=ALU.add),
                 reads=list(tkeys), writes=wkeys)

        for s in range(nseq if stages > 0 else 0):
            for j in range(nblk):
                P.next_segment()
                row0 = s * SEQ + j * TB
                T0 = j * TB
                a_reset()
                xt = [a_alloc(1024, F32), a_alloc(1024, F32)]
                hn = [a_alloc(1024, BF16), a_alloc(1024, BF16)]
                junk = [a_alloc(1024, BF16)] * 2
                ssq = [a_alloc(1, F32), a_alloc(1, F32)]
                rstd = [a_alloc(1, F32), a_alloc(1, F32)]
                ang = a_alloc(NT * 2 * 96, F32)
                angk = a_alloc(NT * 2 * 96, F32)
                angi = a_alloc(NT * 2 * 96, I32)
                ang4 = ang.rearrange("p (t a f) -> p t a f", t=NT, a=2)
                inv = cfv("inv")
                for t in range(0 if "tabs" in _skip else NT):
                    col = s * 16 + j * NT + t
                    P.op("dve", (lambda t, col: lambda e: e.tensor_scalar(
                        out=ang4[:, t, 0, :], in0=inv, scalar1=posf[:, col:col + 1], scalar2=None, op0=ALU.mult))(t, col),
                        reads=CONST, writes=[("A", "ang")])
                    P.op("dve", (lambda t, col: lambda e: e.tensor_scalar(
                        out=ang4[:, t, 1, :], in0=inv, scalar1=posf[:, col:col + 1], scalar2=math.pi / 2,
                        op0=ALU.mult, op1=ALU.add))(t, col),
                        reads=CONST, writes=[("A", "ang")])
                P.op("dve", lambda e: e.tensor_scalar(out=angk, in0=ang, scalar1=1.0 / TWO_PI, scalar2=None, op0=ALU.mult),
                     reads=[("A", "ang")], writes=[("A", "angk")])
                P.op("dve", lambda e: e.tensor_copy(out=angi, in_=angk), reads=[("A", "angk")], writes=[("A", "angi")])
                P.op("dve", lambda e: e.tensor_copy(out=angk, in_=angi), reads=[("A", "angi")], writes=[("A", "angk")])
                P.op("dve", lambda e: e.scalar_tensor_tensor(out=ang, in0=angk, scalar=-C1, in1=ang, op0=ALU.mult, op1=ALU.add),
                     reads=[("A", "angk"), ("A", "ang")], writes=[("A", "ang")])
                P.op("dve", lambda e: e.scalar_tensor_tensor(out=ang, in0=angk, scalar=-C2, in1=ang, op0=ALU.mult, op1=ALU.add),
                     reads=[("A", "angk"), ("A", "ang")], writes=[("A", "ang")])
                P.op("dve", lambda e: e.tensor_scalar(out=ang, in0=ang, scalar1=3.1415925, scalar2=-3.1415925,
                                                      op0=ALU.min, op1=ALU.max),
                     reads=[("A", "ang")], writes=[("A", "ang")])
                P.op("act", lambda e: e.activation(out=tabs[:].rearrange("p t a f -> p (t a f)"), in_=ang, func=AF.Sin),
                     reads=[("A", "ang")], writes=["tabs0"])
                P.op("dve", lambda e: e.tensor_copy(out=cosR[:, :, 0:64], in_=tabs[:, :, 1, 0:64]), reads=["tabs0"], writes=["TABS"])
                P.op("dve", lambda e: e.tensor_copy(out=cosR[:, :, 64:128], in_=tabs[:, :, 1, 0:64]), reads=["tabs0"], writes=["TABS"])
                P.op("dve", lambda e: e.tensor_scalar(out=sinR[:, :, 0:64], in0=tabs[:, :, 0, 0:64], scalar1=-1.0, scalar2=None, op0=ALU.mult),
                     reads=["tabs0"], writes=["TABS"])
                P.op("dve", lambda e: e.tensor_copy(out=sinR[:, :, 64:128], in_=tabs[:, :, 0, 0:64]), reads=["tabs0"], writes=["TABS"])
                P.op("dve", lambda e: e.tensor_copy(out=cosN[:, :, 0:32], in_=tabs[:, :, 1, 64:96]), reads=["tabs0"], writes=["TABS"])
                P.op("dve", lambda e: e.tensor_copy(out=cosN[:, :, 32:64], in_=tabs[:, :, 1, 64:96]), reads=["tabs0"], writes=["TABS"])
                P.op("dve", lambda e: e.tensor_scalar(out=sinN[:, :, 0:32], in0=tabs[:, :, 0, 64:96], scalar1=-1.0, scalar2=None, op0=ALU.mult),
                     reads=["tabs0"], writes=["TABS"])
                P.op("dve", lambda e: e.tensor_copy(out=sinN[:, :, 32:64], in_=tabs[:, :, 0, 64:96]), reads=["tabs0"], writes=["TABS"])
                if dump and "tabs" in dump:
                    do_dump("tabs", tabs[:].rearrange("p t a f -> p (t a f)"), ["tabs0"], [128, NT * 2 * 96])

                for t in range(0 if "norm" in _skip else NT):
                    xb = xt[t % 2]
                    P.op("sp", (lambda t, xb: lambda e: e.dma_start(out=xb, in_=x_d.ap()[row0 + t * 128: row0 + (t + 1) * 128, :]))(t, xb),
                         writes=[("A", "xt", t % 2)], dma_sem="x%d" % (t % 2))
                    rms_to_hT(xb, [("A", "xt", t % 2)], "g_mix", t, 6 + (t % 2), hn, junk, ssq, rstd)
                do_dump("hT", hT[:].rearrange("p k t -> p (k t)"), [("hT", t) for t in range(NT)], [128, 8 * TB], BF16)
                HT = [("hT", t) for t in range(NT)]
                if stages < 2:
                    continue

                sm = a_alloc(512, BF16)
                tmp1s = [a_alloc(512, F32), a_alloc(512, F32)]
                tmp2s = [a_alloc(512, F32), a_alloc(512, F32)]
                nq_tm = a_alloc(1024, BF16)
                nqT = a_alloc(8 * TB, BF16)
                nqT3 = nqT.rearrange("p (i t) -> p i t", i=8)
                sig = a_alloc(NT * 48, F32)
                sig3 = sig.rearrange("p (t c) -> p t c", t=NT)
                w, wk = wload("S1")
                w3 = w[:].rearrange("p (k c) -> p k c", k=8)
                for t in range(NT):
                    gtile = j * NT + t
                    b = t % 4
                    for k in range(8):
                        P.op("pe", (lambda t, k, b: lambda e: e.matmul(bank(b), lhsT=hT[:, k, t * 128:(t + 1) * 128], rhs=w3[:, k, :],
                                                                       start=(k == 0), stop=(k == 7)))(t, k, b),
                             reads=[("hT", t), wk], writes=[PS(b)])
                    rope(None, bank(b, 384), 6, 64, cosN[:, t, :], sinN[:, t, :], sm[:, 0:384],
                         tmp1s[t % 2][:, 0:384], tmp2s[t % 2][:, 0:384], [PS(b)], [("A", "sm")], [("A", "tmp1", t % 2), ("A", "tmp2", t % 2)])
                    P.op("act", (lambda b: lambda e: e.copy(out=sm[:, 384:512], in_=bank(b, 128, 384)))(b),
                         reads=[PS(b)], writes=[("A", "sm2")])
                    tb = 6 + (t % 2)
                    transposes(lambda k: sm[:, k * 128:(k + 1) * 128], 4, tb, [("A", "sm"), ("A", "sm2")])
                    c0 = T0 + t * 128
                    P.op("dve", (lambda tb, c0: lambda e: e.tensor_copy(out=kcT[:, 16 + c0:16 + c0 + 128], in_=bankb(tb, 128, 0)))(tb, c0),
                         reads=[PS(tb)], writes=["kcT"])
                    for g_ in range(2):
                        rs_ = slice(g_ * 64, (g_ + 1) * 64)
                        P.op("dve", lambda e: e.tensor_copy(out=kslT[g_][rs_, c0:c0 + 128], in_=bankb(tb, 128, 128)[rs_, :]),
                             reads=[PS(tb)], writes=[("kslT", gtile)])
                        P.op("dve", lambda e: e.tensor_copy(out=kwT[g_][rs_, c0:c0 + 128], in_=bankb(tb, 128, 256)[rs_, :]),
                             reads=[PS(tb)], writes=[("kwT", gtile)])
                    P.op("dve", (lambda tb, c0: lambda e: e.tensor_copy(out=vcT[:, 16 + c0:16 + c0 + 128], in_=bankb(tb, 128, 384)))(tb, c0),
                         reads=[PS(tb)], writes=["vcT"])
                w, wk = wload("S2")
                w3b = w[:].rearrange("p (k c) -> p k c", k=8)
                for t in range(NT):
                    gtile = j * NT + t
                    b = t % 4
                    for k in range(8):
                        P.op("pe", (lambda t, k, b, w3b: lambda e: e.matmul(bank(b, 304), lhsT=hT[:, k, t * 128:(t + 1) * 128], rhs=w3b[:, k, 0:304],
                                                                            start=(k == 0), stop=(k == 7)))(t, k, b, w3b),
                             reads=[("hT", t), wk], writes=[PS(b)])
                    P.op("act", (lambda b, gtile: lambda e: e.copy(out=vslA[:, gtile, :, 0:64],
                                                                   in_=bank(b, 128, 0).rearrange("p (g d) -> p g d", g=2)))(b, gtile),
                         reads=[PS(b)], writes=[("vslA", gtile)])
                    P.op("dve", (lambda b, gtile: lambda e: e.tensor_copy(out=vwA[:, gtile, :, 0:64],
                                                                          in_=bank(b, 128, 128).rearrange("p (g d) -> p g d", g=2)))(b, gtile),
                         reads=[PS(b)], writes=[("vwA", gtile)])
                    P.op("act", (lambda b, t: lambda e: e.activation(out=sig3[:, t, :], in_=bank(b, 48, 256), func=AF.Sigmoid))(b, t),
                         reads=[PS(b)], writes=[("A", "sig", t)])
                for qi, qn in enumerate(("Q1", "Q2")):
                    w, wk = wload(qn)
                    w3q = w[:].rearrange("p (k c) -> p k c", k=8)
                    for t in range(NT):
                        b = t % 4
                        for k in range(8):
                            P.op("pe", (lambda t, k, b, w3q: lambda e: e.matmul(bank(b), lhsT=hT[:, k, t * 128:(t + 1) * 128], rhs=w3q[:, k, :],
                                                                                start=(k == 0), stop=(k == 7)))(t, k, b, w3q),
                                 reads=[("hT", t), wk], writes=[PS(b)])
                        rope(None, bank(b), 8, 64, cosN[:, t, :], sinN[:, t, :], nq_tm[:, 0:512],
                             tmp1s[t % 2], tmp2s[t % 2], [PS(b)], [("A", "nq_tm")], [("A", "tmp1", t % 2), ("A", "tmp2", t % 2)])
                        tb = 6 + (t % 2)
                        transposes(lambda k: nq_tm[:, k * 128:(k + 1) * 128], 4, tb, [("A", "nq_tm")])
                        P.op("dve", (lambda tb, qi, t: lambda e: e.tensor_copy(
                            out=nqT3[:, qi * 4:(qi + 1) * 4, t * 128:(t + 1) * 128],
                            in_=bankb(tb, 512).rearrange("p (i c) -> p i c", i=4)))(tb, qi, t),
                            reads=[PS(tb)], writes=[("A", "nqT", t)])
                do_dump("kslT", kslT[0][:], [("kslT", j * NT + t) for t in range(NT)], [128, SEQ], BF16)
                do_dump("nqT", nqT, [("A", "nqT", t) for t in range(NT)], [128, 8 * TB], BF16)
                do_dump("sig", sig, [("A", "sig", t) for t in range(NT)], [128, NT * 48])
                if stages < 3:
                    continue

                kpe = a_alloc(32 * 32, BF16)
                kpe3 = kpe.rearrange("p (l c) -> p l c", l=32)
                gl = [a_alloc(128, F32) for _ in range(3)]
                s0 = 32 * j
                for nm, cache, pen, hid, w2n in (("CK", kcT, "pek", hidk, "ck2"), ("CV", vcT, "pev", hidv, "cv2")):
                    src = cache[:, 16 * s0: 16 * s0 + 1]
                    src = bass.AP(src.tensor, src.offset, [list(src.ap[0]), [1, 32], [16, 32]])
                    pe_ap = cfv(pen)
                    P.op("dve", (lambda src, pe_ap: lambda e: e.tensor_tensor(out=kpe3, in0=src, in1=bc(pe_ap, [[1, 32], [0, 32]]), op=ALU.add))(src, pe_ap),
                         reads=[nm[1] == "K" and "kcT" or "vcT", ("const", "cf")], writes=[("A", "kpe")])
                    wA, wkA = wload(nm + "1", hold=True)
                    iA = wload.last
                    wB, wkB = wload(nm + "2", hold=True)
                    iB = wload.last
                    for g in range(2):
                        first = True
                        for hh in range(2):
                            for l in range(32):
                                wsrc, wkey = (wA, wkA) if l < 16 else (wB, wkB)
                                w1v = wsrc[:].rearrange("p (l j) -> p l j", l=16)
                                P.op("pe", lambda e: e.matmul(
                                    bank(g, 32, hh * 32),
                                    lhsT=w1v[g * 64:(g + 1) * 64, l % 16, hh * 128:(hh + 1) * 128],
                                    rhs=kpe3[g * 64:(g + 1) * 64, l, :],
                                    start=first, stop=(l == 31), skip_group_check=True),
                                    reads=[("A", "kpe"), wkey], writes=[PS(g)])
                                first = False
                    wrelease(iA)
                    wrelease(iB)
                    xh = bass.AP(psum, 0, [[4096, 128], [512, 2], [1, 64]])
                    g3 = [t_.rearrange("p (g c) -> p g c", g=2) for t_ in gl]
                    PH = [PS(0), PS(1)]
                    P.op("act", lambda e: e.activation(out=g3[0], in_=xh, func=AF.Square), reads=PH, writes=[("A", "gl0")])
                    P.op("dve", lambda e: e.tensor_scalar(out=gl[0], in0=gl[0], scalar1=0.044715, scalar2=1.0, op0=ALU.mult, op1=ALU.add),
                         reads=[("A", "gl0")], writes=[("A", "gl0")])
                    P.op("dve", lambda e: e.tensor_tensor(out=g3[1], in0=xh, in1=g3[0], op=ALU.mult), reads=PH + [("A", "gl0")], writes=[("A", "gl1")])
                    P.op("act", lambda e: e.activation(out=gl[2], in_=gl[1], func=AF.Sigmoid, scale=1.5957691216), reads=[("A", "gl1")], writes=[("A", "gl2")])
                    xh4 = bass.AP(psum, 0, [[4096, 128], [512, 2], [32, 2], [1, 32]])
                    P.op("dve", lambda e: e.tensor_tensor(out=hid[:, :, :, s0:s0 + 32], in0=xh4,
                                                          in1=gl[2].rearrange("p (g h c) -> p g h c", g=2, h=2), op=ALU.mult),
                         reads=PH + [("A", "gl2")], writes=[nm])
                ck2 = cbv("ck2").rearrange("p (h d) -> p h d", h=2)
                cv2 = cbv("cv2").rearrange("p (h d) -> p h d", h=2)
                for g in range(2):
                    for hh in range(2):
                        P.op("pe", (lambda g, hh: lambda e: e.matmul(bank(2, 128, g * 128), lhsT=ck2[:, hh, :], rhs=hidk[:, g, hh, :],
                                                                     start=(g == 0 and hh == 0), stop=(hh == 1), skip_group_check=True))(g, hh),
                             reads=["CK", ("const", "cb")], writes=[PS(2)])
                for g in range(2):
                    for hh in range(2):
                        P.op("pe", (lambda g, hh: lambda e: e.matmul(bank(3, 64, g * 64), lhsT=hidv[:, g, hh, :], rhs=cv2[:, hh, :],
                                                                     start=(g == 0 and hh == 0), stop=(hh == 1), skip_group_check=True))(g, hh),
                             reads=["CV", ("const", "cb")], writes=[PS(3)])
                for g_ in range(2):
                    rs_ = slice(g_ * 64, (g_ + 1) * 64)
                    P.op("act", lambda e: e.copy(out=KcTz[rs_, g_, :], in_=bank(2, 128, g_ * 128)[rs_, :]), reads=[PS(2)], writes=["KcT"])
                P.op("dve", lambda e: e.tensor_copy(out=VcA[:, :, 0:64], in_=bank(3, 128).rearrange("p (g d) -> p g d", g=2)),
                     reads=[PS(3), "VcA"], writes=["VcA"])
                do_dump("KcT", KcTz[:].rearrange("p g c -> p (g c)"), ["KcT"], [128, 256], BF16)
                do_dump("VcA", VcA[:].rearrange("p g c -> p (g c)"), ["VcA"], [128, 2 * 97], BF16)
                if stages < 4:
                    continue

                pexp = [a_alloc(1024, BF16) for _ in range(3)]
                onsa = a_alloc(1024, F32)
                onsa4 = onsa.rearrange("p (i g d) -> p i g d", i=8, g=2)
                otmp = a_alloc(512, F32)
                onsab = a_alloc(1024, BF16)
                den = a_alloc(8, F32)
                fac = a_alloc(8, F32)
                impt = a_alloc(256, F32)
                imp = a_alloc(32, F32)
                top8 = a_alloc(8, F32)
                selm = a_alloc(32, BF16)
                selT = a_alloc(128, BF16)
                pcount = {"n": 0, "br": 0}
                triB = cbv("tri")
                oldB = cbv("old")
                maskC3 = cbv("maskC").rearrange("p (g q) -> p g q", g=16)
                VM3 = cbv("VM").rearrange("p (g b) -> p g b", g=16)
                AC3 = cfv("AC").rearrange("p (g b) -> p g b", g=16)
                E3 = cbv("E").rearrange("p (k c) -> p k c", k=16)
                P.op("dve", lambda e: e.memset(selT, 0.0), writes=[("A", "selT")])
                P.op("dve", lambda e: e.memset(selT[32:33, :], 1.0), reads=[("A", "selT")], writes=[("A", "selT")])

                def hb(ap2):
                    return bass.AP(ap2.tensor, ap2.offset, [list(ap2.ap[0]), [0, 4], [1, 128]])

                def stage_a(pr):
                    kind, t, g, gt, kt = pr["kind"], pr["t"], pr["g"], pr["gt"], pr["kt"]
                    n = pcount["n"]
                    pcount["n"] += 1
                    sb_ = (n % 2) * 2
                    pt = pexp[n % 3]
                    pk = ("A", "pexp", n % 3)
                    rows = slice(g * 64, (g + 1) * 64)
                    biases = []
                    if kind == "cmp":
                        kT_ap, kkeys = KcTz[:, g, :], ["KcT"]
                        biases.append((ident, hb(maskC3[:, gt, :]), [("const", "cb")]))
                    elif kind == "win":
                        kT_ap, kkeys = kwT[g][:, kt * 128:(kt + 1) * 128], [("kwT", kt)]
                        if kt == gt:
                            biases.append((ident, hb(triB), [("const", "cb")]))
                        elif kt == gt - 4:
                            biases.append((ident, hb(oldB), [("const", "cb")]))
                    else:
                        kT_ap, kkeys = kslT[g][:, kt * 128:(kt + 1) * 128], [("kslT", kt)]
                        biases.append((E3[:, kt, :], hb(selT), [("const", "cb"), ("A", "selT")]))
                        if kt == gt:
                            biases.append((ident, hb(triB), [("const", "cb")]))
                    for half in range(2):
                        for bi, (bl, br_, bkeys) in enumerate(biases):
                            P.op("pe", lambda e: e.matmul(bank(sb_ + half), lhsT=bl, rhs=br_, start=(bi == 0), stop=False),
                                 reads=bkeys, writes=[PS(sb_ + half)])
                        P.op("pe", lambda e: e.matmul(
                            bank(sb_ + half), lhsT=kT_ap,
                            rhs=nqT3[:, half * 4:(half + 1) * 4, t * 128:(t + 1) * 128],
                            start=(len(biases) == 0), stop=True),
                            reads=kkeys + [("A", "nqT", t)], writes=[PS(sb_ + half)])
                    P.op("act", lambda e: e.activation(out=pt, in_=psum[:, sb_ * 512:(sb_ + 2) * 512], func=AF.Exp, scale=0.125),
                         reads=[PS(sb_), PS(sb_ + 1)], writes=[pk])
                    pr["pt"], pr["pk"] = pt, pk

                def evac_branch(g, t, br, ncol, first_branch, ob0):
                    o4 = bass.AP(psum, ob0 * 512, [[4096, 128], [512, 2], [ncol, 4], [1, 64]])
                    d4 = bass.AP(psum, ob0 * 512 + 64, [[4096, 128], [512, 2], [ncol, 4]])
                    den3 = den.rearrange("p (a b) -> p a b", a=2)
                    OB = [PS(ob0), PS(ob0 + 1)]
                    P.op("dve", lambda e: e.tensor_scalar(out=den3, in0=d4, scalar1=1e-30, scalar2=None, op0=ALU.max),
                         reads=OB, writes=[("A", "den")])
                    P.op("dve", lambda e: e.reciprocal(out=den, in_=den), reads=[("A", "den")], writes=[("A", "den")])
                    if br == 0:
                        i4 = bass.AP(psum, ob0 * 512 + 65, [[4096, 128], [512, 2], [97, 4], [1, 32]])
                        db = bass.AP(den.tensor, den.offset, [list(den.ap[0]), [4, 2], [1, 4], [0, 32]])
                        gt = j * NT + t
                        P.op("dve", lambda e: e.tensor_tensor(out=impt.rearrange("p (a b c) -> p a b c", a=2, b=4), in0=i4, in1=db, op=ALU.mult),
                             reads=OB + [("A", "den")], writes=[("A", "impt")])
                        P.op("dve", lambda e: e.tensor_reduce(out=imp, in_=impt.rearrange("p (h c) -> p c h", h=8),
                                                              op=ALU.add, axis=mybir.AxisListType.X),
                             reads=[("A", "impt")], writes=[("A", "imp")])
                        P.op("dve", lambda e: e.tensor_tensor(out=imp, in0=imp, in1=VM3[:, gt, :], op=ALU.mult),
                             reads=[("A", "imp"), ("const", "cb")], writes=[("A", "imp")])
                        P.op("dve", lambda e: e.tensor_tensor(out=imp, in0=imp, in1=AC3[:, gt, :], op=ALU.add),
                             reads=[("A", "imp"), ("const", "cf")], writes=[("A", "imp")])
                        P.op("dve", lambda e: e.max(out=top8, in_=imp), reads=[("A", "imp")], writes=[("A", "top8")])
                        P.op("dve", lambda e: e.tensor_scalar(out=selm, in0=imp, scalar1=top8[:, 7:8], scalar2=None, op0=ALU.is_ge),
                             reads=[("A", "imp"), ("A", "top8")], writes=[("A", "selm")])
                        n = pcount["n"]
                        pcount["n"] += 1
                        tbk = (n % 2) * 2
                        P.op("pe", lambda e: e.transpose(out=bankb(tbk, 128, 0)[0:32, :], in_=selm, identity=ident),
                             reads=[("A", "selm"), ("const", "cb")], writes=[PS(tbk)])
                        P.op("dve", lambda e: e.tensor_copy(out=selT[0:32, :], in_=bankb(tbk, 128, 0)[0:32, :]), reads=[PS(tbk)], writes=[("A", "selT")])
                    gcol = br * 16 + g * 8
                    P.op("dve", lambda e: e.tensor_tensor(out=fac, in0=den, in1=sig3[:, t, gcol:gcol + 8], op=ALU.mult),
                         reads=[("A", "den"), ("A", "sig", t)], writes=[("A", "fac")])
                    fb = bass.AP(fac.tensor, fac.offset, [list(fac.ap[0]), [4, 2], [1, 4], [0, 64]])
                    dst = onsa4[:, :, g, :].rearrange("p (a b) d -> p a b d", a=2)
                    if first_branch:
                        P.op("dve", lambda e: e.tensor_tensor(out=dst, in0=o4, in1=fb, op=ALU.mult),
                             reads=OB + [("A", "fac")], writes=[("A", "onsa", g)])
                    else:
                        ot = otmp.rearrange("p (a b d) -> p a b d", a=2, b=4)
                        P.op("dve", lambda e: e.tensor_tensor(out=ot, in0=o4, in1=fb, op=ALU.mult),
                             reads=OB + [("A", "fac")], writes=[("A", "otmp")])
                        P.op("dve", lambda e: e.tensor_tensor(out=dst, in0=dst, in1=ot, op=ALU.add),
                             reads=[("A", "otmp"), ("A", "onsa", g)], writes=[("A", "onsa", g)])

                def stage_b(pr):
                    kind, t, g, gt, kt = pr["kind"], pr["t"], pr["g"], pr["gt"], pr["kt"]
                    if kind == "fin":
                        P.op("act", lambda e: e.copy(out=onsab, in_=onsa), reads=[("A", "onsa", 0), ("A", "onsa", 1)], writes=[("A", "onsab")])
                        n = pcount["n"]
                        pcount["n"] += 1
                        tbk = (n % 2) * 2
                        transposes(lambda k: onsab[:, k * 128:(k + 1) * 128], 8, tbk, [("A", "onsab")])
                        P.op("dve", lambda e: e.tensor_copy(out=onsaT[:, :, t * 128:(t + 1) * 128],
                                                            in_=bankb(tbk).rearrange("p (k c) -> p k c", k=8)),
                             reads=[PS(tbk)], writes=[("onsaT", t)])
                        return
                    if pr["first"]:
                        pr["ob0"] = 4 + 2 * (pcount["br"] % 2)
                        pcount["br"] += 1
                        cur["ob0"] = pr["ob0"]
                    ob0 = cur["ob0"]
                    pt, pk = pr["pt"], pr["pk"]
                    if kind == "cmp":
                        v_ap, vkeys, ncol = VcA[:, g, :], ["VcA"], 97
                    elif kind == "win":
                        v_ap, vkeys, ncol = vwA[:, kt, g, :], [("vwA", kt)], 65
                    else:
                        v_ap, vkeys, ncol = vslA[:, kt, g, :], [("vslA", kt)], 65
                    for h in range(8):
                        ob = ob0 + h // 4
                        P.op("pe", lambda e: e.matmul(
                            bank(ob, ncol, (h % 4) * ncol), lhsT=pt[:, h * 128:(h + 1) * 128], rhs=v_ap,
                            start=(pr["first"] and h % 4 == 0), stop=pr["last"], skip_group_check=True),
                            reads=[pk] + vkeys, writes=[PS(ob)])
                    if pr["last"]:
                        evac_branch(g, t, {"cmp": 0, "sel": 1, "win": 2}[kind], ncol, kind == "cmp", ob0)

                cur = {}
                plist = []
                for t in range(NT):
                    gt = j * NT + t
                    for g in range(2):
                        plist.append(dict(kind="cmp", t=t, g=g, gt=gt, kt=None, first=True, last=True))
                        kts = list(range(max(0, gt - 4), gt + 1))
                        for ii, kt in enumerate(kts):
                            plist.append(dict(kind="win", t=t, g=g, gt=gt, kt=kt, first=(ii == 0), last=(ii == len(kts) - 1)))
                        for kt in range(gt + 1):
                            plist.append(dict(kind="sel", t=t, g=g, gt=gt, kt=kt, first=(kt == 0), last=(kt == gt)))
                    plist.append(dict(kind="fin", t=t, g=None, gt=gt, kt=None))
                SKEW = 1
                for ii in range(len(plist) + SKEW):
                    if ii < len(plist) and plist[ii]["kind"] != "fin":
                        stage_a(plist[ii])
                    if ii - SKEW >= 0:
                        stage_b(plist[ii - SKEW])
                do_dump("onsaT", onsaT[:].rearrange("p k t -> p (k t)"), [("onsaT", t) for t in range(NT)], [128, 8 * TB], BF16)
                if stages < 5:
                    continue

                a_reset()
                S3 = [dict(rqk=a_alloc(NT * 256, BF16), rv=a_alloc(NT * 256, BF16)) for _ in range(3)]
                S2 = [dict(qT=a_alloc(TB, BF16), kT=a_alloc(TB, BF16), qxT=a_alloc(TB, BF16), kz=a_alloc(NT * 128, BF16),
                           inT=a_alloc(NT * 128, BF16), rbc=a_alloc(NT * 256, BF16)) for _ in range(2)]
                S2b = [dict(osb=a_alloc(NT * 256, F32), y=a_alloc(NT * 256, BF16), st=a_alloc(32, F32)) for _ in range(2)]
                gsg = [a_alloc(NT * 512, BF16) for _ in range(3)]
                rt1s = [a_alloc(256, F32), a_alloc(256, F32)]
                rt2s = [a_alloc(256, F32), a_alloc(256, F32)]
                sqj = a_alloc(256, BF16)
                decT = cbv("decayT").rearrange("p (h n) -> p h n", h=8)
                xi3 = cbv("xi").rearrange("p (h n) -> p h n", h=8)
                zs = cfv("zs")
                gng = cfv("gn_g")
                log_g = [math.log(1.0 - 2.0 ** (-5.0 - h)) for h in range(8)]
                gch = [math.exp(128.0 * lg) for lg in log_g]

                def ret_s0(h):
                    A3 = S3[h % 3]
                    K3 = lambda n, *x: ("A", n, h % 3) + tuple(x)
                    hp = h // 2
                    if h % 2 == 0:
                        w, wk = wload("B%d" % hp)
                        w3g = w[:].rearrange("p (k c) -> p k c", k=8)
                        gs = gsg[hp % 3].rearrange("p (t c) -> p t c", t=NT)
                        for t in range(NT):
                            b = t % 2
                            for k in range(8):
                                P.op("pe", lambda e: e.matmul(bank(b), lhsT=hT[:, k, t * 128:(t + 1) * 128], rhs=w3g[:, k, :],
                                                              start=(k == 0), stop=(k == 7)),
                                     reads=[("hT", t), wk], writes=[PS(b)])
                            P.op("act", lambda e: e.activation(out=gs[:, t, :], in_=bank(b), func=AF.Silu),
                                 reads=[PS(b)], writes=[("A", "gsg", hp % 3, t)])
                    w, wk = wload("A%d" % h)
                    w3a = w[:].rearrange("p (k c) -> p k c", k=8)
                    rqk3 = A3["rqk"].rearrange("p (t c) -> p t c", t=NT)
                    rv3 = A3["rv"].rearrange("p (t c) -> p t c", t=NT)
                    for t in range(NT):
                        b = t % 2
                        for k in range(8):
                            P.op("pe", lambda e: e.matmul(bank(b), lhsT=hT[:, k, t * 128:(t + 1) * 128], rhs=w3a[:, k, :],
                                                          start=(k == 0), stop=(k == 7)),
                                 reads=[("hT", t), wk], writes=[PS(b)])
                        rope(None, bank(b, 256), 2, 128, cosR[:, t, :], sinR[:, t, :], rqk3[:, t, :], rt1s[t % 2], rt2s[t % 2],
                             [PS(b)], [K3("rqk", t)], [("A", "rt1", t % 2), ("A", "rt2", t % 2)])
                        P.op("act", lambda e: e.copy(out=rv3[:, t, :], in_=bank(b, 256, 256)),
                             reads=[PS(b)], writes=[K3("rv", t)])

                def ret_s1(h):
                    A3 = S3[h % 3]
                    B = S2[h % 2]
                    K3 = lambda n, *x: ("A", n, h % 3) + tuple(x)
                    K = lambda n: ("A", n, h % 2)
                    rqk3 = A3["rqk"].rearrange("p (t c) -> p t c", t=NT)
                    rv3 = A3["rv"].rearrange("p (t c) -> p t c", t=NT)
                    for which in range(2):
                        for t in range(NT):
                            P.op("pe", lambda e: e.transpose(out=bankb(7, 128, (which * NT + t) * 128),
                                                             in_=rqk3[:, t, which * 128:(which + 1) * 128], identity=ident),
                                 reads=[K3("rqk", t), ("const", "cb")], writes=[PS(7)])
                    P.op("act", lambda e: e.copy(out=B["qT"], in_=bankb(7, 512, 0)), reads=[PS(7)], writes=[K("qT")])
                    P.op("dve", lambda e: e.tensor_tensor(out=B["qxT"].rearrange("p (t n) -> p t n", t=NT),
                                                          in0=bankb(7, 512, 0).rearrange("p (t n) -> p t n", t=NT),
                                                          in1=bass.AP(xi3.tensor, xi3[:, h, :].offset, [list(xi3.ap[0]), [0, NT], [1, 128]]),
                                                          op=ALU.mult),
                         reads=[PS(7), ("const", "cb")], writes=[K("qxT")])
                    P.op("act", lambda e: e.copy(out=B["kT"], in_=bankb(7, 512, 512)), reads=[PS(7)], writes=[K("kT")])
                    P.op("dve", lambda e: e.tensor_scalar(out=B["kz"].rearrange("p (t d) -> p t d", t=NT), in0=rqk3[:, :, 128:256],
                                                          scalar1=zs[:, h:h + 1], scalar2=None, op0=ALU.mult),
                         reads=[K3("rqk", t_) for t_ in range(NT)] + [("const", "cf")], writes=[K("kz")])
                    for t in range(NT):
                        P.op("pe", lambda e: e.matmul(bank(2, 128, t * 128), lhsT=B["kT"][:, t * 128:(t + 1) * 128],
                                                      rhs=B["qT"][:, t * 128:(t + 1) * 128], start=(t == 0), stop=True,
                                                      skip_group_check=True),
                             reads=[K("kT"), K("qT")], writes=[PS(2)])
                    for t in range(NT):
                        rbk = 3 + t // 2
                        P.op("pe", lambda e: e.matmul(bank(rbk, 256, (t % 2) * 256), lhsT=B["kz"][:, t * 128:(t + 1) * 128],
                                                      rhs=rv3[:, t, :], start=(t % 2 == 0), stop=True, skip_group_check=True),
                             reads=[K("kz"), K3("rv", t)], writes=[PS(rbk)])
                    P.op("dve", lambda e: e.tensor_tensor(out=B["inT"].rearrange("p (t n) -> p t n", t=NT),
                                                          in0=bank(2).rearrange("p (t n) -> p t n", t=NT),
                                                          in1=bass.AP(decT.tensor, decT[:, h, :].offset, [list(decT.ap[0]), [0, NT], [1, 128]]),
                                                          op=ALU.mult),
                         reads=[PS(2), ("const", "cb")], writes=[K("inT")])
                    rbc3 = B["rbc"].rearrange("p (t e) -> p t e", t=NT)
                    for t in range(NT):
                        rbk = 3 + t // 2
                        if t == 0:
                            P.op("dve", lambda e: e.tensor_copy(out=rbc3[:, 0, :], in_=Rst[:, h, :]), reads=[("R", h)], writes=[K("rbc")])
                        P.op("dve", lambda e: e.scalar_tensor_tensor(out=Rst[:, h, :], in0=Rst[:, h, :], scalar=gch[h],
                                                                     in1=bank(rbk, 256, (t % 2) * 256), op0=ALU.mult, op1=ALU.add),
                             reads=[("R", h), PS(rbk), K("rbc")], writes=[("R", h)])
                        if t < NT - 1:
                            P.op("dve", lambda e: e.tensor_copy(out=rbc3[:, t + 1, :], in_=Rst[:, h, :]), reads=[("R", h)], writes=[K("rbc")])

                def ret_s2(h):
                    A3 = S3[h % 3]
                    B = S2[h % 2]
                    C = S2b[h % 2]
                    K3 = lambda n, *x: ("A", n, h % 3) + tuple(x)
                    K = lambda n: ("A", n, h % 2)
                    hp = h // 2
                    rv3 = A3["rv"].rearrange("p (t c) -> p t c", t=NT)
                    rbc3 = B["rbc"].rearrange("p (t e) -> p t e", t=NT)
                    osb3 = C["osb"].rearrange("p (t e) -> p t e", t=NT)
                    y3 = C["y"].rearrange("p (t e) -> p t e", t=NT)
                    gs = gsg[hp % 3].rearrange("p (t c) -> p t c", t=NT)
                    stt = C["st"]
                    for t in range(NT):
                        ob = 5 + t // 2
                        oo = (t % 2) * 256
                        P.op("pe", lambda e: e.matmul(bank(ob, 256, oo), lhsT=B["inT"][:, t * 128:(t + 1) * 128], rhs=rv3[:, t, :],
                                                      start=(t % 2 == 0), stop=False, skip_group_check=True),
                             reads=[K("inT"), K3("rv", t)], writes=[PS(ob)])
                        P.op("pe", lambda e: e.matmul(bank(ob, 256, oo), lhsT=B["qxT"][:, t * 128:(t + 1) * 128], rhs=rbc3[:, t, :],
                                                      start=False, stop=True, skip_group_check=True),
                             reads=[K("qxT"), K("rbc")], writes=[PS(ob)])
                    for t in range(NT):
                        ob = 5 + t // 2
                        oo = (t % 2) * 256
                        P.op("act", lambda e: e.activation(out=osb3[:, t, :], in_=bank(ob, 256, oo), func=AF.Copy,
                                                           accum_out=stt[:, t:t + 1]),
                             reads=[PS(ob)], writes=[K("osb"), K("st")])
                        P.op("act", lambda e: e.activation(out=sqj, in_=bank(ob, 256, oo), func=AF.Square,
                                                           accum_out=stt[:, 4 + t:5 + t]),
                             reads=[PS(ob)], writes=[("A", "sqj"), K("st")])
                    mean = stt[:, 8:12]
                    var = stt[:, 12:16]
                    rs = stt[:, 16:20]
                    nb = stt[:, 20:24]
                    mneg = stt[:, 24:28]
                    KS = [K("st")]
                    P.op("pool", lambda e: e.tensor_scalar(out=mean, in0=stt[:, 0:4], scalar1=1.0 / 256, scalar2=None, op0=ALU.mult), reads=KS, writes=KS)
                    P.op("pool", lambda e: e.tensor_scalar(out=mneg, in0=stt[:, 0:4], scalar1=-1.0 / 256, scalar2=None, op0=ALU.mult), reads=KS, writes=KS)
                    P.op("pool", lambda e: e.tensor_tensor(out=var, in0=mean, in1=mean, op=ALU.mult), reads=KS, writes=KS)
                    P.op("pool", lambda e: e.tensor_scalar(out=rs, in0=stt[:, 4:8], scalar1=1.0 / 256, scalar2=EPS, op0=ALU.mult, op1=ALU.add), reads=KS, writes=KS)
                    P.op("pool", lambda e: e.tensor_tensor(out=var, in0=rs, in1=var, op=ALU.subtract), reads=KS, writes=KS)
                    P.op("pool", lambda e: e.tensor_tensor(out=rs, in0=var, in1=cfv("mhalf")[:, 0:4], op=ALU.pow),
                         reads=KS + [("const", "cf")], writes=KS)
                    P.op("pool", lambda e: e.tensor_tensor(out=nb, in0=mneg, in1=rs, op=ALU.mult), reads=KS, writes=KS)

                def ret_s2b(h):
                    C = S2b[h % 2]
                    K = lambda n: ("A", n, h % 2)
                    hp = h // 2
                    osb3 = C["osb"].rearrange("p (t e) -> p t e", t=NT)
                    y3 = C["y"].rearrange("p (t e) -> p t e", t=NT)
                    gs = gsg[hp % 3].rearrange("p (t c) -> p t c", t=NT)
                    stt = C["st"]
                    rs = stt[:, 16:20]
                    nb = stt[:, 20:24]
                    for t in range(NT):
                        P.op("dve", lambda e: e.tensor_scalar(out=y3[:, t, :], in0=osb3[:, t, :], scalar1=rs[:, t:t + 1], scalar2=nb[:, t:t + 1],
                                                              op0=ALU.mult, op1=ALU.add),
                             reads=[K("osb"), K("st")], writes=[K("y")])
                    P.op("dve", lambda e: e.tensor_tensor(out=y3, in0=y3, in1=gs[:, :, (h % 2) * 256:(h % 2 + 1) * 256], op=ALU.mult),
                         reads=[K("y")] + [("A", "gsg", hp % 3, t) for t in range(NT)], writes=[K("y")])

                def ret_s3(h):
                    C = S2b[h % 2]
                    K = lambda n: ("A", n, h % 2)
                    y3 = C["y"].rearrange("p (t e) -> p t e", t=NT)
                    for kc in range(2):
                        for t in range(NT):
                            P.op("pe", lambda e: e.transpose(out=bankb(7, 128, (kc * NT + t) * 128),
                                                             in_=y3[:, t, kc * 128:(kc + 1) * 128], identity=ident),
                                 reads=[K("y"), ("const", "cb")], writes=[PS(7)])
                    P.op("dve", lambda e: e.tensor_tensor(out=oretT[:, 2 * h:2 * h + 2, :],
                                                          in0=bankb(7).rearrange("p (k c) -> p k c", k=2),
                                                          in1=bass.AP(gng.tensor, gng[:, 2 * h:2 * h + 2].offset, [list(gng.ap[0]), [1, 2], [0, TB]]),
                                                          op=ALU.mult),
                         reads=[PS(7), ("const", "cf")], writes=[("oretT", h)])

                if j == 0:
                    P.op("pool", lambda e: e.memset(Rst[:], 0.0), reads=[("R", h) for h in range(8)], writes=[("R", h) for h in range(8)])
                for it in range(8 + 4):
                    if it < 8:
                        ret_s0(it)
                    if 0 <= it - 1 < 8:
                        ret_s1(it - 1)
                    if 0 <= it - 2 < 8:
                        ret_s2(it - 2)
                    if 0 <= it - 3 < 8:
                        ret_s2b(it - 3)
                    if 0 <= it - 4 < 8:
                        ret_s3(it - 4)
                do_dump("oretT", oretT[:].rearrange("p k t -> p (k t)"), [("oretT", h) for h in range(8)], [128, 16 * TB], BF16)
                if stages < 6:
                    continue

                a_reset()
                U = a_alloc(16 * TB, BF16)
                U3 = U.rearrange("p (k t) -> p k t", k=16)
                mixT = a_alloc(8 * TB, BF16)
                mix3 = mixT.rearrange("p (k t) -> p k t", k=8)
                xres = a_alloc(NT * 1024, F32)
                xres3 = xres.rearrange("p (t c) -> p t c", t=NT)
                pT = a_alloc(2 * TB, BF16)
                pT3 = pT.rearrange("p (k t) -> p k t", k=2)
                pld = [a_alloc(256, F32), a_alloc(256, F32)]
                pbf = a_alloc(256, BF16)
                mt1s = [a_alloc(512, F32), a_alloc(512, F32)]
                mt2s = [a_alloc(512, F32), a_alloc(512, F32)]
                hn = [a_alloc(1024, BF16), a_alloc(1024, BF16)]
                junk = [a_alloc(1024, BF16)] * 2
                ssq = [a_alloc(1, F32), a_alloc(1, F32)]
                rstd = [a_alloc(1, F32), a_alloc(1, F32)]
                sgAs = [a_alloc(512, F32), a_alloc(512, F32)]
                for t in range(NT):
                    P.op("sp", (lambda t: lambda e: e.dma_start(out=xres3[:, t, :], in_=x_d.ap()[row0 + t * 128: row0 + (t + 1) * 128, :]))(t),
                         writes=[("A", "xres", t)], dma_sem="x%d" % (t % 2))
                for i in range(4):
                    w, wk = wload("MG%d" % i)
                    w3m = w[:].rearrange("p (k c) -> p k c", k=8)
                    for n in range(4):
                        b = n % 4
                        for k in range(8):
                            P.op("pe", (lambda n, k, b, w3m: lambda e: e.matmul(bank(b), lhsT=w3m[:, k, n * 128:(n + 1) * 128], rhs=hT[:, k, :],
                                                                                start=(k == 0), stop=(k == 7)))(n, k, b, w3m),
                                 reads=HT + [wk], writes=[PS(b)])
                        P.op("act", (lambda i, n, b: lambda e: e.activation(out=U3[:, i * 4 + n, :], in_=bank(b), func=AF.Sigmoid))(i, n, b),
                             reads=[PS(b)], writes=[("A", "U", i * 4 + n)])
                wno = None
                for i in range(4):
                    if i % 2 == 0:
                        wno, wnok = wload("NO%d" % (i // 2), hold=True)
                        iNO = wload.last
                        wno3 = wno[:].rearrange("p (k c) -> p k c", k=8)
                    wro, wrok = wload("RO%d" % i)
                    wro3 = wro[:].rearrange("p (k c) -> p k c", k=16)
                    for n2 in range(2):
                        n = 2 * i + n2
                        br_, bn_ = 4, 5
                        for k in range(16):
                            P.op("pe", lambda e: e.matmul(bank(4 + 2 * (n % 2)), lhsT=wro3[:, k, n2 * 128:(n2 + 1) * 128], rhs=oretT[:, k, :],
                                                          start=(k == 0), stop=(k == 15)),
                                 reads=[("oretT", k // 2), wrok], writes=[PS(4 + 2 * (n % 2))])
                        cno = (i % 2) * 256 + n2 * 128
                        for k in range(8):
                            P.op("pe", lambda e: e.matmul(bank(5 + 2 * (n % 2)), lhsT=wno3[:, k, cno:cno + 128], rhs=onsaT[:, k, :],
                                                          start=(k == 0), stop=(k == 7)),
                                 reads=[("onsaT", t) for t in range(NT)] + [wnok], writes=[PS(5 + 2 * (n % 2))])
                        bR, bN = 4 + 2 * (n % 2), 5 + 2 * (n % 2)
                        P.op("dve", lambda e: e.tensor_tensor(out=mt1s[n % 2], in0=bank(bR), in1=U3[:, n, :], op=ALU.mult),
                             reads=[PS(bR), ("A", "U", n)], writes=[("A", "mt1", n % 2)])
                        P.op("dve", lambda e: e.tensor_tensor(out=mt2s[n % 2], in0=bank(bN), in1=U3[:, 8 + n, :], op=ALU.mult),
                             reads=[PS(bN), ("A", "U", 8 + n)], writes=[("A", "mt2", n % 2)])
                        P.op("dve", lambda e: e.tensor_tensor(out=mix3[:, n, :], in0=mt1s[n % 2], in1=mt2s[n % 2], op=ALU.add),
                             reads=[("A", "mt1", n % 2), ("A", "mt2", n % 2)], writes=[("A", "mix", n)])
                    if i % 2 == 1:
                        wrelease(iNO)
                MIX = [("A", "mix", n) for n in range(8)]
                for ch in range(2):
                    w, wk = wload("WO%d" % ch)
                    w3o = w[:].rearrange("p (k c) -> p k c", k=8)
                    for t in range(NT):
                        b = t % 4
                        for k in range(8):
                            P.op("pe", (lambda t, k, b, w3o: lambda e: e.matmul(bank(b), lhsT=mix3[:, k, t * 128:(t + 1) * 128], rhs=w3o[:, k, :],
                                                                                start=(k == 0), stop=(k == 7)))(t, k, b, w3o),
                                 reads=MIX + [wk], writes=[PS(b)])
                        P.op("dve", (lambda t, b, ch: lambda e: e.tensor_tensor(out=xres3[:, t, ch * 512:(ch + 1) * 512], in0=bank(b),
                                                                                in1=xres3[:, t, ch * 512:(ch + 1) * 512], op=ALU.add))(t, b, ch),
                             reads=[PS(b), ("A", "xres", t)], writes=[("A", "xres", t)])
                do_dump("x1", xres, [("A", "xres", t) for t in range(NT)], [128, NT * 1024])
                for t in range(NT):
                    rms_to_hT(xres3[:, t, :], [("A", "xres", t)], "g_mlp", t, 6 + (t % 2), hn, junk, ssq, rstd)
                U4 = U.rearrange("p (a k t) -> p a k t", a=2, k=8)
                for qd in range(4):
                    ub = qd % 2
                    for half in range(2):
                        w, wk = wload("UP%d" % (2 * qd + half))
                        w3u = w[:].rearrange("p (k c) -> p k c", k=8)
                        for n in range(4):
                            b = n % 4
                            for k in range(8):
                                P.op("pe", (lambda n, k, b, w3u: lambda e: e.matmul(bank(b), lhsT=w3u[:, k, n * 128:(n + 1) * 128], rhs=hT[:, k, :],
                                                                                    start=(k == 0), stop=(k == 7)))(n, k, b, w3u),
                                     reads=HT + [wk], writes=[PS(b)])
                            ui = half * 4 + n
                            P.op("act", lambda e: e.activation(out=sgAs[n % 2], in_=bank(b), func=AF.Relu),
                                 reads=[PS(b)], writes=[("A", "sgA", n % 2)])
                            P.op("dve", lambda e: e.tensor_tensor(out=U4[:, ub, ui, :], in0=sgAs[n % 2], in1=sgAs[n % 2], op=ALU.mult),
                                 reads=[("A", "sgA", n % 2)], writes=[("A", "U", ub * 8 + ui)])
                    for ch in range(2):
                        w, wk = wload("DN%d" % (2 * qd + ch))
                        w3d = w[:].rearrange("p (k c) -> p k c", k=8)
                        for t in range(NT):
                            b = 4 + t % 2
                            for k in range(8):
                                P.op("pe", (lambda t, k, b, w3d, ub: lambda e: e.matmul(bank(b), lhsT=U4[:, ub, k, t * 128:(t + 1) * 128], rhs=w3d[:, k, :],
                                                                                        start=(k == 0), stop=(k == 7)))(t, k, b, w3d, ub),
                                     reads=[("A", "U", ub * 8 + k), wk], writes=[PS(b)])
                            P.op("dve", (lambda t, b, ch: lambda e: e.tensor_tensor(out=xres3[:, t, ch * 512:(ch + 1) * 512], in0=bank(b),
                                                                                    in1=xres3[:, t, ch * 512:(ch + 1) * 512], op=ALU.add))(t, b, ch),
                                 reads=[PS(b), ("A", "xres", t)], writes=[("A", "xres", t)])
                do_dump("x2", xres, [("A", "xres", t) for t in range(NT)], [128, NT * 1024])
                for t in range(NT):
                    pb_ = pld[t % 2]
                    P.op("sp", (lambda t, pb_: lambda e: e.dma_start(out=pb_, in_=p_d.ap()[row0 + t * 128: row0 + (t + 1) * 128, :]))(t, pb_),
                         writes=[("A", "pld", t % 2)], dma_sem="p%d" % (t % 2))
                    P.op("act", (lambda pb_: lambda e: e.copy(out=pbf, in_=pb_))(pb_), reads=[("A", "pld", t % 2)], writes=[("A", "pbf")])
                    tb = 6 + (t % 2)
                    transposes(lambda k: pbf[:, k * 128:(k + 1) * 128], 2, tb, [("A", "pbf")])
                    P.op("dve", (lambda t, tb: lambda e: e.tensor_copy(out=pT3[:, :, t * 128:(t + 1) * 128],
                                                                       in_=bankb(tb, 256).rearrange("p (k c) -> p k c", k=2)))(t, tb),
                         reads=[PS(tb)], writes=[("A", "pT", t)])
                    rms_to_hT(xres3[:, t, :], [("A", "xres", t)], "g_ple", t, 6 + (t % 2), hn, junk, ssq, rstd)
                wpp, wppk = wload("PP", hold=True)
                iPP = wload.last
                wpp3 = wpp[:, 0:2048].rearrange("p (k c) -> p k c", k=2)
                for ch in range(2):
                    w, wk = wload("PG%d" % ch)
                    w3g = w[:].rearrange("p (k c) -> p k c", k=8)
                    for t in range(NT):
                        ba = t % 2
                        bb = 2 + t % 2
                        for k in range(8):
                            P.op("pe", (lambda t, k, ba, w3g: lambda e: e.matmul(bank(ba), lhsT=hT[:, k, t * 128:(t + 1) * 128], rhs=w3g[:, k, :],
                                                                                 start=(k == 0), stop=(k == 7)))(t, k, ba, w3g),
                                 reads=[("hT", t), wk], writes=[PS(ba)])
                        for k in range(2):
                            P.op("pe", (lambda t, k, bb, ch: lambda e: e.matmul(bank(bb), lhsT=pT3[:, k, t * 128:(t + 1) * 128],
                                                                                rhs=wpp3[:, k, ch * 512:(ch + 1) * 512],
                                                                                start=(k == 0), stop=(k == 1)))(t, k, bb, ch),
                                 reads=[("A", "pT", t), wppk], writes=[PS(bb)])
                        P.op("act", lambda e: e.activation(out=sgAs[t % 2], in_=bank(ba), func=AF.Sigmoid),
                             reads=[PS(ba)], writes=[("A", "sgA", t % 2)])
                        P.op("dve", lambda e: e.tensor_tensor(out=mt1s[t % 2], in0=bank(bb), in1=sgAs[t % 2], op=ALU.mult),
                             reads=[PS(bb), ("A", "sgA", t % 2)], writes=[("A", "mt1", t % 2)])
                        P.op("dve", lambda e: e.tensor_tensor(out=xres3[:, t, ch * 512:(ch + 1) * 512], in0=mt1s[t % 2],
                                                              in1=xres3[:, t, ch * 512:(ch + 1) * 512], op=ALU.add),
                             reads=[("A", "mt1", t % 2), ("A", "xres", t)], writes=[("A", "xres", t)])
                wrelease(iPP)
                gfin = cfv("g_final")
                for t in range(NT):
                    i2 = t % 2
                    P.op("act", lambda e: e.activation(out=junk[i2], in_=xres3[:, t, :], func=AF.Square, accum_out=ssq[i2]),
                         reads=[("A", "xres", t)], writes=[("A", "junk"), ("A", "ssq", i2)])
                    P.op("dve", lambda e: e.tensor_scalar(out=rstd[i2], in0=ssq[i2], scalar1=1.0 / DM, scalar2=EPS, op0=ALU.mult, op1=ALU.add),
                         reads=[("A", "ssq", i2)], writes=[("A", "rstd", i2)])
                    P.op("pool", lambda e: e.tensor_tensor(out=rstd[i2], in0=rstd[i2], in1=cfv("mhalf")[:, 0:1], op=ALU.pow),
                         reads=[("A", "rstd", i2), ("const", "cf")], writes=[("A", "rstd", i2)])
                    P.op("dve", lambda e: e.scalar_tensor_tensor(out=xres3[:, t, :], in0=xres3[:, t, :], scalar=rstd[i2], in1=gfin,
                                                                 op0=ALU.mult, op1=ALU.mult),
                         reads=[("A", "xres", t), ("A", "rstd", i2), ("const", "cf")], writes=[("A", "xres", t)])
                    P.op("sp", lambda e: e.dma_start(out=out_d.ap()[row0 + t * 128: row0 + (t + 1) * 128, :], in_=xres3[:, t, :]),
                         reads=[("A", "xres", t)], dma_sem="out")
        info = P.emit()
    return nc, info, dumps


_CACHE = {}


def prepare_inputs(inputs, ncores=NCORES, nseq=None):
    x = np.asarray(inputs["x"], np.float32)
    p = np.asarray(inputs["p"], np.float32)[0]
    pos = np.asarray(inputs["positions"], np.int32)
    B = x.shape[0]
    nseq = B // ncores if nseq is None else nseq
    cf, cb = host_consts(inputs)
    wpack = host_pack(inputs)
    in_maps = []
    for c in range(ncores):
        xs = np.ascontiguousarray(x[c * nseq:(c + 1) * nseq].reshape(nseq * SEQ, DM))
        ps_ = np.ascontiguousarray(p[c * nseq:(c + 1) * nseq].reshape(nseq * SEQ, 256))
        pl = pos[c * nseq:(c + 1) * nseq].reshape(nseq, 16, 128).transpose(2, 0, 1).reshape(128, nseq * 16)
        in_maps.append({"x": xs, "p": ps_, "posl": np.ascontiguousarray(pl), "wpack": wpack, "cf": cf, "cb": cb})
    return in_maps, nseq


def kernel(**inputs):
    in_maps, nseq = prepare_inputs(inputs)
    key = ("full", nseq)
    if key not in _CACHE:
        _CACHE[key] = build_program(nseq=nseq)[0]
    nc = _CACHE[key]
    res = run_bass_kernel_spmd(nc, in_maps, core_ids=list(range(NCORES)))
    outs = [r["out"].reshape(nseq, SEQ, DM) for r in res.results]
    return np.concatenate(outs, axis=0).astype(np.float32)
```

```python
import math
import numpy as np
from contextlib import ExitStack
import concourse.bass as bass
import concourse.mybir as mybir
from concourse.bass_utils import run_bass_kernel_spmd

F32 = mybir.dt.float32
BF16 = mybir.dt.bfloat16
I32 = mybir.dt.int32
AF = mybir.ActivationFunctionType
ALU = mybir.AluOpType

NCORES = 8
SEQ = 2048
DM = 1024
TB = 512
NT = TB // 128
EPS = 1e-6
TWO_PI = 2.0 * math.pi
C1 = 6.28125
C2 = TWO_PI - C1
PIECE = 4096
BIGM = 30000.0


class _Op:
    __slots__ = ("eng", "fn", "deps", "dma_sem", "seg", "token", "needs_inc", "is_dma", "ninc")

    def __init__(self, eng, fn, deps, dma_sem, seg, ninc):
        self.eng = eng
        self.fn = fn
        self.deps = deps
        self.dma_sem = dma_sem
        self.seg = seg
        self.token = None
        self.needs_inc = False
        self.is_dma = dma_sem is not None
        self.ninc = ninc


class _Rec:
    def __init__(self):
        self.call = None

    def __getattr__(self, name):
        def f(*a, **k):
            self.call = (name, a, k)
            return self
        return f


class Prog:
    def __init__(self, nc, stack):
        self.nc = nc
        self.stack = stack
        self.ops = []
        self.last_write = {}
        self.readers = {}
        self.seg = 0
        self.eng_obj = {"pe": nc.tensor, "act": nc.scalar, "dve": nc.vector,
                        "pool": nc.gpsimd, "sp": nc.sync}
        self.sems = {}
        self.dma_sems = {}
        self.fence_ops = set()
        self.ps_last = {}
        self.touch = {}

    def next_segment(self):
        self.seg += 1

    def dma_sem(self, name):
        if name not in self.dma_sems:
            s = self.stack.enter_context(self.nc.semaphore("d_" + name))
            self.dma_sems[name] = [s, 0]
        return name

    def fence(self):
        per_eng = {}
        dmas = set()
        for k in list(self.touch.keys()):
            ids = []
            w = self.last_write.pop(k, None)
            if w is not None:
                ids.append(w)
            ids.extend(self.readers.pop(k, []))
            for i in ids:
                o = self.ops[i]
                if o.is_dma:
                    dmas.add(i)
                else:
                    if per_eng.get(o.eng, -1) < i:
                        per_eng[o.eng] = i
        for i in self.fence_ops:
            o = self.ops[i]
            if o.is_dma:
                dmas.add(i)
            elif per_eng.get(o.eng, -1) < i:
                per_eng[o.eng] = i
        self.fence_ops = set(per_eng.values()) | dmas
        self.touch = {}

    def op(self, eng, fn, reads=(), writes=(), dma_sem=None, ninc=1):
        import os as _os
        _lim = int(_os.environ.get("LIMIT", "0"))
        if _lim and len(self.ops) >= _lim:
            return None
        deps = set()
        lw = self.last_write
        rd = self.readers
        for k in reads:
            w = lw.get(k)
            if w is not None:
                deps.add(w)
            if isinstance(k, tuple) and k[0] == "A" and k not in self.touch:
                deps.update(self.fence_ops)
        for k in writes:
            w = lw.get(k)
            if w is not None:
                deps.add(w)
            r = rd.get(k)
            if r:
                deps.update(r)
            if isinstance(k, tuple) and k[0] == "A" and k not in self.touch:
                deps.update(self.fence_ops)
        idx = len(self.ops)
        for k in tuple(reads) + tuple(writes):
            if isinstance(k, tuple) and k[0] == "ps":
                ent = self.ps_last.setdefault(k, {})
                for e2, i2 in ent.items():
                    if e2 != eng:
                        deps.add(i2)
                ent[eng] = idx
        rec = _Rec()
        fn(rec)
        assert rec.call is not None
        self.ops.append(_Op(eng, rec.call, deps, dma_sem, self.seg, ninc))
        for k in reads:
            if isinstance(k, tuple) and k[0] == "A":
                self.touch[k] = None
            if isinstance(k, tuple) and k[0] == "const":
                continue
            rd.setdefault(k, []).append(idx)
        for k in writes:
            if isinstance(k, tuple) and k[0] == "A":
                self.touch[k] = None
            lw[k] = idx
            rd[k] = []
        return idx

    @staticmethod
    def _skip(do, o):
        return do.eng == "pe" and o.eng == "pe" and not do.is_dma and not o.is_dma

    def emit(self, final_wait_eng="sp"):
        nc = self.nc
        ops = self.ops
        for o in ops:
            for d in o.deps:
                do = ops[d]
                if self._skip(do, o):
                    continue
                do.needs_inc = True
        counters = {}
        for o in ops:
            if o.is_dma:
                ent = self.dma_sems[o.dma_sem]
                ent[1] += 16 * o.ninc
                o.token = (ent[0], ent[1], o.dma_sem)
                o.needs_inc = True
            elif o.needs_inc:
                key = (o.eng, o.seg if o.eng == "pe" else (o.seg + 3) // 4)
                if key not in self.sems:
                    self.sems[key] = self.stack.enter_context(nc.semaphore("p_%s_%d" % key))
                counters[key] = counters.get(key, 0) + 1
                o.token = (self.sems[key], counters[key], key)
        waited = {e: {} for e in self.eng_obj}
        nwaits = 0
        for o in ops:
            eobj = self.eng_obj[o.eng]
            need = {}
            wd = waited[o.eng]
            for d in o.deps:
                do = ops[d]
                if self._skip(do, o):
                    continue
                sem, val, key = do.token
                if wd.get(key, 0) >= val:
                    continue
                if need.get(key, (None, 0))[1] < val:
                    need[key] = (sem, val)
            for key, (sem, val) in need.items():
                eobj.wait_ge(sem, val)
                wd[key] = val
                nwaits += 1
            mname, margs, mkw = o.fn
            inst = getattr(eobj, mname)(*margs, **mkw)
            if o.is_dma:
                insts = inst if isinstance(inst, (list, tuple)) else [inst]
                assert len(insts) == o.ninc
                for i in insts:
                    i.then_inc(o.token[0], 16)
            elif o.needs_inc:
                inst.then_inc(o.token[0], 1)
        eobj = self.eng_obj[final_wait_eng]
        for name, (sem, val) in self.dma_sems.items():
            if val > 0:
                eobj.wait_ge(sem, val)
        return dict(n_ops=len(ops), n_waits=nwaits, n_sems=len(self.sems) + len(self.dma_sems))


def _layout(names_sizes):
    off = {}
    o = 0
    for n, s in names_sizes:
        off[n] = (o, s)
        o += s
    return off, o


CF_ITEMS = [("g_mix", 8), ("g_mlp", 8), ("g_ple", 8), ("gn_g", 16), ("zs", 8), ("inv", 96),
            ("pek", 32), ("pev", 32), ("AC", 512), ("g_final", 1024), ("mhalf", 8)]
CF_OFF, CF_W = _layout(CF_ITEMS)
CB_ITEMS = [("decayT", 1024), ("xi", 1024), ("tri", 128), ("old", 128), ("maskC", 2048),
            ("VM", 512), ("ident", 128), ("E", 2048), ("ov", 32), ("ck2", 256), ("cv2", 128)]
CB_OFF, CB_W = _layout(CB_ITEMS)


def host_consts(inp):
    f32 = np.float32
    cf = np.zeros((128, CF_W), f32)
    cb = np.zeros((128, CB_W), f32)

    def putf(name, arr):
        o, s = CF_OFF[name]
        cf[:, o:o + s] = np.asarray(arr, f32).reshape(128, s)

    def putb(name, arr):
        o, s = CB_OFF[name]
        cb[:, o:o + s] = np.asarray(arr, f32).reshape(128, s)

    colmaj = lambda g, k: np.asarray(g, f32).reshape(k, 128).T
    putf("g_mix", colmaj(inp["norm_mix_g"][0], 8))
    putf("g_mlp", colmaj(inp["norm_mlp_g"][0], 8))
    putf("g_ple", colmaj(inp["norm_ple_g"][0], 8))
    putf("gn_g", colmaj(inp["ret_gn_g"][0], 16))
    log_g = np.log(1.0 - 2.0 ** (-5.0 - np.arange(8, dtype=f32))).astype(f32)
    idx = np.arange(128, dtype=f32)
    diff = idx[:, None] - idx[None, :]
    decay = np.where(diff[None] >= 0, np.exp(np.maximum(diff, 0.0)[None] * log_g[:, None, None]), 0.0)
    sc = 128.0 ** -0.5
    putb("decayT", np.transpose(decay, (2, 0, 1)) * sc)
    xi = np.exp((idx + 1.0)[None] * log_g[:, None])
    putb("xi", np.broadcast_to(xi[None], (128, 8, 128)))
    zeta = np.exp((127.0 - idx)[None] * log_g[:, None])
    putf("zs", zeta.T * sc)
    inv_r = (f32(10000.0) ** (-np.arange(0, 128, 2, dtype=f32) / f32(128))).astype(f32)
    inv_n = (f32(10000.0) ** (-np.arange(0, 64, 2, dtype=f32) / f32(64))).astype(f32)
    putf("inv", np.broadcast_to(np.concatenate([inv_r, inv_n])[None], (128, 96)))
    pek = np.asarray(inp["cmp_pe_k"][0], f32)
    pev = np.asarray(inp["cmp_pe_v"][0], f32)
    putf("pek", np.concatenate([pek.T, pek.T], 0))
    putf("pev", np.concatenate([pev.T, pev.T], 0))
    putf("g_final", np.broadcast_to(np.asarray(inp["norm_final_g"], f32)[None], (128, 1024)))
    putf("mhalf", np.full((128, 8), -0.5, f32))
    q = np.arange(128)
    putb("tri", np.where(q[:, None] <= q[None, :], 0.0, -BIGM))
    putb("old", np.where(q[:, None] > q[None, :], 0.0, -BIGM))
    slot = np.arange(128)
    c = slot - 1
    gt = np.arange(16)
    t_abs = gt[:, None] * 128 + q[None, :]
    mC = ((16 * c[:, None, None] + 31) <= t_abs[None]) & (slot[:, None, None] >= 1)
    putb("maskC", np.where(mC, 0.0, -BIGM))
    blk = np.arange(32)
    cur = (t_abs.T // 64)
    forced = (blk[None, None] == 0) | (blk[None, None] == cur[..., None]) | (blk[None, None] == cur[..., None] - 1)
    valid = blk[None, None] <= cur[..., None]
    putb("VM", (valid & ~forced).astype(f32))
    putf("AC", np.where(forced, 1e6, np.where(valid, 0.0, -1.0)))
    putb("ident", np.eye(128, dtype=f32))
    E = np.zeros((128, 16, 128), f32)
    key = np.arange(128)
    for kt in range(16):
        for b in range(32):
            E[b, kt, :] = BIGM * (b == 2 * kt + key // 64)
        E[32, kt, :] = -BIGM
    putb("E", E)
    ov = ((16 * c[:, None] < 64 * (blk[None] + 1)) & (16 * c[:, None] + 31 >= 64 * blk[None]) & (slot[:, None] >= 1))
    putb("ov", ov.astype(f32))
    w2k = np.asarray(inp["cmp_k_w2"][0], f32)
    w2v = np.asarray(inp["cmp_v_w2"][0], f32)
    ck2 = np.zeros((128, 2, 128), f32)
    cv2 = np.zeros((128, 2, 64), f32)
    for hh in range(2):
        ck2[:, hh, 0:64] = w2k[hh * 128:(hh + 1) * 128]
        ck2[:, hh, 64:128] = w2k[hh * 128:(hh + 1) * 128]
        cv2[:, hh, :] = w2v[hh * 128:(hh + 1) * 128]
    putb("ck2", ck2)
    putb("cv2", cv2)
    return cf, cb


def piece_names():
    names = ["S1", "S2", "Q1", "Q2", "CK1", "CK2", "CV1", "CV2"]
    for hp in range(4):
        names += ["B%d" % hp, "A%d" % (2 * hp), "A%d" % (2 * hp + 1)]
    names += ["MG0", "MG1", "MG2", "MG3"]
    for i in range(4):
        if i % 2 == 0:
            names.append("NO%d" % (i // 2))
        names.append("RO%d" % i)
    names += ["WO0", "WO1"]
    for qd in range(4):
        names += ["UP%d" % (2 * qd), "UP%d" % (2 * qd + 1), "DN%d" % (2 * qd), "DN%d" % (2 * qd + 1)]
    names += ["PP", "PG0", "PG1"]
    return names


PIECES = piece_names()
PIDX = {n: i for i, n in enumerate(PIECES)}
NP_ = len(PIECES)


def host_pack(inp):
    f32 = np.float32
    W = np.zeros((NP_, 128, PIECE), f32)
    w_in = np.asarray(inp["w_in"][0], f32)
    o = 0
    sl = {}
    for n, s in [("rq", 1024), ("rk", 1024), ("rv", 2048), ("rg", 2048), ("nq", 1024), ("kc", 128),
                 ("vc", 128), ("ksl", 128), ("vsl", 128), ("kw", 128), ("vw", 128), ("ng", 48)]:
        sl[n] = w_in[:, o:o + s]
        o += s

    def kpiece(cols):
        out = np.zeros((128, 8, 512), f32)
        out[:, :, :cols.shape[1]] = cols.reshape(8, 128, -1).transpose(1, 0, 2)
        return out.reshape(128, PIECE)

    W[PIDX["S1"]] = kpiece(np.concatenate([sl["kc"], sl["ksl"], sl["kw"], sl["vc"]], 1))
    W[PIDX["S2"]] = kpiece(np.concatenate([sl["vsl"], sl["vw"], sl["ng"]], 1))
    nq = sl["nq"].reshape(1024, 16, 64)
    order = []
    for i in range(8):
        order += [i, 8 + i]
    nqp = nq[:, order, :].reshape(1024, 1024)
    W[PIDX["Q1"]] = kpiece(nqp[:, 0:512])
    W[PIDX["Q2"]] = kpiece(nqp[:, 512:1024])
    for h in range(8):
        W[PIDX["A%d" % h]] = kpiece(np.concatenate(
            [sl["rq"][:, h * 128:(h + 1) * 128], sl["rk"][:, h * 128:(h + 1) * 128],
             sl["rv"][:, h * 256:(h + 1) * 256]], 1))
    for hp in range(4):
        W[PIDX["B%d" % hp]] = kpiece(sl["rg"][:, hp * 512:(hp + 1) * 512])
    for nm, key in (("CK", "cmp_k_w1"), ("CV", "cmp_v_w1")):
        w1 = np.asarray(inp[key][0], f32).reshape(32, 64, 256)
        for half in range(2):
            blk = w1[half * 16:(half + 1) * 16]
            pc = np.concatenate([blk.transpose(1, 0, 2)] * 2, 0)
            W[PIDX["%s%d" % (nm, half + 1)]] = pc.reshape(128, PIECE)
    wm = np.asarray(inp["w_merge_gate"][0], f32)
    for i in range(4):
        W[PIDX["MG%d" % i]] = kpiece(wm[:, i * 512:(i + 1) * 512])
    wno = np.asarray(inp["w_nsa_o"][0], f32)
    rows = []
    for i in range(8):
        rows += list(range(i * 64, (i + 1) * 64)) + list(range((8 + i) * 64, (9 + i) * 64))
    wno = wno[rows, :]
    for i in range(2):
        W[PIDX["NO%d" % i]] = kpiece(wno[:, i * 512:(i + 1) * 512])
    wro = np.asarray(inp["w_ret_o"][0], f32)
    for i in range(4):
        pc = wro[:, i * 256:(i + 1) * 256].reshape(16, 128, 256).transpose(1, 0, 2)
        W[PIDX["RO%d" % i]] = pc.reshape(128, PIECE)
    wo = np.asarray(inp["w_out"][0], f32)
    for i in range(2):
        W[PIDX["WO%d" % i]] = kpiece(wo[:, i * 512:(i + 1) * 512])
    wu = np.asarray(inp["w_mlp_up"][0], f32)
    wd = np.asarray(inp["w_mlp_down"][0], f32)
    for i in range(8):
        W[PIDX["UP%d" % i]] = kpiece(wu[:, i * 512:(i + 1) * 512])
    for qd in range(4):
        for ch in range(2):
            W[PIDX["DN%d" % (2 * qd + ch)]] = kpiece(wd[qd * 1024:(qd + 1) * 1024, ch * 512:(ch + 1) * 512])
    wg = np.asarray(inp["w_ple_gate"][0], f32)
    for i in range(2):
        W[PIDX["PG%d" % i]] = kpiece(wg[:, i * 512:(i + 1) * 512])
    wp = np.asarray(inp["w_ple_proj"][0], f32)
    pp = np.zeros((128, PIECE), f32)
    pp[:, :2048] = wp.reshape(2, 128, 1024).transpose(1, 0, 2).reshape(128, 2048)
    W[PIDX["PP"]] = pp
    return W


def build_program(nseq=4, nblk=4, dump=None, stages=99):
    nc = bass.Bass("TRN2", target_bir_lowering=False)
    ntok = nseq * SEQ
    x_d = nc.dram_tensor("x", [ntok, DM], F32, kind="ExternalInput")
    p_d = nc.dram_tensor("p", [ntok, 256], F32, kind="ExternalInput")
    pos_d = nc.dram_tensor("posl", [128, nseq * 16], I32, kind="ExternalInput")
    wp_d = nc.dram_tensor("wpack", [NP_, 128, PIECE], F32, kind="ExternalInput")
    cf_d = nc.dram_tensor("cf", [128, CF_W], F32, kind="ExternalInput")
    cb_d = nc.dram_tensor("cb", [128, CB_W], F32, kind="ExternalInput")
    out_d = nc.dram_tensor("out", [ntok, DM], F32, kind="ExternalOutput")
    wbf_d = nc.dram_tensor("wbf", [NP_, 128, PIECE], BF16, kind="ExternalOutput")
    dumps = {}

    st = ExitStack()
    with st:
        P = Prog(nc, st)
        sbt = lambda n, s, d: st.enter_context(nc.sbuf_tensor(n, s, d))
        psum = st.enter_context(nc.psum_tensor("psum", [128, 4096], F32))

        def bank(b, n=512, off=0):
            return psum[:, b * 512 + off: b * 512 + off + n]

        def bankb(b, n=1024, off=0):
            return psum[:, b * 512:(b + 1) * 512].bitcast(BF16)[:, off:off + n]

        PS = lambda b: ("ps", b)

        cf = sbt("cf_s", [128, CF_W], F32)
        cb = sbt("cb_s", [128, CB_W], BF16)
        posi = sbt("posi", [128, nseq * 16], I32)
        posf = sbt("posf", [128, nseq * 16], F32)
        NSLOT = 4
        wring = [sbt("wring%d" % i, [128, PIECE], BF16) for i in range(NSLOT)]
        kslT = [sbt("kslT%d" % g_, [128, SEQ], BF16) for g_ in range(2)]
        kwT = [sbt("kwT%d" % g_, [128, SEQ], BF16) for g_ in range(2)]
        KcTz = sbt("KcTz", [128, 2, 128], BF16)
        vslA = sbt("vslA", [128, 16, 2, 65], BF16)
        vwA = sbt("vwA", [128, 16, 2, 65], BF16)
        kcT = sbt("kcT", [128, 16 + SEQ], BF16)
        vcT = sbt("vcT", [128, 16 + SEQ], BF16)
        hidk = sbt("hidk", [128, 2, 2, 128], BF16)
        hidv = sbt("hidv", [128, 2, 2, 128], BF16)
        VcA = sbt("VcA", [128, 2, 97], BF16)
        Rst = sbt("Rst", [128, 8, 256], F32)
        hT = sbt("hT", [128, 8, TB], BF16)
        oretT = sbt("oretT", [128, 16, TB], BF16)
        onsaT = sbt("onsaT", [128, 8, TB], BF16)
        tabs = sbt("tabs", [128, NT, 2, 96], F32)
        cosR = sbt("cosR", [128, NT, 128], F32)
        sinR = sbt("sinR", [128, NT, 128], F32)
        cosN = sbt("cosN", [128, NT, 64], F32)
        sinN = sbt("sinN", [128, NT, 64], F32)
        ARENA_W = 16 * 1024
        arena = sbt("arena", [128, ARENA_W], F32)
        astate = {"off": 0}

        def cfv(name, *shape):
            o, s = CF_OFF[name]
            v = cf[:, o:o + s]
            return v

        def cbv(name):
            o, s = CB_OFF[name]
            return cb[:, o:o + s]

        def a_reset():
            astate["off"] = 0
            P.fence()

        def a_alloc(n, dtype):
            words = (n * (2 if dtype == BF16 else 4) + 3) // 4
            words = (words + 7) // 8 * 8
            o = astate["off"]
            assert o + words <= ARENA_W, ("arena overflow", o, words)
            astate["off"] = o + words
            v = arena[:, o:o + words]
            if dtype == BF16:
                v = v.bitcast(BF16)
            elif dtype == I32:
                v = v.bitcast(I32)
            return v[:, 0:n]

        for nme in ["cf", "cb", "pos", "x0", "x1", "p0", "p1", "out", "dump"] + ["w%d" % i for i in range(NSLOT)]:
            P.dma_sem(nme)

        def do_dump(name, ap, reads, shape, dtype=F32):
            if dump is None or name not in dump or name in dumps:
                return
            d = nc.dram_tensor("dump_" + name, list(shape), dtype, kind="ExternalOutput")
            dumps[name] = d
            P.op("sp", lambda e: e.dma_start(out=d.ap(), in_=ap), reads=reads, dma_sem="dump")

        P.op("sp", lambda e: e.dma_start(out=cf[:], in_=cf_d.ap()), writes=[("const", "cf")], dma_sem="cf")
        P.op("sp", lambda e: e.dma_start(out=posi[:], in_=pos_d.ap()), writes=["posi"], dma_sem="pos")
        P.op("dve", lambda e: e.tensor_copy(out=posf[:], in_=posi[:]), reads=["posi"], writes=[("const", "posf")])
        cbst = a_alloc(CB_W, F32)
        P.op("sp", lambda e: e.dma_start(out=cbst, in_=cb_d.ap()), writes=[("A", "cbst")], dma_sem="cb")
        P.op("dve", lambda e: e.tensor_copy(out=cb[:], in_=cbst), reads=[("A", "cbst")], writes=[("const", "cb")])
        a_reset()
        import os as _os
        _skip = _os.environ.get("SKIP", "").split(",")
        cstate = {"n": 0}

        def cast_upto(n):
            while cstate["n"] < min(n, NP_):
                i = cstate["n"]
                cstate["n"] += 1
                P.dma_sem("cast%d" % i)
                P.op("pool", lambda e: e.dma_start(out=wbf_d.ap()[i], in_=wp_d.ap()[i]),
                     writes=[("wbf", i)], dma_sem="cast%d" % i)

        cast_upto(6)
        for tname, t in (() if "memset" in _skip else (("kcT", kcT), ("vcT", vcT), ("hidk", hidk), ("hidv", hidv))):
            P.op("pool", (lambda t: lambda e: e.memset(t[:], 0.0))(t), writes=[tname])
        P.op("pool", lambda e: e.memset(VcA[:], 0.0), writes=["VcA"])
        P.op("pool", lambda e: e.memset(KcTz[:], 0.0), writes=["KcT"])
        for g_ in range(2):
            P.op("pool", lambda e: e.memset(kslT[g_][:], 0.0), writes=[("kslT", k_) for k_ in range(16)])
            P.op("pool", lambda e: e.memset(kwT[g_][:], 0.0), writes=[("kwT", k_) for k_ in range(16)])
        P.op("dve", lambda e: e.memset(VcA[:, :, 64:65], 1.0), reads=["VcA"], writes=["VcA"])
        for g in range(2):
            P.op("dve", (lambda g: lambda e: e.tensor_copy(out=VcA[:, g, 65:97], in_=cbv("ov")))(g),
                 reads=[("const", "cb"), "VcA"], writes=["VcA"])
        P.op("pool", lambda e: e.memset(vslA[:], 1.0), writes=["vslA"])
        P.op("pool", lambda e: e.memset(vwA[:], 1.0), writes=["vwA"])

        ncut = {1: 0, 2: 4, 3: 8, 4: 8, 5: 20}.get(stages, NP_)
        border = PIECES[:ncut]
        nstream = len(border) * nseq * nblk
        wstate = {"use": 0, "iss": 0}
        released = set()
        held = set()
        pending = []

        def try_issue(upto):
            while wstate["iss"] < min(upto, nstream):
                n = wstate["iss"]
                if n - NSLOT >= 0 and (n - NSLOT) not in released:
                    break
                wstate["iss"] += 1
                slot = n % NSLOT
                pi = PIDX[border[n % len(border)]]
                cast_upto(pi + 1 + 5)
                P.op("sp", lambda e: e.dma_start(out=wring[slot][:], in_=wbf_d.ap()[pi]),
                     reads=[("wbf", pi)], writes=[("wr", slot)], dma_sem="w%d" % slot)

        def wrelease(i):
            held.discard(i)
            released.add(i)
            try_issue(wstate["use"] + NSLOT)

        def wload(name, hold=False):
            i = wstate["use"]
            assert border[i % len(border)] == name, (name, border[i % len(border)])
            for q in list(pending):
                if q not in held:
                    released.add(q)
                    pending.remove(q)
            wstate["use"] += 1
            try_issue(i + NSLOT)
            assert wstate["iss"] > i, ("weight ring deadlock", name, i)
            pending.append(i)
            if hold:
                held.add(i)
            wload.last = i
            return wring[i % NSLOT], ("wr", i % NSLOT)

        CONST = [("const", "cf"), ("const", "cb"), ("const", "posf")]
        ident = cbv("ident")

        def transposes(src_fn, n, tb, keys_r):
            for k in range(n):
                P.op("pe", (lambda k: lambda e: e.transpose(out=bankb(tb, 128, k * 128), in_=src_fn(k), identity=ident))(k),
                     reads=keys_r + [("const", "cb")], writes=[PS(tb)])

        def bc(ap2d, dims):
            return bass.AP(ap2d.tensor, ap2d.offset, [list(ap2d.ap[0])] + [list(d) for d in dims])

        def rms_to_hT(src_ap, src_keys, gname, t, tb, hn, junk, ssq, rstd):
            i2 = t % 2
            hn, junk, ssq, rstd = hn[i2], junk[i2], ssq[i2], rstd[i2]
            P.op("act", lambda e: e.activation(out=junk, in_=src_ap, func=AF.Square, accum_out=ssq),
                 reads=src_keys, writes=[("A", "junk"), ("A", "ssq", i2)])
            P.op("dve", lambda e: e.tensor_scalar(out=rstd, in0=ssq, scalar1=1.0 / DM, scalar2=EPS,
                                                  op0=ALU.mult, op1=ALU.add),
                 reads=[("A", "ssq", i2)], writes=[("A", "rstd", i2)])
            P.op("pool", lambda e: e.tensor_tensor(out=rstd, in0=rstd, in1=cfv("mhalf")[:, 0:1], op=ALU.pow),
                 reads=[("A", "rstd", i2), ("const", "cf")], writes=[("A", "rstd", i2)])
            P.op("act", lambda e: e.activation(out=hn, in_=src_ap, func=AF.Copy, scale=rstd),
                 reads=src_keys + [("A", "rstd", i2)], writes=[("A", "hn", i2)])
            transposes(lambda k: hn[:, k * 128:(k + 1) * 128], 8, tb, [("A", "hn", i2)])
            g = cfv(gname)
            P.op("dve", lambda e: e.tensor_tensor(
                out=hT[:, :, t * 128:(t + 1) * 128],
                in0=bankb(tb).rearrange("p (k c) -> p k c", k=8),
                in1=bc(g, [[1, 8], [0, 128]]), op=ALU.mult),
                reads=[PS(tb), ("const", "cf")], writes=[("hT", t)])

        def rope(e_unused, psv, nh, hd, cosv, sinv, outv, tmp1, tmp2, rkeys, wkeys, tkeys):
            h2 = hd // 2
            x3 = psv.rearrange("p (h d) -> p h d", h=nh)
            P.op("dve", lambda e: e.tensor_tensor(out=tmp1.rearrange("p (h d) -> p h d", h=nh), in0=x3,
                                                  in1=bc(cosv, [[0, nh], [1, hd]]), op=ALU.mult),
                 reads=rkeys + ["TABS"], writes=[tkeys[0]])
            t23 = tmp2.rearrange("p (h d) -> p h d", h=nh)
            P.op("dve", lambda e: e.tensor_tensor(out=t23[:, :, 0:h2], in0=x3[:, :, h2:hd],
                                                  in1=bc(sinv[:, 0:h2], [[0, nh], [1, h2]]), op=ALU.mult),
                 reads=rkeys + ["TABS"], writes=[tkeys[1]])
            P.op("dve", lambda e: e.tensor_tensor(out=t23[:, :, h2:hd], in0=x3[:, :, 0:h2],
                                                  in1=bc(sinv[:, h2:hd], [[0, nh], [1, h2]]), op=ALU.mult),
                 reads=rkeys + ["TABS"], writes=[tkeys[1]])
            P.op("dve", lambda e: e.tensor_tensor(out=outv, in0=tmp1, in1=tmp2, op=ALU.add),
                 reads=list(tkeys), writes=wkeys)

        for s in range(nseq if stages > 0 else 0):
            for j in range(nblk):
                P.next_segment()
                row0 = s * SEQ + j * TB
                T0 = j * TB
                a_reset()
                xt = [a_alloc(1024, F32), a_alloc(1024, F32)]
                hn = [a_alloc(1024, BF16), a_alloc(1024, BF16)]
                junk = [a_alloc(1024, BF16)] * 2
                ssq = [a_alloc(1, F32), a_alloc(1, F32)]
                rstd = [a_alloc(1, F32), a_alloc(1, F32)]
                ang = a_alloc(NT * 2 * 96, F32)
                angk = a_alloc(NT * 2 * 96, F32)
                angi = a_alloc(NT * 2 * 96, I32)
                ang4 = ang.rearrange("p (t a f) -> p t a f", t=NT, a=2)
                inv = cfv("inv")
                for t in range(0 if "tabs" in _skip else NT):
                    col = s * 16 + j * NT + t
                    P.op("dve", (lambda t, col: lambda e: e.tensor_scalar(
                        out=ang4[:, t, 0, :], in0=inv, scalar1=posf[:, col:col + 1], scalar2=None, op0=ALU.mult))(t, col),
                        reads=CONST, writes=[("A", "ang")])
                    P.op("dve", (lambda t, col: lambda e: e.tensor_scalar(
                        out=ang4[:, t, 1, :], in0=inv, scalar1=posf[:, col:col + 1], scalar2=math.pi / 2,
                        op0=ALU.mult, op1=ALU.add))(t, col),
                        reads=CONST, writes=[("A", "ang")])
                P.op("dve", lambda e: e.tensor_scalar(out=angk, in0=ang, scalar1=1.0 / TWO_PI, scalar2=None, op0=ALU.mult),
                     reads=[("A", "ang")], writes=[("A", "angk")])
                P.op("dve", lambda e: e.tensor_copy(out=angi, in_=angk), reads=[("A", "angk")], writes=[("A", "angi")])
                P.op("dve", lambda e: e.tensor_copy(out=angk, in_=angi), reads=[("A", "angi")], writes=[("A", "angk")])
                P.op("dve", lambda e: e.scalar_tensor_tensor(out=ang, in0=angk, scalar=-C1, in1=ang, op0=ALU.mult, op1=ALU.add),
                     reads=[("A", "angk"), ("A", "ang")], writes=[("A", "ang")])
                P.op("dve", lambda e: e.scalar_tensor_tensor(out=ang, in0=angk, scalar=-C2, in1=ang, op0=ALU.mult, op1=ALU.add),
                     reads=[("A", "angk"), ("A", "ang")], writes=[("A", "ang")])
                P.op("dve", lambda e: e.tensor_scalar(out=ang, in0=ang, scalar1=3.1415925, scalar2=-3.1415925,
                                                      op0=ALU.min, op1=ALU.max),
                     reads=[("A", "ang")], writes=[("A", "ang")])
                P.op("act", lambda e: e.activation(out=tabs[:].rearrange("p t a f -> p (t a f)"), in_=ang, func=AF.Sin),
                     reads=[("A", "ang")], writes=["tabs0"])
                P.op("dve", lambda e: e.tensor_copy(out=cosR[:, :, 0:64], in_=tabs[:, :, 1, 0:64]), reads=["tabs0"], writes=["TABS"])
                P.op("dve", lambda e: e.tensor_copy(out=cosR[:, :, 64:128], in_=tabs[:, :, 1, 0:64]), reads=["tabs0"], writes=["TABS"])
                P.op("dve", lambda e: e.tensor_scalar(out=sinR[:, :, 0:64], in0=tabs[:, :, 0, 0:64], scalar1=-1.0, scalar2=None, op0=ALU.mult),
                     reads=["tabs0"], writes=["TABS"])
                P.op("dve", lambda e: e.tensor_copy(out=sinR[:, :, 64:128], in_=tabs[:, :, 0, 0:64]), reads=["tabs0"], writes=["TABS"])
                P.op("dve", lambda e: e.tensor_copy(out=cosN[:, :, 0:32], in_=tabs[:, :, 1, 64:96]), reads=["tabs0"], writes=["TABS"])
                P.op("dve", lambda e: e.tensor_copy(out=cosN[:, :, 32:64], in_=tabs[:, :, 1, 64:96]), reads=["tabs0"], writes=["TABS"])
                P.op("dve", lambda e: e.tensor_scalar(out=sinN[:, :, 0:32], in0=tabs[:, :, 0, 64:96], scalar1=-1.0, scalar2=None, op0=ALU.mult),
                     reads=["tabs0"], writes=["TABS"])
                P.op("dve", lambda e: e.tensor_copy(out=sinN[:, :, 32:64], in_=tabs[:, :, 0, 64:96]), reads=["tabs0"], writes=["TABS"])
                if dump and "tabs" in dump:
                    do_dump("tabs", tabs[:].rearrange("p t a f -> p (t a f)"), ["tabs0"], [128, NT * 2 * 96])

                for t in range(0 if "norm" in _skip else NT):
                    xb = xt[t % 2]
                    P.op("sp", (lambda t, xb: lambda e: e.dma_start(out=xb, in_=x_d.ap()[row0 + t * 128: row0 + (t + 1) * 128, :]))(t, xb),
                         writes=[("A", "xt", t % 2)], dma_sem="x%d" % (t % 2))
                    rms_to_hT(xb, [("A", "xt", t % 2)], "g_mix", t, 6 + (t % 2), hn, junk, ssq, rstd)
                do_dump("hT", hT[:].rearrange("p k t -> p (k t)"), [("hT", t) for t in range(NT)], [128, 8 * TB], BF16)
                HT = [("hT", t) for t in range(NT)]
                if stages < 2:
                    continue

                sm = a_alloc(512, BF16)
                tmp1s = [a_alloc(512, F32), a_alloc(512, F32)]
                tmp2s = [a_alloc(512, F32), a_alloc(512, F32)]
                nq_tm = a_alloc(1024, BF16)
                nqT = a_alloc(8 * TB, BF16)
                nqT3 = nqT.rearrange("p (i t) -> p i t", i=8)
                sig = a_alloc(NT * 48, F32)
                sig3 = sig.rearrange("p (t c) -> p t c", t=NT)
                w, wk = wload("S1")
                w3 = w[:].rearrange("p (k c) -> p k c", k=8)
                for t in range(NT):
                    gtile = j * NT + t
                    b = t % 4
                    for k in range(8):
                        P.op("pe", (lambda t, k, b: lambda e: e.matmul(bank(b), lhsT=hT[:, k, t * 128:(t + 1) * 128], rhs=w3[:, k, :],
                                                                       start=(k == 0), stop=(k == 7)))(t, k, b),
                             reads=[("hT", t), wk], writes=[PS(b)])
                    rope(None, bank(b, 384), 6, 64, cosN[:, t, :], sinN[:, t, :], sm[:, 0:384],
                         tmp1s[t % 2][:, 0:384], tmp2s[t % 2][:, 0:384], [PS(b)], [("A", "sm")], [("A", "tmp1", t % 2), ("A", "tmp2", t % 2)])
                    P.op("act", (lambda b: lambda e: e.copy(out=sm[:, 384:512], in_=bank(b, 128, 384)))(b),
                         reads=[PS(b)], writes=[("A", "sm2")])
                    tb = 6 + (t % 2)
                    transposes(lambda k: sm[:, k * 128:(k + 1) * 128], 4, tb, [("A", "sm"), ("A", "sm2")])
                    c0 = T0 + t * 128
                    P.op("dve", (lambda tb, c0: lambda e: e.tensor_copy(out=kcT[:, 16 + c0:16 + c0 + 128], in_=bankb(tb, 128, 0)))(tb, c0),
                         reads=[PS(tb)], writes=["kcT"])
                    for g_ in range(2):
                        rs_ = slice(g_ * 64, (g_ + 1) * 64)
                        P.op("dve", lambda e: e.tensor_copy(out=kslT[g_][rs_, c0:c0 + 128], in_=bankb(tb, 128, 128)[rs_, :]),
                             reads=[PS(tb)], writes=[("kslT", gtile)])
                        P.op("dve", lambda e: e.tensor_copy(out=kwT[g_][rs_, c0:c0 + 128], in_=bankb(tb, 128, 256)[rs_, :]),
                             reads=[PS(tb)], writes=[("kwT", gtile)])
                    P.op("dve", (lambda tb, c0: lambda e: e.tensor_copy(out=vcT[:, 16 + c0:16 + c0 + 128], in_=bankb(tb, 128, 384)))(tb, c0),
                         reads=[PS(tb)], writes=["vcT"])
                w, wk = wload("S2")
                w3b = w[:].rearrange("p (k c) -> p k c", k=8)
                for t in range(NT):
                    gtile = j * NT + t
                    b = t % 4
                    for k in range(8):
                        P.op("pe", (lambda t, k, b, w3b: lambda e: e.matmul(bank(b, 304), lhsT=hT[:, k, t * 128:(t + 1) * 128], rhs=w3b[:, k, 0:304],
                                                                            start=(k == 0), stop=(k == 7)))(t, k, b, w3b),
                             reads=[("hT", t), wk], writes=[PS(b)])
                    P.op("act", (lambda b, gtile: lambda e: e.copy(out=vslA[:, gtile, :, 0:64],
                                                                   in_=bank(b, 128, 0).rearrange("p (g d) -> p g d", g=2)))(b, gtile),
                         reads=[PS(b)], writes=[("vslA", gtile)])
                    P.op("dve", (lambda b, gtile: lambda e: e.tensor_copy(out=vwA[:, gtile, :, 0:64],
                                                                          in_=bank(b, 128, 128).rearrange("p (g d) -> p g d", g=2)))(b, gtile),
                         reads=[PS(b)], writes=[("vwA", gtile)])
                    P.op("act", (lambda b, t: lambda e: e.activation(out=sig3[:, t, :], in_=bank(b, 48, 256), func=AF.Sigmoid))(b, t),
                         reads=[PS(b)], writes=[("A", "sig", t)])
                for qi, qn in enumerate(("Q1", "Q2")):
                    w, wk = wload(qn)
                    w3q = w[:].rearrange("p (k c) -> p k c", k=8)
                    for t in range(NT):
                        b = t % 4
                        for k in range(8):
                            P.op("pe", (lambda t, k, b, w3q: lambda e: e.matmul(bank(b), lhsT=hT[:, k, t * 128:(t + 1) * 128], rhs=w3q[:, k, :],
                                                                                start=(k == 0), stop=(k == 7)))(t, k, b, w3q),
                                 reads=[("hT", t), wk], writes=[PS(b)])
                        rope(None, bank(b), 8, 64, cosN[:, t, :], sinN[:, t, :], nq_tm[:, 0:512],
                             tmp1s[t % 2], tmp2s[t % 2], [PS(b)], [("A", "nq_tm")], [("A", "tmp1", t % 2), ("A", "tmp2", t % 2)])
                        tb = 6 + (t % 2)
                        transposes(lambda k: nq_tm[:, k * 128:(k + 1) * 128], 4, tb, [("A", "nq_tm")])
                        P.op("dve", (lambda tb, qi, t: lambda e: e.tensor_copy(
                            out=nqT3[:, qi * 4:(qi + 1) * 4, t * 128:(t + 1) * 128],
                            in_=bankb(tb, 512).rearrange("p (i c) -> p i c", i=4)))(tb, qi, t),
                            reads=[PS(tb)], writes=[("A", "nqT", t)])
                do_dump("kslT", kslT[0][:], [("kslT", j * NT + t) for t in range(NT)], [128, SEQ], BF16)
                do_dump("nqT", nqT, [("A", "nqT", t) for t in range(NT)], [128, 8 * TB], BF16)
                do_dump("sig", sig, [("A", "sig", t) for t in range(NT)], [128, NT * 48])
                if stages < 3:
                    continue

                kpe = a_alloc(32 * 32, BF16)
                kpe3 = kpe.rearrange("p (l c) -> p l c", l=32)
                gl = [a_alloc(128, F32) for _ in range(3)]
                s0 = 32 * j
                for nm, cache, pen, hid, w2n in (("CK", kcT, "pek", hidk, "ck2"), ("CV", vcT, "pev", hidv, "cv2")):
                    src = cache[:, 16 * s0: 16 * s0 + 1]
                    src = bass.AP(src.tensor, src.offset, [list(src.ap[0]), [1, 32], [16, 32]])
                    pe_ap = cfv(pen)
                    P.op("dve", (lambda src, pe_ap: lambda e: e.tensor_tensor(out=kpe3, in0=src, in1=bc(pe_ap, [[1, 32], [0, 32]]), op=ALU.add))(src, pe_ap),
                         reads=[nm[1] == "K" and "kcT" or "vcT", ("const", "cf")], writes=[("A", "kpe")])
                    wA, wkA = wload(nm + "1", hold=True)
                    iA = wload.last
                    wB, wkB = wload(nm + "2", hold=True)
                    iB = wload.last
                    for g in range(2):
                        first = True
                        for hh in range(2):
                            for l in range(32):
                                wsrc, wkey = (wA, wkA) if l < 16 else (wB, wkB)
                                w1v = wsrc[:].rearrange("p (l j) -> p l j", l=16)
                                P.op("pe", lambda e: e.matmul(
                                    bank(g, 32, hh * 32),
                                    lhsT=w1v[g * 64:(g + 1) * 64, l % 16, hh * 128:(hh + 1) * 128],
                                    rhs=kpe3[g * 64:(g + 1) * 64, l, :],
                                    start=first, stop=(l == 31), skip_group_check=True),
                                    reads=[("A", "kpe"), wkey], writes=[PS(g)])
                                first = False
                    wrelease(iA)
                    wrelease(iB)
                    xh = bass.AP(psum, 0, [[4096, 128], [512, 2], [1, 64]])
                    g3 = [t_.rearrange("p (g c) -> p g c", g=2) for t_ in gl]
                    PH = [PS(0), PS(1)]
                    P.op("act", lambda e: e.activation(out=g3[0], in_=xh, func=AF.Square), reads=PH, writes=[("A", "gl0")])
                    P.op("dve", lambda e: e.tensor_scalar(out=gl[0], in0=gl[0], scalar1=0.044715, scalar2=1.0, op0=ALU.mult, op1=ALU.add),
                         reads=[("A", "gl0")], writes=[("A", "gl0")])
                    P.op("dve", lambda e: e.tensor_tensor(out=g3[1], in0=xh, in1=g3[0], op=ALU.mult), reads=PH + [("A", "gl0")], writes=[("A", "gl1")])
                    P.op("act", lambda e: e.activation(out=gl[2], in_=gl[1], func=AF.Sigmoid, scale=1.5957691216), reads=[("A", "gl1")], writes=[("A", "gl2")])
                    xh4 = bass.AP(psum, 0, [[4096, 128], [512, 2], [32, 2], [1, 32]])
                    P.op("dve", lambda e: e.tensor_tensor(out=hid[:, :, :, s0:s0 + 32], in0=xh4,
                                                          in1=gl[2].rearrange("p (g h c) -> p g h c", g=2, h=2), op=ALU.mult),
                         reads=PH + [("A", "gl2")], writes=[nm])
                ck2 = cbv("ck2").rearrange("p (h d) -> p h d", h=2)
                cv2 = cbv("cv2").rearrange("p (h d) -> p h d", h=2)
                for g in range(2):
                    for hh in range(2):
                        P.op("pe", (lambda g, hh: lambda e: e.matmul(bank(2, 128, g * 128), lhsT=ck2[:, hh, :], rhs=hidk[:, g, hh, :],
                                                                     start=(g == 0 and hh == 0), stop=(hh == 1), skip_group_check=True))(g, hh),
                             reads=["CK", ("const", "cb")], writes=[PS(2)])
                for g in range(2):
                    for hh in range(2):
                        P.op("pe", (lambda g, hh: lambda e: e.matmul(bank(3, 64, g * 64), lhsT=hidv[:, g, hh, :], rhs=cv2[:, hh, :],
                                                                     start=(g == 0 and hh == 0), stop=(hh == 1), skip_group_check=True))(g, hh),
                             reads=["CV", ("const", "cb")], writes=[PS(3)])
                for g_ in range(2):
                    rs_ = slice(g_ * 64, (g_ + 1) * 64)
                    P.op("act", lambda e: e.copy(out=KcTz[rs_, g_, :], in_=bank(2, 128, g_ * 128)[rs_, :]), reads=[PS(2)], writes=["KcT"])
                P.op("dve", lambda e: e.tensor_copy(out=VcA[:, :, 0:64], in_=bank(3, 128).rearrange("p (g d) -> p g d", g=2)),
                     reads=[PS(3), "VcA"], writes=["VcA"])
                do_dump("KcT", KcTz[:].rearrange("p g c -> p (g c)"), ["KcT"], [128, 256], BF16)
                do_dump("VcA", VcA[:].rearrange("p g c -> p (g c)"), ["VcA"], [128, 2 * 97], BF16)
                if stages < 4:
                    continue

                pexp = [a_alloc(1024, BF16) for _ in range(3)]
                onsa = a_alloc(1024, F32)
                onsa4 = onsa.rearrange("p (i g d) -> p i g d", i=8, g=2)
                otmp = a_alloc(512, F32)
                onsab = a_alloc(1024, BF16)
                den = a_alloc(8, F32)
                fac = a_alloc(8, F32)
                impt = a_alloc(256, F32)
                imp = a_alloc(32, F32)
                top8 = a_alloc(8, F32)
                selm = a_alloc(32, BF16)
                selT = a_alloc(128, BF16)
                pcount = {"n": 0, "br": 0}
                triB = cbv("tri")
                oldB = cbv("old")
                maskC3 = cbv("maskC").rearrange("p (g q) -> p g q", g=16)
                VM3 = cbv("VM").rearrange("p (g b) -> p g b", g=16)
                AC3 = cfv("AC").rearrange("p (g b) -> p g b", g=16)
                E3 = cbv("E").rearrange("p (k c) -> p k c", k=16)
                P.op("dve", lambda e: e.memset(selT, 0.0), writes=[("A", "selT")])
                P.op("dve", lambda e: e.memset(selT[32:33, :], 1.0), reads=[("A", "selT")], writes=[("A", "selT")])

                def hb(ap2):
                    return bass.AP(ap2.tensor, ap2.offset, [list(ap2.ap[0]), [0, 4], [1, 128]])

                def stage_a(pr):
                    kind, t, g, gt, kt = pr["kind"], pr["t"], pr["g"], pr["gt"], pr["kt"]
                    n = pcount["n"]
                    pcount["n"] += 1
                    sb_ = (n % 2) * 2
                    pt = pexp[n % 3]
                    pk = ("A", "pexp", n % 3)
                    rows = slice(g * 64, (g + 1) * 64)
                    biases = []
                    if kind == "cmp":
                        kT_ap, kkeys = KcTz[:, g, :], ["KcT"]
                        biases.append((ident, hb(maskC3[:, gt, :]), [("const", "cb")]))
                    elif kind == "win":
                        kT_ap, kkeys = kwT[g][:, kt * 128:(kt + 1) * 128], [("kwT", kt)]
                        if kt == gt:
                            biases.append((ident, hb(triB), [("const", "cb")]))
                        elif kt == gt - 4:
                            biases.append((ident, hb(oldB), [("const", "cb")]))
                    else:
                        kT_ap, kkeys = kslT[g][:, kt * 128:(kt + 1) * 128], [("kslT", kt)]
                        biases.append((E3[:, kt, :], hb(selT), [("const", "cb"), ("A", "selT")]))
                        if kt == gt:
                            biases.append((ident, hb(triB), [("const", "cb")]))
                    for half in range(2):
                        for bi, (bl, br_, bkeys) in enumerate(biases):
                            P.op("pe", lambda e: e.matmul(bank(sb_ + half), lhsT=bl, rhs=br_, start=(bi == 0), stop=False),
                                 reads=bkeys, writes=[PS(sb_ + half)])
                        P.op("pe", lambda e: e.matmul(
                            bank(sb_ + half), lhsT=kT_ap,
                            rhs=nqT3[:, half * 4:(half + 1) * 4, t * 128:(t + 1) * 128],
                            start=(len(biases) == 0), stop=True),
                            reads=kkeys + [("A", "nqT", t)], writes=[PS(sb_ + half)])
                    P.op("act", lambda e: e.activation(out=pt, in_=psum[:, sb_ * 512:(sb_ + 2) * 512], func=AF.Exp, scale=0.125),
                         reads=[PS(sb_), PS(sb_ + 1)], writes=[pk])
                    pr["pt"], pr["pk"] = pt, pk

                def evac_branch(g, t, br, ncol, first_branch, ob0):
                    o4 = bass.AP(psum, ob0 * 512, [[4096, 128], [512, 2], [ncol, 4], [1, 64]])
                    d4 = bass.AP(psum, ob0 * 512 + 64, [[4096, 128], [512, 2], [ncol, 4]])
                    den3 = den.rearrange("p (a b) -> p a b", a=2)
                    OB = [PS(ob0), PS(ob0 + 1)]
                    P.op("dve", lambda e: e.tensor_scalar(out=den3, in0=d4, scalar1=1e-30, scalar2=None, op0=ALU.max),
                         reads=OB, writes=[("A", "den")])
                    P.op("dve", lambda e: e.reciprocal(out=den, in_=den), reads=[("A", "den")], writes=[("A", "den")])
                    if br == 0:
                        i4 = bass.AP(psum, ob0 * 512 + 65, [[4096, 128], [512, 2], [97, 4], [1, 32]])
                        db = bass.AP(den.tensor, den.offset, [list(den.ap[0]), [4, 2], [1, 4], [0, 32]])
                        gt = j * NT + t
                        P.op("dve", lambda e: e.tensor_tensor(out=impt.rearrange("p (a b c) -> p a b c", a=2, b=4), in0=i4, in1=db, op=ALU.mult),
                             reads=OB + [("A", "den")], writes=[("A", "impt")])
                        P.op("dve", lambda e: e.tensor_reduce(out=imp, in_=impt.rearrange("p (h c) -> p c h", h=8),
                                                              op=ALU.add, axis=mybir.AxisListType.X),
                             reads=[("A", "impt")], writes=[("A", "imp")])
                        P.op("dve", lambda e: e.tensor_tensor(out=imp, in0=imp, in1=VM3[:, gt, :], op=ALU.mult),
                             reads=[("A", "imp"), ("const", "cb")], writes=[("A", "imp")])
                        P.op("dve", lambda e: e.tensor_tensor(out=imp, in0=imp, in1=AC3[:, gt, :], op=ALU.add),
                             reads=[("A", "imp"), ("const", "cf")], writes=[("A", "imp")])
                        P.op("dve", lambda e: e.max(out=top8, in_=imp), reads=[("A", "imp")], writes=[("A", "top8")])
                        P.op("dve", lambda e: e.tensor_scalar(out=selm, in0=imp, scalar1=top8[:, 7:8], scalar2=None, op0=ALU.is_ge),
                             reads=[("A", "imp"), ("A", "top8")], writes=[("A", "selm")])
                        n = pcount["n"]
                        pcount["n"] += 1
                        tbk = (n % 2) * 2
                        P.op("pe", lambda e: e.transpose(out=bankb(tbk, 128, 0)[0:32, :], in_=selm, identity=ident),
                             reads=[("A", "selm"), ("const", "cb")], writes=[PS(tbk)])
                        P.op("dve", lambda e: e.tensor_copy(out=selT[0:32, :], in_=bankb(tbk, 128, 0)[0:32, :]), reads=[PS(tbk)], writes=[("A", "selT")])
                    gcol = br * 16 + g * 8
                    P.op("dve", lambda e: e.tensor_tensor(out=fac, in0=den, in1=sig3[:, t, gcol:gcol + 8], op=ALU.mult),
                         reads=[("A", "den"), ("A", "sig", t)], writes=[("A", "fac")])
                    fb = bass.AP(fac.tensor, fac.offset, [list(fac.ap[0]), [4, 2], [1, 4], [0, 64]])
                    dst = onsa4[:, :, g, :].rearrange("p (a b) d -> p a b d", a=2)
                    if first_branch:
                        P.op("dve", lambda e: e.tensor_tensor(out=dst, in0=o4, in1=fb, op=ALU.mult),
                             reads=OB + [("A", "fac")], writes=[("A", "onsa", g)])
                    else:
                        ot = otmp.rearrange("p (a b d) -> p a b d", a=2, b=4)
                        P.op("dve", lambda e: e.tensor_tensor(out=ot, in0=o4, in1=fb, op=ALU.mult),
                             reads=OB + [("A", "fac")], writes=[("A", "otmp")])
                        P.op("dve", lambda e: e.tensor_tensor(out=dst, in0=dst, in1=ot, op=ALU.add),
                             reads=[("A", "otmp"), ("A", "onsa", g)], writes=[("A", "onsa", g)])

                def stage_b(pr):
                    kind, t, g, gt, kt = pr["kind"], pr["t"], pr["g"], pr["gt"], pr["kt"]
                    if kind == "fin":
                        P.op("act", lambda e: e.copy(out=onsab, in_=onsa), reads=[("A", "onsa", 0), ("A", "onsa", 1)], writes=[("A", "onsab")])
                        n = pcount["n"]
                        pcount["n"] += 1
                        tbk = (n % 2) * 2
                        transposes(lambda k: onsab[:, k * 128:(k + 1) * 128], 8, tbk, [("A", "onsab")])
                        P.op("dve", lambda e: e.tensor_copy(out=onsaT[:, :, t * 128:(t + 1) * 128],
                                                            in_=bankb(tbk).rearrange("p (k c) -> p k c", k=8)),
                             reads=[PS(tbk)], writes=[("onsaT", t)])
                        return
                    if pr["first"]:
                        pr["ob0"] = 4 + 2 * (pcount["br"] % 2)
                        pcount["br"] += 1
                        cur["ob0"] = pr["ob0"]
                    ob0 = cur["ob0"]
                    pt, pk = pr["pt"], pr["pk"]
                    if kind == "cmp":
                        v_ap, vkeys, ncol = VcA[:, g, :], ["VcA"], 97
                    elif kind == "win":
                        v_ap, vkeys, ncol = vwA[:, kt, g, :], [("vwA", kt)], 65
                    else:
                        v_ap, vkeys, ncol = vslA[:, kt, g, :], [("vslA", kt)], 65
                    for h in range(8):
                        ob = ob0 + h // 4
                        P.op("pe", lambda e: e.matmul(
                            bank(ob, ncol, (h % 4) * ncol), lhsT=pt[:, h * 128:(h + 1) * 128], rhs=v_ap,
                            start=(pr["first"] and h % 4 == 0), stop=pr["last"], skip_group_check=True),
                            reads=[pk] + vkeys, writes=[PS(ob)])
                    if pr["last"]:
                        evac_branch(g, t, {"cmp": 0, "sel": 1, "win": 2}[kind], ncol, kind == "cmp", ob0)

                cur = {}
                plist = []
                for t in range(NT):
                    gt = j * NT + t
                    for g in range(2):
                        plist.append(dict(kind="cmp", t=t, g=g, gt=gt, kt=None, first=True, last=True))
                        kts = list(range(max(0, gt - 4), gt + 1))
                        for ii, kt in enumerate(kts):
                            plist.append(dict(kind="win", t=t, g=g, gt=gt, kt=kt, first=(ii == 0), last=(ii == len(kts) - 1)))
                        for kt in range(gt + 1):
                            plist.append(dict(kind="sel", t=t, g=g, gt=gt, kt=kt, first=(kt == 0), last=(kt == gt)))
                    plist.append(dict(kind="fin", t=t, g=None, gt=gt, kt=None))
                SKEW = 1
                for ii in range(len(plist) + SKEW):
                    if ii < len(plist) and plist[ii]["kind"] != "fin":
                        stage_a(plist[ii])
                    if ii - SKEW >= 0:
                        stage_b(plist[ii - SKEW])
                do_dump("onsaT", onsaT[:].rearrange("p k t -> p (k t)"), [("onsaT", t) for t in range(NT)], [128, 8 * TB], BF16)
                if stages < 5:
                    continue

                a_reset()
                S3 = [dict(rqk=a_alloc(NT * 256, BF16), rv=a_alloc(NT * 256, BF16)) for _ in range(3)]
                S2 = [dict(qT=a_alloc(TB, BF16), kT=a_alloc(TB, BF16), qxT=a_alloc(TB, BF16), kz=a_alloc(NT * 128, BF16),
                           inT=a_alloc(NT * 128, BF16), rbc=a_alloc(NT * 256, BF16)) for _ in range(2)]
                S2b = [dict(osb=a_alloc(NT * 256, F32), y=a_alloc(NT * 256, BF16), st=a_alloc(32, F32)) for _ in range(2)]
                gsg = [a_alloc(NT * 512, BF16) for _ in range(3)]
                rt1s = [a_alloc(256, F32), a_alloc(256, F32)]
                rt2s = [a_alloc(256, F32), a_alloc(256, F32)]
                sqj = a_alloc(256, BF16)
                decT = cbv("decayT").rearrange("p (h n) -> p h n", h=8)
                xi3 = cbv("xi").rearrange("p (h n) -> p h n", h=8)
                zs = cfv("zs")
                gng = cfv("gn_g")
                log_g = [math.log(1.0 - 2.0 ** (-5.0 - h)) for h in range(8)]
                gch = [math.exp(128.0 * lg) for lg in log_g]

                def ret_s0(h):
                    A3 = S3[h % 3]
                    K3 = lambda n, *x: ("A", n, h % 3) + tuple(x)
                    hp = h // 2
                    if h % 2 == 0:
                        w, wk = wload("B%d" % hp)
                        w3g = w[:].rearrange("p (k c) -> p k c", k=8)
                        gs = gsg[hp % 3].rearrange("p (t c) -> p t c", t=NT)
                        for t in range(NT):
                            b = t % 2
                            for k in range(8):
                                P.op("pe", lambda e: e.matmul(bank(b), lhsT=hT[:, k, t * 128:(t + 1) * 128], rhs=w3g[:, k, :],
                                                              start=(k == 0), stop=(k == 7)),
                                     reads=[("hT", t), wk], writes=[PS(b)])
                            P.op("act", lambda e: e.activation(out=gs[:, t, :], in_=bank(b), func=AF.Silu),
                                 reads=[PS(b)], writes=[("A", "gsg", hp % 3, t)])
                    w, wk = wload("A%d" % h)
                    w3a = w[:].rearrange("p (k c) -> p k c", k=8)
                    rqk3 = A3["rqk"].rearrange("p (t c) -> p t c", t=NT)
                    rv3 = A3["rv"].rearrange("p (t c) -> p t c", t=NT)
                    for t in range(NT):
                        b = t % 2
                        for k in range(8):
                            P.op("pe", lambda e: e.matmul(bank(b), lhsT=hT[:, k, t * 128:(t + 1) * 128], rhs=w3a[:, k, :],
                                                          start=(k == 0), stop=(k == 7)),
                                 reads=[("hT", t), wk], writes=[PS(b)])
                        rope(None, bank(b, 256), 2, 128, cosR[:, t, :], sinR[:, t, :], rqk3[:, t, :], rt1s[t % 2], rt2s[t % 2],
                             [PS(b)], [K3("rqk", t)], [("A", "rt1", t % 2), ("A", "rt2", t % 2)])
                        P.op("act", lambda e: e.copy(out=rv3[:, t, :], in_=bank(b, 256, 256)),
                             reads=[PS(b)], writes=[K3("rv", t)])

                def ret_s1(h):
                    A3 = S3[h % 3]
                    B = S2[h % 2]
                    K3 = lambda n, *x: ("A", n, h % 3) + tuple(x)
                    K = lambda n: ("A", n, h % 2)
                    rqk3 = A3["rqk"].rearrange("p (t c) -> p t c", t=NT)
                    rv3 = A3["rv"].rearrange("p (t c) -> p t c", t=NT)
                    for which in range(2):
                        for t in range(NT):
                            P.op("pe", lambda e: e.transpose(out=bankb(7, 128, (which * NT + t) * 128),
                                                             in_=rqk3[:, t, which * 128:(which + 1) * 128], identity=ident),
                                 reads=[K3("rqk", t), ("const", "cb")], writes=[PS(7)])
                    P.op("act", lambda e: e.copy(out=B["qT"], in_=bankb(7, 512, 0)), reads=[PS(7)], writes=[K("qT")])
                    P.op("dve", lambda e: e.tensor_tensor(out=B["qxT"].rearrange("p (t n) -> p t n", t=NT),
                                                          in0=bankb(7, 512, 0).rearrange("p (t n) -> p t n", t=NT),
                                                          in1=bass.AP(xi3.tensor, xi3[:, h, :].offset, [list(xi3.ap[0]), [0, NT], [1, 128]]),
                                                          op=ALU.mult),
                         reads=[PS(7), ("const", "cb")], writes=[K("qxT")])
                    P.op("act", lambda e: e.copy(out=B["kT"], in_=bankb(7, 512, 512)), reads=[PS(7)], writes=[K("kT")])
                    P.op("dve", lambda e: e.tensor_scalar(out=B["kz"].rearrange("p (t d) -> p t d", t=NT), in0=rqk3[:, :, 128:256],
                                                          scalar1=zs[:, h:h + 1], scalar2=None, op0=ALU.mult),
                         reads=[K3("rqk", t_) for t_ in range(NT)] + [("const", "cf")], writes=[K("kz")])
                    for t in range(NT):
                        P.op("pe", lambda e: e.matmul(bank(2, 128, t * 128), lhsT=B["kT"][:, t * 128:(t + 1) * 128],
                                                      rhs=B["qT"][:, t * 128:(t + 1) * 128], start=(t == 0), stop=True,
                                                      skip_group_check=True),
                             reads=[K("kT"), K("qT")], writes=[PS(2)])
                    for t in range(NT):
                        rbk = 3 + t // 2
                        P.op("pe", lambda e: e.matmul(bank(rbk, 256, (t % 2) * 256), lhsT=B["kz"][:, t * 128:(t + 1) * 128],
                                                      rhs=rv3[:, t, :], start=(t % 2 == 0), stop=True, skip_group_check=True),
                             reads=[K("kz"), K3("rv", t)], writes=[PS(rbk)])
                    P.op("dve", lambda e: e.tensor_tensor(out=B["inT"].rearrange("p (t n) -> p t n", t=NT),
                                                          in0=bank(2).rearrange("p (t n) -> p t n", t=NT),
                                                          in1=bass.AP(decT.tensor, decT[:, h, :].offset, [list(decT.ap[0]), [0, NT], [1, 128]]),
                                                          op=ALU.mult),
                         reads=[PS(2), ("const", "cb")], writes=[K("inT")])
                    rbc3 = B["rbc"].rearrange("p (t e) -> p t e", t=NT)
                    for t in range(NT):
                        rbk = 3 + t // 2
                        if t == 0:
                            P.op("dve", lambda e: e.tensor_copy(out=rbc3[:, 0, :], in_=Rst[:, h, :]), reads=[("R", h)], writes=[K("rbc")])
                        P.op("dve", lambda e: e.scalar_tensor_tensor(out=Rst[:, h, :], in0=Rst[:, h, :], scalar=gch[h],
                                                                     in1=bank(rbk, 256, (t % 2) * 256), op0=ALU.mult, op1=ALU.add),
                             reads=[("R", h), PS(rbk), K("rbc")], writes=[("R", h)])
                        if t < NT - 1:
                            P.op("dve", lambda e: e.tensor_copy(out=rbc3[:, t + 1, :], in_=Rst[:, h, :]), reads=[("R", h)], writes=[K("rbc")])

                def ret_s2(h):
                    A3 = S3[h % 3]
                    B = S2[h % 2]
                    C = S2b[h % 2]
                    K3 = lambda n, *x: ("A", n, h % 3) + tuple(x)
                    K = lambda n: ("A", n, h % 2)
                    hp = h // 2
                    rv3 = A3["rv"].rearrange("p (t c) -> p t c", t=NT)
                    rbc3 = B["rbc"].rearrange("p (t e) -> p t e", t=NT)
                    osb3 = C["osb"].rearrange("p (t e) -> p t e", t=NT)
                    y3 = C["y"].rearrange("p (t e) -> p t e", t=NT)
                    gs = gsg[hp % 3].rearrange("p (t c) -> p t c", t=NT)
                    stt = C["st"]
                    for t in range(NT):
                        ob = 5 + t // 2
                        oo = (t % 2) * 256
                        P.op("pe", lambda e: e.matmul(bank(ob, 256, oo), lhsT=B["inT"][:, t * 128:(t + 1) * 128], rhs=rv3[:, t, :],
                                                      start=(t % 2 == 0), stop=False, skip_group_check=True),
                             reads=[K("inT"), K3("rv", t)], writes=[PS(ob)])
                        P.op("pe", lambda e: e.matmul(bank(ob, 256, oo), lhsT=B["qxT"][:, t * 128:(t + 1) * 128], rhs=rbc3[:, t, :],
                                                      start=False, stop=True, skip_group_check=True),
                             reads=[K("qxT"), K("rbc")], writes=[PS(ob)])
                    for t in range(NT):
                        ob = 5 + t // 2
                        oo = (t % 2) * 256
                        P.op("act", lambda e: e.activation(out=osb3[:, t, :], in_=bank(ob, 256, oo), func=AF.Copy,
                                                           accum_out=stt[:, t:t + 1]),
                             reads=[PS(ob)], writes=[K("osb"), K("st")])
                        P.op("act", lambda e: e.activation(out=sqj, in_=bank(ob, 256, oo), func=AF.Square,
                                                           accum_out=stt[:, 4 + t:5 + t]),
                             reads=[PS(ob)], writes=[("A", "sqj"), K("st")])
                    mean = stt[:, 8:12]
                    var = stt[:, 12:16]
                    rs = stt[:, 16:20]
                    nb = stt[:, 20:24]
                    mneg = stt[:, 24:28]
                    KS = [K("st")]
                    P.op("pool", lambda e: e.tensor_scalar(out=mean, in0=stt[:, 0:4], scalar1=1.0 / 256, scalar2=None, op0=ALU.mult), reads=KS, writes=KS)
                    P.op("pool", lambda e: e.tensor_scalar(out=mneg, in0=stt[:, 0:4], scalar1=-1.0 / 256, scalar2=None, op0=ALU.mult), reads=KS, writes=KS)
                    P.op("pool", lambda e: e.tensor_tensor(out=var, in0=mean, in1=mean, op=ALU.mult), reads=KS, writes=KS)
                    P.op("pool", lambda e: e.tensor_scalar(out=rs, in0=stt[:, 4:8], scalar1=1.0 / 256, scalar2=EPS, op0=ALU.mult, op1=ALU.add), reads=KS, writes=KS)
                    P.op("pool", lambda e: e.tensor_tensor(out=var, in0=rs, in1=var, op=ALU.subtract), reads=KS, writes=KS)
                    P.op("pool", lambda e: e.tensor_tensor(out=rs, in0=var, in1=cfv("mhalf")[:, 0:4], op=ALU.pow),
                         reads=KS + [("const", "cf")], writes=KS)
                    P.op("pool", lambda e: e.tensor_tensor(out=nb, in0=mneg, in1=rs, op=ALU.mult), reads=KS, writes=KS)

                def ret_s2b(h):
                    C = S2b[h % 2]
                    K = lambda n: ("A", n, h % 2)
                    hp = h // 2
                    osb3 = C["osb"].rearrange("p (t e) -> p t e", t=NT)
                    y3 = C["y"].rearrange("p (t e) -> p t e", t=NT)
                    gs = gsg[hp % 3].rearrange("p (t c) -> p t c", t=NT)
                    stt = C["st"]
                    rs = stt[:, 16:20]
                    nb = stt[:, 20:24]
                    for t in range(NT):
                        P.op("dve", lambda e: e.tensor_scalar(out=y3[:, t, :], in0=osb3[:, t, :], scalar1=rs[:, t:t + 1], scalar2=nb[:, t:t + 1],
                                                              op0=ALU.mult, op1=ALU.add),
                             reads=[K("osb"), K("st")], writes=[K("y")])
                    P.op("dve", lambda e: e.tensor_tensor(out=y3, in0=y3, in1=gs[:, :, (h % 2) * 256:(h % 2 + 1) * 256], op=ALU.mult),
                         reads=[K("y")] + [("A", "gsg", hp % 3, t) for t in range(NT)], writes=[K("y")])

                def ret_s3(h):
                    C = S2b[h % 2]
                    K = lambda n: ("A", n, h % 2)
                    y3 = C["y"].rearrange("p (t e) -> p t e", t=NT)
                    for kc in range(2):
                        for t in range(NT):
                            P.op("pe", lambda e: e.transpose(out=bankb(7, 128, (kc * NT + t) * 128),
                                                             in_=y3[:, t, kc * 128:(kc + 1) * 128], identity=ident),
                                 reads=[K("y"), ("const", "cb")], writes=[PS(7)])
                    P.op("dve", lambda e: e.tensor_tensor(out=oretT[:, 2 * h:2 * h + 2, :],
                                                          in0=bankb(7).rearrange("p (k c) -> p k c", k=2),
                                                          in1=bass.AP(gng.tensor, gng[:, 2 * h:2 * h + 2].offset, [list(gng.ap[0]), [1, 2], [0, TB]]),
                                                          op=ALU.mult),
                         reads=[PS(7), ("const", "cf")], writes=[("oretT", h)])

                if j == 0:
                    P.op("pool", lambda e: e.memset(Rst[:], 0.0), reads=[("R", h) for h in range(8)], writes=[("R", h) for h in range(8)])
                for it in range(8 + 4):
                    if it < 8:
                        ret_s0(it)
                    if 0 <= it - 1 < 8:
                        ret_s1(it - 1)
                    if 0 <= it - 2 < 8:
                        ret_s2(it - 2)
                    if 0 <= it - 3 < 8:
                        ret_s2b(it - 3)
                    if 0 <= it - 4 < 8:
                        ret_s3(it - 4)
                do_dump("oretT", oretT[:].rearrange("p k t -> p (k t)"), [("oretT", h) for h in range(8)], [128, 16 * TB], BF16)
                if stages < 6:
                    continue

                a_reset()
                U = a_alloc(16 * TB, BF16)
                U3 = U.rearrange("p (k t) -> p k t", k=16)
                mixT = a_alloc(8 * TB, BF16)
                mix3 = mixT.rearrange("p (k t) -> p k t", k=8)
                xres = a_alloc(NT * 1024, F32)
                xres3 = xres.rearrange("p (t c) -> p t c", t=NT)
                pT = a_alloc(2 * TB, BF16)
                pT3 = pT.rearrange("p (k t) -> p k t", k=2)
                pld = [a_alloc(256, F32), a_alloc(256, F32)]
                pbf = a_alloc(256, BF16)
                mt1s = [a_alloc(512, F32), a_alloc(512, F32)]
                mt2s = [a_alloc(512, F32), a_alloc(512, F32)]
                hn = [a_alloc(1024, BF16), a_alloc(1024, BF16)]
                junk = [a_alloc(1024, BF16)] * 2
                ssq = [a_alloc(1, F32), a_alloc(1, F32)]
                rstd = [a_alloc(1, F32), a_alloc(1, F32)]
                sgAs = [a_alloc(512, F32), a_alloc(512, F32)]
                for t in range(NT):
                    P.op("sp", (lambda t: lambda e: e.dma_start(out=xres3[:, t, :], in_=x_d.ap()[row0 + t * 128: row0 + (t + 1) * 128, :]))(t),
                         writes=[("A", "xres", t)], dma_sem="x%d" % (t % 2))
                for i in range(4):
                    w, wk = wload("MG%d" % i)
                    w3m = w[:].rearrange("p (k c) -> p k c", k=8)
                    for n in range(4):
                        b = n % 4
                        for k in range(8):
                            P.op("pe", (lambda n, k, b, w3m: lambda e: e.matmul(bank(b), lhsT=w3m[:, k, n * 128:(n + 1) * 128], rhs=hT[:, k, :],
                                                                                start=(k == 0), stop=(k == 7)))(n, k, b, w3m),
                                 reads=HT + [wk], writes=[PS(b)])
                        P.op("act", (lambda i, n, b: lambda e: e.activation(out=U3[:, i * 4 + n, :], in_=bank(b), func=AF.Sigmoid))(i, n, b),
                             reads=[PS(b)], writes=[("A", "U", i * 4 + n)])
                wno = None
                for i in range(4):
                    if i % 2 == 0:
                        wno, wnok = wload("NO%d" % (i // 2), hold=True)
                        iNO = wload.last
                        wno3 = wno[:].rearrange("p (k c) -> p k c", k=8)
                    wro, wrok = wload("RO%d" % i)
                    wro3 = wro[:].rearrange("p (k c) -> p k c", k=16)
                    for n2 in range(2):
                        n = 2 * i + n2
                        br_, bn_ = 4, 5
                        for k in range(16):
                            P.op("pe", lambda e: e.matmul(bank(4 + 2 * (n % 2)), lhsT=wro3[:, k, n2 * 128:(n2 + 1) * 128], rhs=oretT[:, k, :],
                                                          start=(k == 0), stop=(k == 15)),
                                 reads=[("oretT", k // 2), wrok], writes=[PS(4 + 2 * (n % 2))])
                        cno = (i % 2) * 256 + n2 * 128
                        for k in range(8):
                            P.op("pe", lambda e: e.matmul(bank(5 + 2 * (n % 2)), lhsT=wno3[:, k, cno:cno + 128], rhs=onsaT[:, k, :],
                                                          start=(k == 0), stop=(k == 7)),
                                 reads=[("onsaT", t) for t in range(NT)] + [wnok], writes=[PS(5 + 2 * (n % 2))])
                        bR, bN = 4 + 2 * (n % 2), 5 + 2 * (n % 2)
                        P.op("dve", lambda e: e.tensor_tensor(out=mt1s[n % 2], in0=bank(bR), in1=U3[:, n, :], op=ALU.mult),
                             reads=[PS(bR), ("A", "U", n)], writes=[("A", "mt1", n % 2)])
                        P.op("dve", lambda e: e.tensor_tensor(out=mt2s[n % 2], in0=bank(bN), in1=U3[:, 8 + n, :], op=ALU.mult),
                             reads=[PS(bN), ("A", "U", 8 + n)], writes=[("A", "mt2", n % 2)])
                        P.op("dve", lambda e: e.tensor_tensor(out=mix3[:, n, :], in0=mt1s[n % 2], in1=mt2s[n % 2], op=ALU.add),
                             reads=[("A", "mt1", n % 2), ("A", "mt2", n % 2)], writes=[("A", "mix", n)])
                    if i % 2 == 1:
                        wrelease(iNO)
                MIX = [("A", "mix", n) for n in range(8)]
                for ch in range(2):
                    w, wk = wload("WO%d" % ch)
                    w3o = w[:].rearrange("p (k c) -> p k c", k=8)
                    for t in range(NT):
                        b = t % 4
                        for k in range(8):
                            P.op("pe", (lambda t, k, b, w3o: lambda e: e.matmul(bank(b), lhsT=mix3[:, k, t * 128:(t + 1) * 128], rhs=w3o[:, k, :],
                                                                                start=(k == 0), stop=(k == 7)))(t, k, b, w3o),
                                 reads=MIX + [wk], writes=[PS(b)])
                        P.op("dve", (lambda t, b, ch: lambda e: e.tensor_tensor(out=xres3[:, t, ch * 512:(ch + 1) * 512], in0=bank(b),
                                                                                in1=xres3[:, t, ch * 512:(ch + 1) * 512], op=ALU.add))(t, b, ch),
                             reads=[PS(b), ("A", "xres", t)], writes=[("A", "xres", t)])
                do_dump("x1", xres, [("A", "xres", t) for t in range(NT)], [128, NT * 1024])
                for t in range(NT):
                    rms_to_hT(xres3[:, t, :], [("A", "xres", t)], "g_mlp", t, 6 + (t % 2), hn, junk, ssq, rstd)
                U4 = U.rearrange("p (a k t) -> p a k t", a=2, k=8)
                for qd in range(4):
                    ub = qd % 2
                    for half in range(2):
                        w, wk = wload("UP%d" % (2 * qd + half))
                        w3u = w[:].rearrange("p (k c) -> p k c", k=8)
                        for n in range(4):
                            b = n % 4
                            for k in range(8):
                                P.op("pe", (lambda n, k, b, w3u: lambda e: e.matmul(bank(b), lhsT=w3u[:, k, n * 128:(n + 1) * 128], rhs=hT[:, k, :],
                                                                                    start=(k == 0), stop=(k == 7)))(n, k, b, w3u),
                                     reads=HT + [wk], writes=[PS(b)])
                            ui = half * 4 + n
                            P.op("act", lambda e: e.activation(out=sgAs[n % 2], in_=bank(b), func=AF.Relu),
                                 reads=[PS(b)], writes=[("A", "sgA", n % 2)])
                            P.op("dve", lambda e: e.tensor_tensor(out=U4[:, ub, ui, :], in0=sgAs[n % 2], in1=sgAs[n % 2], op=ALU.mult),
                                 reads=[("A", "sgA", n % 2)], writes=[("A", "U", ub * 8 + ui)])
                    for ch in range(2):
                        w, wk = wload("DN%d" % (2 * qd + ch))
                        w3d = w[:].rearrange("p (k c) -> p k c", k=8)
                        for t in range(NT):
                            b = 4 + t % 2
                            for k in range(8):
                                P.op("pe", (lambda t, k, b, w3d, ub: lambda e: e.matmul(bank(b), lhsT=U4[:, ub, k, t * 128:(t + 1) * 128], rhs=w3d[:, k, :],
                                                                                        start=(k == 0), stop=(k == 7)))(t, k, b, w3d, ub),
                                     reads=[("A", "U", ub * 8 + k), wk], writes=[PS(b)])
                            P.op("dve", (lambda t, b, ch: lambda e: e.tensor_tensor(out=xres3[:, t, ch * 512:(ch + 1) * 512], in0=bank(b),
                                                                                    in1=xres3[:, t, ch * 512:(ch + 1) * 512], op=ALU.add))(t, b, ch),
                                 reads=[PS(b), ("A", "xres", t)], writes=[("A", "xres", t)])
                do_dump("x2", xres, [("A", "xres", t) for t in range(NT)], [128, NT * 1024])
                for t in range(NT):
                    pb_ = pld[t % 2]
                    P.op("sp", (lambda t, pb_: lambda e: e.dma_start(out=pb_, in_=p_d.ap()[row0 + t * 128: row0 + (t + 1) * 128, :]))(t, pb_),
                         writes=[("A", "pld", t % 2)], dma_sem="p%d" % (t % 2))
                    P.op("act", (lambda pb_: lambda e: e.copy(out=pbf, in_=pb_))(pb_), reads=[("A", "pld", t % 2)], writes=[("A", "pbf")])
                    tb = 6 + (t % 2)
                    transposes(lambda k: pbf[:, k * 128:(k + 1) * 128], 2, tb, [("A", "pbf")])
                    P.op("dve", (lambda t, tb: lambda e: e.tensor_copy(out=pT3[:, :, t * 128:(t + 1) * 128],
                                                                       in_=bankb(tb, 256).rearrange("p (k c) -> p k c", k=2)))(t, tb),
                         reads=[PS(tb)], writes=[("A", "pT", t)])
                    rms_to_hT(xres3[:, t, :], [("A", "xres", t)], "g_ple", t, 6 + (t % 2), hn, junk, ssq, rstd)
                wpp, wppk = wload("PP", hold=True)
                iPP = wload.last
                wpp3 = wpp[:, 0:2048].rearrange("p (k c) -> p k c", k=2)
                for ch in range(2):
                    w, wk = wload("PG%d" % ch)
                    w3g = w[:].rearrange("p (k c) -> p k c", k=8)
                    for t in range(NT):
                        ba = t % 2
                        bb = 2 + t % 2
                        for k in range(8):
                            P.op("pe", (lambda t, k, ba, w3g: lambda e: e.matmul(bank(ba), lhsT=hT[:, k, t * 128:(t + 1) * 128], rhs=w3g[:, k, :],
                                                                                 start=(k == 0), stop=(k == 7)))(t, k, ba, w3g),
                                 reads=[("hT", t), wk], writes=[PS(ba)])
                        for k in range(2):
                            P.op("pe", (lambda t, k, bb, ch: lambda e: e.matmul(bank(bb), lhsT=pT3[:, k, t * 128:(t + 1) * 128],
                                                                                rhs=wpp3[:, k, ch * 512:(ch + 1) * 512],
                                                                                start=(k == 0), stop=(k == 1)))(t, k, bb, ch),
                                 reads=[("A", "pT", t), wppk], writes=[PS(bb)])
                        P.op("act", lambda e: e.activation(out=sgAs[t % 2], in_=bank(ba), func=AF.Sigmoid),
                             reads=[PS(ba)], writes=[("A", "sgA", t % 2)])
                        P.op("dve", lambda e: e.tensor_tensor(out=mt1s[t % 2], in0=bank(bb), in1=sgAs[t % 2], op=ALU.mult),
                             reads=[PS(bb), ("A", "sgA", t % 2)], writes=[("A", "mt1", t % 2)])
                        P.op("dve", lambda e: e.tensor_tensor(out=xres3[:, t, ch * 512:(ch + 1) * 512], in0=mt1s[t % 2],
                                                              in1=xres3[:, t, ch * 512:(ch + 1) * 512], op=ALU.add),
                             reads=[("A", "mt1", t % 2), ("A", "xres", t)], writes=[("A", "xres", t)])
                wrelease(iPP)
                gfin = cfv("g_final")
                for t in range(NT):
                    i2 = t % 2
                    P.op("act", lambda e: e.activation(out=junk[i2], in_=xres3[:, t, :], func=AF.Square, accum_out=ssq[i2]),
                         reads=[("A", "xres", t)], writes=[("A", "junk"), ("A", "ssq", i2)])
                    P.op("dve", lambda e: e.tensor_scalar(out=rstd[i2], in0=ssq[i2], scalar1=1.0 / DM, scalar2=EPS, op0=ALU.mult, op1=ALU.add),
                         reads=[("A", "ssq", i2)], writes=[("A", "rstd", i2)])
                    P.op("pool", lambda e: e.tensor_tensor(out=rstd[i2], in0=rstd[i2], in1=cfv("mhalf")[:, 0:1], op=ALU.pow),
                         reads=[("A", "rstd", i2), ("const", "cf")], writes=[("A", "rstd", i2)])
                    P.op("dve", lambda e: e.scalar_tensor_tensor(out=xres3[:, t, :], in0=xres3[:, t, :], scalar=rstd[i2], in1=gfin,
                                                                 op0=ALU.mult, op1=ALU.mult),
                         reads=[("A", "xres", t), ("A", "rstd", i2), ("const", "cf")], writes=[("A", "xres", t)])
                    P.op("sp", lambda e: e.dma_start(out=out_d.ap()[row0 + t * 128: row0 + (t + 1) * 128, :], in_=xres3[:, t, :]),
                         reads=[("A", "xres", t)], dma_sem="out")
        info = P.emit()
    return nc, info, dumps


_CACHE = {}


def prepare_inputs(inputs, ncores=NCORES, nseq=None):
    x = np.asarray(inputs["x"], np.float32)
    p = np.asarray(inputs["p"], np.float32)[0]
    pos = np.asarray(inputs["positions"], np.int32)
    B = x.shape[0]
    nseq = B // ncores if nseq is None else nseq
    cf, cb = host_consts(inputs)
    wpack = host_pack(inputs)
    in_maps = []
    for c in range(ncores):
        xs = np.ascontiguousarray(x[c * nseq:(c + 1) * nseq].reshape(nseq * SEQ, DM))
        ps_ = np.ascontiguousarray(p[c * nseq:(c + 1) * nseq].reshape(nseq * SEQ, 256))
        pl = pos[c * nseq:(c + 1) * nseq].reshape(nseq, 16, 128).transpose(2, 0, 1).reshape(128, nseq * 16)
        in_maps.append({"x": xs, "p": ps_, "posl": np.ascontiguousarray(pl), "wpack": wpack, "cf": cf, "cb": cb})
    return in_maps, nseq


def kernel(**inputs):
    in_maps, nseq = prepare_inputs(inputs)
    key = ("full", nseq)
    if key not in _CACHE:
        _CACHE[key] = build_program(nseq=nseq)[0]
    nc = _CACHE[key]
    res = run_bass_kernel_spmd(nc, in_maps, core_ids=list(range(NCORES)))
    outs = [r["out"].reshape(nseq, SEQ, DM) for r in res.results]
    return np.concatenate(outs, axis=0).astype(np.float32)
```

```python
import math
import numpy as np
from contextlib import ExitStack
import concourse.bass as bass
import concourse.mybir as mybir
from concourse.bass_utils import run_bass_kernel_spmd

F32 = mybir.dt.float32
BF16 = mybir.dt.bfloat16
I32 = mybir.dt.int32
AF = mybir.ActivationFunctionType
ALU = mybir.AluOpType

NCORES = 8
SEQ = 2048
DM = 1024
TB = 512
NT = TB // 128
EPS = 1e-6
TWO_PI = 2.0 * math.pi
C1 = 6.28125
C2 = TWO_PI - C1
PIECE = 4096
BIGM = 30000.0


class _Op:
    __slots__ = ("eng", "fn", "deps", "dma_sem", "seg", "token", "needs_inc", "is_dma", "ninc")

    def __init__(self, eng, fn, deps, dma_sem, seg, ninc):
        self.eng = eng
        self.fn = fn
        self.deps = deps
        self.dma_sem = dma_sem
        self.seg = seg
        self.token = None
        self.needs_inc = False
        self.is_dma = dma_sem is not None
        self.ninc = ninc


class _Rec:
    def __init__(self):
        self.call = None

    def __getattr__(self, name):
        def f(*a, **k):
            self.call = (name, a, k)
            return self
        return f


class Prog:
    def __init__(self, nc, stack):
        self.nc = nc
        self.stack = stack
        self.ops = []
        self.last_write = {}
        self.readers = {}
        self.seg = 0
        self.eng_obj = {"pe": nc.tensor, "act": nc.scalar, "dve": nc.vector,
                        "pool": nc.gpsimd, "sp": nc.sync}
        self.sems = {}
        self.dma_sems = {}
        self.fence_ops = set()
        self.ps_last = {}
        self.touch = {}

    def next_segment(self):
        self.seg += 1

    def dma_sem(self, name):
        if name not in self.dma_sems:
            s = self.stack.enter_context(self.nc.semaphore("d_" + name))
            self.dma_sems[name] = [s, 0]
        return name

    def fence(self):
        per_eng = {}
        dmas = set()
        for k in list(self.touch.keys()):
            ids = []
            w = self.last_write.pop(k, None)
            if w is not None:
                ids.append(w)
            ids.extend(self.readers.pop(k, []))
            for i in ids:
                o = self.ops[i]
                if o.is_dma:
                    dmas.add(i)
                else:
                    if per_eng.get(o.eng, -1) < i:
                        per_eng[o.eng] = i
        for i in self.fence_ops:
            o = self.ops[i]
            if o.is_dma:
                dmas.add(i)
            elif per_eng.get(o.eng, -1) < i:
                per_eng[o.eng] = i
        self.fence_ops = set(per_eng.values()) | dmas
        self.touch = {}

    def op(self, eng, fn, reads=(), writes=(), dma_sem=None, ninc=1):
        import os as _os
        _lim = int(_os.environ.get("LIMIT", "0"))
        if _lim and len(self.ops) >= _lim:
            return None
        deps = set()
        lw = self.last_write
        rd = self.readers
        for k in reads:
            w = lw.get(k)
            if w is not None:
                deps.add(w)
            if isinstance(k, tuple) and k[0] == "A" and k not in self.touch:
                deps.update(self.fence_ops)
        for k in writes:
            w = lw.get(k)
            if w is not None:
                deps.add(w)
            r = rd.get(k)
            if r:
                deps.update(r)
            if isinstance(k, tuple) and k[0] == "A" and k not in self.touch:
                deps.update(self.fence_ops)
        idx = len(self.ops)
        for k in tuple(reads) + tuple(writes):
            if isinstance(k, tuple) and k[0] == "ps":
                ent = self.ps_last.setdefault(k, {})
                for e2, i2 in ent.items():
                    if e2 != eng:
                        deps.add(i2)
                ent[eng] = idx
        rec = _Rec()
        fn(rec)
        assert rec.call is not None
        self.ops.append(_Op(eng, rec.call, deps, dma_sem, self.seg, ninc))
        for k in reads:
            if isinstance(k, tuple) and k[0] == "A":
                self.touch[k] = None
            if isinstance(k, tuple) and k[0] == "const":
                continue
            rd.setdefault(k, []).append(idx)
        for k in writes:
            if isinstance(k, tuple) and k[0] == "A":
                self.touch[k] = None
            lw[k] = idx
            rd[k] = []
        return idx

    @staticmethod
    def _skip(do, o):
        return do.eng == "pe" and o.eng == "pe" and not do.is_dma and not o.is_dma

    def emit(self, final_wait_eng="sp"):
        nc = self.nc
        ops = self.ops
        for o in ops:
            for d in o.deps:
                do = ops[d]
                if self._skip(do, o):
                    continue
                do.needs_inc = True
        counters = {}
        for o in ops:
            if o.is_dma:
                ent = self.dma_sems[o.dma_sem]
                ent[1] += 16 * o.ninc
                o.token = (ent[0], ent[1], o.dma_sem)
                o.needs_inc = True
            elif o.needs_inc:
                key = (o.eng, o.seg if o.eng == "pe" else (o.seg + 3) // 4)
                if key not in self.sems:
                    self.sems[key] = self.stack.enter_context(nc.semaphore("p_%s_%d" % key))
                counters[key] = counters.get(key, 0) + 1
                o.token = (self.sems[key], counters[key], key)
        waited = {e: {} for e in self.eng_obj}
        nwaits = 0
        for o in ops:
            eobj = self.eng_obj[o.eng]
            need = {}
            wd = waited[o.eng]
            for d in o.deps:
                do = ops[d]
                if self._skip(do, o):
                    continue
                sem, val, key = do.token
                if wd.get(key, 0) >= val:
                    continue
                if need.get(key, (None, 0))[1] < val:
                    need[key] = (sem, val)
            for key, (sem, val) in need.items():
                eobj.wait_ge(sem, val)
                wd[key] = val
                nwaits += 1
            mname, margs, mkw = o.fn
            inst = getattr(eobj, mname)(*margs, **mkw)
            if o.is_dma:
                insts = inst if isinstance(inst, (list, tuple)) else [inst]
                assert len(insts) == o.ninc
                for i in insts:
                    i.then_inc(o.token[0], 16)
            elif o.needs_inc:
                inst.then_inc(o.token[0], 1)
        eobj = self.eng_obj[final_wait_eng]
        for name, (sem, val) in self.dma_sems.items():
            if val > 0:
                eobj.wait_ge(sem, val)
        return dict(n_ops=len(ops), n_waits=nwaits, n_sems=len(self.sems) + len(self.dma_sems))


def _layout(names_sizes):
    off = {}
    o = 0
    for n, s in names_sizes:
        off[n] = (o, s)
        o += s
    return off, o


CF_ITEMS = [("g_mix", 8), ("g_mlp", 8), ("g_ple", 8), ("gn_g", 16), ("zs", 8), ("inv", 96),
            ("pek", 32), ("pev", 32), ("AC", 512), ("g_final", 1024), ("mhalf", 8)]
CF_OFF, CF_W = _layout(CF_ITEMS)
CB_ITEMS = [("decayT", 1024), ("xi", 1024), ("tri", 128), ("old", 128), ("maskC", 2048),
            ("VM", 512), ("ident", 128), ("E", 2048), ("ov", 32), ("ck2", 256), ("cv2", 128)]
CB_OFF, CB_W = _layout(CB_ITEMS)


def host_consts(inp):
    f32 = np.float32
    cf = np.zeros((128, CF_W), f32)
    cb = np.zeros((128, CB_W), f32)

    def putf(name, arr):
        o, s = CF_OFF[name]
        cf[:, o:o + s] = np.asarray(arr, f32).reshape(128, s)

    def putb(name, arr):
        o, s = CB_OFF[name]
        cb[:, o:o + s] = np.asarray(arr, f32).reshape(128, s)

    colmaj = lambda g, k: np.asarray(g, f32).reshape(k, 128).T
    putf("g_mix", colmaj(inp["norm_mix_g"][0], 8))
    putf("g_mlp", colmaj(inp["norm_mlp_g"][0], 8))
    putf("g_ple", colmaj(inp["norm_ple_g"][0], 8))
    putf("gn_g", colmaj(inp["ret_gn_g"][0], 16))
    log_g = np.log(1.0 - 2.0 ** (-5.0 - np.arange(8, dtype=f32))).astype(f32)
    idx = np.arange(128, dtype=f32)
    diff = idx[:, None] - idx[None, :]
    decay = np.where(diff[None] >= 0, np.exp(np.maximum(diff, 0.0)[None] * log_g[:, None, None]), 0.0)
    sc = 128.0 ** -0.5
    putb("decayT", np.transpose(decay, (2, 0, 1)) * sc)
    xi = np.exp((idx + 1.0)[None] * log_g[:, None])
    putb("xi", np.broadcast_to(xi[None], (128, 8, 128)))
    zeta = np.exp((127.0 - idx)[None] * log_g[:, None])
    putf("zs", zeta.T * sc)
    inv_r = (f32(10000.0) ** (-np.arange(0, 128, 2, dtype=f32) / f32(128))).astype(f32)
    inv_n = (f32(10000.0) ** (-np.arange(0, 64, 2, dtype=f32) / f32(64))).astype(f32)
    putf("inv", np.broadcast_to(np.concatenate([inv_r, inv_n])[None], (128, 96)))
    pek = np.asarray(inp["cmp_pe_k"][0], f32)
    pev = np.asarray(inp["cmp_pe_v"][0], f32)
    putf("pek", np.concatenate([pek.T, pek.T], 0))
    putf("pev", np.concatenate([pev.T, pev.T], 0))
    putf("g_final", np.broadcast_to(np.asarray(inp["norm_final_g"], f32)[None], (128, 1024)))
    putf("mhalf", np.full((128, 8), -0.5, f32))
    q = np.arange(128)
    putb("tri", np.where(q[:, None] <= q[None, :], 0.0, -BIGM))
    putb("old", np.where(q[:, None] > q[None, :], 0.0, -BIGM))
    slot = np.arange(128)
    c = slot - 1
    gt = np.arange(16)
    t_abs = gt[:, None] * 128 + q[None, :]
    mC = ((16 * c[:, None, None] + 31) <= t_abs[None]) & (slot[:, None, None] >= 1)
    putb("maskC", np.where(mC, 0.0, -BIGM))
    blk = np.arange(32)
    cur = (t_abs.T // 64)
    forced = (blk[None, None] == 0) | (blk[None, None] == cur[..., None]) | (blk[None, None] == cur[..., None] - 1)
    valid = blk[None, None] <= cur[..., None]
    putb("VM", (valid & ~forced).astype(f32))
    putf("AC", np.where(forced, 1e6, np.where(valid, 0.0, -1.0)))
    putb("ident", np.eye(128, dtype=f32))
    E = np.zeros((128, 16, 128), f32)
    key = np.arange(128)
    for kt in range(16):
        for b in range(32):
            E[b, kt, :] = BIGM * (b == 2 * kt + key // 64)
        E[32, kt, :] = -BIGM
    putb("E", E)
    ov = ((16 * c[:, None] < 64 * (blk[None] + 1)) & (16 * c[:, None] + 31 >= 64 * blk[None]) & (slot[:, None] >= 1))
    putb("ov", ov.astype(f32))
    w2k = np.asarray(inp["cmp_k_w2"][0], f32)
    w2v = np.asarray(inp["cmp_v_w2"][0], f32)
    ck2 = np.zeros((128, 2, 128), f32)
    cv2 = np.zeros((128, 2, 64), f32)
    for hh in range(2):
        ck2[:, hh, 0:64] = w2k[hh * 128:(hh + 1) * 128]
        ck2[:, hh, 64:128] = w2k[hh * 128:(hh + 1) * 128]
        cv2[:, hh, :] = w2v[hh * 128:(hh + 1) * 128]
    putb("ck2", ck2)
    putb("cv2", cv2)
    return cf, cb


def piece_names():
    names = ["S1", "S2", "Q1", "Q2", "CK1", "CK2", "CV1", "CV2"]
    for hp in range(4):
        names += ["B%d" % hp, "A%d" % (2 * hp), "A%d" % (2 * hp + 1)]
    names += ["MG0", "MG1", "MG2", "MG3"]
    for i in range(4):
        if i % 2 == 0:
            names.append("NO%d" % (i // 2))
        names.append("RO%d" % i)
    names += ["WO0", "WO1"]
    for qd in range(4):
        names += ["UP%d" % (2 * qd), "UP%d" % (2 * qd + 1), "DN%d" % (2 * qd), "DN%d" % (2 * qd + 1)]
    names += ["PP", "PG0", "PG1"]
    return names


PIECES = piece_names()
PIDX = {n: i for i, n in enumerate(PIECES)}
NP_ = len(PIECES)


def host_pack(inp):
    f32 = np.float32
    W = np.zeros((NP_, 128, PIECE), f32)
    w_in = np.asarray(inp["w_in"][0], f32)
    o = 0
    sl = {}
    for n, s in [("rq", 1024), ("rk", 1024), ("rv", 2048), ("rg", 2048), ("nq", 1024), ("kc", 128),
                 ("vc", 128), ("ksl", 128), ("vsl", 128), ("kw", 128), ("vw", 128), ("ng", 48)]:
        sl[n] = w_in[:, o:o + s]
        o += s

    def kpiece(cols):
        out = np.zeros((128, 8, 512), f32)
        out[:, :, :cols.shape[1]] = cols.reshape(8, 128, -1).transpose(1, 0, 2)
        return out.reshape(128, PIECE)

    W[PIDX["S1"]] = kpiece(np.concatenate([sl["kc"], sl["ksl"], sl["kw"], sl["vc"]], 1))
    W[PIDX["S2"]] = kpiece(np.concatenate([sl["vsl"], sl["vw"], sl["ng"]], 1))
    nq = sl["nq"].reshape(1024, 16, 64)
    order = []
    for i in range(8):
        order += [i, 8 + i]
    nqp = nq[:, order, :].reshape(1024, 1024)
    W[PIDX["Q1"]] = kpiece(nqp[:, 0:512])
    W[PIDX["Q2"]] = kpiece(nqp[:, 512:1024])
    for h in range(8):
        W[PIDX["A%d" % h]] = kpiece(np.concatenate(
            [sl["rq"][:, h * 128:(h + 1) * 128], sl["rk"][:, h * 128:(h + 1) * 128],
             sl["rv"][:, h * 256:(h + 1) * 256]], 1))
    for hp in range(4):
        W[PIDX["B%d" % hp]] = kpiece(sl["rg"][:, hp * 512:(hp + 1) * 512])
    for nm, key in (("CK", "cmp_k_w1"), ("CV", "cmp_v_w1")):
        w1 = np.asarray(inp[key][0], f32).reshape(32, 64, 256)
        for half in range(2):
            blk = w1[half * 16:(half + 1) * 16]
            pc = np.concatenate([blk.transpose(1, 0, 2)] * 2, 0)
            W[PIDX["%s%d" % (nm, half + 1)]] = pc.reshape(128, PIECE)
    wm = np.asarray(inp["w_merge_gate"][0], f32)
    for i in range(4):
        W[PIDX["MG%d" % i]] = kpiece(wm[:, i * 512:(i + 1) * 512])
    wno = np.asarray(inp["w_nsa_o"][0], f32)
    rows = []
    for i in range(8):
        rows += list(range(i * 64, (i + 1) * 64)) + list(range((8 + i) * 64, (9 + i) * 64))
    wno = wno[rows, :]
    for i in range(2):
        W[PIDX["NO%d" % i]] = kpiece(wno[:, i * 512:(i + 1) * 512])
    wro = np.asarray(inp["w_ret_o"][0], f32)
    for i in range(4):
        pc = wro[:, i * 256:(i + 1) * 256].reshape(16, 128, 256).transpose(1, 0, 2)
        W[PIDX["RO%d" % i]] = pc.reshape(128, PIECE)
    wo = np.asarray(inp["w_out"][0], f32)
    for i in range(2):
        W[PIDX["WO%d" % i]] = kpiece(wo[:, i * 512:(i + 1) * 512])
    wu = np.asarray(inp["w_mlp_up"][0], f32)
    wd = np.asarray(inp["w_mlp_down"][0], f32)
    for i in range(8):
        W[PIDX["UP%d" % i]] = kpiece(wu[:, i * 512:(i + 1) * 512])
    for qd in range(4):
        for ch in range(2):
            W[PIDX["DN%d" % (2 * qd + ch)]] = kpiece(wd[qd * 1024:(qd + 1) * 1024, ch * 512:(ch + 1) * 512])
    wg = np.asarray(inp["w_ple_gate"][0], f32)
    for i in range(2):
        W[PIDX["PG%d" % i]] = kpiece(wg[:, i * 512:(i + 1) * 512])
    wp = np.asarray(inp["w_ple_proj"][0], f32)
    pp = np.zeros((128, PIECE), f32)
    pp[:, :2048] = wp.reshape(2, 128, 1024).transpose(1, 0, 2).reshape(128, 2048)
    W[PIDX["PP"]] = pp
    return W


def build_program(nseq=4, nblk=4, dump=None, stages=99):
    nc = bass.Bass("TRN2", target_bir_lowering=False)
    ntok = nseq * SEQ
    x_d = nc.dram_tensor("x", [ntok, DM], F32, kind="ExternalInput")
    p_d = nc.dram_tensor("p", [ntok, 256], F32, kind="ExternalInput")
    pos_d = nc.dram_tensor("posl", [128, nseq * 16], I32, kind="ExternalInput")
    wp_d = nc.dram_tensor("wpack", [NP_, 128, PIECE], F32, kind="ExternalInput")
    cf_d = nc.dram_tensor("cf", [128, CF_W], F32, kind="ExternalInput")
    cb_d = nc.dram_tensor("cb", [128, CB_W], F32, kind="ExternalInput")
    out_d = nc.dram_tensor("out", [ntok, DM], F32, kind="ExternalOutput")
    wbf_d = nc.dram_tensor("wbf", [NP_, 128, PIECE], BF16, kind="ExternalOutput")
    dumps = {}

    st = ExitStack()
    with st:
        P = Prog(nc, st)
        sbt = lambda n, s, d: st.enter_context(nc.sbuf_tensor(n, s, d))
        psum = st.enter_context(nc.psum_tensor("psum", [128, 4096], F32))

        def bank(b, n=512, off=0):
            return psum[:, b * 512 + off: b * 512 + off + n]

        def bankb(b, n=1024, off=0):
            return psum[:, b * 512:(b + 1) * 512].bitcast(BF16)[:, off:off + n]

        PS = lambda b: ("ps", b)

        cf = sbt("cf_s", [128, CF_W], F32)
        cb = sbt("cb_s", [128, CB_W], BF16)
        posi = sbt("posi", [128, nseq * 16], I32)
        posf = sbt("posf", [128, nseq * 16], F32)
        NSLOT = 4
        wring = [sbt("wring%d" % i, [128, PIECE], BF16) for i in range(NSLOT)]
        kslT = [sbt("kslT%d" % g_, [128, SEQ], BF16) for g_ in range(2)]
        kwT = [sbt("kwT%d" % g_, [128, SEQ], BF16) for g_ in range(2)]
        KcTz = sbt("KcTz", [128, 2, 128], BF16)
        vslA = sbt("vslA", [128, 16, 2, 65], BF16)
        vwA = sbt("vwA", [128, 16, 2, 65], BF16)
        kcT = sbt("kcT", [128, 16 + SEQ], BF16)
        vcT = sbt("vcT", [128, 16 + SEQ], BF16)
        hidk = sbt("hidk", [128, 2, 2, 128], BF16)
        hidv = sbt("hidv", [128, 2, 2, 128], BF16)
        VcA = sbt("VcA", [128, 2, 97], BF16)
        Rst = sbt("Rst", [128, 8, 256], F32)
        hT = sbt("hT", [128, 8, TB], BF16)
        oretT = sbt("oretT", [128, 16, TB], BF16)
        onsaT = sbt("onsaT", [128, 8, TB], BF16)
        tabs = sbt("tabs", [128, NT, 2, 96], F32)
        cosR = sbt("cosR", [128, NT, 128], F32)
        sinR = sbt("sinR", [128, NT, 128], F32)
        cosN = sbt("cosN", [128, NT, 64], F32)
        sinN = sbt("sinN", [128, NT, 64], F32)
        ARENA_W = 16 * 1024
        arena = sbt("arena", [128, ARENA_W], F32)
        astate = {"off": 0}

        def cfv(name, *shape):
            o, s = CF_OFF[name]
            v = cf[:, o:o + s]
            return v

        def cbv(name):
            o, s = CB_OFF[name]
            return cb[:, o:o + s]

        def a_reset():
            astate["off"] = 0
            P.fence()

        def a_alloc(n, dtype):
            words = (n * (2 if dtype == BF16 else 4) + 3) // 4
            words = (words + 7) // 8 * 8
            o = astate["off"]
            assert o + words <= ARENA_W, ("arena overflow", o, words)
            astate["off"] = o + words
            v = arena[:, o:o + words]
            if dtype == BF16:
                v = v.bitcast(BF16)
            elif dtype == I32:
                v = v.bitcast(I32)
            return v[:, 0:n]

        for nme in ["cf", "cb", "pos", "x0", "x1", "p0", "p1", "out", "dump"] + ["w%d" % i for i in range(NSLOT)]:
            P.dma_sem(nme)

        def do_dump(name, ap, reads, shape, dtype=F32):
            if dump is None or name not in dump or name in dumps:
                return
            d = nc.dram_tensor("dump_" + name, list(shape), dtype, kind="ExternalOutput")
            dumps[name] = d
            P.op("sp", lambda e: e.dma_start(out=d.ap(), in_=ap), reads=reads, dma_sem="dump")

        P.op("sp", lambda e: e.dma_start(out=cf[:], in_=cf_d.ap()), writes=[("const", "cf")], dma_sem="cf")
        P.op("sp", lambda e: e.dma_start(out=posi[:], in_=pos_d.ap()), writes=["posi"], dma_sem="pos")
        P.op("dve", lambda e: e.tensor_copy(out=posf[:], in_=posi[:]), reads=["posi"], writes=[("const", "posf")])
        cbst = a_alloc(CB_W, F32)
        P.op("sp", lambda e: e.dma_start(out=cbst, in_=cb_d.ap()), writes=[("A", "cbst")], dma_sem="cb")
        P.op("dve", lambda e: e.tensor_copy(out=cb[:], in_=cbst), reads=[("A", "cbst")], writes=[("const", "cb")])
        a_reset()
        import os as _os
        _skip = _os.environ.get("SKIP", "").split(",")
        cstate = {"n": 0}

        def cast_upto(n):
            while cstate["n"] < min(n, NP_):
                i = cstate["n"]
                cstate["n"] += 1
                P.dma_sem("cast%d" % i)
                P.op("pool", lambda e: e.dma_start(out=wbf_d.ap()[i], in_=wp_d.ap()[i]),
                     writes=[("wbf", i)], dma_sem="cast%d" % i)

        cast_upto(6)
        for tname, t in (() if "memset" in _skip else (("kcT", kcT), ("vcT", vcT), ("hidk", hidk), ("hidv", hidv))):
            P.op("pool", (lambda t: lambda e: e.memset(t[:], 0.0))(t), writes=[tname])
        P.op("pool", lambda e: e.memset(VcA[:], 0.0), writes=["VcA"])
        P.op("pool", lambda e: e.memset(KcTz[:], 0.0), writes=["KcT"])
        for g_ in range(2):
            P.op("pool", lambda e: e.memset(kslT[g_][:], 0.0), writes=[("kslT", k_) for k_ in range(16)])
            P.op("pool", lambda e: e.memset(kwT[g_][:], 0.0), writes=[("kwT", k_) for k_ in range(16)])
        P.op("dve", lambda e: e.memset(VcA[:, :, 64:65], 1.0), reads=["VcA"], writes=["VcA"])
        for g in range(2):
            P.op("dve", (lambda g: lambda e: e.tensor_copy(out=VcA[:, g, 65:97], in_=cbv("ov")))(g),
                 reads=[("const", "cb"), "VcA"], writes=["VcA"])
        P.op("pool", lambda e: e.memset(vslA[:], 1.0), writes=["vslA"])
        P.op("pool", lambda e: e.memset(vwA[:], 1.0), writes=["vwA"])

        ncut = {1: 0, 2: 4, 3: 8, 4: 8, 5: 20}.get(stages, NP_)
        border = PIECES[:ncut]
        nstream = len(border) * nseq * nblk
        wstate = {"use": 0, "iss": 0}
        released = set()
        held = set()
        pending = []

        def try_issue(upto):
            while wstate["iss"] < min(upto, nstream):
                n = wstate["iss"]
                if n - NSLOT >= 0 and (n - NSLOT) not in released:
                    break
                wstate["iss"] += 1
                slot = n % NSLOT
                pi = PIDX[border[n % len(border)]]
                cast_upto(pi + 1 + 5)
                P.op("sp", lambda e: e.dma_start(out=wring[slot][:], in_=wbf_d.ap()[pi]),
                     reads=[("wbf", pi)], writes=[("wr", slot)], dma_sem="w%d" % slot)

        def wrelease(i):
            held.discard(i)
            released.add(i)
            try_issue(wstate["use"] + NSLOT)

        def wload(name, hold=False):
            i = wstate["use"]
            assert border[i % len(border)] == name, (name, border[i % len(border)])
            for q in list(pending):
                if q not in held:
                    released.add(q)
                    pending.remove(q)
            wstate["use"] += 1
            try_issue(i + NSLOT)
            assert wstate["iss"] > i, ("weight ring deadlock", name, i)
            pending.append(i)
            if hold:
                held.add(i)
            wload.last = i
            return wring[i % NSLOT], ("wr", i % NSLOT)

        CONST = [("const", "cf"), ("const", "cb"), ("const", "posf")]
        ident = cbv("ident")

        def transposes(src_fn, n, tb, keys_r):
            for k in range(n):
                P.op("pe", (lambda k: lambda e: e.transpose(out=bankb(tb, 128, k * 128), in_=src_fn(k), identity=ident))(k),
                     reads=keys_r + [("const", "cb")], writes=[PS(tb)])

        def bc(ap2d, dims):
            return bass.AP(ap2d.tensor, ap2d.offset, [list(ap2d.ap[0])] + [list(d) for d in dims])

        def rms_to_hT(src_ap, src_keys, gname, t, tb, hn, junk, ssq, rstd):
            i2 = t % 2
            hn, junk, ssq, rstd = hn[i2], junk[i2], ssq[i2], rstd[i2]
            P.op("act", lambda e: e.activation(out=junk, in_=src_ap, func=AF.Square, accum_out=ssq),
                 reads=src_keys, writes=[("A", "junk"), ("A", "ssq", i2)])
            P.op("dve", lambda e: e.tensor_scalar(out=rstd, in0=ssq, scalar1=1.0 / DM, scalar2=EPS,
                                                  op0=ALU.mult, op1=ALU.add),
                 reads=[("A", "ssq", i2)], writes=[("A", "rstd", i2)])
            P.op("pool", lambda e: e.tensor_tensor(out=rstd, in0=rstd, in1=cfv("mhalf")[:, 0:1], op=ALU.pow),
                 reads=[("A", "rstd", i2), ("const", "cf")], writes=[("A", "rstd", i2)])
            P.op("act", lambda e: e.activation(out=hn, in_=src_ap, func=AF.Copy, scale=rstd),
                 reads=src_keys + [("A", "rstd", i2)], writes=[("A", "hn", i2)])
            transposes(lambda k: hn[:, k * 128:(k + 1) * 128], 8, tb, [("A", "hn", i2)])
            g = cfv(gname)
            P.op("dve", lambda e: e.tensor_tensor(
                out=hT[:, :, t * 128:(t + 1) * 128],
                in0=bankb(tb).rearrange("p (k c) -> p k c", k=8),
                in1=bc(g, [[1, 8], [0, 128]]), op=ALU.mult),
                reads=[PS(tb), ("const", "cf")], writes=[("hT", t)])

        def rope(e_unused, psv, nh, hd, cosv, sinv, outv, tmp1, tmp2, rkeys, wkeys, tkeys):
            h2 = hd // 2
            x3 = psv.rearrange("p (h d) -> p h d", h=nh)
            P.op("dve", lambda e: e.tensor_tensor(out=tmp1.rearrange("p (h d) -> p h d", h=nh), in0=x3,
                                                  in1=bc(cosv, [[0, nh], [1, hd]]), op=ALU.mult),
                 reads=rkeys + ["TABS"], writes=[tkeys[0]])
            t23 = tmp2.rearrange("p (h d) -> p h d", h=nh)
            P.op("dve", lambda e: e.tensor_tensor(out=t23[:, :, 0:h2], in0=x3[:, :, h2:hd],
                                                  in1=bc(sinv[:, 0:h2], [[0, nh], [1, h2]]), op=ALU.mult),
                 reads=rkeys + ["TABS"], writes=[tkeys[1]])
            P.op("dve", lambda e: e.tensor_tensor(out=t23[:, :, h2:hd], in0=x3[:, :, 0:h2],
                                                  in1=bc(sinv[:, h2:hd], [[0, nh], [1, h2]]), op=ALU.mult),
                 reads=rkeys + ["TABS"], writes=[tkeys[1]])
            P.op("dve", lambda e: e.tensor_tensor(out=outv, in0=tmp1, in1=tmp2, op=ALU.add),
                 reads=list(tkeys), writes=wkeys)

        for s in range(nseq if stages > 0 else 0):
            for j in range(nblk):
                P.next_segment()
                row0 = s * SEQ + j * TB
                T0 = j * TB
                a_reset()
                xt = [a_alloc(1024, F32), a_alloc(1024, F32)]
                hn = [a_alloc(1024, BF16), a_alloc(1024, BF16)]
                junk = [a_alloc(1024, BF16)] * 2
                ssq = [a_alloc(1, F32), a_alloc(1, F32)]
                rstd = [a_alloc(1, F32), a_alloc(1, F32)]
                ang = a_alloc(NT * 2 * 96, F32)
                angk = a_alloc(NT * 2 * 96, F32)
                angi = a_alloc(NT * 2 * 96, I32)
                ang4 = ang.rearrange("p (t a f) -> p t a f", t=NT, a=2)
                inv = cfv("inv")
                for t in range(0 if "tabs" in _skip else NT):
                    col = s * 16 + j * NT + t
                    P.op("dve", (lambda t, col: lambda e: e.tensor_scalar(
                        out=ang4[:, t, 0, :], in0=inv, scalar1=posf[:, col:col + 1], scalar2=None, op0=ALU.mult))(t, col),
                        reads=CONST, writes=[("A", "ang")])
                    P.op("dve", (lambda t, col: lambda e: e.tensor_scalar(
                        out=ang4[:, t, 1, :], in0=inv, scalar1=posf[:, col:col + 1], scalar2=math.pi / 2,
                        op0=ALU.mult, op1=ALU.add))(t, col),
                        reads=CONST, writes=[("A", "ang")])
                P.op("dve", lambda e: e.tensor_scalar(out=angk, in0=ang, scalar1=1.0 / TWO_PI, scalar2=None, op0=ALU.mult),
                     reads=[("A", "ang")], writes=[("A", "angk")])
                P.op("dve", lambda e: e.tensor_copy(out=angi, in_=angk), reads=[("A", "angk")], writes=[("A", "angi")])
                P.op("dve", lambda e: e.tensor_copy(out=angk, in_=angi), reads=[("A", "angi")], writes=[("A", "angk")])
                P.op("dve", lambda e: e.scalar_tensor_tensor(out=ang, in0=angk, scalar=-C1, in1=ang, op0=ALU.mult, op1=ALU.add),
                     reads=[("A", "angk"), ("A", "ang")], writes=[("A", "ang")])
                P.op("dve", lambda e: e.scalar_tensor_tensor(out=ang, in0=angk, scalar=-C2, in1=ang, op0=ALU.mult, op1=ALU.add),
                     reads=[("A", "angk"), ("A", "ang")], writes=[("A", "ang")])
                P.op("dve", lambda e: e.tensor_scalar(out=ang, in0=ang, scalar1=3.1415925, scalar2=-3.1415925,
                                                      op0=ALU.min, op1=ALU.max),
                     reads=[("A", "ang")], writes=[("A", "ang")])
                P.op("act", lambda e: e.activation(out=tabs[:].rearrange("p t a f -> p (t a f)"), in_=ang, func=AF.Sin),
                     reads=[("A", "ang")], writes=["tabs0"])
                P.op("dve", lambda e: e.tensor_copy(out=cosR[:, :, 0:64], in_=tabs[:, :, 1, 0:64]), reads=["tabs0"], writes=["TABS"])
                P.op("dve", lambda e: e.tensor_copy(out=cosR[:, :, 64:128], in_=tabs[:, :, 1, 0:64]), reads=["tabs0"], writes=["TABS"])
                P.op("dve", lambda e: e.tensor_scalar(out=sinR[:, :, 0:64], in0=tabs[:, :, 0, 0:64], scalar1=-1.0, scalar2=None, op0=ALU.mult),
                     reads=["tabs0"], writes=["TABS"])
                P.op("dve", lambda e: e.tensor_copy(out=sinR[:, :, 64:128], in_=tabs[:, :, 0, 0:64]), reads=["tabs0"], writes=["TABS"])
                P.op("dve", lambda e: e.tensor_copy(out=cosN[:, :, 0:32], in_=tabs[:, :, 1, 64:96]), reads=["tabs0"], writes=["TABS"])
                P.op("dve", lambda e: e.tensor_copy(out=cosN[:, :, 32:64], in_=tabs[:, :, 1, 64:96]), reads=["tabs0"], writes=["TABS"])
                P.op("dve", lambda e: e.tensor_scalar(out=sinN[:, :, 0:32], in0=tabs[:, :, 0, 64:96], scalar1=-1.0, scalar2=None, op0=ALU.mult),
                     reads=["tabs0"], writes=["TABS"])
                P.op("dve", lambda e: e.tensor_copy(out=sinN[:, :, 32:64], in_=tabs[:, :, 0, 64:96]), reads=["tabs0"], writes=["TABS"])
                if dump and "tabs" in dump:
                    do_dump("tabs", tabs[:].rearrange("p t a f -> p (t a f)"), ["tabs0"], [128, NT * 2 * 96])

                for t in range(0 if "norm" in _skip else NT):
                    xb = xt[t % 2]
                    P.op("sp", (lambda t, xb: lambda e: e.dma_start(out=xb, in_=x_d.ap()[row0 + t * 128: row0 + (t + 1) * 128, :]))(t, xb),
                         writes=[("A", "xt", t % 2)], dma_sem="x%d" % (t % 2))
                    rms_to_hT(xb, [("A", "xt", t % 2)], "g_mix", t, 6 + (t % 2), hn, junk, ssq, rstd)
                do_dump("hT", hT[:].rearrange("p k t -> p (k t)"), [("hT", t) for t in range(NT)], [128, 8 * TB], BF16)
                HT = [("hT", t) for t in range(NT)]
                if stages < 2:
                    continue

                sm = a_alloc(512, BF16)
                tmp1s = [a_alloc(512, F32), a_alloc(512, F32)]
                tmp2s = [a_alloc(512, F32), a_alloc(512, F32)]
                nq_tm = a_alloc(1024, BF16)
                nqT = a_alloc(8 * TB, BF16)
                nqT3 = nqT.rearrange("p (i t) -> p i t", i=8)
                sig = a_alloc(NT * 48, F32)
                sig3 = sig.rearrange("p (t c) -> p t c", t=NT)
                w, wk = wload("S1")
                w3 = w[:].rearrange("p (k c) -> p k c", k=8)
                for t in range(NT):
                    gtile = j * NT + t
                    b = t % 4
                    for k in range(8):
                        P.op("pe", (lambda t, k, b: lambda e: e.matmul(bank(b), lhsT=hT[:, k, t * 128:(t + 1) * 128], rhs=w3[:, k, :],
                                                                       start=(k == 0), stop=(k == 7)))(t, k, b),
                             reads=[("hT", t), wk], writes=[PS(b)])
                    rope(None, bank(b, 384), 6, 64, cosN[:, t, :], sinN[:, t, :], sm[:, 0:384],
                         tmp1s[t % 2][:, 0:384], tmp2s[t % 2][:, 0:384], [PS(b)], [("A", "sm")], [("A", "tmp1", t % 2), ("A", "tmp2", t % 2)])
                    P.op("act", (lambda b: lambda e: e.copy(out=sm[:, 384:512], in_=bank(b, 128, 384)))(b),
                         reads=[PS(b)], writes=[("A", "sm2")])
                    tb = 6 + (t % 2)
                    transposes(lambda k: sm[:, k * 128:(k + 1) * 128], 4, tb, [("A", "sm"), ("A", "sm2")])
                    c0 = T0 + t * 128
                    P.op("dve", (lambda tb, c0: lambda e: e.tensor_copy(out=kcT[:, 16 + c0:16 + c0 + 128], in_=bankb(tb, 128, 0)))(tb, c0),
                         reads=[PS(tb)], writes=["kcT"])
                    for g_ in range(2):
                        rs_ = slice(g_ * 64, (g_ + 1) * 64)
                        P.op("dve", lambda e: e.tensor_copy(out=kslT[g_][rs_, c0:c0 + 128], in_=bankb(tb, 128, 128)[rs_, :]),
                             reads=[PS(tb)], writes=[("kslT", gtile)])
                        P.op("dve", lambda e: e.tensor_copy(out=kwT[g_][rs_, c0:c0 + 128], in_=bankb(tb, 128, 256)[rs_, :]),
                             reads=[PS(tb)], writes=[("kwT", gtile)])
                    P.op("dve", (lambda tb, c0: lambda e: e.tensor_copy(out=vcT[:, 16 + c0:16 + c0 + 128], in_=bankb(tb, 128, 384)))(tb, c0),
                         reads=[PS(tb)], writes=["vcT"])
                w, wk = wload("S2")
                w3b = w[:].rearrange("p (k c) -> p k c", k=8)
                for t in range(NT):
                    gtile = j * NT + t
                    b = t % 4
                    for k in range(8):
                        P.op("pe", (lambda t, k, b, w3b: lambda e: e.matmul(bank(b, 304), lhsT=hT[:, k, t * 128:(t + 1) * 128], rhs=w3b[:, k, 0:304],
                                                                            start=(k == 0), stop=(k == 7)))(t, k, b, w3b),
                             reads=[("hT", t), wk], writes=[PS(b)])
                    P.op("act", (lambda b, gtile: lambda e: e.copy(out=vslA[:, gtile, :, 0:64],
                                                                   in_=bank(b, 128, 0).rearrange("p (g d) -> p g d", g=2)))(b, gtile),
                         reads=[PS(b)], writes=[("vslA", gtile)])
                    P.op("dve", (lambda b, gtile: lambda e: e.tensor_copy(out=vwA[:, gtile, :, 0:64],
                                                                          in_=bank(b, 128, 128).rearrange("p (g d) -> p g d", g=2)))(b, gtile),
                         reads=[PS(b)], writes=[("vwA", gtile)])
                    P.op("act", (lambda b, t: lambda e: e.activation(out=sig3[:, t, :], in_=bank(b, 48, 256), func=AF.Sigmoid))(b, t),
                         reads=[PS(b)], writes=[("A", "sig", t)])
                for qi, qn in enumerate(("Q1", "Q2")):
                    w, wk = wload(qn)
                    w3q = w[:].rearrange("p (k c) -> p k c", k=8)
                    for t in range(NT):
                        b = t % 4
                        for k in range(8):
                            P.op("pe", (lambda t, k, b, w3q: lambda e: e.matmul(bank(b), lhsT=hT[:, k, t * 128:(t + 1) * 128], rhs=w3q[:, k, :],
                                                                                start=(k == 0), stop=(k == 7)))(t, k, b, w3q),
                                 reads=[("hT", t), wk], writes=[PS(b)])
                        rope(None, bank(b), 8, 64, cosN[:, t, :], sinN[:, t, :], nq_tm[:, 0:512],
                             tmp1s[t % 2], tmp2s[t % 2], [PS(b)], [("A", "nq_tm")], [("A", "tmp1", t % 2), ("A", "tmp2", t % 2)])
                        tb = 6 + (t % 2)
                        transposes(lambda k: nq_tm[:, k * 128:(k + 1) * 128], 4, tb, [("A", "nq_tm")])
                        P.op("dve", (lambda tb, qi, t: lambda e: e.tensor_copy(
                            out=nqT3[:, qi * 4:(qi + 1) * 4, t * 128:(t + 1) * 128],
                            in_=bankb(tb, 512).rearrange("p (i c) -> p i c", i=4)))(tb, qi, t),
                            reads=[PS(tb)], writes=[("A", "nqT", t)])
                do_dump("kslT", kslT[0][:], [("kslT", j * NT + t) for t in range(NT)], [128, SEQ], BF16)
                do_dump("nqT", nqT, [("A", "nqT", t) for t in range(NT)], [128, 8 * TB], BF16)
                do_dump("sig", sig, [("A", "sig", t) for t in range(NT)], [128, NT * 48])
                if stages < 3:
                    continue

                kpe = a_alloc(32 * 32, BF16)
                kpe3 = kpe.rearrange("p (l c) -> p l c", l=32)
                gl = [a_alloc(128, F32) for _ in range(3)]
                s0 = 32 * j
                for nm, cache, pen, hid, w2n in (("CK", kcT, "pek", hidk, "ck2"), ("CV", vcT, "pev", hidv, "cv2")):
                    src = cache[:, 16 * s0: 16 * s0 + 1]
                    src = bass.AP(src.tensor, src.offset, [list(src.ap[0]), [1, 32], [16, 32]])
                    pe_ap = cfv(pen)
                    P.op("dve", (lambda src, pe_ap: lambda e: e.tensor_tensor(out=kpe3, in0=src, in1=bc(pe_ap, [[1, 32], [0, 32]]), op=ALU.add))(src, pe_ap),
                         reads=[nm[1] == "K" and "kcT" or "vcT", ("const", "cf")], writes=[("A", "kpe")])
                    wA, wkA = wload(nm + "1", hold=True)
                    iA = wload.last
                    wB, wkB = wload(nm + "2", hold=True)
                    iB = wload.last
                    for g in range(2):
                        first = True
                        for hh in range(2):
                            for l in range(32):
                                wsrc, wkey = (wA, wkA) if l < 16 else (wB, wkB)
                                w1v = wsrc[:].rearrange("p (l j) -> p l j", l=16)
                                P.op("pe", lambda e: e.matmul(
                                    bank(g, 32, hh * 32),
                                    lhsT=w1v[g * 64:(g + 1) * 64, l % 16, hh * 128:(hh + 1) * 128],
                                    rhs=kpe3[g * 64:(g + 1) * 64, l, :],
                                    start=first, stop=(l == 31), skip_group_check=True),
                                    reads=[("A", "kpe"), wkey], writes=[PS(g)])
                                first = False
                    wrelease(iA)
                    wrelease(iB)
                    xh = bass.AP(psum, 0, [[4096, 128], [512, 2], [1, 64]])
                    g3 = [t_.rearrange("p (g c) -> p g c", g=2) for t_ in gl]
                    PH = [PS(0), PS(1)]
                    P.op("act", lambda e: e.activation(out=g3[0], in_=xh, func=AF.Square), reads=PH, writes=[("A", "gl0")])
                    P.op("dve", lambda e: e.tensor_scalar(out=gl[0], in0=gl[0], scalar1=0.044715, scalar2=1.0, op0=ALU.mult, op1=ALU.add),
                         reads=[("A", "gl0")], writes=[("A", "gl0")])
                    P.op("dve", lambda e: e.tensor_tensor(out=g3[1], in0=xh, in1=g3[0], op=ALU.mult), reads=PH + [("A", "gl0")], writes=[("A", "gl1")])
                    P.op("act", lambda e: e.activation(out=gl[2], in_=gl[1], func=AF.Sigmoid, scale=1.5957691216), reads=[("A", "gl1")], writes=[("A", "gl2")])
                    xh4 = bass.AP(psum, 0, [[4096, 128], [512, 2], [32, 2], [1, 32]])
                    P.op("dve", lambda e: e.tensor_tensor(out=hid[:, :, :, s0:s0 + 32], in0=xh4,
                                                          in1=gl[2].rearrange("p (g h c) -> p g h c", g=2, h=2), op=ALU.mult),
                         reads=PH + [("A", "gl2")], writes=[nm])
                ck2 = cbv("ck2").rearrange("p (h d) -> p h d", h=2)
                cv2 = cbv("cv2").rearrange("p (h d) -> p h d", h=2)
                for g in range(2):
                    for hh in range(2):
                        P.op("pe", (lambda g, hh: lambda e: e.matmul(bank(2, 128, g * 128), lhsT=ck2[:, hh, :], rhs=hidk[:, g, hh, :],
                                                                     start=(g == 0 and hh == 0), stop=(hh == 1), skip_group_check=True))(g, hh),
                             reads=["CK", ("const", "cb")], writes=[PS(2)])
                for g in range(2):
                    for hh in range(2):
                        P.op("pe", (lambda g, hh: lambda e: e.matmul(bank(3, 64, g * 64), lhsT=hidv[:, g, hh, :], rhs=cv2[:, hh, :],
                                                                     start=(g == 0 and hh == 0), stop=(hh == 1), skip_group_check=True))(g, hh),
                             reads=["CV", ("const", "cb")], writes=[PS(3)])
                for g_ in range(2):
                    rs_ = slice(g_ * 64, (g_ + 1) * 64)
                    P.op("act", lambda e: e.copy(out=KcTz[rs_, g_, :], in_=bank(2, 128, g_ * 128)[rs_, :]), reads=[PS(2)], writes=["KcT"])
                P.op("dve", lambda e: e.tensor_copy(out=VcA[:, :, 0:64], in_=bank(3, 128).rearrange("p (g d) -> p g d", g=2)),
                     reads=[PS(3), "VcA"], writes=["VcA"])
                do_dump("KcT", KcTz[:].rearrange("p g c -> p (g c)"), ["KcT"], [128, 256], BF16)
                do_dump("VcA", VcA[:].rearrange("p g c -> p (g c)"), ["VcA"], [128, 2 * 97], BF16)
                if stages < 4:
                    continue

                pexp = [a_alloc(1024, BF16) for _ in range(3)]
                onsa = a_alloc(1024, F32)
                onsa4 = onsa.rearrange("p (i g d) -> p i g d", i=8, g=2)
                otmp = a_alloc(512, F32)
                onsab = a_alloc(1024, BF16)
                den = a_alloc(8, F32)
                fac = a_alloc(8, F32)
                impt = a_alloc(256, F32)
                imp = a_alloc(32, F32)
                top8 = a_alloc(8, F32)
                selm = a_alloc(32, BF16)
                selT = a_alloc(128, BF16)
                pcount = {"n": 0, "br": 0}
                triB = cbv("tri")
                oldB = cbv("old")
                maskC3 = cbv("maskC").rearrange("p (g q) -> p g q", g=16)
                VM3 = cbv("VM").rearrange("p (g b) -> p g b", g=16)
                AC3 = cfv("AC").rearrange("p (g b) -> p g b", g=16)
                E3 = cbv("E").rearrange("p (k c) -> p k c", k=16)
                P.op("dve", lambda e: e.memset(selT, 0.0), writes=[("A", "selT")])
                P.op("dve", lambda e: e.memset(selT[32:33, :], 1.0), reads=[("A", "selT")], writes=[("A", "selT")])

                def hb(ap2):
                    return bass.AP(ap2.tensor, ap2.offset, [list(ap2.ap[0]), [0, 4], [1, 128]])

                def stage_a(pr):
                    kind, t, g, gt, kt = pr["kind"], pr["t"], pr["g"], pr["gt"], pr["kt"]
                    if kind == "tk":
                        n = pcount["n"]
                        pcount["n"] += 1
                        tbk = (n % 2) * 2
                        P.op("pe", lambda e: e.transpose(out=bankb(tbk, 128, 0)[0:32, :], in_=selm, identity=ident),
                             reads=[("A", "selm"), ("const", "cb")], writes=[PS(tbk)])
                        P.op("dve", lambda e: e.tensor_copy(out=selT[0:32, :], in_=bankb(tbk, 128, 0)[0:32, :]), reads=[PS(tbk)], writes=[("A", "selT")])
                        return
                    n = pcount["n"]
                    pcount["n"] += 1
                    sb_ = (n % 2) * 2
                    pt = pexp[n % 3]
                    pk = ("A", "pexp", n % 3)
                    rows = slice(g * 64, (g + 1) * 64)
                    biases = []
                    if kind == "cmp":
                        kT_ap, kkeys = KcTz[:, g, :], ["KcT"]
                        biases.append((ident, hb(maskC3[:, gt, :]), [("const", "cb")]))
                    elif kind == "win":
                        kT_ap, kkeys = kwT[g][:, kt * 128:(kt + 1) * 128], [("kwT", kt)]
                        if kt == gt:
                            biases.append((ident, hb(triB), [("const", "cb")]))
                        elif kt == gt - 4:
                            biases.append((ident, hb(oldB), [("const", "cb")]))
                    else:
                        kT_ap, kkeys = kslT[g][:, kt * 128:(kt + 1) * 128], [("kslT", kt)]
                        biases.append((E3[:, kt, :], hb(selT), [("const", "cb"), ("A", "selT")]))
                        if kt == gt:
                            biases.append((ident, hb(triB), [("const", "cb")]))
                    for half in range(2):
                        for bi, (bl, br_, bkeys) in enumerate(biases):
                            P.op("pe", lambda e: e.matmul(bank(sb_ + half), lhsT=bl, rhs=br_, start=(bi == 0), stop=False),
                                 reads=bkeys, writes=[PS(sb_ + half)])
                        P.op("pe", lambda e: e.matmul(
                            bank(sb_ + half), lhsT=kT_ap,
                            rhs=nqT3[:, half * 4:(half + 1) * 4, t * 128:(t + 1) * 128],
                            start=(len(biases) == 0), stop=True),
                            reads=kkeys + [("A", "nqT", t)], writes=[PS(sb_ + half)])
                    P.op("act", lambda e: e.activation(out=pt, in_=psum[:, sb_ * 512:(sb_ + 2) * 512], func=AF.Exp, scale=0.125),
                         reads=[PS(sb_), PS(sb_ + 1)], writes=[pk])
                    pr["pt"], pr["pk"] = pt, pk

                def evac_branch(g, t, br, ncol, first_branch, ob0):
                    o4 = bass.AP(psum, ob0 * 512, [[4096, 128], [512, 2], [ncol, 4], [1, 64]])
                    d4 = bass.AP(psum, ob0 * 512 + 64, [[4096, 128], [512, 2], [ncol, 4]])
                    den3 = den.rearrange("p (a b) -> p a b", a=2)
                    OB = [PS(ob0), PS(ob0 + 1)]
                    P.op("dve", lambda e: e.tensor_scalar(out=den3, in0=d4, scalar1=1e-30, scalar2=None, op0=ALU.max),
                         reads=OB, writes=[("A", "den")])
                    P.op("dve", lambda e: e.reciprocal(out=den, in_=den), reads=[("A", "den")], writes=[("A", "den")])
                    if br == 0:
                        i4 = bass.AP(psum, ob0 * 512 + 65, [[4096, 128], [512, 2], [97, 4], [1, 32]])
                        db = bass.AP(den.tensor, den.offset, [list(den.ap[0]), [4, 2], [1, 4], [0, 32]])
                        gt = j * NT + t
                        P.op("dve", lambda e: e.tensor_tensor(out=impt.rearrange("p (a b c) -> p a b c", a=2, b=4), in0=i4, in1=db, op=ALU.mult),
                             reads=OB + [("A", "den")], writes=[("A", "impt")])
                        P.op("dve", lambda e: e.tensor_reduce(out=imp, in_=impt.rearrange("p (h c) -> p c h", h=8),
                                                              op=ALU.add, axis=mybir.AxisListType.X),
                             reads=[("A", "impt")], writes=[("A", "imp")])
                        P.op("dve", lambda e: e.tensor_tensor(out=imp, in0=imp, in1=VM3[:, gt, :], op=ALU.mult),
                             reads=[("A", "imp"), ("const", "cb")], writes=[("A", "imp")])
                        P.op("dve", lambda e: e.tensor_tensor(out=imp, in0=imp, in1=AC3[:, gt, :], op=ALU.add),
                             reads=[("A", "imp"), ("const", "cf")], writes=[("A", "imp")])
                        P.op("dve", lambda e: e.max(out=top8, in_=imp), reads=[("A", "imp")], writes=[("A", "top8")])
                        P.op("dve", lambda e: e.tensor_scalar(out=selm, in0=imp, scalar1=top8[:, 7:8], scalar2=None, op0=ALU.is_ge),
                             reads=[("A", "imp"), ("A", "top8")], writes=[("A", "selm")])
                    gcol = br * 16 + g * 8
                    P.op("dve", lambda e: e.tensor_tensor(out=fac, in0=den, in1=sig3[:, t, gcol:gcol + 8], op=ALU.mult),
                         reads=[("A", "den"), ("A", "sig", t)], writes=[("A", "fac")])
                    fb = bass.AP(fac.tensor, fac.offset, [list(fac.ap[0]), [4, 2], [1, 4], [0, 64]])
                    dst = onsa4[:, :, g, :].rearrange("p (a b) d -> p a b d", a=2)
                    if first_branch:
                        P.op("dve", lambda e: e.tensor_tensor(out=dst, in0=o4, in1=fb, op=ALU.mult),
                             reads=OB + [("A", "fac")], writes=[("A", "onsa", g)])
                    else:
                        ot = otmp.rearrange("p (a b d) -> p a b d", a=2, b=4)
                        P.op("dve", lambda e: e.tensor_tensor(out=ot, in0=o4, in1=fb, op=ALU.mult),
                             reads=OB + [("A", "fac")], writes=[("A", "otmp")])
                        P.op("dve", lambda e: e.tensor_tensor(out=dst, in0=dst, in1=ot, op=ALU.add),
                             reads=[("A", "otmp"), ("A", "onsa", g)], writes=[("A", "onsa", g)])

                def stage_b(pr):
                    kind, t, g, gt, kt = pr["kind"], pr["t"], pr["g"], pr["gt"], pr["kt"]
                    if kind == "tk":
                        return
                    if kind == "fin":
                        P.op("act", lambda e: e.copy(out=onsab, in_=onsa), reads=[("A", "onsa", 0), ("A", "onsa", 1)], writes=[("A", "onsab")])
                        n = pcount["n"]
                        pcount["n"] += 1
                        tbk = (n % 2) * 2
                        transposes(lambda k: onsab[:, k * 128:(k + 1) * 128], 8, tbk, [("A", "onsab")])
                        P.op("dve", lambda e: e.tensor_copy(out=onsaT[:, :, t * 128:(t + 1) * 128],
                                                            in_=bankb(tbk).rearrange("p (k c) -> p k c", k=8)),
                             reads=[PS(tbk)], writes=[("onsaT", t)])
                        return
                    if pr["first"]:
                        pr["ob0"] = 4 + 2 * (pcount["br"] % 2)
                        pcount["br"] += 1
                        cur["ob0"] = pr["ob0"]
                    ob0 = cur["ob0"]
                    pt, pk = pr["pt"], pr["pk"]
                    if kind == "cmp":
                        v_ap, vkeys, ncol = VcA[:, g, :], ["VcA"], 97
                    elif kind == "win":
                        v_ap, vkeys, ncol = vwA[:, kt, g, :], [("vwA", kt)], 65
                    else:
                        v_ap, vkeys, ncol = vslA[:, kt, g, :], [("vslA", kt)], 65
                    for h in range(8):
                        ob = ob0 + h // 4
                        P.op("pe", lambda e: e.matmul(
                            bank(ob, ncol, (h % 4) * ncol), lhsT=pt[:, h * 128:(h + 1) * 128], rhs=v_ap,
                            start=(pr["first"] and h % 4 == 0), stop=pr["last"], skip_group_check=True),
                            reads=[pk] + vkeys, writes=[PS(ob)])
                    if pr["last"]:
                        evac_branch(g, t, {"cmp": 0, "sel": 1, "win": 2}[kind], ncol, kind == "cmp", ob0)

                cur = {}
                plist = []
                for t in range(NT):
                    gt = j * NT + t
                    for g in range(2):
                        plist.append(dict(kind="cmp", t=t, g=g, gt=gt, kt=None, first=True, last=True))
                        kts = list(range(max(0, gt - 4), gt + 1))
                        for ii, kt in enumerate(kts):
                            plist.append(dict(kind="win", t=t, g=g, gt=gt, kt=kt, first=(ii == 0), last=(ii == len(kts) - 1)))
                        plist.append(dict(kind="tk", t=t, g=g, gt=gt, kt=None))
                        for kt in range(gt + 1):
                            plist.append(dict(kind="sel", t=t, g=g, gt=gt, kt=kt, first=(kt == 0), last=(kt == gt)))
                    plist.append(dict(kind="fin", t=t, g=None, gt=gt, kt=None))
                SKEW = 1
                for ii in range(len(plist) + SKEW):
                    if ii < len(plist) and plist[ii]["kind"] != "fin":
                        stage_a(plist[ii])
                    if ii - SKEW >= 0:
                        stage_b(plist[ii - SKEW])
                do_dump("onsaT", onsaT[:].rearrange("p k t -> p (k t)"), [("onsaT", t) for t in range(NT)], [128, 8 * TB], BF16)
                if stages < 5:
                    continue

                a_reset()
                S3 = [dict(rqk=a_alloc(NT * 256, BF16), rv=a_alloc(NT * 256, BF16)) for _ in range(3)]
                S2 = [dict(qT=a_alloc(TB, BF16), kT=a_alloc(TB, BF16), qxT=a_alloc(TB, BF16), kz=a_alloc(NT * 128, BF16),
                           inT=a_alloc(NT * 128, BF16), rbc=a_alloc(NT * 256, BF16)) for _ in range(2)]
                S2b = [dict(osb=a_alloc(NT * 256, F32), y=a_alloc(NT * 256, BF16), st=a_alloc(32, F32)) for _ in range(2)]
                gsg = [a_alloc(NT * 512, BF16) for _ in range(3)]
                rt1s = [a_alloc(256, F32), a_alloc(256, F32)]
                rt2s = [a_alloc(256, F32), a_alloc(256, F32)]
                sqj = a_alloc(256, BF16)
                decT = cbv("decayT").rearrange("p (h n) -> p h n", h=8)
                xi3 = cbv("xi").rearrange("p (h n) -> p h n", h=8)
                zs = cfv("zs")
                gng = cfv("gn_g")
                log_g = [math.log(1.0 - 2.0 ** (-5.0 - h)) for h in range(8)]
                gch = [math.exp(128.0 * lg) for lg in log_g]

                def ret_s0(h):
                    A3 = S3[h % 3]
                    K3 = lambda n, *x: ("A", n, h % 3) + tuple(x)
                    hp = h // 2
                    if h % 2 == 0:
                        w, wk = wload("B%d" % hp)
                        w3g = w[:].rearrange("p (k c) -> p k c", k=8)
                        gs = gsg[hp % 3].rearrange("p (t c) -> p t c", t=NT)
                        for t in range(NT):
                            b = t % 2
                            for k in range(8):
                                P.op("pe", lambda e: e.matmul(bank(b), lhsT=hT[:, k, t * 128:(t + 1) * 128], rhs=w3g[:, k, :],
                                                              start=(k == 0), stop=(k == 7)),
                                     reads=[("hT", t), wk], writes=[PS(b)])
                            P.op("act", lambda e: e.activation(out=gs[:, t, :], in_=bank(b), func=AF.Silu),
                                 reads=[PS(b)], writes=[("A", "gsg", hp % 3, t)])
                    w, wk = wload("A%d" % h)
                    w3a = w[:].rearrange("p (k c) -> p k c", k=8)
                    rqk3 = A3["rqk"].rearrange("p (t c) -> p t c", t=NT)
                    rv3 = A3["rv"].rearrange("p (t c) -> p t c", t=NT)
                    for t in range(NT):
                        b = t % 2
                        for k in range(8):
                            P.op("pe", lambda e: e.matmul(bank(b), lhsT=hT[:, k, t * 128:(t + 1) * 128], rhs=w3a[:, k, :],
                                                          start=(k == 0), stop=(k == 7)),
                                 reads=[("hT", t), wk], writes=[PS(b)])
                        rope(None, bank(b, 256), 2, 128, cosR[:, t, :], sinR[:, t, :], rqk3[:, t, :], rt1s[t % 2], rt2s[t % 2],
                             [PS(b)], [K3("rqk", t)], [("A", "rt1", t % 2), ("A", "rt2", t % 2)])
                        P.op("act", lambda e: e.copy(out=rv3[:, t, :], in_=bank(b, 256, 256)),
                             reads=[PS(b)], writes=[K3("rv", t)])

                def ret_s1(h):
                    A3 = S3[h % 3]
                    B = S2[h % 2]
                    K3 = lambda n, *x: ("A", n, h % 3) + tuple(x)
                    K = lambda n: ("A", n, h % 2)
                    rqk3 = A3["rqk"].rearrange("p (t c) -> p t c", t=NT)
                    rv3 = A3["rv"].rearrange("p (t c) -> p t c", t=NT)
                    for which in range(2):
                        for t in range(NT):
                            P.op("pe", lambda e: e.transpose(out=bankb(7, 128, (which * NT + t) * 128),
                                                             in_=rqk3[:, t, which * 128:(which + 1) * 128], identity=ident),
                                 reads=[K3("rqk", t), ("const", "cb")], writes=[PS(7)])
                    P.op("act", lambda e: e.copy(out=B["qT"], in_=bankb(7, 512, 0)), reads=[PS(7)], writes=[K("qT")])
                    P.op("dve", lambda e: e.tensor_tensor(out=B["qxT"].rearrange("p (t n) -> p t n", t=NT),
                                                          in0=bankb(7, 512, 0).rearrange("p (t n) -> p t n", t=NT),
                                                          in1=bass.AP(xi3.tensor, xi3[:, h, :].offset, [list(xi3.ap[0]), [0, NT], [1, 128]]),
                                                          op=ALU.mult),
                         reads=[PS(7), ("const", "cb")], writes=[K("qxT")])
                    P.op("act", lambda e: e.copy(out=B["kT"], in_=bankb(7, 512, 512)), reads=[PS(7)], writes=[K("kT")])
                    P.op("dve", lambda e: e.tensor_scalar(out=B["kz"].rearrange("p (t d) -> p t d", t=NT), in0=rqk3[:, :, 128:256],
                                                          scalar1=zs[:, h:h + 1], scalar2=None, op0=ALU.mult),
                         reads=[K3("rqk", t_) for t_ in range(NT)] + [("const", "cf")], writes=[K("kz")])
                    for t in range(NT):
                        P.op("pe", lambda e: e.matmul(bank(2, 128, t * 128), lhsT=B["kT"][:, t * 128:(t + 1) * 128],
                                                      rhs=B["qT"][:, t * 128:(t + 1) * 128], start=(t == 0), stop=True,
                                                      skip_group_check=True),
                             reads=[K("kT"), K("qT")], writes=[PS(2)])
                    for t in range(NT):
                        rbk = 3 + t // 2
                        P.op("pe", lambda e: e.matmul(bank(rbk, 256, (t % 2) * 256), lhsT=B["kz"][:, t * 128:(t + 1) * 128],
                                                      rhs=rv3[:, t, :], start=(t % 2 == 0), stop=True, skip_group_check=True),
                             reads=[K("kz"), K3("rv", t)], writes=[PS(rbk)])
                    P.op("dve", lambda e: e.tensor_tensor(out=B["inT"].rearrange("p (t n) -> p t n", t=NT),
                                                          in0=bank(2).rearrange("p (t n) -> p t n", t=NT),
                                                          in1=bass.AP(decT.tensor, decT[:, h, :].offset, [list(decT.ap[0]), [0, NT], [1, 128]]),
                                                          op=ALU.mult),
                         reads=[PS(2), ("const", "cb")], writes=[K("inT")])
                    rbc3 = B["rbc"].rearrange("p (t e) -> p t e", t=NT)
                    for t in range(NT):
                        rbk = 3 + t // 2
                        if t == 0:
                            P.op("dve", lambda e: e.tensor_copy(out=rbc3[:, 0, :], in_=Rst[:, h, :]), reads=[("R", h)], writes=[K("rbc")])
                        P.op("dve", lambda e: e.scalar_tensor_tensor(out=Rst[:, h, :], in0=Rst[:, h, :], scalar=gch[h],
                                                                     in1=bank(rbk, 256, (t % 2) * 256), op0=ALU.mult, op1=ALU.add),
                             reads=[("R", h), PS(rbk), K("rbc")], writes=[("R", h)])
                        if t < NT - 1:
                            P.op("dve", lambda e: e.tensor_copy(out=rbc3[:, t + 1, :], in_=Rst[:, h, :]), reads=[("R", h)], writes=[K("rbc")])

                def ret_s2(h):
                    A3 = S3[h % 3]
                    B = S2[h % 2]
                    C = S2b[h % 2]
                    K3 = lambda n, *x: ("A", n, h % 3) + tuple(x)
                    K = lambda n: ("A", n, h % 2)
                    hp = h // 2
                    rv3 = A3["rv"].rearrange("p (t c) -> p t c", t=NT)
                    rbc3 = B["rbc"].rearrange("p (t e) -> p t e", t=NT)
                    osb3 = C["osb"].rearrange("p (t e) -> p t e", t=NT)
                    y3 = C["y"].rearrange("p (t e) -> p t e", t=NT)
                    gs = gsg[hp % 3].rearrange("p (t c) -> p t c", t=NT)
                    stt = C["st"]
                    for t in range(NT):
                        ob = 5 + t // 2
                        oo = (t % 2) * 256
                        P.op("pe", lambda e: e.matmul(bank(ob, 256, oo), lhsT=B["inT"][:, t * 128:(t + 1) * 128], rhs=rv3[:, t, :],
                                                      start=(t % 2 == 0), stop=False, skip_group_check=True),
                             reads=[K("inT"), K3("rv", t)], writes=[PS(ob)])
                        P.op("pe", lambda e: e.matmul(bank(ob, 256, oo), lhsT=B["qxT"][:, t * 128:(t + 1) * 128], rhs=rbc3[:, t, :],
                                                      start=False, stop=True, skip_group_check=True),
                             reads=[K("qxT"), K("rbc")], writes=[PS(ob)])
                    for t in range(NT):
                        ob = 5 + t // 2
                        oo = (t % 2) * 256
                        P.op("act", lambda e: e.activation(out=osb3[:, t, :], in_=bank(ob, 256, oo), func=AF.Copy,
                                                           accum_out=stt[:, t:t + 1]),
                             reads=[PS(ob)], writes=[K("osb"), K("st")])
                        P.op("act", lambda e: e.activation(out=sqj, in_=bank(ob, 256, oo), func=AF.Square,
                                                           accum_out=stt[:, 4 + t:5 + t]),
                             reads=[PS(ob)], writes=[("A", "sqj"), K("st")])
                    mean = stt[:, 8:12]
                    var = stt[:, 12:16]
                    rs = stt[:, 16:20]
                    nb = stt[:, 20:24]
                    mneg = stt[:, 24:28]
                    KS = [K("st")]
                    P.op("pool", lambda e: e.tensor_scalar(out=mean, in0=stt[:, 0:4], scalar1=1.0 / 256, scalar2=None, op0=ALU.mult), reads=KS, writes=KS)
                    P.op("pool", lambda e: e.tensor_scalar(out=mneg, in0=stt[:, 0:4], scalar1=-1.0 / 256, scalar2=None, op0=ALU.mult), reads=KS, writes=KS)
                    P.op("pool", lambda e: e.tensor_tensor(out=var, in0=mean, in1=mean, op=ALU.mult), reads=KS, writes=KS)
                    P.op("pool", lambda e: e.tensor_scalar(out=rs, in0=stt[:, 4:8], scalar1=1.0 / 256, scalar2=EPS, op0=ALU.mult, op1=ALU.add), reads=KS, writes=KS)
                    P.op("pool", lambda e: e.tensor_tensor(out=var, in0=rs, in1=var, op=ALU.subtract), reads=KS, writes=KS)
                    P.op("pool", lambda e: e.tensor_tensor(out=rs, in0=var, in1=cfv("mhalf")[:, 0:4], op=ALU.pow),
                         reads=KS + [("const", "cf")], writes=KS)
                    P.op("pool", lambda e: e.tensor_tensor(out=nb, in0=mneg, in1=rs, op=ALU.mult), reads=KS, writes=KS)

                def ret_s2b(h):
                    C = S2b[h % 2]
                    K = lambda n: ("A", n, h % 2)
                    hp = h // 2
                    osb3 = C["osb"].rearrange("p (t e) -> p t e", t=NT)
                    y3 = C["y"].rearrange("p (t e) -> p t e", t=NT)
                    gs = gsg[hp % 3].rearrange("p (t c) -> p t c", t=NT)
                    stt = C["st"]
                    rs = stt[:, 16:20]
                    nb = stt[:, 20:24]
                    for t in range(NT):
                        P.op("dve", lambda e: e.tensor_scalar(out=y3[:, t, :], in0=osb3[:, t, :], scalar1=rs[:, t:t + 1], scalar2=nb[:, t:t + 1],
                                                              op0=ALU.mult, op1=ALU.add),
                             reads=[K("osb"), K("st")], writes=[K("y")])
                    P.op("dve", lambda e: e.tensor_tensor(out=y3, in0=y3, in1=gs[:, :, (h % 2) * 256:(h % 2 + 1) * 256], op=ALU.mult),
                         reads=[K("y")] + [("A", "gsg", hp % 3, t) for t in range(NT)], writes=[K("y")])

                def ret_s3(h):
                    C = S2b[h % 2]
                    K = lambda n: ("A", n, h % 2)
                    y3 = C["y"].rearrange("p (t e) -> p t e", t=NT)
                    for kc in range(2):
                        for t in range(NT):
                            P.op("pe", lambda e: e.transpose(out=bankb(7, 128, (kc * NT + t) * 128),
                                                             in_=y3[:, t, kc * 128:(kc + 1) * 128], identity=ident),
                                 reads=[K("y"), ("const", "cb")], writes=[PS(7)])
                    P.op("dve", lambda e: e.tensor_tensor(out=oretT[:, 2 * h:2 * h + 2, :],
                                                          in0=bankb(7).rearrange("p (k c) -> p k c", k=2),
                                                          in1=bass.AP(gng.tensor, gng[:, 2 * h:2 * h + 2].offset, [list(gng.ap[0]), [1, 2], [0, TB]]),
                                                          op=ALU.mult),
                         reads=[PS(7), ("const", "cf")], writes=[("oretT", h)])

                if j == 0:
                    P.op("pool", lambda e: e.memset(Rst[:], 0.0), reads=[("R", h) for h in range(8)], writes=[("R", h) for h in range(8)])
                for it in range(8 + 4):
                    if it < 8:
                        ret_s0(it)
                    if 0 <= it - 1 < 8:
                        ret_s1(it - 1)
                    if 0 <= it - 2 < 8:
                        ret_s2(it - 2)
                    if 0 <= it - 3 < 8:
                        ret_s2b(it - 3)
                    if 0 <= it - 4 < 8:
                        ret_s3(it - 4)
                do_dump("oretT", oretT[:].rearrange("p k t -> p (k t)"), [("oretT", h) for h in range(8)], [128, 16 * TB], BF16)
                if stages < 6:
                    continue

                a_reset()
                U = a_alloc(16 * TB, BF16)
                U3 = U.rearrange("p (k t) -> p k t", k=16)
                mixT = a_alloc(8 * TB, BF16)
                mix3 = mixT.rearrange("p (k t) -> p k t", k=8)
                xres = a_alloc(NT * 1024, F32)
                xres3 = xres.rearrange("p (t c) -> p t c", t=NT)
                pT = a_alloc(2 * TB, BF16)
                pT3 = pT.rearrange("p (k t) -> p k t", k=2)
                pld = [a_alloc(256, F32), a_alloc(256, F32)]
                pbf = a_alloc(256, BF16)
                mt1s = [a_alloc(512, F32), a_alloc(512, F32)]
                mt2s = [a_alloc(512, F32), a_alloc(512, F32)]
                hn = [a_alloc(1024, BF16), a_alloc(1024, BF16)]
                junk = [a_alloc(1024, BF16)] * 2
                ssq = [a_alloc(1, F32), a_alloc(1, F32)]
                rstd = [a_alloc(1, F32), a_alloc(1, F32)]
                sgAs = [a_alloc(512, F32), a_alloc(512, F32)]
                for t in range(NT):
                    P.op("sp", (lambda t: lambda e: e.dma_start(out=xres3[:, t, :], in_=x_d.ap()[row0 + t * 128: row0 + (t + 1) * 128, :]))(t),
                         writes=[("A", "xres", t)], dma_sem="x%d" % (t % 2))
                for i in range(4):
                    w, wk = wload("MG%d" % i)
                    w3m = w[:].rearrange("p (k c) -> p k c", k=8)
                    for n in range(4):
                        b = n % 4
                        for k in range(8):
                            P.op("pe", (lambda n, k, b, w3m: lambda e: e.matmul(bank(b), lhsT=w3m[:, k, n * 128:(n + 1) * 128], rhs=hT[:, k, :],
                                                                                start=(k == 0), stop=(k == 7)))(n, k, b, w3m),
                                 reads=HT + [wk], writes=[PS(b)])
                        P.op("act", (lambda i, n, b: lambda e: e.activation(out=U3[:, i * 4 + n, :], in_=bank(b), func=AF.Sigmoid))(i, n, b),
                             reads=[PS(b)], writes=[("A", "U", i * 4 + n)])
                wno = None
                for i in range(4):
                    if i % 2 == 0:
                        wno, wnok = wload("NO%d" % (i // 2), hold=True)
                        iNO = wload.last
                        wno3 = wno[:].rearrange("p (k c) -> p k c", k=8)
                    wro, wrok = wload("RO%d" % i)
                    wro3 = wro[:].rearrange("p (k c) -> p k c", k=16)
                    for n2 in range(2):
                        n = 2 * i + n2
                        br_, bn_ = 4, 5
                        for k in range(16):
                            P.op("pe", lambda e: e.matmul(bank(4 + 2 * (n % 2)), lhsT=wro3[:, k, n2 * 128:(n2 + 1) * 128], rhs=oretT[:, k, :],
                                                          start=(k == 0), stop=(k == 15)),
                                 reads=[("oretT", k // 2), wrok], writes=[PS(4 + 2 * (n % 2))])
                        cno = (i % 2) * 256 + n2 * 128
                        for k in range(8):
                            P.op("pe", lambda e: e.matmul(bank(5 + 2 * (n % 2)), lhsT=wno3[:, k, cno:cno + 128], rhs=onsaT[:, k, :],
                                                          start=(k == 0), stop=(k == 7)),
                                 reads=[("onsaT", t) for t in range(NT)] + [wnok], writes=[PS(5 + 2 * (n % 2))])
                        bR, bN = 4 + 2 * (n % 2), 5 + 2 * (n % 2)
                        P.op("dve", lambda e: e.tensor_tensor(out=mt1s[n % 2], in0=bank(bR), in1=U3[:, n, :], op=ALU.mult),
                             reads=[PS(bR), ("A", "U", n)], writes=[("A", "mt1", n % 2)])
                        P.op("dve", lambda e: e.tensor_tensor(out=mt2s[n % 2], in0=bank(bN), in1=U3[:, 8 + n, :], op=ALU.mult),
                             reads=[PS(bN), ("A", "U", 8 + n)], writes=[("A", "mt2", n % 2)])
                        P.op("dve", lambda e: e.tensor_tensor(out=mix3[:, n, :], in0=mt1s[n % 2], in1=mt2s[n % 2], op=ALU.add),
                             reads=[("A", "mt1", n % 2), ("A", "mt2", n % 2)], writes=[("A", "mix", n)])
                    if i % 2 == 1:
                        wrelease(iNO)
                MIX = [("A", "mix", n) for n in range(8)]
                for ch in range(2):
                    w, wk = wload("WO%d" % ch)
                    w3o = w[:].rearrange("p (k c) -> p k c", k=8)
                    for t in range(NT):
                        b = t % 4
                        for k in range(8):
                            P.op("pe", (lambda t, k, b, w3o: lambda e: e.matmul(bank(b), lhsT=mix3[:, k, t * 128:(t + 1) * 128], rhs=w3o[:, k, :],
                                                                                start=(k == 0), stop=(k == 7)))(t, k, b, w3o),
                                 reads=MIX + [wk], writes=[PS(b)])
                        P.op("dve", (lambda t, b, ch: lambda e: e.tensor_tensor(out=xres3[:, t, ch * 512:(ch + 1) * 512], in0=bank(b),
                                                                                in1=xres3[:, t, ch * 512:(ch + 1) * 512], op=ALU.add))(t, b, ch),
                             reads=[PS(b), ("A", "xres", t)], writes=[("A", "xres", t)])
                do_dump("x1", xres, [("A", "xres", t) for t in range(NT)], [128, NT * 1024])
                for t in range(NT):
                    rms_to_hT(xres3[:, t, :], [("A", "xres", t)], "g_mlp", t, 6 + (t % 2), hn, junk, ssq, rstd)
                U4 = U.rearrange("p (a k t) -> p a k t", a=2, k=8)
                for qd in range(4):
                    ub = qd % 2
                    for half in range(2):
                        w, wk = wload("UP%d" % (2 * qd + half))
                        w3u = w[:].rearrange("p (k c) -> p k c", k=8)
                        for n in range(4):
                            b = n % 4
                            for k in range(8):
                                P.op("pe", (lambda n, k, b, w3u: lambda e: e.matmul(bank(b), lhsT=w3u[:, k, n * 128:(n + 1) * 128], rhs=hT[:, k, :],
                                                                                    start=(k == 0), stop=(k == 7)))(n, k, b, w3u),
                                     reads=HT + [wk], writes=[PS(b)])
                            ui = half * 4 + n
                            P.op("act", lambda e: e.activation(out=sgAs[n % 2], in_=bank(b), func=AF.Relu),
                                 reads=[PS(b)], writes=[("A", "sgA", n % 2)])
                            P.op("dve", lambda e: e.tensor_tensor(out=U4[:, ub, ui, :], in0=sgAs[n % 2], in1=sgAs[n % 2], op=ALU.mult),
                                 reads=[("A", "sgA", n % 2)], writes=[("A", "U", ub * 8 + ui)])
                    for ch in range(2):
                        w, wk = wload("DN%d" % (2 * qd + ch))
                        w3d = w[:].rearrange("p (k c) -> p k c", k=8)
                        for t in range(NT):
                            b = 4 + t % 2
                            for k in range(8):
                                P.op("pe", (lambda t, k, b, w3d, ub: lambda e: e.matmul(bank(b), lhsT=U4[:, ub, k, t * 128:(t + 1) * 128], rhs=w3d[:, k, :],
                                                                                        start=(k == 0), stop=(k == 7)))(t, k, b, w3d, ub),
                                     reads=[("A", "U", ub * 8 + k), wk], writes=[PS(b)])
                            P.op("dve", (lambda t, b, ch: lambda e: e.tensor_tensor(out=xres3[:, t, ch * 512:(ch + 1) * 512], in0=bank(b),
                                                                                    in1=xres3[:, t, ch * 512:(ch + 1) * 512], op=ALU.add))(t, b, ch),
                                 reads=[PS(b), ("A", "xres", t)], writes=[("A", "xres", t)])
                do_dump("x2", xres, [("A", "xres", t) for t in range(NT)], [128, NT * 1024])
                for t in range(NT):
                    pb_ = pld[t % 2]
                    P.op("sp", (lambda t, pb_: lambda e: e.dma_start(out=pb_, in_=p_d.ap()[row0 + t * 128: row0 + (t + 1) * 128, :]))(t, pb_),
                         writes=[("A", "pld", t % 2)], dma_sem="p%d" % (t % 2))
                    P.op("act", (lambda pb_: lambda e: e.copy(out=pbf, in_=pb_))(pb_), reads=[("A", "pld", t % 2)], writes=[("A", "pbf")])
                    tb = 6 + (t % 2)
                    transposes(lambda k: pbf[:, k * 128:(k + 1) * 128], 2, tb, [("A", "pbf")])
                    P.op("dve", (lambda t, tb: lambda e: e.tensor_copy(out=pT3[:, :, t * 128:(t + 1) * 128],
                                                                       in_=bankb(tb, 256).rearrange("p (k c) -> p k c", k=2)))(t, tb),
                         reads=[PS(tb)], writes=[("A", "pT", t)])
                    rms_to_hT(xres3[:, t, :], [("A", "xres", t)], "g_ple", t, 6 + (t % 2), hn, junk, ssq, rstd)
                wpp, wppk = wload("PP", hold=True)
                iPP = wload.last
                wpp3 = wpp[:, 0:2048].rearrange("p (k c) -> p k c", k=2)
                for ch in range(2):
                    w, wk = wload("PG%d" % ch)
                    w3g = w[:].rearrange("p (k c) -> p k c", k=8)
                    for t in range(NT):
                        ba = t % 2
                        bb = 2 + t % 2
                        for k in range(8):
                            P.op("pe", (lambda t, k, ba, w3g: lambda e: e.matmul(bank(ba), lhsT=hT[:, k, t * 128:(t + 1) * 128], rhs=w3g[:, k, :],
                                                                                 start=(k == 0), stop=(k == 7)))(t, k, ba, w3g),
                                 reads=[("hT", t), wk], writes=[PS(ba)])
                        for k in range(2):
                            P.op("pe", (lambda t, k, bb, ch: lambda e: e.matmul(bank(bb), lhsT=pT3[:, k, t * 128:(t + 1) * 128],
                                                                                rhs=wpp3[:, k, ch * 512:(ch + 1) * 512],
                                                                                start=(k == 0), stop=(k == 1)))(t, k, bb, ch),
                                 reads=[("A", "pT", t), wppk], writes=[PS(bb)])
                        P.op("act", lambda e: e.activation(out=sgAs[t % 2], in_=bank(ba), func=AF.Sigmoid),
                             reads=[PS(ba)], writes=[("A", "sgA", t % 2)])
                        P.op("dve", lambda e: e.tensor_tensor(out=mt1s[t % 2], in0=bank(bb), in1=sgAs[t % 2], op=ALU.mult),
                             reads=[PS(bb), ("A", "sgA", t % 2)], writes=[("A", "mt1", t % 2)])
                        P.op("dve", lambda e: e.tensor_tensor(out=xres3[:, t, ch * 512:(ch + 1) * 512], in0=mt1s[t % 2],
                                                              in1=xres3[:, t, ch * 512:(ch + 1) * 512], op=ALU.add),
                             reads=[("A", "mt1", t % 2), ("A", "xres", t)], writes=[("A", "xres", t)])
                wrelease(iPP)
                gfin = cfv("g_final")
                for t in range(NT):
                    i2 = t % 2
                    P.op("act", lambda e: e.activation(out=junk[i2], in_=xres3[:, t, :], func=AF.Square, accum_out=ssq[i2]),
                         reads=[("A", "xres", t)], writes=[("A", "junk"), ("A", "ssq", i2)])
                    P.op("dve", lambda e: e.tensor_scalar(out=rstd[i2], in0=ssq[i2], scalar1=1.0 / DM, scalar2=EPS, op0=ALU.mult, op1=ALU.add),
                         reads=[("A", "ssq", i2)], writes=[("A", "rstd", i2)])
                    P.op("pool", lambda e: e.tensor_tensor(out=rstd[i2], in0=rstd[i2], in1=cfv("mhalf")[:, 0:1], op=ALU.pow),
                         reads=[("A", "rstd", i2), ("const", "cf")], writes=[("A", "rstd", i2)])
                    P.op("dve", lambda e: e.scalar_tensor_tensor(out=xres3[:, t, :], in0=xres3[:, t, :], scalar=rstd[i2], in1=gfin,
                                                                 op0=ALU.mult, op1=ALU.mult),
                         reads=[("A", "xres", t), ("A", "rstd", i2), ("const", "cf")], writes=[("A", "xres", t)])
                    P.op("sp", lambda e: e.dma_start(out=out_d.ap()[row0 + t * 128: row0 + (t + 1) * 128, :], in_=xres3[:, t, :]),
                         reads=[("A", "xres", t)], dma_sem="out")
        info = P.emit()
    return nc, info, dumps


_CACHE = {}


def prepare_inputs(inputs, ncores=NCORES, nseq=None):
    x = np.asarray(inputs["x"], np.float32)
    p = np.asarray(inputs["p"], np.float32)[0]
    pos = np.asarray(inputs["positions"], np.int32)
    B = x.shape[0]
    nseq = B // ncores if nseq is None else nseq
    cf, cb = host_consts(inputs)
    wpack = host_pack(inputs)
    in_maps = []
    for c in range(ncores):
        xs = np.ascontiguousarray(x[c * nseq:(c + 1) * nseq].reshape(nseq * SEQ, DM))
        ps_ = np.ascontiguousarray(p[c * nseq:(c + 1) * nseq].reshape(nseq * SEQ, 256))
        pl = pos[c * nseq:(c + 1) * nseq].reshape(nseq, 16, 128).transpose(2, 0, 1).reshape(128, nseq * 16)
        in_maps.append({"x": xs, "p": ps_, "posl": np.ascontiguousarray(pl), "wpack": wpack, "cf": cf, "cb": cb})
    return in_maps, nseq


def kernel(**inputs):
    in_maps, nseq = prepare_inputs(inputs)
    key = ("full", nseq)
    if key not in _CACHE:
        _CACHE[key] = build_program(nseq=nseq)[0]
    nc = _CACHE[key]
    res = run_bass_kernel_spmd(nc, in_maps, core_ids=list(range(NCORES)))
    outs = [r["out"].reshape(nseq, SEQ, DM) for r in res.results]
    return np.concatenate(outs, axis=0).astype(np.float32)
```
